# Optimizing a Trainium2 kernel written in Bass

```python
import jax, jax.numpy as jnp
from jax import lax
import numpy as np

D_MODEL = 1024
BATCH = 8
SEQ = 4096
DEPTH = 2

QBLOCK = 128
WINDOW = 128
EPS = 1e-6
ROPE_THETA = 10000.0
MLA_HEADS = D_MODEL // 128
MLA_Q_RANK = D_MODEL // 4
MLA_KV_RANK = D_MODEL // 8
MLA_NOPE = 64
MLA_ROPE = 32
MLA_V = 64
SWA_HEADS = D_MODEL // 128
SWA_KV_HEADS = SWA_HEADS // 4
SWA_HEAD_DIM = 64
FOX_HEADS = D_MODEL // 64
FOX_HEAD_DIM = 64
FOX_WIDTH = FOX_HEADS * FOX_HEAD_DIM
FORGET_BIAS_INIT = 2.0

EVEN_WIDTH = MLA_HEADS * MLA_V + SWA_HEADS * SWA_HEAD_DIM
EVEN_SPLITS = (MLA_Q_RANK, MLA_KV_RANK, MLA_ROPE, SWA_HEADS * SWA_HEAD_DIM,
               SWA_KV_HEADS * SWA_HEAD_DIM, SWA_KV_HEADS * SWA_HEAD_DIM, EVEN_WIDTH)
ODD_SPLITS = (FOX_WIDTH, FOX_WIDTH, FOX_WIDTH, FOX_HEADS, FOX_WIDTH)
N_EVEN = (DEPTH + 1) // 2
N_ODD = DEPTH // 2

kernel_name = "hybrid_mla_swa_fox_gated"


def rms_norm(x, g):
    xf = x.astype(jnp.float32)
    y = xf * lax.rsqrt(jnp.mean(xf * xf, axis=-1, keepdims=True) + EPS)
    return (y * g.astype(jnp.float32)).astype(x.dtype)


def split_cols(z, sizes):
    idx = [int(v) for v in np.cumsum(sizes)[:-1]]
    return jnp.split(z, idx, axis=-1)


def rope_angles(positions, dim):
    inv_freq = 1.0 / (ROPE_THETA ** (jnp.arange(0, dim, 2, dtype=jnp.float32) / dim))
    return positions.astype(jnp.float32)[..., None] * inv_freq


def apply_rope(x, ang):
    cos, sin = jnp.cos(ang), jnp.sin(ang)
    xf = x.astype(jnp.float32)
    x1, x2 = jnp.split(xf, 2, axis=-1)
    return jnp.concatenate([x1 * cos - x2 * sin, x2 * cos + x1 * sin], axis=-1).astype(x.dtype)


def alibi_slopes(n):
    return 2.0 ** (-8.0 * (jnp.arange(n, dtype=jnp.float32) + 1.0) / n)


def causal_block_attention(q, k, v, log_cum=None):
    B, S, H, Dk = q.shape
    Dv = v.shape[-1]
    nb = S // QBLOCK
    scale = Dk ** -0.5
    kpos = jnp.arange(S)
    lc_t = None if log_cum is None else jnp.transpose(log_cum, (0, 2, 1))

    def one_block(i):
        start = i * QBLOCK
        qi = lax.dynamic_slice_in_dim(q, start, QBLOCK, axis=1)
        s = jnp.einsum('bqhd,bkhd->bhqk', qi, k, preferred_element_type=jnp.float32) * scale
        if lc_t is not None:
            ci = lax.dynamic_slice_in_dim(lc_t, start, QBLOCK, axis=2)
            s = s + ci[..., :, None] - lc_t[..., None, :]
        qpos = start + jnp.arange(QBLOCK)
        mask = kpos[None, :] <= qpos[:, None]
        s = jnp.where(mask, s, -jnp.inf)
        p = jax.nn.softmax(s, axis=-1)
        return jnp.einsum('bhqk,bkhd->bqhd', p.astype(v.dtype), v)

    out = lax.map(one_block, jnp.arange(nb))
    return jnp.transpose(out, (1, 0, 2, 3, 4)).reshape(B, S, H, Dv)


def sliding_window_sink_attention(q, k, v, sinks, slopes):
    B, S, H, D = q.shape
    KV = k.shape[2]
    G = H // KV
    W = WINDOW
    nb = S // W
    qb = q.reshape(B, nb, W, KV, G, D)
    pad = ((0, 0), (W, 0), (0, 0), (0, 0))
    kp = jnp.pad(k, pad).reshape(B, nb + 1, W, KV, D)
    vp = jnp.pad(v, pad).reshape(B, nb + 1, W, KV, D)
    kb = jnp.concatenate([kp[:, :-1], kp[:, 1:]], axis=2)
    vb = jnp.concatenate([vp[:, :-1], vp[:, 1:]], axis=2)
    s = jnp.einsum('bnqkgd,bnckd->bnkgqc', qb, kb, preferred_element_type=jnp.float32) * (D ** -0.5)
    a = jnp.arange(W)[:, None]
    c = jnp.arange(2 * W)[None, :]
    dist = (W + a - c).astype(jnp.float32)
    blk = jnp.arange(nb)[:, None, None]
    valid = (dist >= 0) & (dist < W) & (blk * W + c[None] - W >= 0)
    s = s - slopes.reshape(KV, G)[:, :, None, None] * dist
    s = jnp.where(valid[None, :, None, None], s, -jnp.inf)
    sink = sinks.astype(jnp.float32).reshape(KV, G)[:, :, None, None]
    m = jnp.maximum(jnp.max(s, axis=-1, keepdims=True), sink)
    p = jnp.exp(s - m)
    p = p / (jnp.sum(p, axis=-1, keepdims=True) + jnp.exp(sink - m))
    o = jnp.einsum('bnkgqc,bnckd->bnqkgd', p.astype(v.dtype), vb)
    return o.reshape(B, S, H, D)


def mla_swa_layer(x, positions, g_in, w_in, g_q_a, w_q_up, g_kv_a, w_kv_up, sinks, w_out):
    B, S, _ = x.shape
    h = rms_norm(x, g_in)
    z = h @ w_in
    cq, ckv, kpe, q_s, k_s, v_s, gate = split_cols(z, EVEN_SPLITS)
    q = (rms_norm(cq, g_q_a) @ w_q_up).reshape(B, S, MLA_HEADS, MLA_NOPE + MLA_ROPE)
    q_nope, q_pe = q[..., :MLA_NOPE], q[..., MLA_NOPE:]
    kv = (rms_norm(ckv, g_kv_a) @ w_kv_up).reshape(B, S, MLA_HEADS, MLA_NOPE + MLA_V)
    k_nope, v_m = kv[..., :MLA_NOPE], kv[..., MLA_NOPE:]
    ang = rope_angles(positions, MLA_ROPE)
    q_pe = apply_rope(q_pe, ang[:, :, None, :])
    k_pe = apply_rope(kpe, ang)[:, :, None, :]
    qm = jnp.concatenate([q_nope, q_pe], axis=-1)
    km = jnp.concatenate([k_nope, jnp.broadcast_to(k_pe, (B, S, MLA_HEADS, MLA_ROPE))], axis=-1)
    o_mla = causal_block_attention(qm, km, v_m)
    o_swa = sliding_window_sink_attention(
        q_s.reshape(B, S, SWA_HEADS, SWA_HEAD_DIM),
        k_s.reshape(B, S, SWA_KV_HEADS, SWA_HEAD_DIM),
        v_s.reshape(B, S, SWA_KV_HEADS, SWA_HEAD_DIM),
        sinks, alibi_slopes(SWA_HEADS))
    o = jnp.concatenate([o_mla.reshape(B, S, -1), o_swa.reshape(B, S, -1)], axis=-1)
    return x + (o * jax.nn.silu(gate)) @ w_out


def fox_layer(x, g_in, w_in, b_f, w_out):
    B, S, _ = x.shape
    h = rms_norm(x, g_in)
    z = h @ w_in
    q, k, v, f_logit, gate = split_cols(z, ODD_SPLITS)
    log_f = jax.nn.log_sigmoid(f_logit.astype(jnp.float32) + b_f.astype(jnp.float32))
    log_cum = jnp.cumsum(log_f, axis=1)
    o = causal_block_attention(q.reshape(B, S, FOX_HEADS, FOX_HEAD_DIM),
                               k.reshape(B, S, FOX_HEADS, FOX_HEAD_DIM),
                               v.reshape(B, S, FOX_HEADS, FOX_HEAD_DIM), log_cum=log_cum)
    return x + (o.reshape(B, S, FOX_WIDTH) * jax.nn.silu(gate)) @ w_out


def setup_inputs(seed: int = 0) -> dict:
    key = jax.random.key(seed)
    ks = jax.random.split(key, 16)
    f32 = jnp.float32

    def w(k, shape, fan_in):
        return jax.random.normal(k, shape, f32) * (fan_in ** -0.5)

    def gain(k, shape):
        return 1.0 + 0.05 * jax.random.normal(k, shape, f32)

    x = jax.random.normal(ks[0], (BATCH, SEQ, D_MODEL), f32)
    positions = jnp.broadcast_to(jnp.arange(SEQ, dtype=jnp.int32), (BATCH, SEQ))
    return {
        "x": x,
        "positions": positions,
        "e_g_in": gain(ks[1], (N_EVEN, D_MODEL)),
        "e_w_in": w(ks[2], (N_EVEN, D_MODEL, sum(EVEN_SPLITS)), D_MODEL),
        "e_g_q_a": gain(ks[3], (N_EVEN, MLA_Q_RANK)),
        "e_w_q_up": w(ks[4], (N_EVEN, MLA_Q_RANK, MLA_HEADS * (MLA_NOPE + MLA_ROPE)), MLA_Q_RANK),
        "e_g_kv_a": gain(ks[5], (N_EVEN, MLA_KV_RANK)),
        "e_w_kv_up": w(ks[6], (N_EVEN, MLA_KV_RANK, MLA_HEADS * (MLA_NOPE + MLA_V)), MLA_KV_RANK),
        "e_sinks": jax.random.normal(ks[7], (N_EVEN, SWA_HEADS), f32),
        "e_w_out": w(ks[8], (N_EVEN, EVEN_WIDTH, D_MODEL), EVEN_WIDTH),
        "o_g_in": gain(ks[9], (N_ODD, D_MODEL)),
        "o_w_in": w(ks[10], (N_ODD, D_MODEL, sum(ODD_SPLITS)), D_MODEL),
        "o_b_f": FORGET_BIAS_INIT + 0.5 * jax.random.normal(ks[11], (N_ODD, FOX_HEADS), f32),
        "o_w_out": w(ks[12], (N_ODD, FOX_WIDTH, D_MODEL), FOX_WIDTH),
        "g_final": gain(ks[13], (D_MODEL,)),
    }


def reference(x, positions, e_g_in, e_w_in, e_g_q_a, e_w_q_up, e_g_kv_a, e_w_kv_up, e_sinks,
              e_w_out, o_g_in, o_w_in, o_b_f, o_w_out, g_final):
    for layer in range(DEPTH):
        j = layer // 2
        if layer % 2 == 0:
            x = mla_swa_layer(x, positions, e_g_in[j], e_w_in[j], e_g_q_a[j], e_w_q_up[j],
                              e_g_kv_a[j], e_w_kv_up[j], e_sinks[j], e_w_out[j])
        else:
            x = fox_layer(x, o_g_in[j], o_w_in[j], o_b_f[j], o_w_out[j])
    return rms_norm(x, g_final)
```

```python
import math
import os
DBG = os.environ.get('KDBG', '')
import numpy as np
import concourse.bass as bass
import concourse.mybir as mybir
from concourse.bass_utils import run_bass_kernel_spmd

F32 = mybir.dt.float32
BF16 = mybir.dt.bfloat16
I32 = mybir.dt.int32
AF = mybir.ActivationFunctionType
ALU = mybir.AluOpType

S_LEN = 4096
D = 1024
NT = 32
NB = 8
EPS = 1e-6
SEM_ROT = 12000


class Tok:
    __slots__ = ("name", "w", "r", "dsem")

    def __init__(self, name=""):
        self.name = name
        self.w = None
        self.r = {}
        self.dsem = None


class _Rec:
    def __init__(self):
        self.call = None

    def __getattr__(self, name):
        def f(*a, **k):
            assert self.call is None
            self.call = (name, a, k)
            return self
        return f


def _freeze(fn):
    r = _Rec()
    fn(r)
    assert r.call is not None
    return r.call


class Sched:
    ENGS = ("pe", "act", "dve", "pool", "sp")

    def __init__(self, nc):
        self.nc = nc
        self.sems = []
        self.semeng = {}
        self.ops = {e: [] for e in self.ENGS}
        self.esem = {}
        self.ecnt = {}
        self.seen = {e: {} for e in self.ENGS}
        self.semval = {}
        self.unsig = {e: False for e in self.ENGS}
        self.noself = {"pe"}
        for e in ("pe", "act", "dve", "pool"):
            self._new_esem(e)

    def _alloc(self, name, eng=None):
        h = self.nc.alloc_semaphore(name=name)
        self.sems.append(h)
        sid = len(self.sems) - 1
        self.semval[sid] = 0
        self.semeng[sid] = eng
        return sid

    def _new_esem(self, e):
        self.esem[e] = self._alloc(f"s_{e}_{len(self.sems)}", e)
        self.ecnt[e] = 0

    def tok(self, name=""):
        return Tok(name)

    def toks(self, n, name=""):
        return [Tok(f"{name}{i}") for i in range(n)]

    def _collect(self, e, reads, writes):
        need = {}

        def add(s, v):
            if need.get(s, 0) < v:
                need[s] = v
        for t in reads:
            if t.w is not None:
                add(*t.w)
        for t in writes:
            if t.w is not None:
                add(*t.w)
            for s, v in t.r.items():
                add(s, v)
        waits = []
        seen = self.seen[e]
        for s, v in need.items():
            if e in self.noself and self.semeng[s] == e:
                continue
            if seen.get(s, 0) >= v:
                continue
            seen[s] = v
            waits.append((s, v))
        return waits

    def _mark(self, s, val, reads, writes):
        for t in reads:
            if t.r.get(s, 0) < val:
                t.r[s] = val
        for t in writes:
            t.w = (s, val)
            t.r = {}

    def op(self, e, fn, reads=(), writes=(), sig=True):
        waits = self._collect(e, reads, writes)
        if sig and self.ecnt[e] >= SEM_ROT and not self.unsig[e]:
            self._new_esem(e)
        self.unsig[e] = not sig
        s = self.esem[e]
        val = self.ecnt[e] + 1
        if sig:
            self.ecnt[e] = val
            self.semval[s] = val
        self.ops[e].append((waits, _freeze(fn), (s, 1) if sig else None))
        self._mark(s, val, reads, writes)

    def barrier(self):
        cur = [(s, v) for s, v in self.semval.items() if v > 0]
        for e in self.ENGS:
            waits = []
            for s, v in cur:
                if (self.semeng[s] == e and e in self.noself) or self.seen[e].get(s, 0) >= v:
                    continue
                self.seen[e][s] = v
                waits.append((s, v))
            self.ops[e].append((waits, None, None))

    def dma(self, q, fn, reads=(), writes=(), owner=None):
        waits = self._collect(q, reads, writes)
        if owner is None:
            owner = writes[0] if writes else reads[0]
        if owner.dsem is None or self.semval[owner.dsem] >= SEM_ROT * 2:
            owner.dsem = self._alloc(f"d_{len(self.sems)}")
        s = owner.dsem
        self.semval[s] += 16
        val = self.semval[s]
        self.ops[q].append((waits, _freeze(fn), (s, 16)))
        self._mark(s, val, reads, writes)

    def wait_all(self, e, toks):
        waits = self._collect(e, [], toks)
        self.ops[e].append((waits, None, None))

    def emit(self):
        nc = self.nc
        sems = self.sems
        ops = self.ops

        def replay(eng, lst):
            for waits, fn, inc in lst:
                for s, v in waits:
                    eng.wait_ge(sems[s], v)
                if fn is None:
                    continue
                name, a, k = fn
                ins = getattr(eng, name)(*a, **k)
                if inc is not None:
                    ins.then_inc(sems[inc[0]], inc[1])

        with nc.Block() as block:
            @block.tensor
            def _(eng):
                replay(eng, ops["pe"])

            @block.scalar
            def _(eng):
                replay(eng, ops["act"])

            @block.vector
            def _(eng):
                replay(eng, ops["dve"])

            @block.gpsimd
            def _(eng):
                replay(eng, ops["pool"])

            @block.sync
            def _(eng):
                replay(eng, ops["sp"])


def _constants():
    c = {}
    c["c_ident"] = np.eye(128, dtype=np.float32)
    k = np.arange(128)[:, None]
    q = np.arange(128)[None, :]
    c["c_tri"] = (k <= q).astype(np.float32)
    c["c_ones"] = np.ones((128, 128), np.float32)
    qq = np.arange(512)[None, None, :]
    kk = np.arange(128)[:, None, None]
    rr = np.arange(4)[None, :, None]
    c["c_mask"] = ((rr * 128 + kk) <= qq).astype(np.float32)
    slopes = 2.0 ** (-8.0 * (np.arange(8, dtype=np.float64) + 1.0) / 8)
    t = np.zeros((128, 8, 2, 128), np.float64)
    kq = np.arange(128)[:, None]
    qv = np.arange(128)[None, :]
    for h in range(8):
        dist0 = 128 + qv - kq
        t[:, h, 0, :] = np.where(dist0 < 128, np.exp(-slopes[h] * dist0), 0.0)
        dist1 = qv - kq
        t[:, h, 1, :] = np.where(dist1 >= 0, np.exp(-slopes[h] * np.maximum(dist1, 0)), 0.0)
    c["c_swt"] = t.astype(np.float32)
    invf = 1.0 / (10000.0 ** (np.arange(0, 32, 2, dtype=np.float32) / 32))
    v = np.zeros((128, 2), np.float32)
    for p in range(64, 128):
        v[p, 0] = invf[(p - 64) % 16]
        v[p, 1] = (math.pi / 2) if p < 96 else 0.0
    c["c_rope"] = v
    return c


CONST_SHAPES = {"c_ident": [128, 128], "c_tri": [128, 128], "c_ones": [128, 128],
                "c_mask": [128, 4, 512], "c_swt": [128, 8, 2, 128], "c_rope": [128, 2]}


class _Stop(Exception):
    pass


def build_program(mode="full", stop=0):
    nc = bass.Bass("TRN2", target_bir_lowering=False)
    S = Sched(nc)
    do0 = mode in ("full", "l0")
    do1 = mode in ("full", "l1")

    def din(name, shape, dt=F32):
        return nc.dram_tensor(name, shape, dt, kind="ExternalInput").ap()

    cst = {k: din(k, v) for k, v in CONST_SHAPES.items()}
    if do0:
        x_d = din("x", [S_LEN, D])
        pos_d = din("positions", [S_LEN], I32)
        e_g_in = din("e_g_in", [D])
        e_w_in = din("e_w_in", [D, 2208])
        e_g_q_a = din("e_g_q_a", [256])
        e_w_q_up = din("e_w_q_up", [256, 768])
        e_g_kv_a = din("e_g_kv_a", [128])
        e_w_kv_up = din("e_w_kv_up", [128, 1024])
        e_sinks = din("e_sinks", [8])
        e_w_out = din("e_w_out", [D, D])
    if do1:
        o_g_in = din("o_g_in", [D])
        o_w_in = din("o_w_in", [D, 4112])
        o_b_f = din("o_b_f", [16])
        o_w_out = din("o_w_out", [D, D])
        g_final = din("g_final", [D])
        out_d = nc.dram_tensor("out", [S_LEN, D], F32, kind="ExternalOutput").ap()
    if mode == "full":
        x1_d = nc.dram_tensor("x1s", [S_LEN, D], F32).ap()
    elif mode == "l0":
        x1_d = nc.dram_tensor("x1", [S_LEN, D], F32, kind="ExternalOutput").ap()
    else:
        x1_d = din("x1", [S_LEN, D])
    t_x1 = S.toks(NT, "x1d")
    t_x1own = S.tok("x1own")
    t_outown = S.toks(2, "outown")
    t_out = S.toks(NT, "outd")

    def region(nbytes):
        st, _ = nc.bump_sbuf(nbytes)
        return st

    def at(name, shape, dt, off):
        return nc.alloc_sbuf_tensor_at(name, shape, dt, offset=off)

    def sb(name, shape, dt):
        return nc.alloc_sbuf_tensor(name, shape, dt)

    hT = sb("hT", [128, 8, S_LEN], BF16)
    t_hT = S.toks(NT, "hT")
    o_aot = region(65536)
    AOT = at("AOT", [128, 8, S_LEN], BF16, o_aot)
    t_AOT = [[S.tok(f"aot{c}_{b}") for b in range(NB)] for c in range(8)]
    o_big = region(32768)
    BIG = at("BIG", [128, 32 * 4 * 128], BF16, o_big)
    t_VA = S.tok("VA")
    t_LAT = S.toks(NB, "LAT")
    o_trig = region(8192)
    TRIG = at("TRIG", [128, S_LEN], BF16, o_trig)
    t_TRIG = S.tok("TRIG")
    o_pool = region(19456)
    QT = at("QT", [128, S_LEN], BF16, o_pool)
    t_QT = S.toks(NB, "QT")
    t_QTaug = S.tok("QTaug")
    KT = at("KT", [128, S_LEN], BF16, o_pool + 8192)
    t_KT = S.tok("KT")
    t_KTaug = S.tok("KTaug")
    NPT = 5
    PT = [at(f"pt{i}", [128, 512], BF16, o_pool + 16384 + 1024 * i) for i in range(3)]
    PT += [sb(f"ptx{i}", [128, 512], BF16) for i in range(NPT - 3)]
    t_PT = S.toks(NPT, "pt")
    XT = [at(f"xt{i}", [128, D], F32, o_pool + 4096 * i) for i in range(3)]
    t_XT = S.toks(3, "xt")
    XB = [at(f"xb{i}", [128, D], BF16, o_pool + 12288 + 2048 * i) for i in range(2)]
    t_XB = S.toks(2, "xb")
    ET = [at(f"et{i}", [128, 512], F32, o_pool + 2048 * i) for i in range(3)]
    t_ET = S.toks(3, "et")
    o_pf = region(1024)
    PF = [at(f"pf{i}", [128, 128], F32, o_pf + 512 * i) for i in range(2)]
    t_PF = S.toks(2, "pf")
    o_rc = region(4096)
    RC = [at(f"rc{i}", [128, 512], F32, o_rc + 2048 * i) for i in range(2)]
    t_RC = S.toks(2, "rc")
    GB = at("GB", [128, D], F32, o_rc)
    t_GB = S.tok("GB")
    o_wa = region(8192)
    WA = at("WA", [128, 3328], BF16, o_wa)
    t_WA = S.tok("WA")
    WB = sb("WB", [128, 1024], BF16)
    t_WB = S.tok("WB")
    MASK = sb("MASK", [128, 128], BF16)
    t_MASK = S.tok("MASK")
    identb = sb("identb", [128, 128], BF16)
    onesf = sb("onesf", [128, 128], F32)
    t_cst = S.tok("cst")
    stat = sb("stat", [128, 8], F32)
    t_statS = S.toks(2, "stat")
    epsb = sb("epsb", [128, 2], F32)
    t_eps = S.tok("eps")

    PS = [nc.alloc_psum_tensor(f"ps{i}", [128, 512], F32) for i in range(7)]
    t_PS = S.toks(7, "ps")
    PST = nc.alloc_psum_tensor("pst", [128, 8, 128], BF16)
    t_PST = S.tok("pst")

    S.dma("sp", lambda e: e.dma_start(out=onesf[:], in_=cst["c_ones"][:, :]), writes=[t_cst])
    t_cstb = S.tok("cstb")
    S.dma("pool", lambda e: e.dma_start(out=identb[:], in_=cst["c_ident"][:, :]), writes=[t_cstb])
    S.dma("pool", lambda e: e.dma_start(out=MASK[:], in_=cst["c_tri"][:, :]), writes=[t_MASK])
    S.op("dve", lambda e: e.memset(epsb[:, 0:1], EPS), writes=[t_eps])
    S.op("dve", lambda e: e.memset(epsb[:, 1:2], 1.0), writes=[t_eps])

    cnt = {"ev": 0}

    def evac_engine():
        cnt["ev"] += 1
        return "act" if cnt["ev"] % 2 else "dve"

    def copy_op(eng, out, in_, scale=None):
        if eng == "act":
            if scale is None:
                return lambda e: e.copy(out=out, in_=in_)
            return lambda e: e.mul(out=out, in_=in_, mul=scale)
        if scale is None:
            return lambda e: e.tensor_copy(out=out, in_=in_)
        return lambda e: e.tensor_scalar(out=out, in0=in_, scalar1=scale, scalar2=None, op0=ALU.mult)

    def load_gain(g_d):
        S.dma("sp", lambda e: e.dma_start(out=GB[:], in_=bass.AP(g_d.tensor, 0, [[0, 128], [1, D]])),
              writes=[t_GB])

    def rstd_from_ms(col, tst):
        S.op("act", lambda e: e.activation(out=stat[:, col + 1:col + 2], in_=stat[:, col:col + 1], func=AF.Ln,
                                           bias=epsb[:, 0:1], scale=1.0), reads=[t_eps], writes=[tst])
        S.op("act", lambda e: e.activation(out=stat[:, col + 1:col + 2], in_=stat[:, col + 1:col + 2],
                                           func=AF.Exp, scale=-0.5), writes=[tst])

    def norm_tile(xt, t_xt, out_ap, t_outs, junk_ap, t_junk, slot=0):
        c0 = 4 * slot
        tst = t_statS[slot]
        S.op("dve", lambda e: e.memset(stat[:, c0:c0 + 1], 0.0), writes=[tst])
        S.op("act", lambda e: e.activation(out=junk_ap, in_=xt[:], func=AF.Square, scale=1.0 / 32,
                                           accum_out=stat[:, c0:c0 + 1]), reads=[t_xt], writes=[t_junk, tst])
        rstd_from_ms(c0, tst)
        S.op("dve", lambda e: e.scalar_tensor_tensor(out=out_ap, in0=xt[:], scalar=stat[:, c0 + 1:c0 + 2], in1=GB[:],
                                                     op0=ALU.mult, op1=ALU.mult),
             reads=[t_xt, tst, t_GB], writes=list(t_outs))

    def to_hT(xt, t_xt, tt):
        i = tt % 2
        norm_tile(xt, t_xt, XB[i][:], [t_XB[i]], XB[i][:], t_XB[i], slot=i)
        for c in range(8):
            S.op("pe", lambda e, c=c: e.transpose(out=PST[:, c, :], in_=XB[i][:, c * 128:(c + 1) * 128],
                                                   identity=identb[:]),
                 reads=[t_XB[i], t_cstb], writes=[t_PST])
        eng = evac_engine()
        S.op(eng, copy_op(eng, hT[:, :, tt * 128:(tt + 1) * 128], PST[:, :, :]), writes=[t_PST, t_hT[tt]])

    def phase_A(src_d):
        for tt in range(NT):
            i = tt % 2
            S.dma("sp", lambda e, tt=tt, i=i: e.dma_start(out=XT[i][:], in_=src_d[tt * 128:(tt + 1) * 128, :]),
                  writes=[t_XT[i]])
            to_hT(XT[i], t_XT[i], tt)

    def wview(w_d, c0, c1):
        return w_d.rearrange("(c p) n -> p c n", p=128)[:, :, c0:c1]

    def proj_fm(ps, t_ps, w_ap_fn, nk, src_fn, src_toks, w_toks, M):
        for k in range(nk):
            S.op("pe", lambda e, k=k: e.matmul(ps[0:M, :], lhsT=w_ap_fn(k), rhs=src_fn(k),
                                               start=(k == 0), stop=(k == nk - 1)),
                 reads=list(src_toks) + list(w_toks), writes=[t_ps])

    def attention(KR, groups, bias_fn, scale, va_fn, out_fn, den_add=None, fp32_tables=False, rd_extra=(),
                  KTt=None, pre_group=None, bg=(), va_tok=None, one_rc=False):
        steps = []
        for gi, (q0, QW, kbs, gidx) in enumerate(groups):
            for n, (j, c0, tbl) in enumerate(kbs):
                steps.append((gi, q0, QW, j, c0, tbl, n == 0, n == len(kbs) - 1, gidx))

        SB_ = (0, 1, 6)
        LA = 2
        KTx, t_KTx = (KT, [t_KT, t_KTaug]) if KTt is None else KTt
        t_VAx = t_VA if va_tok is None else va_tok
        bg = list(bg)
        nsteps = sum(len(g_[2]) for g_ in groups)
        bg_stride = max(1, nsteps // (len(bg) + 1)) if bg else 0
        seen_groups = set()

        def emit_qk(i):
            gi, q0, QW, j, c0, tbl, first, last, gidx = steps[i]
            sp = PS[SB_[i % 3]]
            S.op("pe", lambda e: e.matmul(sp[:, c0:QW], lhsT=KTx[0:KR, j * 128:(j + 1) * 128],
                                          rhs=QT[0:KR, q0 + c0:q0 + QW], start=True, stop=True),
                 reads=list(t_KTx) + [t_QT[q0 // 512], t_QTaug], writes=[t_PS[SB_[i % 3]]])

        first_idx = {}
        for i_, st_ in enumerate(steps):
            first_idx.setdefault(st_[0], i_)
        if pre_group is not None:
            pre_group(groups[0][3])
        for i0 in range(min(LA, len(steps))):
            emit_qk(i0)
        for i, (gi, q0, QW, j, c0, tbl, first, last, gidx) in enumerate(steps):
            if i + LA < len(steps):
                emit_qk(i + LA)
            if pre_group is not None and i == first_idx[gi] + 1 and gi + 1 < len(groups):
                pre_group(groups[gi + 1][3])
            if bg and i > 0 and i % bg_stride == 0:
                bg.pop(0)()
            sp = PS[SB_[i % 3]]
            tsp = t_PS[SB_[i % 3]]
            pt = PT[i % NPT]
            tpt = t_PT[i % NPT]
            kw = {"scale": scale}
            b = bias_fn(j) if bias_fn is not None else None
            if b is not None:
                kw["bias"] = b
            if tbl is not None and fp32_tables:
                pf = PF[i % 2]
                S.op("act", lambda e, kw=kw, pf=pf, sp=sp, QW=QW: e.activation(out=pf[:, 0:QW], in_=sp[:, 0:QW],
                                                                          func=AF.Exp, **kw),
                     reads=list(rd_extra), writes=[tsp, t_PF[i % 2]])
                S.op("dve", lambda e, pf=pf, pt=pt, tbl=tbl, QW=QW: e.tensor_tensor(out=pt[:, 0:QW], in0=pf[:, 0:QW],
                                                                              in1=tbl, op=ALU.mult),
                     reads=[t_PF[i % 2]] + list(rd_extra), writes=[tpt])
            else:
                S.op("act", lambda e, kw=kw, pt=pt, sp=sp, QW=QW, c0=c0: e.activation(
                    out=pt[:, c0:QW], in_=sp[:, c0:QW], func=AF.Exp, **kw),
                    reads=list(rd_extra), writes=[tsp, tpt])
                if tbl is not None:
                    S.op("dve", lambda e, pt=pt, tbl=tbl, c0=c0: e.tensor_tensor(
                        out=pt[:, c0:c0 + 128], in0=pt[:, c0:c0 + 128], in1=tbl, op=ALU.mult),
                        reads=[t_MASK], writes=[tpt])
            op_ = PS[2 + gi % 2]
            top = t_PS[2 + gi % 2]
            S.op("pe", lambda e, op_=op_, pt=pt, j=j, first=first, last=last, QW=QW, c0=c0:
                 e.matmul(op_[:, c0:QW], lhsT=va_fn(j), rhs=pt[:, c0:QW], start=first, stop=last),
                 reads=[tpt, t_VAx], writes=[top])
            if last:
                rc = RC[0 if one_rc else gi % 2]
                trc = t_RC[0 if one_rc else gi % 2]
                if den_add is not None:
                    S.op("dve", lambda e, rc=rc, op_=op_, QW=QW: e.tensor_scalar(
                        out=rc[64:128, 0:QW], in0=op_[64:128, 0:QW], scalar1=den_add, scalar2=None, op0=ALU.add),
                        reads=list(rd_extra), writes=[top, trc])
                    S.op("dve", lambda e, rc=rc, QW=QW: e.reciprocal(out=rc[64:128, 0:QW], in_=rc[64:128, 0:QW]),
                         writes=[trc])
                else:
                    S.op("dve", lambda e, rc=rc, op_=op_, QW=QW: e.reciprocal(out=rc[64:128, 0:QW],
                                                                         in_=op_[64:128, 0:QW]),
                         writes=[top, trc])
                o_ap, o_toks = out_fn(gidx)
                S.op("dve", lambda e, rc=rc, op_=op_, o_ap=o_ap, QW=QW: e.tensor_tensor(
                    out=o_ap, in0=op_[0:64, 0:QW], in1=rc[64:128, 0:QW], op=ALU.mult),
                    reads=[trc], writes=[top] + list(o_toks))
        while bg:
            bg.pop(0)()

    def _unused():
        pass

    def dense_groups():
        gs = []
        for g in range(NB):
            kbs = [(j, 0, None) for j in range(4 * g)] + [(4 * g + r, r * 128, MASK[:, :]) for r in range(4)]
            gs.append((g * 512, 512, kbs, g))
        return gs

    def gate_phase(w_in_d, goff):
        for c in range(8):
            S.dma("pool", lambda e, c=c: e.dma_start(
                out=WB[:, 0:1024].rearrange("p (c n) -> p c n", c=8),
                in_=wview(w_in_d, goff + c * 128, goff + (c + 1) * 128)), writes=[t_WB])
            for b in range(NB):
                ps = PS[4 + b % 2]
                tps = t_PS[4 + b % 2]
                proj_fm(ps, tps, lambda k: WB[:, k * 128:(k + 1) * 128], 8,
                        lambda k, b=b: hT[:, k, b * 512:(b + 1) * 512], t_hT[4 * b:4 * b + 4], [t_WB], 128)
                gt = PT[b % NPT]
                S.op("act", lambda e, gt=gt, ps=ps: e.activation(out=gt[:], in_=ps[:], func=AF.Silu),
                     writes=[tps, t_PT[b % NPT]])
                S.op("dve", lambda e, gt=gt, c=c, b=b: e.tensor_tensor(
                    out=AOT[:, c, b * 512:(b + 1) * 512], in0=AOT[:, c, b * 512:(b + 1) * 512], in1=gt[:],
                    op=ALU.mult), reads=[t_PT[b % NPT]], writes=[t_AOT[c][b]])

    def out_phase(w_out_d, res_d, t_res, last_layer, next_gain_d=None):
        S.barrier()
        WO = at("WO_%d" % int(last_layer), [128, 8 * 1024], BF16, o_big)
        t_WO = S.tok("WO")
        S.dma("pool", lambda e: e.dma_start(out=WO[:].rearrange("p (c n) -> p c n", c=8),
                                            in_=wview(w_out_d, 0, D)), writes=[t_WO])
        if last_layer:
            load_gain(g_final)
        elif next_gain_d is not None:
            load_gain(next_gain_d)
        t_x1o = S.toks(3, "x1own")
        t_outo = S.toks(3, "outown")
        def stage1(tt):
            i = tt % 3
            b = tt // 4
            pp = 2 * (tt % 2)
            S.dma("sp", lambda e: e.dma_start(out=XT[i][:], in_=res_d[tt * 128:(tt + 1) * 128, :]),
                  reads=[t_res[tt]] if t_res is not None else [], writes=[t_XT[i]])
            for half in range(2):
                ps = PS[pp + half]
                tps = t_PS[pp + half]
                for c in range(8):
                    S.op("pe", lambda e: e.matmul(
                        ps[:, :], lhsT=AOT[:, c, tt * 128:(tt + 1) * 128],
                        rhs=WO[:, c * 1024 + half * 512: c * 1024 + (half + 1) * 512],
                        start=(c == 0), stop=(c == 7)),
                        reads=[t_AOT[c][b], t_WO], writes=[tps])

        def stage1b(tt):
            i = tt % 3
            pp = 2 * (tt % 2)
            for half in range(2):
                ps = PS[pp + half]
                tps = t_PS[pp + half]
                S.op("dve", lambda e: e.tensor_tensor(
                    out=XT[i][:, half * 512:(half + 1) * 512], in0=ps[:, :], in1=XT[i][:, half * 512:(half + 1) * 512],
                    op=ALU.add), writes=[tps, t_XT[i]])

        def stage2(tt):
            i = tt % 3
            if last_layer:
                norm_tile(XT[i], t_XT[i], XT[i][:], [t_XT[i]], XB[tt % 2][:], t_XB[tt % 2], slot=tt % 2)
                S.dma("sp", lambda e: e.dma_start(out=out_d[tt * 128:(tt + 1) * 128, :], in_=XT[i][:]),
                      reads=[t_XT[i]], writes=[t_out[tt]], owner=t_outo[i])
            else:
                S.dma("sp", lambda e: e.dma_start(out=x1_d[tt * 128:(tt + 1) * 128, :], in_=XT[i][:]),
                      reads=[t_XT[i]], writes=[t_x1[tt]], owner=t_x1o[i])
                if mode == "full":
                    to_hT(XT[i], t_XT[i], tt)

        for tt in range(NT + 1):
            if tt < NT:
                stage1(tt)
            if tt >= 1:
                stage2(tt - 1)
            if tt < NT:
                stage1b(tt)
        S.barrier()

    def _layers():
        if do0:
            VA0 = at("VA0", [128, 32, 128], BF16, o_big)
            LATt = at("LATt", [128, 3, S_LEN], BF16, o_big + 8192)
            SWT = at("SWT", [128, 2048], F32, o_big + 8192)
            posi = at("posi", [128, S_LEN], I32, o_aot)
            kf = at("kf", [128, S_LEN], F32, o_aot + 16384)
            ANG = at("ANG", [128, S_LEN], F32, o_aot + 32768)
            t_pos, t_kf, t_ang = S.tok("posi"), S.tok("kf"), S.tok("ang")
            load_gain(e_g_in)
            phase_A(x_d)

            gq = sb("gq", [128, 2], F32)
            gkv = sb("gkv", [128, 1], F32)
            esink = sb("esink", [128, 8], F32)
            ropec = sb("ropec", [128, 2], F32)
            t_sm = S.tok("small0")
            for c2 in range(2):
                S.dma("sp", lambda e, c2=c2: e.dma_start(
                    out=gq[:, c2:c2 + 1], in_=e_g_q_a[c2 * 128:(c2 + 1) * 128].rearrange("(p o) -> p o", o=1)),
                    writes=[t_sm])
            S.dma("sp", lambda e: e.dma_start(out=gkv[:], in_=e_g_kv_a.rearrange("(p o) -> p o", o=1)), writes=[t_sm])
            S.dma("sp", lambda e: e.dma_start(out=esink[:], in_=bass.AP(e_sinks.tensor, 0, [[0, 128], [1, 8]])),
                  writes=[t_sm])
            S.dma("sp", lambda e: e.dma_start(out=ropec[:], in_=cst["c_rope"][:, :]), writes=[t_sm])
            S.op("act", lambda e: e.activation(out=esink[:], in_=esink[:], func=AF.Exp), writes=[t_sm])

            S.dma("sp", lambda e: e.dma_start(out=posi[64:128, :], in_=bass.AP(pos_d.tensor, 0, [[0, 64], [1, S_LEN]])),
                  writes=[t_pos])
            P6 = slice(64, 128)
            S.op("dve", lambda e: e.tensor_copy(out=ANG[P6, :], in_=posi[P6, :]), reads=[t_pos], writes=[t_ang])
            S.op("dve", lambda e: e.tensor_scalar(out=ANG[P6, :], in0=ANG[P6, :], scalar1=ropec[P6, 0:1],
                                                  scalar2=ropec[P6, 1:2], op0=ALU.mult, op1=ALU.add),
                 reads=[t_sm], writes=[t_ang])
            S.op("dve", lambda e: e.tensor_scalar(out=kf[P6, :], in0=ANG[P6, :], scalar1=1.0 / (2 * math.pi),
                                                  scalar2=0.5, op0=ALU.mult, op1=ALU.add),
                 reads=[t_ang], writes=[t_kf])
            S.op("dve", lambda e: e.tensor_copy(out=posi[P6, :], in_=kf[P6, :]), reads=[t_kf], writes=[t_pos])
            S.op("dve", lambda e: e.tensor_copy(out=kf[P6, :], in_=posi[P6, :]), reads=[t_pos], writes=[t_kf])
            C1 = 6.28125
            C2 = 2 * math.pi - C1
            S.op("dve", lambda e: e.scalar_tensor_tensor(out=ANG[P6, :], in0=kf[P6, :], scalar=-C1, in1=ANG[P6, :],
                                                         op0=ALU.mult, op1=ALU.add), reads=[t_kf], writes=[t_ang])
            S.op("dve", lambda e: e.scalar_tensor_tensor(out=ANG[P6, :], in0=kf[P6, :], scalar=-C2, in1=ANG[P6, :],
                                                         op0=ALU.mult, op1=ALU.add), reads=[t_kf], writes=[t_ang])
            S.op("dve", lambda e: e.tensor_single_scalar(out=kf[P6, :], in_=ANG[P6, :], scalar=-math.pi, op=ALU.is_lt),
                 reads=[t_ang], writes=[t_kf])
            S.op("dve", lambda e: e.scalar_tensor_tensor(out=ANG[P6, :], in0=kf[P6, :], scalar=2 * math.pi,
                                                         in1=ANG[P6, :], op0=ALU.mult, op1=ALU.add),
                 reads=[t_kf], writes=[t_ang])
            S.op("dve", lambda e: e.tensor_scalar(out=ANG[P6, :], in0=ANG[P6, :], scalar1=-3.1415925, scalar2=3.1415925,
                                                  op0=ALU.max, op1=ALU.min), writes=[t_ang])
            S.op("act", lambda e: e.activation(out=TRIG[P6, :], in_=ANG[P6, :], func=AF.Sin), reads=[t_ang],
                 writes=[t_TRIG])
            S.barrier()
            checkpoint(1)

            S.dma("sp", lambda e: e.dma_start(out=SWT[:, :], in_=cst["c_swt"].rearrange("p h r q -> p (h r q)")),
                  writes=[t_kf])
            S.op("dve", lambda e: e.memset(VA0[:, :, 64:128], 1.0), writes=[t_VA])
            SWA_Q0, SWA_K0, SWA_V0 = 416, 928, 1056
            WBkv = WB[:, 0:1024].rearrange("p (c s n) -> p c s n", c=8, s=2)
            for h in range(8):
                kv = h // 4
                if h % 4 == 0:
                    S.dma("pool", lambda e, kv=kv: e.dma_start(
                        out=WBkv[:, :, 0, :], in_=wview(e_w_in, SWA_K0 + kv * 64, SWA_K0 + (kv + 1) * 64)), writes=[t_WB])
                    S.dma("pool", lambda e, kv=kv: e.dma_start(
                        out=WBkv[:, :, 1, :], in_=wview(e_w_in, SWA_V0 + kv * 64, SWA_V0 + (kv + 1) * 64)), writes=[t_WB])
                    for b in range(NB):
                        ps = PS[4 + b % 2]
                        tps = t_PS[4 + b % 2]
                        proj_fm(ps, tps, lambda k: WB[:, k * 128:k * 128 + 64], 8,
                                lambda k, b=b: hT[:, k, b * 512:(b + 1) * 512], t_hT[4 * b:4 * b + 4], [t_WB], 64)
                        eng = evac_engine()
                        S.op(eng, copy_op(eng, KT[0:64, b * 512:(b + 1) * 512], ps[0:64, :]), writes=[tps, t_KT])
                    for t8 in range(4):
                        ps = PS[4 + t8 % 2]
                        tps = t_PS[4 + t8 % 2]
                        for ti in range(8):
                            tt = t8 * 8 + ti
                            for k in range(8):
                                S.op("pe", lambda e, k=k, tt=tt, ti=ti, ps=ps: e.matmul(
                                    ps[:, ti * 64:(ti + 1) * 64], lhsT=hT[:, k, tt * 128:(tt + 1) * 128],
                                    rhs=WB[:, k * 128 + 64:k * 128 + 128], start=(k == 0), stop=(k == 7)),
                                    reads=[t_hT[tt], t_WB], writes=[tps])
                        eng = evac_engine()
                        S.op(eng, copy_op(eng, VA0[:, t8 * 8:(t8 + 1) * 8, 0:64],
                                          ps[:, :].rearrange("p (t d) -> p t d", t=8)), writes=[tps, t_VA])
                S.dma("pool", lambda e, h=h: e.dma_start(
                    out=WA[:, 0:512].rearrange("p (c n) -> p c n", c=8),
                    in_=wview(e_w_in, SWA_Q0 + h * 64, SWA_Q0 + (h + 1) * 64)), writes=[t_WA])
                for b in range(NB):
                    ps = PS[4 + b % 2]
                    tps = t_PS[4 + b % 2]
                    proj_fm(ps, tps, lambda k: WA[:, k * 64:(k + 1) * 64], 8,
                            lambda k, b=b: hT[:, k, b * 512:(b + 1) * 512], t_hT[4 * b:4 * b + 4], [t_WA], 64)
                    eng = evac_engine()
                    S.op(eng, copy_op(eng, QT[0:64, b * 512:(b + 1) * 512], ps[0:64, :], scale=0.125),
                         writes=[tps, t_QT[b]])
                groups = []
                for g in range(NT):
                    kbs = []
                    if g > 0:
                        kbs.append((g - 1, 0, SWT[:, (h * 2 + 0) * 128:(h * 2 + 1) * 128]))
                    kbs.append((g, 0, SWT[:, (h * 2 + 1) * 128:(h * 2 + 2) * 128]))
                    groups.append((g * 128, 128, kbs, g))
                c = 4 + h // 2
                po = (h % 2) * 64

                def out_fn(g, c=c, po=po):
                    return AOT[po:po + 64, c, g * 128:(g + 1) * 128], [t_AOT[c][g // 4]]
                attention(64, groups, None, 1.0, lambda j: VA0[:, j, :], out_fn,
                          den_add=esink[64:128, h:h + 1], fp32_tables=True, rd_extra=[t_sm, t_kf])
            S.barrier()
            checkpoint(2)

            W416 = WA[:, 0:8 * 416].rearrange("p (c n) -> p c n", c=8)
            S.dma("pool", lambda e: e.dma_start(out=W416, in_=wview(e_w_in, 0, 416)), writes=[t_WA])
            WROT = WB[:, 0:8 * 96].rearrange("p (c n) -> p c n", c=8)
            S.op("dve", lambda e: e.memset(WROT[:, :, 0:64], 0.0), writes=[t_WB])
            S.op("dve", lambda e: e.tensor_scalar(out=WROT[:, :, 64:80], in0=W416[:, :, 400:416], scalar1=-1.0,
                                                  scalar2=None, op0=ALU.mult), reads=[t_WA], writes=[t_WB])
            S.op("dve", lambda e: e.tensor_copy(out=WROT[:, :, 80:96], in_=W416[:, :, 384:400]), reads=[t_WA],
                 writes=[t_WB])
            for b in range(NB):
                bs = slice(b * 512, (b + 1) * 512)
                hsrc = lambda k, b=b: hT[:, k, b * 512:(b + 1) * 512]
                ht = t_hT[4 * b:4 * b + 4]
                proj_fm(PS[0], t_PS[0], lambda k: W416[:, k, 0:128], 8, hsrc, ht, [t_WA], 128)
                proj_fm(PS[1], t_PS[1], lambda k: W416[:, k, 128:256], 8, hsrc, ht, [t_WA], 128)
                proj_fm(PS[2], t_PS[2], lambda k: W416[:, k, 256:384], 8, hsrc, ht, [t_WA], 128)
                proj_fm(PS[3], t_PS[3], lambda k: W416[:, k, 320:416], 8, hsrc, ht, [t_WA], 96)
                proj_fm(PS[4], t_PS[4], lambda k: WROT[:, k, :], 8, hsrc, ht, [t_WB], 96)
                for n in range(3):
                    S.op("act", lambda e, n=n: e.activation(out=ET[n][:], in_=PS[n][:], func=AF.Square),
                         writes=[t_PS[n], t_ET[n]])
                S.op("pe", lambda e: e.matmul(PS[5][:, :], lhsT=onesf[:], rhs=ET[0][:], start=True, stop=False),
                     reads=[t_ET[0], t_cst], writes=[t_PS[5]])
                S.op("pe", lambda e: e.matmul(PS[5][:, :], lhsT=onesf[:], rhs=ET[1][:], start=False, stop=True),
                     reads=[t_ET[1], t_cst], writes=[t_PS[5]])
                S.op("pe", lambda e: e.matmul(PS[6][:, :], lhsT=onesf[:], rhs=ET[2][:], start=True, stop=True),
                     reads=[t_ET[2], t_cst], writes=[t_PS[6]])
                for (pi, n_, n) in ((5, 256.0, 0), (6, 128.0, 1)):
                    S.op("act", lambda e, pi=pi, n_=n_, n=n: e.activation(out=ET[n][:], in_=PS[pi][:], func=AF.Ln,
                                                                    bias=epsb[:, 0:1], scale=1.0 / n_),
                         reads=[t_eps], writes=[t_PS[pi], t_ET[n]])
                    S.op("act", lambda e, n=n: e.activation(out=ET[n][:], in_=ET[n][:], func=AF.Exp, scale=-0.5),
                         writes=[t_ET[n]])
                for (pi, chunk, gsc, n) in ((0, 0, gq[:, 0:1], 0), (1, 1, gq[:, 1:2], 0), (2, 2, gkv[:, 0:1], 1)):
                    S.op("dve", lambda e, pi=pi, chunk=chunk, gsc=gsc, n=n, bs=bs: e.scalar_tensor_tensor(
                        out=LATt[:, chunk, bs], in0=PS[pi][:], scalar=gsc, in1=ET[n][:], op0=ALU.mult, op1=ALU.mult),
                        reads=[t_ET[n], t_sm], writes=[t_PS[pi], t_LAT[b]])
                S.op("dve", lambda e, bs=bs: e.tensor_tensor(out=RC[0][64:96, :], in0=PS[3][64:96, :], in1=TRIG[64:96, bs],
                                                         op=ALU.mult), reads=[t_TRIG], writes=[t_PS[3], t_RC[0]])
                S.op("dve", lambda e, bs=bs: e.tensor_tensor(out=RC[1][64:96, :], in0=PS[4][64:96, :], in1=TRIG[96:128, bs],
                                                         op=ALU.mult), reads=[t_TRIG], writes=[t_PS[4], t_RC[1]])
                S.op("dve", lambda e, bs=bs: e.tensor_tensor(out=KT[64:96, bs], in0=RC[0][64:96, :], in1=RC[1][64:96, :],
                                                         op=ALU.add), reads=[t_RC[0], t_RC[1]], writes=[t_KTaug])
            S.barrier()
            checkpoint(3)

            WQ = WA[:, 0:2 * 768].rearrange("p (c n) -> p c n", c=2)
            S.dma("pool", lambda e: e.dma_start(out=WQ, in_=wview(e_w_q_up, 0, 768)), writes=[t_WA])
            WQR = WA[:, 1536:1536 + 2 * 768].rearrange("p (c n) -> p c n", c=2)
            WKV = WB[:, 0:1024]
            S.dma("pool", lambda e: e.dma_start(out=WKV, in_=e_w_kv_up[:, :]), writes=[t_WB])
            S.op("dve", lambda e: e.memset(WA[:, 1536:1536 + 2 * 768], 0.0), writes=[t_WA])
            for c2 in range(2):
                src = WQ[:, c2, :].rearrange("p (h d) -> p h d", h=8)
                dst = WQR[:, c2, :].rearrange("p (h d) -> p h d", h=8)
                S.op("dve", lambda e, src=src, dst=dst: e.tensor_scalar(out=dst[:, :, 64:80], in0=src[:, :, 80:96],
                                                                  scalar1=-1.0, scalar2=None, op0=ALU.mult),
                     writes=[t_WA])
                S.op("dve", lambda e, src=src, dst=dst: e.tensor_copy(out=dst[:, :, 80:96], in_=src[:, :, 64:80]),
                     writes=[t_WA])
            mla_scale = 96.0 ** -0.5
            for h in range(8):
                for b in range(NB):
                    ps = PS[4 + b % 2]
                    tps = t_PS[4 + b % 2]
                    S.op("pe", lambda e, b=b, h=h, ps=ps: e.matmul(ps[0:64, :], lhsT=WKV[:, h * 128:h * 128 + 64],
                                                             rhs=LATt[:, 2, b * 512:(b + 1) * 512], start=True, stop=True),
                         reads=[t_LAT[b], t_WB], writes=[tps])
                    eng = evac_engine()
                    S.op(eng, copy_op(eng, KT[0:64, b * 512:(b + 1) * 512], ps[0:64, :]), writes=[tps, t_KT])
                for t8 in range(4):
                    ps = PS[4 + t8 % 2]
                    tps = t_PS[4 + t8 % 2]
                    for ti in range(8):
                        tt = t8 * 8 + ti
                        S.op("pe", lambda e, tt=tt, ti=ti, h=h, ps=ps: e.matmul(
                            ps[:, ti * 64:(ti + 1) * 64], lhsT=LATt[:, 2, tt * 128:(tt + 1) * 128],
                            rhs=WKV[:, h * 128 + 64:h * 128 + 128], start=True, stop=True),
                            reads=[t_LAT[tt // 4], t_WB], writes=[tps])
                    eng = evac_engine()
                    S.op(eng, copy_op(eng, VA0[:, t8 * 8:(t8 + 1) * 8, 0:64],
                                      ps[:, :].rearrange("p (t d) -> p t d", t=8)), writes=[tps, t_VA])
                for b in range(NB):
                    bs = slice(b * 512, (b + 1) * 512)
                    p1, tp1 = PS[4], t_PS[4]
                    p2, tp2 = PS[5], t_PS[5]
                    for k in range(2):
                        S.op("pe", lambda e, k=k, h=h, bs=bs: e.matmul(p1[0:96, :], lhsT=WQ[:, k, h * 96:(h + 1) * 96],
                                                                rhs=LATt[:, k, bs], start=(k == 0), stop=(k == 1)),
                             reads=[t_LAT[b], t_WA], writes=[tp1])
                    for k in range(2):
                        S.op("pe", lambda e, k=k, h=h, bs=bs: e.matmul(p2[0:96, :], lhsT=WQR[:, k, h * 96:(h + 1) * 96],
                                                                rhs=LATt[:, k, bs], start=(k == 0), stop=(k == 1)),
                             reads=[t_LAT[b], t_WA], writes=[tp2])
                    S.op("act", lambda e, bs=bs: e.copy(out=QT[0:64, bs], in_=p1[0:64, :]),
                         writes=[tp1, t_QT[b]])
                    S.op("dve", lambda e, bs=bs: e.tensor_tensor(out=RC[0][64:96, :], in0=p1[64:96, :], in1=TRIG[64:96, bs],
                                                             op=ALU.mult), reads=[t_TRIG], writes=[tp1, t_RC[0]])
                    S.op("dve", lambda e, bs=bs: e.tensor_tensor(out=RC[1][64:96, :], in0=p2[64:96, :], in1=TRIG[96:128, bs],
                                                             op=ALU.mult), reads=[t_TRIG], writes=[tp2, t_RC[1]])
                    S.op("dve", lambda e, bs=bs: e.tensor_tensor(out=QT[64:96, bs], in0=RC[0][64:96, :], in1=RC[1][64:96, :],
                                                             op=ALU.add), reads=[t_RC[0], t_RC[1]], writes=[t_QT[b]])
                c = h // 2
                po = (h % 2) * 64

                def out_fn(g, c=c, po=po):
                    return AOT[po:po + 64, c, g * 512:(g + 1) * 512], [t_AOT[c][g]]
                attention(96, dense_groups(), None, mla_scale, lambda j: VA0[:, j, :], out_fn)

            checkpoint(4)
            gate_phase(e_w_in, 1184)
            checkpoint(5)
            out_phase(e_w_out, x_d, None, last_layer=False, next_gain_d=(o_g_in if do1 else None))
            checkpoint(6)

        if do1:
            VA4 = at("VA4", [128, 32, 4, 128], BF16, o_big)
            if not do0:
                load_gain(o_g_in)
                phase_A(x1_d)
                S.barrier()
            Q0, K0, V0, F0, G0 = 0, 1024, 2048, 3072, 3088
            WF = WB[:, 0:128].rearrange("p (c n) -> p c n", c=8)
            S.dma("pool", lambda e: e.dma_start(out=WF, in_=wview(o_w_in, F0, F0 + 16)), writes=[t_WB])
            NL = at("NL", [128, 32, 16], F32, o_wa)
            trif = at("trif", [128, 128], F32, o_wa + 2048)
            identf = at("identf", [128, 128], F32, o_wa + 2560)
            CTf = at("CTf", [16, S_LEN], F32, o_big)
            r1 = at("r1", [16, S_LEN], F32, o_big + 16384)
            CS = at("CS", [128, S_LEN], BF16, o_trig)
            bfb = sb("bfb", [128, 16], F32)
            Ct = sb("Ct", [128, 32, 16], F32)
            Rs = sb("Rs", [128, 16], F32)
            zt = sb("zt", [128, 16], F32)
            t_bfb, t_NL, t_Ct, t_Rs, t_zt = S.tok("bfb"), S.toks(NT, "NL"), S.toks(NT, "Ct"), S.tok("Rs"), S.tok("zt")
            t_c1 = S.tok("cst1")
            S.dma("sp", lambda e: e.dma_start(out=trif[:], in_=cst["c_tri"][:, :]), writes=[t_c1])
            S.dma("sp", lambda e: e.dma_start(out=identf[:], in_=cst["c_ident"][:, :]), writes=[t_c1])
            S.dma("sp", lambda e: e.dma_start(out=bfb[:], in_=bass.AP(o_b_f.tensor, 0, [[0, 128], [1, 16]])),
                  writes=[t_bfb])
            S.op("dve", lambda e: e.memset(Rs[:], 0.0), writes=[t_Rs])
            for tt in range(NT):
                ps = PS[6]
                tps = t_PS[6]
                for k in range(8):
                    S.op("pe", lambda e, k=k, tt=tt: e.matmul(ps[:, 0:16], lhsT=hT[:, k, tt * 128:(tt + 1) * 128],
                                                          rhs=WF[:, k, :], start=(k == 0), stop=(k == 7)),
                         reads=[t_hT[tt], t_WB], writes=[tps])
                S.op("dve", lambda e: e.tensor_tensor(out=zt[:], in0=ps[:, 0:16], in1=bfb[:], op=ALU.add),
                     reads=[t_bfb], writes=[tps, t_zt])
                S.op("act", lambda e: e.activation(out=zt[:], in_=zt[:], func=AF.Exp, scale=-1.0), writes=[t_zt])
                S.op("act", lambda e, tt=tt: e.activation(out=NL[:, tt, :], in_=zt[:], func=AF.Ln, bias=epsb[:, 1:2],
                                                         scale=1.0), reads=[t_eps], writes=[t_zt, t_NL[tt]])
                pc = PS[5]
                tpc = t_PS[5]
                S.op("pe", lambda e, tt=tt: e.matmul(pc[:, 0:16], lhsT=trif[:], rhs=NL[:, tt, :], start=True, stop=False),
                     reads=[t_NL[tt], t_c1], writes=[tpc])
                S.op("pe", lambda e: e.matmul(pc[:, 0:16], lhsT=onesf[:], rhs=Rs[:], start=False, stop=True),
                     reads=[t_Rs, t_cst], writes=[tpc])
                S.op("dve", lambda e, tt=tt: e.tensor_copy(out=Ct[:, tt, :], in_=pc[:, 0:16]), writes=[tpc, t_Ct[tt]])
                S.op("dve", lambda e, tt=tt: e.tensor_tensor(out=Rs[:], in0=Rs[:], in1=NL[:, tt, :], op=ALU.add),
                     reads=[t_NL[tt]], writes=[t_Rs])
            t_CS, t_r1, t_ctf = S.tok("CS"), S.tok("r1"), S.tok("ctf")
            for g4 in range(8):
                ps = PS[6]
                tps = t_PS[6]
                for ti in range(4):
                    tt = g4 * 4 + ti
                    S.op("pe", lambda e, tt=tt, ti=ti: e.matmul(ps[0:16, ti * 128:(ti + 1) * 128], lhsT=Ct[:, tt, :],
                                                            rhs=identf[:], start=True, stop=True),
                         reads=[t_Ct[tt], t_c1], writes=[tps])
                S.op("dve", lambda e, g4=g4: e.tensor_scalar(out=CTf[:, g4 * 512:(g4 + 1) * 512], in0=ps[0:16, :],
                                                          scalar1=-1.0, scalar2=None, op0=ALU.mult),
                     writes=[tps, t_ctf])
            tmpb = at("tmpb", [16, S_LEN], BF16, o_aot)
            t_tmpb = S.tok("tmpb")
            S.op("dve", lambda e: e.tensor_copy(out=CS[0:16, :], in_=CTf[:, :]), reads=[t_ctf], writes=[t_CS])
            S.op("dve", lambda e: e.tensor_tensor(out=r1[:, :], in0=CTf[:, :], in1=CS[0:16, :], op=ALU.subtract),
                 reads=[t_ctf, t_CS], writes=[t_r1])
            S.op("dve", lambda e: e.tensor_copy(out=tmpb[:, :], in_=r1[:, :]), reads=[t_r1], writes=[t_tmpb])
            S.op("dve", lambda e: e.tensor_copy(out=CS[32:48, :], in_=tmpb[:, :]), reads=[t_tmpb], writes=[t_CS])
            S.op("dve", lambda e: e.tensor_tensor(out=r1[:, :], in0=r1[:, :], in1=tmpb[:, :], op=ALU.subtract),
                 reads=[t_tmpb], writes=[t_r1])
            S.op("dve", lambda e: e.tensor_copy(out=CS[64:80, :], in_=r1[:, :]), reads=[t_r1], writes=[t_CS])
            S.barrier()
            checkpoint(7)
            if 'g' in DBG:
                S.op("pe", lambda e: e.matmul(PS[6][0:64, 0:16], lhsT=hT[:, 0, 0:64], rhs=hT[:, 0, 0:16],
                                              start=True, stop=True), reads=[t_hT[0]], writes=[t_PS[6]])
            if 'a' not in DBG:
                S.op("dve", lambda e: e.memset(KT[64:67, :], 1.0), writes=[t_KTaug])
            WBqk = WB[:, 0:1024].rearrange("p (c s n) -> p c s n", c=8, s=2)
            checkpoint(71)

            KT2 = at("KT2", [128, S_LEN], BF16, o_wa)
            KTb = [KT, KT2]
            t_KTb = [[S.tok("ktb0"), S.tok("ktb0aug")], [S.tok("ktb1"), S.tok("ktb1aug")]]
            S.op("dve", lambda e: e.memset(KT2[64:67, :], 1.0), writes=[t_KTb[1][1]])
            t_KTb[0][1] = t_KTaug
            WQb = WB[:, 0:512]
            WKb = WB[:, 512:1024]
            WVb = at("WVb", [128, 1024], BF16, o_rc + 2048)
            t_WQb, t_WKb, t_WVb = S.tok("wqb"), S.tok("wkb"), S.tok("wvb")
            t_VAp = S.toks(2, "vap")
            S.op("dve", lambda e: e.memset(VA4[:, :, 0:2, 64:128], 1.0), writes=[t_VAp[0], t_VA])
            S.op("dve", lambda e: e.memset(VA4[:, :, 2:4, 64:128], 1.0), writes=[t_VAp[1], t_VA])
            rot = {"n": 0}

            def bank():
                rot["n"] += 1
                return 4 + rot["n"] % 2

            def load_wk(h):
                S.dma("pool", lambda e: e.dma_start(out=WKb.rearrange("p (c n) -> p c n", c=8),
                                                    in_=wview(o_w_in, K0 + h * 64, K0 + (h + 1) * 64)),
                      writes=[t_WKb])

            def load_wq(h):
                S.dma("pool", lambda e: e.dma_start(out=WQb.rearrange("p (c n) -> p c n", c=8),
                                                    in_=wview(o_w_in, Q0 + h * 64, Q0 + (h + 1) * 64)),
                      writes=[t_WQb])

            def load_wv(p):
                S.dma("pool", lambda e: e.dma_start(out=WVb[:, :].rearrange("p (c n) -> p c n", c=8),
                                                    in_=wview(o_w_in, V0 + p * 128, V0 + (p + 1) * 128)),
                      writes=[t_WVb])

            def k_block(h, b):
                pb = bank()
                proj_fm(PS[pb], t_PS[pb], lambda k: WKb[:, k * 64:(k + 1) * 64], 8,
                        lambda k: hT[:, k, b * 512:(b + 1) * 512], t_hT[4 * b:4 * b + 4], [t_WKb], 64)
                S.op("dve", copy_op("dve", KTb[h % 2][0:64, b * 512:(b + 1) * 512], PS[pb][0:64, :]),
                     writes=[t_PS[pb], t_KTb[h % 2][0]])

            def q_block(h, b):
                pb = bank()
                proj_fm(PS[pb], t_PS[pb], lambda k: WQb[:, k * 64:(k + 1) * 64], 8,
                        lambda k: hT[:, k, b * 512:(b + 1) * 512], t_hT[4 * b:4 * b + 4], [t_WQb], 64)
                S.op("dve", copy_op("dve", QT[0:64, b * 512:(b + 1) * 512], PS[pb][0:64, :], scale=0.125),
                     writes=[t_PS[pb], t_QT[b]])

            def v_tiles(p, t4):
                pb = bank()
                ps = PS[pb]
                s0 = 2 * (p % 2)
                for ti in range(4):
                    tt = t4 * 4 + ti
                    for k in range(8):
                        S.op("pe", lambda e, k=k, tt=tt, ti=ti: e.matmul(
                            ps[:, ti * 128:(ti + 1) * 128], lhsT=hT[:, k, tt * 128:(tt + 1) * 128],
                            rhs=WVb[:, k * 128:(k + 1) * 128], start=(k == 0), stop=(k == 7)),
                            reads=[t_hT[tt], t_WVb], writes=[t_PS[pb]])
                for ti in range(4):
                    tt = t4 * 4 + ti
                    S.op("dve", copy_op("dve", VA4[:, tt, s0:s0 + 2, 0:64],
                                        ps[:, ti * 128:(ti + 1) * 128].rearrange("p (h d) -> p h d", h=2)),
                         writes=[t_PS[pb], t_VAp[p % 2]])

            load_wv(0)
            for t4 in range(8):
                v_tiles(0, t4)
            load_wk(0)
            for b in range(NB):
                k_block(0, b)
            for h in range(16):
                hh = h % 4
                load_wq(h)
                for r_ in range(3):
                    S.dma("sp", lambda e, h=h, r_=r_: e.dma_start(out=QT[64 + r_:65 + r_, :],
                                                                 in_=CS[32 * r_ + h:32 * r_ + h + 1, :]),
                          reads=[t_CS], writes=[t_QTaug])
                bgl = []
                if h + 1 < 16:
                    load_wk(h + 1)
                    bgl += [(lambda h=h, b=b: k_block(h + 1, b)) for b in range(NB)]
                if h % 2 == 1 and h + 1 < 16:
                    load_wv((h + 1) // 2)
                    bgl += [(lambda h=h, t4=t4: v_tiles((h + 1) // 2, t4)) for t4 in range(8)]
                c = h // 2
                po = (h % 2) * 64

                def out_fn(g, c=c, po=po):
                    return AOT[po:po + 64, c, g * 512:(g + 1) * 512], [t_AOT[c][g]]
                attention(67, dense_groups(), lambda j, h=h: Ct[:, j, h:h + 1], 1.0,
                          lambda j, hh=hh: VA4[:, j, hh, :], out_fn, rd_extra=t_Ct,
                          KTt=(KTb[h % 2], t_KTb[h % 2]), pre_group=(lambda g, h=h: q_block(h, g)),
                          bg=bgl, va_tok=t_VAp[(h // 2) % 2], one_rc=True)

            checkpoint(8)
            S.barrier()
            gate_phase(o_w_in, G0)
            checkpoint(9)
            out_phase(o_w_out, x1_d, (t_x1 if mode == "full" else None), last_layer=True)
            S.wait_all("sp", t_out)
        else:
            S.wait_all("sp", t_x1)


    def checkpoint(k):
        if stop == k:
            raise _Stop()

    try:
        _layers()
    except _Stop:
        pass
    S.emit()
    return nc


_CACHE = {}


def _get(mode):
    if mode not in _CACHE:
        _CACHE[mode] = build_program(mode)
    return _CACHE[mode]


L0_KEYS = ["e_g_in", "e_w_in", "e_g_q_a", "e_w_q_up", "e_g_kv_a", "e_w_kv_up", "e_sinks", "e_w_out"]
L1_KEYS = ["o_g_in", "o_w_in", "o_b_f", "o_w_out"]


def _maps(inputs, n, mode, x1=None):
    consts = _constants()
    maps = []
    for b in range(n):
        m = dict(consts)
        if mode in ("full", "l0"):
            m["x"] = np.ascontiguousarray(inputs["x"][b])
            m["positions"] = np.ascontiguousarray(inputs["positions"][b]).astype(np.int32)
            for k in L0_KEYS:
                m[k] = np.ascontiguousarray(inputs[k][0])
        if mode in ("full", "l1"):
            for k in L1_KEYS:
                m[k] = np.ascontiguousarray(inputs[k][0])
            m["g_final"] = np.ascontiguousarray(inputs["g_final"])
        if mode == "l1":
            m["x1"] = np.ascontiguousarray(x1[b])
        maps.append(m)
    return maps


def kernel(**inputs):
    n = inputs["x"].shape[0]
    inputs = {k: np.asarray(v) for k, v in inputs.items()}
    nc = _get("full")
    res = run_bass_kernel_spmd(nc, _maps(inputs, n, "full"), core_ids=list(range(n)))
    return np.stack([np.asarray(r["out"]) for r in res.results], axis=0).astype(np.float32)
```

```python
import math
import os
DBG = os.environ.get('KDBG', '')
import numpy as np
import concourse.bass as bass
import concourse.mybir as mybir
from concourse.bass_utils import run_bass_kernel_spmd

F32 = mybir.dt.float32
BF16 = mybir.dt.bfloat16
I32 = mybir.dt.int32
AF = mybir.ActivationFunctionType
ALU = mybir.AluOpType

S_LEN = 4096
D = 1024
NT = 32
NB = 8
EPS = 1e-6
SEM_ROT = 12000


class Tok:
    __slots__ = ("name", "w", "r", "dsem")

    def __init__(self, name=""):
        self.name = name
        self.w = None
        self.r = {}
        self.dsem = None


class _Rec:
    def __init__(self):
        self.call = None

    def __getattr__(self, name):
        def f(*a, **k):
            assert self.call is None
            self.call = (name, a, k)
            return self
        return f


def _freeze(fn):
    r = _Rec()
    fn(r)
    assert r.call is not None
    return r.call


class Sched:
    ENGS = ("pe", "act", "dve", "pool", "sp")

    def __init__(self, nc):
        self.nc = nc
        self.sems = []
        self.semeng = {}
        self.ops = {e: [] for e in self.ENGS}
        self.esem = {}
        self.ecnt = {}
        self.seen = {e: {} for e in self.ENGS}
        self.semval = {}
        self.unsig = {e: False for e in self.ENGS}
        self.noself = {"pe"}
        for e in ("pe", "act", "dve", "pool"):
            self._new_esem(e)

    def _alloc(self, name, eng=None):
        h = self.nc.alloc_semaphore(name=name)
        self.sems.append(h)
        sid = len(self.sems) - 1
        self.semval[sid] = 0
        self.semeng[sid] = eng
        return sid

    def _new_esem(self, e):
        self.esem[e] = self._alloc(f"s_{e}_{len(self.sems)}", e)
        self.ecnt[e] = 0

    def tok(self, name=""):
        return Tok(name)

    def toks(self, n, name=""):
        return [Tok(f"{name}{i}") for i in range(n)]

    def _collect(self, e, reads, writes):
        need = {}

        def add(s, v):
            if need.get(s, 0) < v:
                need[s] = v
        for t in reads:
            if t.w is not None:
                add(*t.w)
        for t in writes:
            if t.w is not None:
                add(*t.w)
            for s, v in t.r.items():
                add(s, v)
        waits = []
        seen = self.seen[e]
        for s, v in need.items():
            if e in self.noself and self.semeng[s] == e:
                continue
            if seen.get(s, 0) >= v:
                continue
            seen[s] = v
            waits.append((s, v))
        return waits

    def _mark(self, s, val, reads, writes):
        for t in reads:
            if t.r.get(s, 0) < val:
                t.r[s] = val
        for t in writes:
            t.w = (s, val)
            t.r = {}

    def op(self, e, fn, reads=(), writes=(), sig=True):
        waits = self._collect(e, reads, writes)
        if sig and self.ecnt[e] >= SEM_ROT and not self.unsig[e]:
            self._new_esem(e)
        self.unsig[e] = not sig
        s = self.esem[e]
        val = self.ecnt[e] + 1
        if sig:
            self.ecnt[e] = val
            self.semval[s] = val
        self.ops[e].append((waits, _freeze(fn), (s, 1) if sig else None))
        self._mark(s, val, reads, writes)

    def barrier(self):
        cur = [(s, v) for s, v in self.semval.items() if v > 0]
        for e in self.ENGS:
            waits = []
            for s, v in cur:
                if (self.semeng[s] == e and e in self.noself) or self.seen[e].get(s, 0) >= v:
                    continue
                self.seen[e][s] = v
                waits.append((s, v))
            self.ops[e].append((waits, None, None))

    def dma(self, q, fn, reads=(), writes=(), owner=None):
        waits = self._collect(q, reads, writes)
        if owner is None:
            owner = writes[0] if writes else reads[0]
        if owner.dsem is None or self.semval[owner.dsem] >= SEM_ROT * 2:
            owner.dsem = self._alloc(f"d_{len(self.sems)}")
        s = owner.dsem
        self.semval[s] += 16
        val = self.semval[s]
        self.ops[q].append((waits, _freeze(fn), (s, 16)))
        self._mark(s, val, reads, writes)

    def wait_all(self, e, toks):
        waits = self._collect(e, [], toks)
        self.ops[e].append((waits, None, None))

    def emit(self):
        nc = self.nc
        sems = self.sems
        ops = self.ops

        def replay(eng, lst):
            for waits, fn, inc in lst:
                for s, v in waits:
                    eng.wait_ge(sems[s], v)
                if fn is None:
                    continue
                name, a, k = fn
                ins = getattr(eng, name)(*a, **k)
                if inc is not None:
                    ins.then_inc(sems[inc[0]], inc[1])

        with nc.Block() as block:
            @block.tensor
            def _(eng):
                replay(eng, ops["pe"])

            @block.scalar
            def _(eng):
                replay(eng, ops["act"])

            @block.vector
            def _(eng):
                replay(eng, ops["dve"])

            @block.gpsimd
            def _(eng):
                replay(eng, ops["pool"])

            @block.sync
            def _(eng):
                replay(eng, ops["sp"])


def _constants():
    c = {}
    c["c_ident"] = np.eye(128, dtype=np.float32)
    k = np.arange(128)[:, None]
    q = np.arange(128)[None, :]
    c["c_tri"] = (k <= q).astype(np.float32)
    c["c_ones"] = np.ones((128, 128), np.float32)
    qq = np.arange(512)[None, None, :]
    kk = np.arange(128)[:, None, None]
    rr = np.arange(4)[None, :, None]
    c["c_mask"] = ((rr * 128 + kk) <= qq).astype(np.float32)
    slopes = 2.0 ** (-8.0 * (np.arange(8, dtype=np.float64) + 1.0) / 8)
    t = np.zeros((128, 8, 2, 128), np.float64)
    kq = np.arange(128)[:, None]
    qv = np.arange(128)[None, :]
    for h in range(8):
        dist0 = 128 + qv - kq
        t[:, h, 0, :] = np.where(dist0 < 128, np.exp(-slopes[h] * dist0), 0.0)
        dist1 = qv - kq
        t[:, h, 1, :] = np.where(dist1 >= 0, np.exp(-slopes[h] * np.maximum(dist1, 0)), 0.0)
    c["c_swt"] = t.astype(np.float32)
    invf = 1.0 / (10000.0 ** (np.arange(0, 32, 2, dtype=np.float32) / 32))
    v = np.zeros((128, 2), np.float32)
    for p in range(64, 128):
        v[p, 0] = invf[(p - 64) % 16]
        v[p, 1] = (math.pi / 2) if p < 96 else 0.0
    c["c_rope"] = v
    return c


CONST_SHAPES = {"c_ident": [128, 128], "c_tri": [128, 128], "c_ones": [128, 128],
                "c_mask": [128, 4, 512], "c_swt": [128, 8, 2, 128], "c_rope": [128, 2]}


class _Stop(Exception):
    pass


def build_program(mode="full", stop=0):
    nc = bass.Bass("TRN2", target_bir_lowering=False)
    S = Sched(nc)
    do0 = mode in ("full", "l0")
    do1 = mode in ("full", "l1")

    def din(name, shape, dt=F32):
        return nc.dram_tensor(name, shape, dt, kind="ExternalInput").ap()

    cst = {k: din(k, v) for k, v in CONST_SHAPES.items()}
    if do0:
        x_d = din("x", [S_LEN, D])
        pos_d = din("positions", [S_LEN], I32)
        e_g_in = din("e_g_in", [D])
        e_w_in = din("e_w_in", [D, 2208])
        e_g_q_a = din("e_g_q_a", [256])
        e_w_q_up = din("e_w_q_up", [256, 768])
        e_g_kv_a = din("e_g_kv_a", [128])
        e_w_kv_up = din("e_w_kv_up", [128, 1024])
        e_sinks = din("e_sinks", [8])
        e_w_out = din("e_w_out", [D, D])
    if do1:
        o_g_in = din("o_g_in", [D])
        o_w_in = din("o_w_in", [D, 4112])
        o_b_f = din("o_b_f", [16])
        o_w_out = din("o_w_out", [D, D])
        g_final = din("g_final", [D])
        out_d = nc.dram_tensor("out", [S_LEN, D], F32, kind="ExternalOutput").ap()
    if mode == "full":
        x1_d = nc.dram_tensor("x1s", [S_LEN, D], F32).ap()
    elif mode == "l0":
        x1_d = nc.dram_tensor("x1", [S_LEN, D], F32, kind="ExternalOutput").ap()
    else:
        x1_d = din("x1", [S_LEN, D])
    t_x1 = S.toks(NT, "x1d")
    t_x1own = S.tok("x1own")
    t_outown = S.toks(2, "outown")
    t_out = S.toks(NT, "outd")

    def region(nbytes):
        st, _ = nc.bump_sbuf(nbytes)
        return st

    def at(name, shape, dt, off):
        return nc.alloc_sbuf_tensor_at(name, shape, dt, offset=off)

    def sb(name, shape, dt):
        return nc.alloc_sbuf_tensor(name, shape, dt)

    hT = sb("hT", [128, 8, S_LEN], BF16)
    t_hT = S.toks(NT, "hT")
    o_aot = region(65536)
    AOT = at("AOT", [128, 8, S_LEN], BF16, o_aot)
    t_AOT = [[S.tok(f"aot{c}_{b}") for b in range(NB)] for c in range(8)]
    o_big = region(32768)
    BIG = at("BIG", [128, 32 * 4 * 128], BF16, o_big)
    t_VA = S.tok("VA")
    t_LAT = S.toks(NB, "LAT")
    o_trig = region(8192)
    TRIG = at("TRIG", [128, S_LEN], BF16, o_trig)
    t_TRIG = S.tok("TRIG")
    o_pool = region(19456)
    QT = at("QT", [128, S_LEN], BF16, o_pool)
    t_QT = S.toks(NB, "QT")
    t_QTaug = S.tok("QTaug")
    KT = at("KT", [128, S_LEN], BF16, o_pool + 8192)
    t_KT = S.tok("KT")
    t_KTaug = S.tok("KTaug")
    NPT = 5
    PT = [at(f"pt{i}", [128, 512], BF16, o_pool + 16384 + 1024 * i) for i in range(3)]
    PT += [sb(f"ptx{i}", [128, 512], BF16) for i in range(NPT - 3)]
    t_PT = S.toks(NPT, "pt")
    XT = [at(f"xt{i}", [128, D], F32, o_pool + 4096 * i) for i in range(3)]
    t_XT = S.toks(3, "xt")
    XB = [at(f"xb{i}", [128, D], BF16, o_pool + 12288 + 2048 * i) for i in range(2)]
    t_XB = S.toks(2, "xb")
    ET = [at(f"et{i}", [128, 512], F32, o_pool + 2048 * i) for i in range(3)]
    t_ET = S.toks(3, "et")
    o_pf = region(1024)
    PF = [at(f"pf{i}", [128, 128], F32, o_pf + 512 * i) for i in range(2)]
    t_PF = S.toks(2, "pf")
    o_rc = region(4096)
    RC = [at(f"rc{i}", [128, 512], F32, o_rc + 2048 * i) for i in range(2)]
    t_RC = S.toks(2, "rc")
    GB = at("GB", [128, D], F32, o_rc)
    t_GB = S.tok("GB")
    o_wa = region(8192)
    WA = at("WA", [128, 3328], BF16, o_wa)
    t_WA = S.tok("WA")
    WB = sb("WB", [128, 1024], BF16)
    t_WB = S.tok("WB")
    MASK = sb("MASK", [128, 128], BF16)
    t_MASK = S.tok("MASK")
    identb = sb("identb", [128, 128], BF16)
    onesf = sb("onesf", [128, 128], F32)
    t_cst = S.tok("cst")
    stat = sb("stat", [128, 8], F32)
    t_statS = S.toks(2, "stat")
    epsb = sb("epsb", [128, 2], F32)
    t_eps = S.tok("eps")

    PS = [nc.alloc_psum_tensor(f"ps{i}", [128, 512], F32) for i in range(7)]
    t_PS = S.toks(7, "ps")
    PST = nc.alloc_psum_tensor("pst", [128, 8, 128], BF16)
    t_PST = S.tok("pst")

    S.dma("sp", lambda e: e.dma_start(out=onesf[:], in_=cst["c_ones"][:, :]), writes=[t_cst])
    t_cstb = S.tok("cstb")
    S.dma("pool", lambda e: e.dma_start(out=identb[:], in_=cst["c_ident"][:, :]), writes=[t_cstb])
    S.dma("pool", lambda e: e.dma_start(out=MASK[:], in_=cst["c_tri"][:, :]), writes=[t_MASK])
    S.op("dve", lambda e: e.memset(epsb[:, 0:1], EPS), writes=[t_eps])
    S.op("dve", lambda e: e.memset(epsb[:, 1:2], 1.0), writes=[t_eps])

    cnt = {"ev": 0}

    def evac_engine():
        cnt["ev"] += 1
        return "act" if cnt["ev"] % 2 else "dve"

    def copy_op(eng, out, in_, scale=None):
        if eng == "act":
            if scale is None:
                return lambda e: e.copy(out=out, in_=in_)
            return lambda e: e.mul(out=out, in_=in_, mul=scale)
        if scale is None:
            return lambda e: e.tensor_copy(out=out, in_=in_)
        return lambda e: e.tensor_scalar(out=out, in0=in_, scalar1=scale, scalar2=None, op0=ALU.mult)

    def load_gain(g_d):
        S.dma("sp", lambda e: e.dma_start(out=GB[:], in_=bass.AP(g_d.tensor, 0, [[0, 128], [1, D]])),
              writes=[t_GB])

    def rstd_from_ms(col, tst):
        S.op("act", lambda e: e.activation(out=stat[:, col + 1:col + 2], in_=stat[:, col:col + 1], func=AF.Ln,
                                           bias=epsb[:, 0:1], scale=1.0), reads=[t_eps], writes=[tst])
        S.op("act", lambda e: e.activation(out=stat[:, col + 1:col + 2], in_=stat[:, col + 1:col + 2],
                                           func=AF.Exp, scale=-0.5), writes=[tst])

    def norm_tile(xt, t_xt, out_ap, t_outs, junk_ap, t_junk, slot=0):
        c0 = 4 * slot
        tst = t_statS[slot]
        S.op("dve", lambda e: e.memset(stat[:, c0:c0 + 1], 0.0), writes=[tst])
        S.op("act", lambda e: e.activation(out=junk_ap, in_=xt[:], func=AF.Square, scale=1.0 / 32,
                                           accum_out=stat[:, c0:c0 + 1]), reads=[t_xt], writes=[t_junk, tst])
        rstd_from_ms(c0, tst)
        S.op("dve", lambda e: e.scalar_tensor_tensor(out=out_ap, in0=xt[:], scalar=stat[:, c0 + 1:c0 + 2], in1=GB[:],
                                                     op0=ALU.mult, op1=ALU.mult),
             reads=[t_xt, tst, t_GB], writes=list(t_outs))

    def to_hT(xt, t_xt, tt):
        i = tt % 2
        norm_tile(xt, t_xt, XB[i][:], [t_XB[i]], XB[i][:], t_XB[i], slot=i)
        for c in range(8):
            S.op("pe", lambda e, c=c: e.transpose(out=PST[:, c, :], in_=XB[i][:, c * 128:(c + 1) * 128],
                                                   identity=identb[:]),
                 reads=[t_XB[i], t_cstb], writes=[t_PST])
        eng = evac_engine()
        S.op(eng, copy_op(eng, hT[:, :, tt * 128:(tt + 1) * 128], PST[:, :, :]), writes=[t_PST, t_hT[tt]])

    def phase_A(src_d):
        for tt in range(NT):
            i = tt % 2
            S.dma("sp", lambda e, tt=tt, i=i: e.dma_start(out=XT[i][:], in_=src_d[tt * 128:(tt + 1) * 128, :]),
                  writes=[t_XT[i]])
            to_hT(XT[i], t_XT[i], tt)

    def wview(w_d, c0, c1):
        return w_d.rearrange("(c p) n -> p c n", p=128)[:, :, c0:c1]

    def proj_fm(ps, t_ps, w_ap_fn, nk, src_fn, src_toks, w_toks, M):
        for k in range(nk):
            S.op("pe", lambda e, k=k: e.matmul(ps[0:M, :], lhsT=w_ap_fn(k), rhs=src_fn(k),
                                               start=(k == 0), stop=(k == nk - 1)),
                 reads=list(src_toks) + list(w_toks), writes=[t_ps])

    def attention(KR, groups, bias_fn, scale, va_fn, out_fn, den_add=None, fp32_tables=False, rd_extra=(),
                  KTt=None, pre_group=None, bg=(), va_tok=None, one_rc=False):
        steps = []
        for gi, (q0, QW, kbs, gidx) in enumerate(groups):
            for n, (j, c0, tbl) in enumerate(kbs):
                steps.append((gi, q0, QW, j, c0, tbl, n == 0, n == len(kbs) - 1, gidx))

        SB_ = (0, 1, 6)
        LA = 2
        KTx, t_KTx = (KT, [t_KT, t_KTaug]) if KTt is None else KTt
        t_VAx = t_VA if va_tok is None else va_tok
        bg = list(bg)
        nsteps = sum(len(g_[2]) for g_ in groups)
        bg_stride = max(1, nsteps // (len(bg) + 1)) if bg else 0
        seen_groups = set()

        def emit_qk(i):
            gi, q0, QW, j, c0, tbl, first, last, gidx = steps[i]
            sp = PS[SB_[i % 3]]
            S.op("pe", lambda e: e.matmul(sp[:, c0:QW], lhsT=KTx[0:KR, j * 128:(j + 1) * 128],
                                          rhs=QT[0:KR, q0 + c0:q0 + QW], start=True, stop=True),
                 reads=list(t_KTx) + [t_QT[q0 // 512], t_QTaug], writes=[t_PS[SB_[i % 3]]])

        first_idx = {}
        for i_, st_ in enumerate(steps):
            first_idx.setdefault(st_[0], i_)
        if pre_group is not None:
            pre_group(groups[0][3])
        for i0 in range(min(LA, len(steps))):
            emit_qk(i0)
        for i, (gi, q0, QW, j, c0, tbl, first, last, gidx) in enumerate(steps):
            if i + LA < len(steps):
                emit_qk(i + LA)
            if pre_group is not None and i == first_idx[gi] + 1 and gi + 1 < len(groups):
                pre_group(groups[gi + 1][3])
            if bg and i > 0 and i % bg_stride == 0:
                bg.pop(0)()
            sp = PS[SB_[i % 3]]
            tsp = t_PS[SB_[i % 3]]
            pt = PT[i % NPT]
            tpt = t_PT[i % NPT]
            kw = {"scale": scale}
            b = bias_fn(j) if bias_fn is not None else None
            if b is not None:
                kw["bias"] = b
            if tbl is not None and fp32_tables:
                pf = PF[i % 2]
                S.op("act", lambda e, kw=kw, pf=pf, sp=sp, QW=QW: e.activation(out=pf[:, 0:QW], in_=sp[:, 0:QW],
                                                                          func=AF.Exp, **kw),
                     reads=list(rd_extra), writes=[tsp, t_PF[i % 2]])
                S.op("dve", lambda e, pf=pf, pt=pt, tbl=tbl, QW=QW: e.tensor_tensor(out=pt[:, 0:QW], in0=pf[:, 0:QW],
                                                                              in1=tbl, op=ALU.mult),
                     reads=[t_PF[i % 2]] + list(rd_extra), writes=[tpt])
            else:
                S.op("act", lambda e, kw=kw, pt=pt, sp=sp, QW=QW, c0=c0: e.activation(
                    out=pt[:, c0:QW], in_=sp[:, c0:QW], func=AF.Exp, **kw),
                    reads=list(rd_extra), writes=[tsp, tpt])
                if tbl is not None:
                    S.op("dve", lambda e, pt=pt, tbl=tbl, c0=c0: e.tensor_tensor(
                        out=pt[:, c0:c0 + 128], in0=pt[:, c0:c0 + 128], in1=tbl, op=ALU.mult),
                        reads=[t_MASK], writes=[tpt])
            op_ = PS[2 + gi % 2]
            top = t_PS[2 + gi % 2]
            S.op("pe", lambda e, op_=op_, pt=pt, j=j, first=first, last=last, QW=QW, c0=c0:
                 e.matmul(op_[:, c0:QW], lhsT=va_fn(j), rhs=pt[:, c0:QW], start=first, stop=last),
                 reads=[tpt, t_VAx], writes=[top])
            if last:
                rc = RC[0 if one_rc else gi % 2]
                trc = t_RC[0 if one_rc else gi % 2]
                if den_add is not None:
                    S.op("dve", lambda e, rc=rc, op_=op_, QW=QW: e.tensor_scalar(
                        out=rc[64:128, 0:QW], in0=op_[64:128, 0:QW], scalar1=den_add, scalar2=None, op0=ALU.add),
                        reads=list(rd_extra), writes=[top, trc])
                    S.op("dve", lambda e, rc=rc, QW=QW: e.reciprocal(out=rc[64:128, 0:QW], in_=rc[64:128, 0:QW]),
                         writes=[trc])
                else:
                    S.op("dve", lambda e, rc=rc, op_=op_, QW=QW: e.reciprocal(out=rc[64:128, 0:QW],
                                                                         in_=op_[64:128, 0:QW]),
                         writes=[top, trc])
                o_ap, o_toks = out_fn(gidx)
                S.op("dve", lambda e, rc=rc, op_=op_, o_ap=o_ap, QW=QW: e.tensor_tensor(
                    out=o_ap, in0=op_[0:64, 0:QW], in1=rc[64:128, 0:QW], op=ALU.mult),
                    reads=[trc], writes=[top] + list(o_toks))
        while bg:
            bg.pop(0)()

    def _unused():
        pass

    def dense_groups():
        gs = []
        for g in range(NB):
            kbs = [(j, 0, None) for j in range(4 * g)] + [(4 * g + r, r * 128, MASK[:, :]) for r in range(4)]
            gs.append((g * 512, 512, kbs, g))
        return gs

    def gate_phase(w_in_d, goff):
        for c in range(8):
            S.dma("pool", lambda e, c=c: e.dma_start(
                out=WB[:, 0:1024].rearrange("p (c n) -> p c n", c=8),
                in_=wview(w_in_d, goff + c * 128, goff + (c + 1) * 128)), writes=[t_WB])
            for b in range(NB):
                ps = PS[4 + b % 2]
                tps = t_PS[4 + b % 2]
                proj_fm(ps, tps, lambda k: WB[:, k * 128:(k + 1) * 128], 8,
                        lambda k, b=b: hT[:, k, b * 512:(b + 1) * 512], t_hT[4 * b:4 * b + 4], [t_WB], 128)
                gt = PT[b % NPT]
                S.op("act", lambda e, gt=gt, ps=ps: e.activation(out=gt[:], in_=ps[:], func=AF.Silu),
                     writes=[tps, t_PT[b % NPT]])
                S.op("dve", lambda e, gt=gt, c=c, b=b: e.tensor_tensor(
                    out=AOT[:, c, b * 512:(b + 1) * 512], in0=AOT[:, c, b * 512:(b + 1) * 512], in1=gt[:],
                    op=ALU.mult), reads=[t_PT[b % NPT]], writes=[t_AOT[c][b]])

    def out_phase(w_out_d, res_d, t_res, last_layer, next_gain_d=None):
        S.barrier()
        WO = at("WO_%d" % int(last_layer), [128, 8 * 1024], BF16, o_big)
        t_WO = S.tok("WO")
        S.dma("pool", lambda e: e.dma_start(out=WO[:].rearrange("p (c n) -> p c n", c=8),
                                            in_=wview(w_out_d, 0, D)), writes=[t_WO])
        if last_layer:
            load_gain(g_final)
        elif next_gain_d is not None:
            load_gain(next_gain_d)
        t_x1o = S.toks(3, "x1own")
        t_outo = S.toks(3, "outown")
        def stage1(tt):
            i = tt % 3
            b = tt // 4
            pp = 2 * (tt % 2)
            S.dma("sp", lambda e: e.dma_start(out=XT[i][:], in_=res_d[tt * 128:(tt + 1) * 128, :]),
                  reads=[t_res[tt]] if t_res is not None else [], writes=[t_XT[i]])
            for half in range(2):
                ps = PS[pp + half]
                tps = t_PS[pp + half]
                for c in range(8):
                    S.op("pe", lambda e: e.matmul(
                        ps[:, :], lhsT=AOT[:, c, tt * 128:(tt + 1) * 128],
                        rhs=WO[:, c * 1024 + half * 512: c * 1024 + (half + 1) * 512],
                        start=(c == 0), stop=(c == 7)),
                        reads=[t_AOT[c][b], t_WO], writes=[tps])

        def stage1b(tt):
            i = tt % 3
            pp = 2 * (tt % 2)
            for half in range(2):
                ps = PS[pp + half]
                tps = t_PS[pp + half]
                S.op("dve", lambda e: e.tensor_tensor(
                    out=XT[i][:, half * 512:(half + 1) * 512], in0=ps[:, :], in1=XT[i][:, half * 512:(half + 1) * 512],
                    op=ALU.add), writes=[tps, t_XT[i]])

        def stage2(tt):
            i = tt % 3
            if last_layer:
                norm_tile(XT[i], t_XT[i], XT[i][:], [t_XT[i]], XB[tt % 2][:], t_XB[tt % 2], slot=tt % 2)
                S.dma("sp", lambda e: e.dma_start(out=out_d[tt * 128:(tt + 1) * 128, :], in_=XT[i][:]),
                      reads=[t_XT[i]], writes=[t_out[tt]], owner=t_outo[i])
            else:
                S.dma("sp", lambda e: e.dma_start(out=x1_d[tt * 128:(tt + 1) * 128, :], in_=XT[i][:]),
                      reads=[t_XT[i]], writes=[t_x1[tt]], owner=t_x1o[i])
                if mode == "full":
                    to_hT(XT[i], t_XT[i], tt)

        for tt in range(NT + 1):
            if tt < NT:
                stage1(tt)
            if tt >= 1:
                stage2(tt - 1)
            if tt < NT:
                stage1b(tt)
        S.barrier()

    def _layers():
        if do0:
            VA0 = at("VA0", [128, 32, 128], BF16, o_big)
            LATt = at("LATt", [128, 3, S_LEN], BF16, o_big + 8192)
            SWT = at("SWT", [128, 2048], F32, o_big + 8192)
            posi = at("posi", [128, S_LEN], I32, o_aot)
            kf = at("kf", [128, S_LEN], F32, o_aot + 16384)
            ANG = at("ANG", [128, S_LEN], F32, o_aot + 32768)
            t_pos, t_kf, t_ang = S.tok("posi"), S.tok("kf"), S.tok("ang")
            load_gain(e_g_in)
            phase_A(x_d)

            gq = sb("gq", [128, 2], F32)
            gkv = sb("gkv", [128, 1], F32)
            esink = sb("esink", [128, 8], F32)
            ropec = sb("ropec", [128, 2], F32)
            t_sm = S.tok("small0")
            for c2 in range(2):
                S.dma("sp", lambda e, c2=c2: e.dma_start(
                    out=gq[:, c2:c2 + 1], in_=e_g_q_a[c2 * 128:(c2 + 1) * 128].rearrange("(p o) -> p o", o=1)),
                    writes=[t_sm])
            S.dma("sp", lambda e: e.dma_start(out=gkv[:], in_=e_g_kv_a.rearrange("(p o) -> p o", o=1)), writes=[t_sm])
            S.dma("sp", lambda e: e.dma_start(out=esink[:], in_=bass.AP(e_sinks.tensor, 0, [[0, 128], [1, 8]])),
                  writes=[t_sm])
            S.dma("sp", lambda e: e.dma_start(out=ropec[:], in_=cst["c_rope"][:, :]), writes=[t_sm])
            S.op("act", lambda e: e.activation(out=esink[:], in_=esink[:], func=AF.Exp), writes=[t_sm])

            S.dma("sp", lambda e: e.dma_start(out=posi[64:128, :], in_=bass.AP(pos_d.tensor, 0, [[0, 64], [1, S_LEN]])),
                  writes=[t_pos])
            P6 = slice(64, 128)
            S.op("dve", lambda e: e.tensor_copy(out=ANG[P6, :], in_=posi[P6, :]), reads=[t_pos], writes=[t_ang])
            S.op("dve", lambda e: e.tensor_scalar(out=ANG[P6, :], in0=ANG[P6, :], scalar1=ropec[P6, 0:1],
                                                  scalar2=ropec[P6, 1:2], op0=ALU.mult, op1=ALU.add),
                 reads=[t_sm], writes=[t_ang])
            S.op("dve", lambda e: e.tensor_scalar(out=kf[P6, :], in0=ANG[P6, :], scalar1=1.0 / (2 * math.pi),
                                                  scalar2=0.5, op0=ALU.mult, op1=ALU.add),
                 reads=[t_ang], writes=[t_kf])
            S.op("dve", lambda e: e.tensor_copy(out=posi[P6, :], in_=kf[P6, :]), reads=[t_kf], writes=[t_pos])
            S.op("dve", lambda e: e.tensor_copy(out=kf[P6, :], in_=posi[P6, :]), reads=[t_pos], writes=[t_kf])
            C1 = 6.28125
            C2 = 2 * math.pi - C1
            S.op("dve", lambda e: e.scalar_tensor_tensor(out=ANG[P6, :], in0=kf[P6, :], scalar=-C1, in1=ANG[P6, :],
                                                         op0=ALU.mult, op1=ALU.add), reads=[t_kf], writes=[t_ang])
            S.op("dve", lambda e: e.scalar_tensor_tensor(out=ANG[P6, :], in0=kf[P6, :], scalar=-C2, in1=ANG[P6, :],
                                                         op0=ALU.mult, op1=ALU.add), reads=[t_kf], writes=[t_ang])
            S.op("dve", lambda e: e.tensor_single_scalar(out=kf[P6, :], in_=ANG[P6, :], scalar=-math.pi, op=ALU.is_lt),
                 reads=[t_ang], writes=[t_kf])
            S.op("dve", lambda e: e.scalar_tensor_tensor(out=ANG[P6, :], in0=kf[P6, :], scalar=2 * math.pi,
                                                         in1=ANG[P6, :], op0=ALU.mult, op1=ALU.add),
                 reads=[t_kf], writes=[t_ang])
            S.op("dve", lambda e: e.tensor_scalar(out=ANG[P6, :], in0=ANG[P6, :], scalar1=-3.1415925, scalar2=3.1415925,
                                                  op0=ALU.max, op1=ALU.min), writes=[t_ang])
            S.op("act", lambda e: e.activation(out=TRIG[P6, :], in_=ANG[P6, :], func=AF.Sin), reads=[t_ang],
                 writes=[t_TRIG])
            S.barrier()
            checkpoint(1)

            S.dma("sp", lambda e: e.dma_start(out=SWT[:, :], in_=cst["c_swt"].rearrange("p h r q -> p (h r q)")),
                  writes=[t_kf])
            S.op("dve", lambda e: e.memset(VA0[:, :, 64:128], 1.0), writes=[t_VA])
            SWA_Q0, SWA_K0, SWA_V0 = 416, 928, 1056
            WBkv = WB[:, 0:1024].rearrange("p (c s n) -> p c s n", c=8, s=2)
            for h in range(8):
                kv = h // 4
                if h % 4 == 0:
                    S.dma("pool", lambda e, kv=kv: e.dma_start(
                        out=WBkv[:, :, 0, :], in_=wview(e_w_in, SWA_K0 + kv * 64, SWA_K0 + (kv + 1) * 64)), writes=[t_WB])
                    S.dma("pool", lambda e, kv=kv: e.dma_start(
                        out=WBkv[:, :, 1, :], in_=wview(e_w_in, SWA_V0 + kv * 64, SWA_V0 + (kv + 1) * 64)), writes=[t_WB])
                    for b in range(NB):
                        ps = PS[4 + b % 2]
                        tps = t_PS[4 + b % 2]
                        proj_fm(ps, tps, lambda k: WB[:, k * 128:k * 128 + 64], 8,
                                lambda k, b=b: hT[:, k, b * 512:(b + 1) * 512], t_hT[4 * b:4 * b + 4], [t_WB], 64)
                        eng = evac_engine()
                        S.op(eng, copy_op(eng, KT[0:64, b * 512:(b + 1) * 512], ps[0:64, :]), writes=[tps, t_KT])
                    for t8 in range(4):
                        ps = PS[4 + t8 % 2]
                        tps = t_PS[4 + t8 % 2]
                        for ti in range(8):
                            tt = t8 * 8 + ti
                            for k in range(8):
                                S.op("pe", lambda e, k=k, tt=tt, ti=ti, ps=ps: e.matmul(
                                    ps[:, ti * 64:(ti + 1) * 64], lhsT=hT[:, k, tt * 128:(tt + 1) * 128],
                                    rhs=WB[:, k * 128 + 64:k * 128 + 128], start=(k == 0), stop=(k == 7)),
                                    reads=[t_hT[tt], t_WB], writes=[tps])
                        eng = evac_engine()
                        S.op(eng, copy_op(eng, VA0[:, t8 * 8:(t8 + 1) * 8, 0:64],
                                          ps[:, :].rearrange("p (t d) -> p t d", t=8)), writes=[tps, t_VA])
                S.dma("pool", lambda e, h=h: e.dma_start(
                    out=WA[:, 0:512].rearrange("p (c n) -> p c n", c=8),
                    in_=wview(e_w_in, SWA_Q0 + h * 64, SWA_Q0 + (h + 1) * 64)), writes=[t_WA])
                for b in range(NB):
                    ps = PS[4 + b % 2]
                    tps = t_PS[4 + b % 2]
                    proj_fm(ps, tps, lambda k: WA[:, k * 64:(k + 1) * 64], 8,
                            lambda k, b=b: hT[:, k, b * 512:(b + 1) * 512], t_hT[4 * b:4 * b + 4], [t_WA], 64)
                    eng = evac_engine()
                    S.op(eng, copy_op(eng, QT[0:64, b * 512:(b + 1) * 512], ps[0:64, :], scale=0.125),
                         writes=[tps, t_QT[b]])
                groups = []
                for g in range(NT):
                    kbs = []
                    if g > 0:
                        kbs.append((g - 1, 0, SWT[:, (h * 2 + 0) * 128:(h * 2 + 1) * 128]))
                    kbs.append((g, 0, SWT[:, (h * 2 + 1) * 128:(h * 2 + 2) * 128]))
                    groups.append((g * 128, 128, kbs, g))
                c = 4 + h // 2
                po = (h % 2) * 64

                def out_fn(g, c=c, po=po):
                    return AOT[po:po + 64, c, g * 128:(g + 1) * 128], [t_AOT[c][g // 4]]
                attention(64, groups, None, 1.0, lambda j: VA0[:, j, :], out_fn,
                          den_add=esink[64:128, h:h + 1], fp32_tables=True, rd_extra=[t_sm, t_kf])
            S.barrier()
            checkpoint(2)

            W416 = WA[:, 0:8 * 416].rearrange("p (c n) -> p c n", c=8)
            S.dma("pool", lambda e: e.dma_start(out=W416, in_=wview(e_w_in, 0, 416)), writes=[t_WA])
            WROT = WB[:, 0:8 * 96].rearrange("p (c n) -> p c n", c=8)
            S.op("dve", lambda e: e.memset(WROT[:, :, 0:64], 0.0), writes=[t_WB])
            S.op("dve", lambda e: e.tensor_scalar(out=WROT[:, :, 64:80], in0=W416[:, :, 400:416], scalar1=-1.0,
                                                  scalar2=None, op0=ALU.mult), reads=[t_WA], writes=[t_WB])
            S.op("dve", lambda e: e.tensor_copy(out=WROT[:, :, 80:96], in_=W416[:, :, 384:400]), reads=[t_WA],
                 writes=[t_WB])
            for b in range(NB):
                bs = slice(b * 512, (b + 1) * 512)
                hsrc = lambda k, b=b: hT[:, k, b * 512:(b + 1) * 512]
                ht = t_hT[4 * b:4 * b + 4]
                proj_fm(PS[0], t_PS[0], lambda k: W416[:, k, 0:128], 8, hsrc, ht, [t_WA], 128)
                proj_fm(PS[1], t_PS[1], lambda k: W416[:, k, 128:256], 8, hsrc, ht, [t_WA], 128)
                proj_fm(PS[2], t_PS[2], lambda k: W416[:, k, 256:384], 8, hsrc, ht, [t_WA], 128)
                proj_fm(PS[3], t_PS[3], lambda k: W416[:, k, 320:416], 8, hsrc, ht, [t_WA], 96)
                proj_fm(PS[4], t_PS[4], lambda k: WROT[:, k, :], 8, hsrc, ht, [t_WB], 96)
                for n in range(3):
                    S.op("act", lambda e, n=n: e.activation(out=ET[n][:], in_=PS[n][:], func=AF.Square),
                         writes=[t_PS[n], t_ET[n]])
                S.op("pe", lambda e: e.matmul(PS[5][:, :], lhsT=onesf[:], rhs=ET[0][:], start=True, stop=False),
                     reads=[t_ET[0], t_cst], writes=[t_PS[5]])
                S.op("pe", lambda e: e.matmul(PS[5][:, :], lhsT=onesf[:], rhs=ET[1][:], start=False, stop=True),
                     reads=[t_ET[1], t_cst], writes=[t_PS[5]])
                S.op("pe", lambda e: e.matmul(PS[6][:, :], lhsT=onesf[:], rhs=ET[2][:], start=True, stop=True),
                     reads=[t_ET[2], t_cst], writes=[t_PS[6]])
                for (pi, n_, n) in ((5, 256.0, 0), (6, 128.0, 1)):
                    S.op("act", lambda e, pi=pi, n_=n_, n=n: e.activation(out=ET[n][:], in_=PS[pi][:], func=AF.Ln,
                                                                    bias=epsb[:, 0:1], scale=1.0 / n_),
                         reads=[t_eps], writes=[t_PS[pi], t_ET[n]])
                    S.op("act", lambda e, n=n: e.activation(out=ET[n][:], in_=ET[n][:], func=AF.Exp, scale=-0.5),
                         writes=[t_ET[n]])
                for (pi, chunk, gsc, n) in ((0, 0, gq[:, 0:1], 0), (1, 1, gq[:, 1:2], 0), (2, 2, gkv[:, 0:1], 1)):
                    S.op("dve", lambda e, pi=pi, chunk=chunk, gsc=gsc, n=n, bs=bs: e.scalar_tensor_tensor(
                        out=LATt[:, chunk, bs], in0=PS[pi][:], scalar=gsc, in1=ET[n][:], op0=ALU.mult, op1=ALU.mult),
                        reads=[t_ET[n], t_sm], writes=[t_PS[pi], t_LAT[b]])
                S.op("dve", lambda e, bs=bs: e.tensor_tensor(out=RC[0][64:96, :], in0=PS[3][64:96, :], in1=TRIG[64:96, bs],
                                                         op=ALU.mult), reads=[t_TRIG], writes=[t_PS[3], t_RC[0]])
                S.op("dve", lambda e, bs=bs: e.tensor_tensor(out=RC[1][64:96, :], in0=PS[4][64:96, :], in1=TRIG[96:128, bs],
                                                         op=ALU.mult), reads=[t_TRIG], writes=[t_PS[4], t_RC[1]])
                S.op("dve", lambda e, bs=bs: e.tensor_tensor(out=KT[64:96, bs], in0=RC[0][64:96, :], in1=RC[1][64:96, :],
                                                         op=ALU.add), reads=[t_RC[0], t_RC[1]], writes=[t_KTaug])
            S.barrier()
            checkpoint(3)

            WQ = WA[:, 0:2 * 768].rearrange("p (c n) -> p c n", c=2)
            S.dma("pool", lambda e: e.dma_start(out=WQ, in_=wview(e_w_q_up, 0, 768)), writes=[t_WA])
            WQR = WA[:, 1536:1536 + 2 * 768].rearrange("p (c n) -> p c n", c=2)
            WKV = WB[:, 0:1024]
            S.dma("pool", lambda e: e.dma_start(out=WKV, in_=e_w_kv_up[:, :]), writes=[t_WB])
            S.op("dve", lambda e: e.memset(WA[:, 1536:1536 + 2 * 768], 0.0), writes=[t_WA])
            for c2 in range(2):
                src = WQ[:, c2, :].rearrange("p (h d) -> p h d", h=8)
                dst = WQR[:, c2, :].rearrange("p (h d) -> p h d", h=8)
                S.op("dve", lambda e, src=src, dst=dst: e.tensor_scalar(out=dst[:, :, 64:80], in0=src[:, :, 80:96],
                                                                  scalar1=-1.0, scalar2=None, op0=ALU.mult),
                     writes=[t_WA])
                S.op("dve", lambda e, src=src, dst=dst: e.tensor_copy(out=dst[:, :, 80:96], in_=src[:, :, 64:80]),
                     writes=[t_WA])
            mla_scale = 96.0 ** -0.5
            for h in range(8):
                for b in range(NB):
                    ps = PS[4 + b % 2]
                    tps = t_PS[4 + b % 2]
                    S.op("pe", lambda e, b=b, h=h, ps=ps: e.matmul(ps[0:64, :], lhsT=WKV[:, h * 128:h * 128 + 64],
                                                             rhs=LATt[:, 2, b * 512:(b + 1) * 512], start=True, stop=True),
                         reads=[t_LAT[b], t_WB], writes=[tps])
                    eng = evac_engine()
                    S.op(eng, copy_op(eng, KT[0:64, b * 512:(b + 1) * 512], ps[0:64, :]), writes=[tps, t_KT])
                for t8 in range(4):
                    ps = PS[4 + t8 % 2]
                    tps = t_PS[4 + t8 % 2]
                    for ti in range(8):
                        tt = t8 * 8 + ti
                        S.op("pe", lambda e, tt=tt, ti=ti, h=h, ps=ps: e.matmul(
                            ps[:, ti * 64:(ti + 1) * 64], lhsT=LATt[:, 2, tt * 128:(tt + 1) * 128],
                            rhs=WKV[:, h * 128 + 64:h * 128 + 128], start=True, stop=True),
                            reads=[t_LAT[tt // 4], t_WB], writes=[tps])
                    eng = evac_engine()
                    S.op(eng, copy_op(eng, VA0[:, t8 * 8:(t8 + 1) * 8, 0:64],
                                      ps[:, :].rearrange("p (t d) -> p t d", t=8)), writes=[tps, t_VA])
                for b in range(NB):
                    bs = slice(b * 512, (b + 1) * 512)
                    p1, tp1 = PS[4], t_PS[4]
                    p2, tp2 = PS[5], t_PS[5]
                    for k in range(2):
                        S.op("pe", lambda e, k=k, h=h, bs=bs: e.matmul(p1[0:96, :], lhsT=WQ[:, k, h * 96:(h + 1) * 96],
                                                                rhs=LATt[:, k, bs], start=(k == 0), stop=(k == 1)),
                             reads=[t_LAT[b], t_WA], writes=[tp1])
                    for k in range(2):
                        S.op("pe", lambda e, k=k, h=h, bs=bs: e.matmul(p2[0:96, :], lhsT=WQR[:, k, h * 96:(h + 1) * 96],
                                                                rhs=LATt[:, k, bs], start=(k == 0), stop=(k == 1)),
                             reads=[t_LAT[b], t_WA], writes=[tp2])
                    S.op("act", lambda e, bs=bs: e.copy(out=QT[0:64, bs], in_=p1[0:64, :]),
                         writes=[tp1, t_QT[b]])
                    S.op("dve", lambda e, bs=bs: e.tensor_tensor(out=RC[0][64:96, :], in0=p1[64:96, :], in1=TRIG[64:96, bs],
                                                             op=ALU.mult), reads=[t_TRIG], writes=[tp1, t_RC[0]])
                    S.op("dve", lambda e, bs=bs: e.tensor_tensor(out=RC[1][64:96, :], in0=p2[64:96, :], in1=TRIG[96:128, bs],
                                                             op=ALU.mult), reads=[t_TRIG], writes=[tp2, t_RC[1]])
                    S.op("dve", lambda e, bs=bs: e.tensor_tensor(out=QT[64:96, bs], in0=RC[0][64:96, :], in1=RC[1][64:96, :],
                                                             op=ALU.add), reads=[t_RC[0], t_RC[1]], writes=[t_QT[b]])
                c = h // 2
                po = (h % 2) * 64

                def out_fn(g, c=c, po=po):
                    return AOT[po:po + 64, c, g * 512:(g + 1) * 512], [t_AOT[c][g]]
                attention(96, dense_groups(), None, mla_scale, lambda j: VA0[:, j, :], out_fn)

            checkpoint(4)
            gate_phase(e_w_in, 1184)
            checkpoint(5)
            out_phase(e_w_out, x_d, None, last_layer=False, next_gain_d=(o_g_in if do1 else None))
            checkpoint(6)

        if do1:
            VA4 = at("VA4", [128, 32, 4, 128], BF16, o_big)
            if not do0:
                load_gain(o_g_in)
                phase_A(x1_d)
                S.barrier()
            Q0, K0, V0, F0, G0 = 0, 1024, 2048, 3072, 3088
            WF = WB[:, 0:128].rearrange("p (c n) -> p c n", c=8)
            S.dma("pool", lambda e: e.dma_start(out=WF, in_=wview(o_w_in, F0, F0 + 16)), writes=[t_WB])
            NL = at("NL", [128, 32, 16], F32, o_wa)
            trif = at("trif", [128, 128], F32, o_wa + 2048)
            identf = at("identf", [128, 128], F32, o_wa + 2560)
            CTf = at("CTf", [16, S_LEN], F32, o_big)
            r1 = at("r1", [16, S_LEN], F32, o_big + 16384)
            CS = at("CS", [128, S_LEN], BF16, o_trig)
            bfb = sb("bfb", [128, 16], F32)
            Ct = sb("Ct", [128, 32, 16], F32)
            Rs = sb("Rs", [128, 16], F32)
            zt = sb("zt", [128, 16], F32)
            t_bfb, t_NL, t_Ct, t_Rs, t_zt = S.tok("bfb"), S.toks(NT, "NL"), S.toks(NT, "Ct"), S.tok("Rs"), S.tok("zt")
            t_c1 = S.tok("cst1")
            S.dma("sp", lambda e: e.dma_start(out=trif[:], in_=cst["c_tri"][:, :]), writes=[t_c1])
            S.dma("sp", lambda e: e.dma_start(out=identf[:], in_=cst["c_ident"][:, :]), writes=[t_c1])
            S.dma("sp", lambda e: e.dma_start(out=bfb[:], in_=bass.AP(o_b_f.tensor, 0, [[0, 128], [1, 16]])),
                  writes=[t_bfb])
            S.op("dve", lambda e: e.memset(Rs[:], 0.0), writes=[t_Rs])
            for tt in range(NT):
                ps = PS[6]
                tps = t_PS[6]
                for k in range(8):
                    S.op("pe", lambda e, k=k, tt=tt: e.matmul(ps[:, 0:16], lhsT=hT[:, k, tt * 128:(tt + 1) * 128],
                                                          rhs=WF[:, k, :], start=(k == 0), stop=(k == 7)),
                         reads=[t_hT[tt], t_WB], writes=[tps])
                S.op("dve", lambda e: e.tensor_tensor(out=zt[:], in0=ps[:, 0:16], in1=bfb[:], op=ALU.add),
                     reads=[t_bfb], writes=[tps, t_zt])
                S.op("act", lambda e: e.activation(out=zt[:], in_=zt[:], func=AF.Exp, scale=-1.0), writes=[t_zt])
                S.op("act", lambda e, tt=tt: e.activation(out=NL[:, tt, :], in_=zt[:], func=AF.Ln, bias=epsb[:, 1:2],
                                                         scale=1.0), reads=[t_eps], writes=[t_zt, t_NL[tt]])
                pc = PS[5]
                tpc = t_PS[5]
                S.op("pe", lambda e, tt=tt: e.matmul(pc[:, 0:16], lhsT=trif[:], rhs=NL[:, tt, :], start=True, stop=False),
                     reads=[t_NL[tt], t_c1], writes=[tpc])
                S.op("pe", lambda e: e.matmul(pc[:, 0:16], lhsT=onesf[:], rhs=Rs[:], start=False, stop=True),
                     reads=[t_Rs, t_cst], writes=[tpc])
                S.op("dve", lambda e, tt=tt: e.tensor_copy(out=Ct[:, tt, :], in_=pc[:, 0:16]), writes=[tpc, t_Ct[tt]])
                S.op("dve", lambda e, tt=tt: e.tensor_tensor(out=Rs[:], in0=Rs[:], in1=NL[:, tt, :], op=ALU.add),
                     reads=[t_NL[tt]], writes=[t_Rs])
            t_CS, t_r1, t_ctf = S.tok("CS"), S.tok("r1"), S.tok("ctf")
            for g4 in range(8):
                ps = PS[6]
                tps = t_PS[6]
                for ti in range(4):
                    tt = g4 * 4 + ti
                    S.op("pe", lambda e, tt=tt, ti=ti: e.matmul(ps[0:16, ti * 128:(ti + 1) * 128], lhsT=Ct[:, tt, :],
                                                            rhs=identf[:], start=True, stop=True),
                         reads=[t_Ct[tt], t_c1], writes=[tps])
                S.op("dve", lambda e, g4=g4: e.tensor_scalar(out=CTf[:, g4 * 512:(g4 + 1) * 512], in0=ps[0:16, :],
                                                          scalar1=-1.0, scalar2=None, op0=ALU.mult),
                     writes=[tps, t_ctf])
            tmpb = at("tmpb", [16, S_LEN], BF16, o_aot)
            t_tmpb = S.tok("tmpb")
            S.op("dve", lambda e: e.tensor_copy(out=CS[0:16, :], in_=CTf[:, :]), reads=[t_ctf], writes=[t_CS])
            S.op("dve", lambda e: e.tensor_tensor(out=r1[:, :], in0=CTf[:, :], in1=CS[0:16, :], op=ALU.subtract),
                 reads=[t_ctf, t_CS], writes=[t_r1])
            S.op("dve", lambda e: e.tensor_copy(out=tmpb[:, :], in_=r1[:, :]), reads=[t_r1], writes=[t_tmpb])
            S.op("dve", lambda e: e.tensor_copy(out=CS[32:48, :], in_=tmpb[:, :]), reads=[t_tmpb], writes=[t_CS])
            S.op("dve", lambda e: e.tensor_tensor(out=r1[:, :], in0=r1[:, :], in1=tmpb[:, :], op=ALU.subtract),
                 reads=[t_tmpb], writes=[t_r1])
            S.op("dve", lambda e: e.tensor_copy(out=CS[64:80, :], in_=r1[:, :]), reads=[t_r1], writes=[t_CS])
            S.barrier()
            checkpoint(7)
            if 'g' in DBG:
                S.op("pe", lambda e: e.matmul(PS[6][0:64, 0:16], lhsT=hT[:, 0, 0:64], rhs=hT[:, 0, 0:16],
                                              start=True, stop=True), reads=[t_hT[0]], writes=[t_PS[6]])
            if 'a' not in DBG:
                S.op("dve", lambda e: e.memset(KT[64:67, :], 1.0), writes=[t_KTaug])
            WBqk = WB[:, 0:1024].rearrange("p (c s n) -> p c s n", c=8, s=2)
            checkpoint(71)

            KT2 = at("KT2", [128, S_LEN], BF16, o_wa)
            KTb = [KT, KT2]
            t_KTb = [[S.tok("ktb0"), S.tok("ktb0aug")], [S.tok("ktb1"), S.tok("ktb1aug")]]
            S.op("dve", lambda e: e.memset(KT2[64:67, :], 1.0), writes=[t_KTb[1][1]])
            t_KTb[0][1] = t_KTaug
            WVb = at("WVb", [128, 1024], BF16, o_rc + 2048)
            t_WQb, t_WKb, t_WVb = S.tok("wqb"), S.tok("wkb"), S.tok("wvb")
            t_VAp = S.toks(2, "vap")
            S.op("dve", lambda e: e.memset(VA4[:, :, 0:2, 64:128], 1.0), writes=[t_VAp[0], t_VA])
            S.op("dve", lambda e: e.memset(VA4[:, :, 2:4, 64:128], 1.0), writes=[t_VAp[1], t_VA])
            rot = {"n": 0}

            def bank():
                rot["n"] += 1
                return 4 + rot["n"] % 2

            def load_wk(h):
                S.dma("pool", lambda e: e.dma_start(out=WBqk[:, :, 1, :],
                                                    in_=wview(o_w_in, K0 + h * 64, K0 + (h + 1) * 64)),
                      writes=[t_WKb])

            def load_wq(h):
                S.dma("pool", lambda e: e.dma_start(out=WBqk[:, :, 0, :],
                                                    in_=wview(o_w_in, Q0 + h * 64, Q0 + (h + 1) * 64)),
                      writes=[t_WQb])

            def load_wv(p):
                S.dma("pool", lambda e: e.dma_start(out=WVb[:, :].rearrange("p (c n) -> p c n", c=8),
                                                    in_=wview(o_w_in, V0 + p * 128, V0 + (p + 1) * 128)),
                      writes=[t_WVb])

            def k_block(h, b):
                pb = bank()
                proj_fm(PS[pb], t_PS[pb], lambda k: WB[:, k * 128 + 64:(k + 1) * 128], 8,
                        lambda k: hT[:, k, b * 512:(b + 1) * 512], t_hT[4 * b:4 * b + 4], [t_WKb], 64)
                S.op("dve", copy_op("dve", KTb[h % 2][0:64, b * 512:(b + 1) * 512], PS[pb][0:64, :]),
                     writes=[t_PS[pb], t_KTb[h % 2][0]])

            def q_block(h, b):
                pb = bank()
                both = h + 1 < 16
                M = 128 if both else 64
                proj_fm(PS[pb], t_PS[pb], lambda k: WB[:, k * 128:k * 128 + M], 8,
                        lambda k: hT[:, k, b * 512:(b + 1) * 512], t_hT[4 * b:4 * b + 4],
                        [t_WQb, t_WKb] if both else [t_WQb], M)
                S.op("dve", copy_op("dve", QT[0:64, b * 512:(b + 1) * 512], PS[pb][0:64, :], scale=0.125),
                     writes=[t_PS[pb], t_QT[b]])
                if both:
                    S.op("dve", copy_op("dve", KTb[(h + 1) % 2][0:64, b * 512:(b + 1) * 512], PS[pb][64:128, :]),
                         writes=[t_PS[pb], t_KTb[(h + 1) % 2][0]])

            def v_tiles(p, t4):
                pb = bank()
                ps = PS[pb]
                s0 = 2 * (p % 2)
                for ti in range(4):
                    tt = t4 * 4 + ti
                    for k in range(8):
                        S.op("pe", lambda e, k=k, tt=tt, ti=ti: e.matmul(
                            ps[:, ti * 128:(ti + 1) * 128], lhsT=hT[:, k, tt * 128:(tt + 1) * 128],
                            rhs=WVb[:, k * 128:(k + 1) * 128], start=(k == 0), stop=(k == 7)),
                            reads=[t_hT[tt], t_WVb], writes=[t_PS[pb]])
                for ti in range(4):
                    tt = t4 * 4 + ti
                    S.op("dve", copy_op("dve", VA4[:, tt, s0:s0 + 2, 0:64],
                                        ps[:, ti * 128:(ti + 1) * 128].rearrange("p (h d) -> p h d", h=2)),
                         writes=[t_PS[pb], t_VAp[p % 2]])

            load_wv(0)
            for t4 in range(8):
                v_tiles(0, t4)
            load_wk(0)
            for b in range(NB):
                k_block(0, b)
            for h in range(16):
                hh = h % 4
                load_wq(h)
                for r_ in range(3):
                    S.dma("sp", lambda e, h=h, r_=r_: e.dma_start(out=QT[64 + r_:65 + r_, :],
                                                                 in_=CS[32 * r_ + h:32 * r_ + h + 1, :]),
                          reads=[t_CS], writes=[t_QTaug])
                bgl = []
                if h + 1 < 16:
                    load_wk(h + 1)
                if h % 2 == 1 and h + 1 < 16:
                    load_wv((h + 1) // 2)
                    bgl += [(lambda h=h, t4=t4: v_tiles((h + 1) // 2, t4)) for t4 in range(8)]
                c = h // 2
                po = (h % 2) * 64

                def out_fn(g, c=c, po=po):
                    return AOT[po:po + 64, c, g * 512:(g + 1) * 512], [t_AOT[c][g]]
                attention(67, dense_groups(), lambda j, h=h: Ct[:, j, h:h + 1], 1.0,
                          lambda j, hh=hh: VA4[:, j, hh, :], out_fn, rd_extra=t_Ct,
                          KTt=(KTb[h % 2], t_KTb[h % 2]), pre_group=(lambda g, h=h: q_block(h, g)),
                          bg=bgl, va_tok=t_VAp[(h // 2) % 2], one_rc=True)

            checkpoint(8)
            S.barrier()
            gate_phase(o_w_in, G0)
            checkpoint(9)
            out_phase(o_w_out, x1_d, (t_x1 if mode == "full" else None), last_layer=True)
            S.wait_all("sp", t_out)
        else:
            S.wait_all("sp", t_x1)


    def checkpoint(k):
        if stop == k:
            raise _Stop()

    try:
        _layers()
    except _Stop:
        pass
    S.emit()
    return nc


_CACHE = {}


def _get(mode):
    if mode not in _CACHE:
        _CACHE[mode] = build_program(mode)
    return _CACHE[mode]


L0_KEYS = ["e_g_in", "e_w_in", "e_g_q_a", "e_w_q_up", "e_g_kv_a", "e_w_kv_up", "e_sinks", "e_w_out"]
L1_KEYS = ["o_g_in", "o_w_in", "o_b_f", "o_w_out"]


def _maps(inputs, n, mode, x1=None):
    consts = _constants()
    maps = []
    for b in range(n):
        m = dict(consts)
        if mode in ("full", "l0"):
            m["x"] = np.ascontiguousarray(inputs["x"][b])
            m["positions"] = np.ascontiguousarray(inputs["positions"][b]).astype(np.int32)
            for k in L0_KEYS:
                m[k] = np.ascontiguousarray(inputs[k][0])
        if mode in ("full", "l1"):
            for k in L1_KEYS:
                m[k] = np.ascontiguousarray(inputs[k][0])
            m["g_final"] = np.ascontiguousarray(inputs["g_final"])
        if mode == "l1":
            m["x1"] = np.ascontiguousarray(x1[b])
        maps.append(m)
    return maps


def kernel(**inputs):
    n = inputs["x"].shape[0]
    inputs = {k: np.asarray(v) for k, v in inputs.items()}
    nc = _get("full")
    res = run_bass_kernel_spmd(nc, _maps(inputs, n, "full"), core_ids=list(range(n)))
    return np.stack([np.asarray(r["out"]) for r in res.results], axis=0).astype(np.float32)
```

```python
import math
import os
DBG = os.environ.get('KDBG', '')
import numpy as np
import concourse.bass as bass
import concourse.mybir as mybir
from concourse.bass_utils import run_bass_kernel_spmd

F32 = mybir.dt.float32
BF16 = mybir.dt.bfloat16
I32 = mybir.dt.int32
AF = mybir.ActivationFunctionType
ALU = mybir.AluOpType

S_LEN = 4096
D = 1024
NT = 32
NB = 8
EPS = 1e-6
SEM_ROT = 12000


class Tok:
    __slots__ = ("name", "w", "r", "dsem")

    def __init__(self, name=""):
        self.name = name
        self.w = None
        self.r = {}
        self.dsem = None


class _Rec:
    def __init__(self):
        self.call = None

    def __getattr__(self, name):
        def f(*a, **k):
            assert self.call is None
            self.call = (name, a, k)
            return self
        return f


def _freeze(fn):
    r = _Rec()
    fn(r)
    assert r.call is not None
    return r.call


class Sched:
    ENGS = ("pe", "act", "dve", "pool", "sp")

    def __init__(self, nc):
        self.nc = nc
        self.sems = []
        self.semeng = {}
        self.ops = {e: [] for e in self.ENGS}
        self.esem = {}
        self.ecnt = {}
        self.seen = {e: {} for e in self.ENGS}
        self.semval = {}
        self.unsig = {e: False for e in self.ENGS}
        self.noself = {"pe"}
        for e in ("pe", "act", "dve", "pool"):
            self._new_esem(e)

    def _alloc(self, name, eng=None):
        h = self.nc.alloc_semaphore(name=name)
        self.sems.append(h)
        sid = len(self.sems) - 1
        self.semval[sid] = 0
        self.semeng[sid] = eng
        return sid

    def _new_esem(self, e):
        self.esem[e] = self._alloc(f"s_{e}_{len(self.sems)}", e)
        self.ecnt[e] = 0

    def tok(self, name=""):
        return Tok(name)

    def toks(self, n, name=""):
        return [Tok(f"{name}{i}") for i in range(n)]

    def _collect(self, e, reads, writes):
        need = {}

        def add(s, v):
            if need.get(s, 0) < v:
                need[s] = v
        for t in reads:
            if t.w is not None:
                add(*t.w)
        for t in writes:
            if t.w is not None:
                add(*t.w)
            for s, v in t.r.items():
                add(s, v)
        waits = []
        seen = self.seen[e]
        for s, v in need.items():
            if e in self.noself and self.semeng[s] == e:
                continue
            if seen.get(s, 0) >= v:
                continue
            seen[s] = v
            waits.append((s, v))
        return waits

    def _mark(self, s, val, reads, writes):
        for t in reads:
            if t.r.get(s, 0) < val:
                t.r[s] = val
        for t in writes:
            t.w = (s, val)
            t.r = {}

    def op(self, e, fn, reads=(), writes=(), sig=True):
        waits = self._collect(e, reads, writes)
        if sig and self.ecnt[e] >= SEM_ROT and not self.unsig[e]:
            self._new_esem(e)
        self.unsig[e] = not sig
        s = self.esem[e]
        val = self.ecnt[e] + 1
        if sig:
            self.ecnt[e] = val
            self.semval[s] = val
        self.ops[e].append((waits, _freeze(fn), (s, 1) if sig else None))
        self._mark(s, val, reads, writes)

    def barrier(self):
        cur = [(s, v) for s, v in self.semval.items() if v > 0]
        for e in self.ENGS:
            waits = []
            for s, v in cur:
                if (self.semeng[s] == e and e in self.noself) or self.seen[e].get(s, 0) >= v:
                    continue
                self.seen[e][s] = v
                waits.append((s, v))
            self.ops[e].append((waits, None, None))

    def dma(self, q, fn, reads=(), writes=(), owner=None):
        waits = self._collect(q, reads, writes)
        if owner is None:
            owner = writes[0] if writes else reads[0]
        if owner.dsem is None or self.semval[owner.dsem] >= SEM_ROT * 2:
            owner.dsem = self._alloc(f"d_{len(self.sems)}")
        s = owner.dsem
        self.semval[s] += 16
        val = self.semval[s]
        self.ops[q].append((waits, _freeze(fn), (s, 16)))
        self._mark(s, val, reads, writes)

    def wait_all(self, e, toks):
        waits = self._collect(e, [], toks)
        self.ops[e].append((waits, None, None))

    def emit(self):
        nc = self.nc
        sems = self.sems
        ops = self.ops

        def replay(eng, lst):
            for waits, fn, inc in lst:
                for s, v in waits:
                    eng.wait_ge(sems[s], v)
                if fn is None:
                    continue
                name, a, k = fn
                ins = getattr(eng, name)(*a, **k)
                if inc is not None:
                    ins.then_inc(sems[inc[0]], inc[1])

        with nc.Block() as block:
            @block.tensor
            def _(eng):
                replay(eng, ops["pe"])

            @block.scalar
            def _(eng):
                replay(eng, ops["act"])

            @block.vector
            def _(eng):
                replay(eng, ops["dve"])

            @block.gpsimd
            def _(eng):
                replay(eng, ops["pool"])

            @block.sync
            def _(eng):
                replay(eng, ops["sp"])


def _constants():
    c = {}
    c["c_ident"] = np.eye(128, dtype=np.float32)
    k = np.arange(128)[:, None]
    q = np.arange(128)[None, :]
    c["c_tri"] = (k <= q).astype(np.float32)
    c["c_ones"] = np.ones((128, 128), np.float32)
    qq = np.arange(512)[None, None, :]
    kk = np.arange(128)[:, None, None]
    rr = np.arange(4)[None, :, None]
    c["c_mask"] = ((rr * 128 + kk) <= qq).astype(np.float32)
    slopes = 2.0 ** (-8.0 * (np.arange(8, dtype=np.float64) + 1.0) / 8)
    t = np.zeros((128, 8, 2, 128), np.float64)
    kq = np.arange(128)[:, None]
    qv = np.arange(128)[None, :]
    for h in range(8):
        dist0 = 128 + qv - kq
        t[:, h, 0, :] = np.where(dist0 < 128, np.exp(-slopes[h] * dist0), 0.0)
        dist1 = qv - kq
        t[:, h, 1, :] = np.where(dist1 >= 0, np.exp(-slopes[h] * np.maximum(dist1, 0)), 0.0)
    c["c_swt"] = t.astype(np.float32)
    invf = 1.0 / (10000.0 ** (np.arange(0, 32, 2, dtype=np.float32) / 32))
    v = np.zeros((128, 2), np.float32)
    for p in range(64, 128):
        v[p, 0] = invf[(p - 64) % 16]
        v[p, 1] = (math.pi / 2) if p < 96 else 0.0
    c["c_rope"] = v
    return c


CONST_SHAPES = {"c_ident": [128, 128], "c_tri": [128, 128], "c_ones": [128, 128],
                "c_mask": [128, 4, 512], "c_swt": [128, 8, 2, 128], "c_rope": [128, 2]}


class _Stop(Exception):
    pass


def build_program(mode="full", stop=0):
    nc = bass.Bass("TRN2", target_bir_lowering=False)
    S = Sched(nc)
    do0 = mode in ("full", "l0")
    do1 = mode in ("full", "l1")

    def din(name, shape, dt=F32):
        return nc.dram_tensor(name, shape, dt, kind="ExternalInput").ap()

    cst = {k: din(k, v) for k, v in CONST_SHAPES.items()}
    if do0:
        x_d = din("x", [S_LEN, D])
        pos_d = din("positions", [S_LEN], I32)
        e_g_in = din("e_g_in", [D])
        e_w_in = din("e_w_in", [D, 2208])
        e_g_q_a = din("e_g_q_a", [256])
        e_w_q_up = din("e_w_q_up", [256, 768])
        e_g_kv_a = din("e_g_kv_a", [128])
        e_w_kv_up = din("e_w_kv_up", [128, 1024])
        e_sinks = din("e_sinks", [8])
        e_w_out = din("e_w_out", [D, D])
    if do1:
        o_g_in = din("o_g_in", [D])
        o_w_in = din("o_w_in", [D, 4112])
        o_b_f = din("o_b_f", [16])
        o_w_out = din("o_w_out", [D, D])
        g_final = din("g_final", [D])
        out_d = nc.dram_tensor("out", [S_LEN, D], F32, kind="ExternalOutput").ap()
    if mode == "full":
        x1_d = nc.dram_tensor("x1s", [S_LEN, D], F32).ap()
    elif mode == "l0":
        x1_d = nc.dram_tensor("x1", [S_LEN, D], F32, kind="ExternalOutput").ap()
    else:
        x1_d = din("x1", [S_LEN, D])
    t_x1 = S.toks(NT, "x1d")
    t_x1own = S.tok("x1own")
    t_outown = S.toks(2, "outown")
    t_out = S.toks(NT, "outd")

    def region(nbytes):
        st, _ = nc.bump_sbuf(nbytes)
        return st

    def at(name, shape, dt, off):
        return nc.alloc_sbuf_tensor_at(name, shape, dt, offset=off)

    def sb(name, shape, dt):
        return nc.alloc_sbuf_tensor(name, shape, dt)

    hT = sb("hT", [128, 8, S_LEN], BF16)
    t_hT = S.toks(NT, "hT")
    o_aot = region(65536)
    AOT = at("AOT", [128, 8, S_LEN], BF16, o_aot)
    t_AOT = [[S.tok(f"aot{c}_{b}") for b in range(NB)] for c in range(8)]
    o_big = region(32768)
    BIG = at("BIG", [128, 32 * 4 * 128], BF16, o_big)
    t_VA = S.tok("VA")
    t_LAT = S.toks(NB, "LAT")
    o_trig = region(8192)
    TRIG = at("TRIG", [128, S_LEN], BF16, o_trig)
    t_TRIG = S.tok("TRIG")
    o_pool = region(19456)
    QT = at("QT", [128, S_LEN], BF16, o_pool)
    t_QT = S.toks(NB, "QT")
    t_QTaug = S.tok("QTaug")
    KT = at("KT", [128, S_LEN], BF16, o_pool + 8192)
    t_KT = S.tok("KT")
    t_KTaug = S.tok("KTaug")
    NPT = 5
    PT = [at(f"pt{i}", [128, 512], BF16, o_pool + 16384 + 1024 * i) for i in range(3)]
    PT += [sb(f"ptx{i}", [128, 512], BF16) for i in range(NPT - 3)]
    t_PT = S.toks(NPT, "pt")
    XT = [at(f"xt{i}", [128, D], F32, o_pool + 4096 * i) for i in range(3)]
    t_XT = S.toks(3, "xt")
    XB = [at(f"xb{i}", [128, D], BF16, o_pool + 12288 + 2048 * i) for i in range(2)]
    t_XB = S.toks(2, "xb")
    ET = [at(f"et{i}", [128, 512], F32, o_pool + 2048 * i) for i in range(3)]
    t_ET = S.toks(3, "et")
    o_pf = region(1024)
    PF = [at(f"pf{i}", [128, 128], F32, o_pf + 512 * i) for i in range(2)]
    t_PF = S.toks(2, "pf")
    o_rc = region(4096)
    RC = [at(f"rc{i}", [128, 512], F32, o_rc + 2048 * i) for i in range(2)]
    t_RC = S.toks(2, "rc")
    GB = at("GB", [128, D], F32, o_rc)
    t_GB = S.tok("GB")
    o_wa = region(8192)
    WA = at("WA", [128, 3328], BF16, o_wa)
    t_WA = S.tok("WA")
    WB = sb("WB", [128, 1024], BF16)
    t_WB = S.tok("WB")
    MASK = sb("MASK", [128, 128], BF16)
    t_MASK = S.tok("MASK")
    identb = sb("identb", [128, 128], BF16)
    onesf = sb("onesf", [128, 128], F32)
    t_cst = S.tok("cst")
    stat = sb("stat", [128, 8], F32)
    t_statS = S.toks(2, "stat")
    epsb = sb("epsb", [128, 2], F32)
    t_eps = S.tok("eps")

    PS = [nc.alloc_psum_tensor(f"ps{i}", [128, 512], F32) for i in range(7)]
    t_PS = S.toks(7, "ps")
    PST = nc.alloc_psum_tensor("pst", [128, 8, 128], BF16)
    t_PST = S.tok("pst")

    S.dma("sp", lambda e: e.dma_start(out=onesf[:], in_=cst["c_ones"][:, :]), writes=[t_cst])
    t_cstb = S.tok("cstb")
    S.dma("pool", lambda e: e.dma_start(out=identb[:], in_=cst["c_ident"][:, :]), writes=[t_cstb])
    S.dma("pool", lambda e: e.dma_start(out=MASK[:], in_=cst["c_tri"][:, :]), writes=[t_MASK])
    S.op("dve", lambda e: e.memset(epsb[:, 0:1], EPS), writes=[t_eps])
    S.op("dve", lambda e: e.memset(epsb[:, 1:2], 1.0), writes=[t_eps])

    cnt = {"ev": 0}

    def evac_engine():
        cnt["ev"] += 1
        return "act" if cnt["ev"] % 2 else "dve"

    def copy_op(eng, out, in_, scale=None):
        if eng == "act":
            if scale is None:
                return lambda e: e.copy(out=out, in_=in_)
            return lambda e: e.mul(out=out, in_=in_, mul=scale)
        if scale is None:
            return lambda e: e.tensor_copy(out=out, in_=in_)
        return lambda e: e.tensor_scalar(out=out, in0=in_, scalar1=scale, scalar2=None, op0=ALU.mult)

    def load_gain(g_d):
        S.dma("sp", lambda e: e.dma_start(out=GB[:], in_=bass.AP(g_d.tensor, 0, [[0, 128], [1, D]])),
              writes=[t_GB])

    def rstd_from_ms(col, tst):
        S.op("act", lambda e: e.activation(out=stat[:, col + 1:col + 2], in_=stat[:, col:col + 1], func=AF.Ln,
                                           bias=epsb[:, 0:1], scale=1.0), reads=[t_eps], writes=[tst])
        S.op("act", lambda e: e.activation(out=stat[:, col + 1:col + 2], in_=stat[:, col + 1:col + 2],
                                           func=AF.Exp, scale=-0.5), writes=[tst])

    def norm_tile(xt, t_xt, out_ap, t_outs, junk_ap, t_junk, slot=0):
        c0 = 4 * slot
        tst = t_statS[slot]
        S.op("dve", lambda e: e.memset(stat[:, c0:c0 + 1], 0.0), writes=[tst])
        S.op("act", lambda e: e.activation(out=junk_ap, in_=xt[:], func=AF.Square, scale=1.0 / 32,
                                           accum_out=stat[:, c0:c0 + 1]), reads=[t_xt], writes=[t_junk, tst])
        rstd_from_ms(c0, tst)
        S.op("dve", lambda e: e.scalar_tensor_tensor(out=out_ap, in0=xt[:], scalar=stat[:, c0 + 1:c0 + 2], in1=GB[:],
                                                     op0=ALU.mult, op1=ALU.mult),
             reads=[t_xt, tst, t_GB], writes=list(t_outs))

    def to_hT(xt, t_xt, tt):
        i = tt % 2
        norm_tile(xt, t_xt, XB[i][:], [t_XB[i]], XB[i][:], t_XB[i], slot=i)
        for c in range(8):
            S.op("pe", lambda e, c=c: e.transpose(out=PST[:, c, :], in_=XB[i][:, c * 128:(c + 1) * 128],
                                                   identity=identb[:]),
                 reads=[t_XB[i], t_cstb], writes=[t_PST])
        eng = evac_engine()
        S.op(eng, copy_op(eng, hT[:, :, tt * 128:(tt + 1) * 128], PST[:, :, :]), writes=[t_PST, t_hT[tt]])

    def phase_A(src_d):
        for tt in range(NT):
            i = tt % 2
            S.dma("sp", lambda e, tt=tt, i=i: e.dma_start(out=XT[i][:], in_=src_d[tt * 128:(tt + 1) * 128, :]),
                  writes=[t_XT[i]])
            to_hT(XT[i], t_XT[i], tt)

    def wview(w_d, c0, c1):
        return w_d.rearrange("(c p) n -> p c n", p=128)[:, :, c0:c1]

    def proj_fm(ps, t_ps, w_ap_fn, nk, src_fn, src_toks, w_toks, M):
        for k in range(nk):
            S.op("pe", lambda e, k=k: e.matmul(ps[0:M, :], lhsT=w_ap_fn(k), rhs=src_fn(k),
                                               start=(k == 0), stop=(k == nk - 1)),
                 reads=list(src_toks) + list(w_toks), writes=[t_ps])

    def attention(KR, groups, bias_fn, scale, va_fn, out_fn, den_add=None, fp32_tables=False, rd_extra=(),
                  KTt=None, pre_group=None, bg=(), va_tok=None, one_rc=False):
        steps = []
        for gi, (q0, QW, kbs, gidx) in enumerate(groups):
            for n, (j, c0, tbl) in enumerate(kbs):
                steps.append((gi, q0, QW, j, c0, tbl, n == 0, n == len(kbs) - 1, gidx))

        SB_ = (0, 1, 6)
        LA = 2
        KTx, t_KTx = (KT, [t_KT, t_KTaug]) if KTt is None else KTt
        t_VAx = t_VA if va_tok is None else va_tok
        bg = list(bg)
        nsteps = sum(len(g_[2]) for g_ in groups)
        bg_stride = max(1, nsteps // (len(bg) + 1)) if bg else 0
        seen_groups = set()

        def emit_qk(i):
            gi, q0, QW, j, c0, tbl, first, last, gidx = steps[i]
            sp = PS[SB_[i % 3]]
            S.op("pe", lambda e: e.matmul(sp[:, c0:QW], lhsT=KTx[0:KR, j * 128:(j + 1) * 128],
                                          rhs=QT[0:KR, q0 + c0:q0 + QW], start=True, stop=True),
                 reads=list(t_KTx) + [t_QT[q0 // 512], t_QTaug], writes=[t_PS[SB_[i % 3]]])

        first_idx = {}
        for i_, st_ in enumerate(steps):
            first_idx.setdefault(st_[0], i_)
        if pre_group is not None:
            pre_group(groups[0][3])
        for i0 in range(min(LA, len(steps))):
            emit_qk(i0)
        for i, (gi, q0, QW, j, c0, tbl, first, last, gidx) in enumerate(steps):
            if i + LA < len(steps):
                emit_qk(i + LA)
            if pre_group is not None and i == first_idx[gi] + 1 and gi + 1 < len(groups):
                pre_group(groups[gi + 1][3])
            if bg and i > 0 and i % bg_stride == 0:
                bg.pop(0)()
            sp = PS[SB_[i % 3]]
            tsp = t_PS[SB_[i % 3]]
            pt = PT[i % NPT]
            tpt = t_PT[i % NPT]
            kw = {"scale": scale}
            b = bias_fn(j) if bias_fn is not None else None
            if b is not None:
                kw["bias"] = b
            if tbl is not None and fp32_tables:
                pf = PF[i % 2]
                S.op("act", lambda e, kw=kw, pf=pf, sp=sp, QW=QW: e.activation(out=pf[:, 0:QW], in_=sp[:, 0:QW],
                                                                          func=AF.Exp, **kw),
                     reads=list(rd_extra), writes=[tsp, t_PF[i % 2]])
                S.op("dve", lambda e, pf=pf, pt=pt, tbl=tbl, QW=QW: e.tensor_tensor(out=pt[:, 0:QW], in0=pf[:, 0:QW],
                                                                              in1=tbl, op=ALU.mult),
                     reads=[t_PF[i % 2]] + list(rd_extra), writes=[tpt])
            else:
                S.op("act", lambda e, kw=kw, pt=pt, sp=sp, QW=QW, c0=c0: e.activation(
                    out=pt[:, c0:QW], in_=sp[:, c0:QW], func=AF.Exp, **kw),
                    reads=list(rd_extra), writes=[tsp, tpt])
                if tbl is not None:
                    S.op("dve", lambda e, pt=pt, tbl=tbl, c0=c0: e.tensor_tensor(
                        out=pt[:, c0:c0 + 128], in0=pt[:, c0:c0 + 128], in1=tbl, op=ALU.mult),
                        reads=[t_MASK], writes=[tpt])
            op_ = PS[2 + gi % 2]
            top = t_PS[2 + gi % 2]
            S.op("pe", lambda e, op_=op_, pt=pt, j=j, first=first, last=last, QW=QW, c0=c0:
                 e.matmul(op_[:, c0:QW], lhsT=va_fn(j), rhs=pt[:, c0:QW], start=first, stop=last),
                 reads=[tpt, t_VAx], writes=[top])
            if last:
                rc = RC[0 if one_rc else gi % 2]
                trc = t_RC[0 if one_rc else gi % 2]
                if den_add is not None:
                    S.op("dve", lambda e, rc=rc, op_=op_, QW=QW: e.tensor_scalar(
                        out=rc[64:128, 0:QW], in0=op_[64:128, 0:QW], scalar1=den_add, scalar2=None, op0=ALU.add),
                        reads=list(rd_extra), writes=[top, trc])
                    S.op("act", lambda e, rc=rc, QW=QW: e.activation(out=rc[64:128, 0:QW], in_=rc[64:128, 0:QW],
                                                                    func=AF.Ln), writes=[trc])
                    S.op("act", lambda e, rc=rc, QW=QW: e.activation(out=rc[64:128, 0:QW], in_=rc[64:128, 0:QW],
                                                                    func=AF.Exp, scale=-1.0), writes=[trc])
                else:
                    S.op("dve", lambda e, rc=rc, op_=op_, QW=QW: e.reciprocal(out=rc[64:128, 0:QW],
                                                                         in_=op_[64:128, 0:QW]),
                         writes=[top, trc])
                o_ap, o_toks = out_fn(gidx)
                S.op("dve", lambda e, rc=rc, op_=op_, o_ap=o_ap, QW=QW: e.tensor_tensor(
                    out=o_ap, in0=op_[0:64, 0:QW], in1=rc[64:128, 0:QW], op=ALU.mult),
                    reads=[trc], writes=[top] + list(o_toks))
        while bg:
            bg.pop(0)()

    def _unused():
        pass

    def dense_groups():
        gs = []
        for g in range(NB):
            kbs = [(j, 0, None) for j in range(4 * g)] + [(4 * g + r, r * 128, MASK[:, :]) for r in range(4)]
            gs.append((g * 512, 512, kbs, g))
        return gs

    def gate_phase(w_in_d, goff):
        for c in range(8):
            S.dma("pool", lambda e, c=c: e.dma_start(
                out=WB[:, 0:1024].rearrange("p (c n) -> p c n", c=8),
                in_=wview(w_in_d, goff + c * 128, goff + (c + 1) * 128)), writes=[t_WB])
            for b in range(NB):
                ps = PS[4 + b % 2]
                tps = t_PS[4 + b % 2]
                proj_fm(ps, tps, lambda k: WB[:, k * 128:(k + 1) * 128], 8,
                        lambda k, b=b: hT[:, k, b * 512:(b + 1) * 512], t_hT[4 * b:4 * b + 4], [t_WB], 128)
                gt = PT[b % NPT]
                S.op("act", lambda e, gt=gt, ps=ps: e.activation(out=gt[:], in_=ps[:], func=AF.Silu),
                     writes=[tps, t_PT[b % NPT]])
                S.op("dve", lambda e, gt=gt, c=c, b=b: e.tensor_tensor(
                    out=AOT[:, c, b * 512:(b + 1) * 512], in0=AOT[:, c, b * 512:(b + 1) * 512], in1=gt[:],
                    op=ALU.mult), reads=[t_PT[b % NPT]], writes=[t_AOT[c][b]])

    def out_phase(w_out_d, res_d, t_res, last_layer, next_gain_d=None):
        S.barrier()
        WO = at("WO_%d" % int(last_layer), [128, 8 * 1024], BF16, o_big)
        t_WO = S.tok("WO")
        S.dma("pool", lambda e: e.dma_start(out=WO[:].rearrange("p (c n) -> p c n", c=8),
                                            in_=wview(w_out_d, 0, D)), writes=[t_WO])
        if last_layer:
            load_gain(g_final)
        elif next_gain_d is not None:
            load_gain(next_gain_d)
        t_x1o = S.toks(3, "x1own")
        t_outo = S.toks(3, "outown")
        def stage1(tt):
            i = tt % 3
            b = tt // 4
            pp = 2 * (tt % 2)
            S.dma("sp", lambda e: e.dma_start(out=XT[i][:], in_=res_d[tt * 128:(tt + 1) * 128, :]),
                  reads=[t_res[tt]] if t_res is not None else [], writes=[t_XT[i]])
            for half in range(2):
                ps = PS[pp + half]
                tps = t_PS[pp + half]
                for c in range(8):
                    S.op("pe", lambda e: e.matmul(
                        ps[:, :], lhsT=AOT[:, c, tt * 128:(tt + 1) * 128],
                        rhs=WO[:, c * 1024 + half * 512: c * 1024 + (half + 1) * 512],
                        start=(c == 0), stop=(c == 7)),
                        reads=[t_AOT[c][b], t_WO], writes=[tps])

        def stage1b(tt):
            i = tt % 3
            pp = 2 * (tt % 2)
            for half in range(2):
                ps = PS[pp + half]
                tps = t_PS[pp + half]
                S.op("dve", lambda e: e.tensor_tensor(
                    out=XT[i][:, half * 512:(half + 1) * 512], in0=ps[:, :], in1=XT[i][:, half * 512:(half + 1) * 512],
                    op=ALU.add), writes=[tps, t_XT[i]])

        def stage2(tt):
            i = tt % 3
            if last_layer:
                norm_tile(XT[i], t_XT[i], XT[i][:], [t_XT[i]], XB[tt % 2][:], t_XB[tt % 2], slot=tt % 2)
                S.dma("sp", lambda e: e.dma_start(out=out_d[tt * 128:(tt + 1) * 128, :], in_=XT[i][:]),
                      reads=[t_XT[i]], writes=[t_out[tt]], owner=t_outo[i])
            else:
                S.dma("sp", lambda e: e.dma_start(out=x1_d[tt * 128:(tt + 1) * 128, :], in_=XT[i][:]),
                      reads=[t_XT[i]], writes=[t_x1[tt]], owner=t_x1o[i])
                if mode == "full":
                    to_hT(XT[i], t_XT[i], tt)

        for tt in range(NT + 1):
            if tt < NT:
                stage1(tt)
            if tt >= 1:
                stage2(tt - 1)
            if tt < NT:
                stage1b(tt)
        S.barrier()

    def _layers():
        if do0:
            VA0 = at("VA0", [128, 32, 128], BF16, o_big)
            LATt = at("LATt", [128, 3, S_LEN], BF16, o_big + 8192)
            SWT = at("SWT", [128, 2048], F32, o_big + 8192)
            posi = at("posi", [128, S_LEN], I32, o_aot)
            kf = at("kf", [128, S_LEN], F32, o_aot + 16384)
            ANG = at("ANG", [128, S_LEN], F32, o_aot + 32768)
            t_pos, t_kf, t_ang = S.tok("posi"), S.tok("kf"), S.tok("ang")
            load_gain(e_g_in)
            phase_A(x_d)

            gq = sb("gq", [128, 2], F32)
            gkv = sb("gkv", [128, 1], F32)
            esink = sb("esink", [128, 8], F32)
            ropec = sb("ropec", [128, 2], F32)
            t_sm = S.tok("small0")
            for c2 in range(2):
                S.dma("sp", lambda e, c2=c2: e.dma_start(
                    out=gq[:, c2:c2 + 1], in_=e_g_q_a[c2 * 128:(c2 + 1) * 128].rearrange("(p o) -> p o", o=1)),
                    writes=[t_sm])
            S.dma("sp", lambda e: e.dma_start(out=gkv[:], in_=e_g_kv_a.rearrange("(p o) -> p o", o=1)), writes=[t_sm])
            S.dma("sp", lambda e: e.dma_start(out=esink[:], in_=bass.AP(e_sinks.tensor, 0, [[0, 128], [1, 8]])),
                  writes=[t_sm])
            S.dma("sp", lambda e: e.dma_start(out=ropec[:], in_=cst["c_rope"][:, :]), writes=[t_sm])
            S.op("act", lambda e: e.activation(out=esink[:], in_=esink[:], func=AF.Exp), writes=[t_sm])

            S.dma("sp", lambda e: e.dma_start(out=posi[64:128, :], in_=bass.AP(pos_d.tensor, 0, [[0, 64], [1, S_LEN]])),
                  writes=[t_pos])
            P6 = slice(64, 128)
            S.op("dve", lambda e: e.tensor_copy(out=ANG[P6, :], in_=posi[P6, :]), reads=[t_pos], writes=[t_ang])
            S.op("dve", lambda e: e.tensor_scalar(out=ANG[P6, :], in0=ANG[P6, :], scalar1=ropec[P6, 0:1],
                                                  scalar2=ropec[P6, 1:2], op0=ALU.mult, op1=ALU.add),
                 reads=[t_sm], writes=[t_ang])
            S.op("dve", lambda e: e.tensor_scalar(out=kf[P6, :], in0=ANG[P6, :], scalar1=1.0 / (2 * math.pi),
                                                  scalar2=0.5, op0=ALU.mult, op1=ALU.add),
                 reads=[t_ang], writes=[t_kf])
            S.op("dve", lambda e: e.tensor_copy(out=posi[P6, :], in_=kf[P6, :]), reads=[t_kf], writes=[t_pos])
            S.op("dve", lambda e: e.tensor_copy(out=kf[P6, :], in_=posi[P6, :]), reads=[t_pos], writes=[t_kf])
            C1 = 6.28125
            C2 = 2 * math.pi - C1
            S.op("dve", lambda e: e.scalar_tensor_tensor(out=ANG[P6, :], in0=kf[P6, :], scalar=-C1, in1=ANG[P6, :],
                                                         op0=ALU.mult, op1=ALU.add), reads=[t_kf], writes=[t_ang])
            S.op("dve", lambda e: e.scalar_tensor_tensor(out=ANG[P6, :], in0=kf[P6, :], scalar=-C2, in1=ANG[P6, :],
                                                         op0=ALU.mult, op1=ALU.add), reads=[t_kf], writes=[t_ang])
            S.op("dve", lambda e: e.tensor_single_scalar(out=kf[P6, :], in_=ANG[P6, :], scalar=-math.pi, op=ALU.is_lt),
                 reads=[t_ang], writes=[t_kf])
            S.op("dve", lambda e: e.scalar_tensor_tensor(out=ANG[P6, :], in0=kf[P6, :], scalar=2 * math.pi,
                                                         in1=ANG[P6, :], op0=ALU.mult, op1=ALU.add),
                 reads=[t_kf], writes=[t_ang])
            S.op("dve", lambda e: e.tensor_scalar(out=ANG[P6, :], in0=ANG[P6, :], scalar1=-3.1415925, scalar2=3.1415925,
                                                  op0=ALU.max, op1=ALU.min), writes=[t_ang])
            S.op("act", lambda e: e.activation(out=TRIG[P6, :], in_=ANG[P6, :], func=AF.Sin), reads=[t_ang],
                 writes=[t_TRIG])
            S.barrier()
            checkpoint(1)

            S.dma("sp", lambda e: e.dma_start(out=SWT[:, :], in_=cst["c_swt"].rearrange("p h r q -> p (h r q)")),
                  writes=[t_kf])
            S.op("dve", lambda e: e.memset(VA0[:, :, 64:128], 1.0), writes=[t_VA])
            SWA_Q0, SWA_K0, SWA_V0 = 416, 928, 1056
            WBkv = WB[:, 0:1024].rearrange("p (c s n) -> p c s n", c=8, s=2)
            for h in range(8):
                kv = h // 4
                if h % 4 == 0:
                    S.dma("pool", lambda e, kv=kv: e.dma_start(
                        out=WBkv[:, :, 0, :], in_=wview(e_w_in, SWA_K0 + kv * 64, SWA_K0 + (kv + 1) * 64)), writes=[t_WB])
                    S.dma("pool", lambda e, kv=kv: e.dma_start(
                        out=WBkv[:, :, 1, :], in_=wview(e_w_in, SWA_V0 + kv * 64, SWA_V0 + (kv + 1) * 64)), writes=[t_WB])
                    for b in range(NB):
                        ps = PS[4 + b % 2]
                        tps = t_PS[4 + b % 2]
                        proj_fm(ps, tps, lambda k: WB[:, k * 128:k * 128 + 64], 8,
                                lambda k, b=b: hT[:, k, b * 512:(b + 1) * 512], t_hT[4 * b:4 * b + 4], [t_WB], 64)
                        eng = evac_engine()
                        S.op(eng, copy_op(eng, KT[0:64, b * 512:(b + 1) * 512], ps[0:64, :]), writes=[tps, t_KT])
                    for t8 in range(4):
                        ps = PS[4 + t8 % 2]
                        tps = t_PS[4 + t8 % 2]
                        for ti in range(8):
                            tt = t8 * 8 + ti
                            for k in range(8):
                                S.op("pe", lambda e, k=k, tt=tt, ti=ti, ps=ps: e.matmul(
                                    ps[:, ti * 64:(ti + 1) * 64], lhsT=hT[:, k, tt * 128:(tt + 1) * 128],
                                    rhs=WB[:, k * 128 + 64:k * 128 + 128], start=(k == 0), stop=(k == 7)),
                                    reads=[t_hT[tt], t_WB], writes=[tps])
                        eng = evac_engine()
                        S.op(eng, copy_op(eng, VA0[:, t8 * 8:(t8 + 1) * 8, 0:64],
                                          ps[:, :].rearrange("p (t d) -> p t d", t=8)), writes=[tps, t_VA])
                S.dma("pool", lambda e, h=h: e.dma_start(
                    out=WA[:, 0:512].rearrange("p (c n) -> p c n", c=8),
                    in_=wview(e_w_in, SWA_Q0 + h * 64, SWA_Q0 + (h + 1) * 64)), writes=[t_WA])
                for b in range(NB):
                    ps = PS[4 + b % 2]
                    tps = t_PS[4 + b % 2]
                    proj_fm(ps, tps, lambda k: WA[:, k * 64:(k + 1) * 64], 8,
                            lambda k, b=b: hT[:, k, b * 512:(b + 1) * 512], t_hT[4 * b:4 * b + 4], [t_WA], 64)
                    eng = evac_engine()
                    S.op(eng, copy_op(eng, QT[0:64, b * 512:(b + 1) * 512], ps[0:64, :], scale=0.125),
                         writes=[tps, t_QT[b]])
                groups = []
                for g in range(NT):
                    kbs = []
                    if g > 0:
                        kbs.append((g - 1, 0, SWT[:, (h * 2 + 0) * 128:(h * 2 + 1) * 128]))
                    kbs.append((g, 0, SWT[:, (h * 2 + 1) * 128:(h * 2 + 2) * 128]))
                    groups.append((g * 128, 128, kbs, g))
                c = 4 + h // 2
                po = (h % 2) * 64

                def out_fn(g, c=c, po=po):
                    return AOT[po:po + 64, c, g * 128:(g + 1) * 128], [t_AOT[c][g // 4]]
                attention(64, groups, None, 1.0, lambda j: VA0[:, j, :], out_fn,
                          den_add=esink[64:128, h:h + 1], fp32_tables=True, rd_extra=[t_sm, t_kf])
            S.barrier()
            checkpoint(2)

            W416 = WA[:, 0:8 * 416].rearrange("p (c n) -> p c n", c=8)
            S.dma("pool", lambda e: e.dma_start(out=W416, in_=wview(e_w_in, 0, 416)), writes=[t_WA])
            WROT = WB[:, 0:8 * 96].rearrange("p (c n) -> p c n", c=8)
            S.op("dve", lambda e: e.memset(WROT[:, :, 0:64], 0.0), writes=[t_WB])
            S.op("dve", lambda e: e.tensor_scalar(out=WROT[:, :, 64:80], in0=W416[:, :, 400:416], scalar1=-1.0,
                                                  scalar2=None, op0=ALU.mult), reads=[t_WA], writes=[t_WB])
            S.op("dve", lambda e: e.tensor_copy(out=WROT[:, :, 80:96], in_=W416[:, :, 384:400]), reads=[t_WA],
                 writes=[t_WB])
            for b in range(NB):
                bs = slice(b * 512, (b + 1) * 512)
                hsrc = lambda k, b=b: hT[:, k, b * 512:(b + 1) * 512]
                ht = t_hT[4 * b:4 * b + 4]
                proj_fm(PS[0], t_PS[0], lambda k: W416[:, k, 0:128], 8, hsrc, ht, [t_WA], 128)
                proj_fm(PS[1], t_PS[1], lambda k: W416[:, k, 128:256], 8, hsrc, ht, [t_WA], 128)
                proj_fm(PS[2], t_PS[2], lambda k: W416[:, k, 256:384], 8, hsrc, ht, [t_WA], 128)
                proj_fm(PS[3], t_PS[3], lambda k: W416[:, k, 320:416], 8, hsrc, ht, [t_WA], 96)
                proj_fm(PS[4], t_PS[4], lambda k: WROT[:, k, :], 8, hsrc, ht, [t_WB], 96)
                for n in range(3):
                    S.op("act", lambda e, n=n: e.activation(out=ET[n][:], in_=PS[n][:], func=AF.Square),
                         writes=[t_PS[n], t_ET[n]])
                S.op("pe", lambda e: e.matmul(PS[5][:, :], lhsT=onesf[:], rhs=ET[0][:], start=True, stop=False),
                     reads=[t_ET[0], t_cst], writes=[t_PS[5]])
                S.op("pe", lambda e: e.matmul(PS[5][:, :], lhsT=onesf[:], rhs=ET[1][:], start=False, stop=True),
                     reads=[t_ET[1], t_cst], writes=[t_PS[5]])
                S.op("pe", lambda e: e.matmul(PS[6][:, :], lhsT=onesf[:], rhs=ET[2][:], start=True, stop=True),
                     reads=[t_ET[2], t_cst], writes=[t_PS[6]])
                for (pi, n_, n) in ((5, 256.0, 0), (6, 128.0, 1)):
                    S.op("act", lambda e, pi=pi, n_=n_, n=n: e.activation(out=ET[n][:], in_=PS[pi][:], func=AF.Ln,
                                                                    bias=epsb[:, 0:1], scale=1.0 / n_),
                         reads=[t_eps], writes=[t_PS[pi], t_ET[n]])
                    S.op("act", lambda e, n=n: e.activation(out=ET[n][:], in_=ET[n][:], func=AF.Exp, scale=-0.5),
                         writes=[t_ET[n]])
                for (pi, chunk, gsc, n) in ((0, 0, gq[:, 0:1], 0), (1, 1, gq[:, 1:2], 0), (2, 2, gkv[:, 0:1], 1)):
                    S.op("dve", lambda e, pi=pi, chunk=chunk, gsc=gsc, n=n, bs=bs: e.scalar_tensor_tensor(
                        out=LATt[:, chunk, bs], in0=PS[pi][:], scalar=gsc, in1=ET[n][:], op0=ALU.mult, op1=ALU.mult),
                        reads=[t_ET[n], t_sm], writes=[t_PS[pi], t_LAT[b]])
                S.op("dve", lambda e, bs=bs: e.tensor_tensor(out=RC[0][64:96, :], in0=PS[3][64:96, :], in1=TRIG[64:96, bs],
                                                         op=ALU.mult), reads=[t_TRIG], writes=[t_PS[3], t_RC[0]])
                S.op("dve", lambda e, bs=bs: e.tensor_tensor(out=RC[1][64:96, :], in0=PS[4][64:96, :], in1=TRIG[96:128, bs],
                                                         op=ALU.mult), reads=[t_TRIG], writes=[t_PS[4], t_RC[1]])
                S.op("dve", lambda e, bs=bs: e.tensor_tensor(out=KT[64:96, bs], in0=RC[0][64:96, :], in1=RC[1][64:96, :],
                                                         op=ALU.add), reads=[t_RC[0], t_RC[1]], writes=[t_KTaug])
            S.barrier()
            checkpoint(3)

            WQ = WA[:, 0:2 * 768].rearrange("p (c n) -> p c n", c=2)
            S.dma("pool", lambda e: e.dma_start(out=WQ, in_=wview(e_w_q_up, 0, 768)), writes=[t_WA])
            WQR = WA[:, 1536:1536 + 2 * 768].rearrange("p (c n) -> p c n", c=2)
            WKV = WB[:, 0:1024]
            S.dma("pool", lambda e: e.dma_start(out=WKV, in_=e_w_kv_up[:, :]), writes=[t_WB])
            S.op("dve", lambda e: e.memset(WA[:, 1536:1536 + 2 * 768], 0.0), writes=[t_WA])
            for c2 in range(2):
                src = WQ[:, c2, :].rearrange("p (h d) -> p h d", h=8)
                dst = WQR[:, c2, :].rearrange("p (h d) -> p h d", h=8)
                S.op("dve", lambda e, src=src, dst=dst: e.tensor_scalar(out=dst[:, :, 64:80], in0=src[:, :, 80:96],
                                                                  scalar1=-1.0, scalar2=None, op0=ALU.mult),
                     writes=[t_WA])
                S.op("dve", lambda e, src=src, dst=dst: e.tensor_copy(out=dst[:, :, 80:96], in_=src[:, :, 64:80]),
                     writes=[t_WA])
            mla_scale = 96.0 ** -0.5
            for h in range(8):
                for b in range(NB):
                    ps = PS[4 + b % 2]
                    tps = t_PS[4 + b % 2]
                    S.op("pe", lambda e, b=b, h=h, ps=ps: e.matmul(ps[0:64, :], lhsT=WKV[:, h * 128:h * 128 + 64],
                                                             rhs=LATt[:, 2, b * 512:(b + 1) * 512], start=True, stop=True),
                         reads=[t_LAT[b], t_WB], writes=[tps])
                    eng = evac_engine()
                    S.op(eng, copy_op(eng, KT[0:64, b * 512:(b + 1) * 512], ps[0:64, :]), writes=[tps, t_KT])
                for t8 in range(4):
                    ps = PS[4 + t8 % 2]
                    tps = t_PS[4 + t8 % 2]
                    for ti in range(8):
                        tt = t8 * 8 + ti
                        S.op("pe", lambda e, tt=tt, ti=ti, h=h, ps=ps: e.matmul(
                            ps[:, ti * 64:(ti + 1) * 64], lhsT=LATt[:, 2, tt * 128:(tt + 1) * 128],
                            rhs=WKV[:, h * 128 + 64:h * 128 + 128], start=True, stop=True),
                            reads=[t_LAT[tt // 4], t_WB], writes=[tps])
                    eng = evac_engine()
                    S.op(eng, copy_op(eng, VA0[:, t8 * 8:(t8 + 1) * 8, 0:64],
                                      ps[:, :].rearrange("p (t d) -> p t d", t=8)), writes=[tps, t_VA])
                for b in range(NB):
                    bs = slice(b * 512, (b + 1) * 512)
                    p1, tp1 = PS[4], t_PS[4]
                    p2, tp2 = PS[5], t_PS[5]
                    for k in range(2):
                        S.op("pe", lambda e, k=k, h=h, bs=bs: e.matmul(p1[0:96, :], lhsT=WQ[:, k, h * 96:(h + 1) * 96],
                                                                rhs=LATt[:, k, bs], start=(k == 0), stop=(k == 1)),
                             reads=[t_LAT[b], t_WA], writes=[tp1])
                    for k in range(2):
                        S.op("pe", lambda e, k=k, h=h, bs=bs: e.matmul(p2[0:96, :], lhsT=WQR[:, k, h * 96:(h + 1) * 96],
                                                                rhs=LATt[:, k, bs], start=(k == 0), stop=(k == 1)),
                             reads=[t_LAT[b], t_WA], writes=[tp2])
                    S.op("act", lambda e, bs=bs: e.copy(out=QT[0:64, bs], in_=p1[0:64, :]),
                         writes=[tp1, t_QT[b]])
                    S.op("dve", lambda e, bs=bs: e.tensor_tensor(out=RC[0][64:96, :], in0=p1[64:96, :], in1=TRIG[64:96, bs],
                                                             op=ALU.mult), reads=[t_TRIG], writes=[tp1, t_RC[0]])
                    S.op("dve", lambda e, bs=bs: e.tensor_tensor(out=RC[1][64:96, :], in0=p2[64:96, :], in1=TRIG[96:128, bs],
                                                             op=ALU.mult), reads=[t_TRIG], writes=[tp2, t_RC[1]])
                    S.op("dve", lambda e, bs=bs: e.tensor_tensor(out=QT[64:96, bs], in0=RC[0][64:96, :], in1=RC[1][64:96, :],
                                                             op=ALU.add), reads=[t_RC[0], t_RC[1]], writes=[t_QT[b]])
                c = h // 2
                po = (h % 2) * 64

                def out_fn(g, c=c, po=po):
                    return AOT[po:po + 64, c, g * 512:(g + 1) * 512], [t_AOT[c][g]]
                attention(96, dense_groups(), None, mla_scale, lambda j: VA0[:, j, :], out_fn)

            checkpoint(4)
            gate_phase(e_w_in, 1184)
            checkpoint(5)
            out_phase(e_w_out, x_d, None, last_layer=False, next_gain_d=(o_g_in if do1 else None))
            checkpoint(6)

        if do1:
            VA4 = at("VA4", [128, 32, 4, 128], BF16, o_big)
            if not do0:
                load_gain(o_g_in)
                phase_A(x1_d)
                S.barrier()
            Q0, K0, V0, F0, G0 = 0, 1024, 2048, 3072, 3088
            WF = WB[:, 0:128].rearrange("p (c n) -> p c n", c=8)
            S.dma("pool", lambda e: e.dma_start(out=WF, in_=wview(o_w_in, F0, F0 + 16)), writes=[t_WB])
            NL = at("NL", [128, 32, 16], F32, o_wa)
            trif = at("trif", [128, 128], F32, o_wa + 2048)
            identf = at("identf", [128, 128], F32, o_wa + 2560)
            CTf = at("CTf", [16, S_LEN], F32, o_big)
            r1 = at("r1", [16, S_LEN], F32, o_big + 16384)
            CS = at("CS", [128, S_LEN], BF16, o_trig)
            bfb = sb("bfb", [128, 16], F32)
            Ct = sb("Ct", [128, 32, 16], F32)
            Rs = sb("Rs", [128, 16], F32)
            zt = sb("zt", [128, 16], F32)
            t_bfb, t_NL, t_Ct, t_Rs, t_zt = S.tok("bfb"), S.toks(NT, "NL"), S.toks(NT, "Ct"), S.tok("Rs"), S.tok("zt")
            t_c1 = S.tok("cst1")
            S.dma("sp", lambda e: e.dma_start(out=trif[:], in_=cst["c_tri"][:, :]), writes=[t_c1])
            S.dma("sp", lambda e: e.dma_start(out=identf[:], in_=cst["c_ident"][:, :]), writes=[t_c1])
            S.dma("sp", lambda e: e.dma_start(out=bfb[:], in_=bass.AP(o_b_f.tensor, 0, [[0, 128], [1, 16]])),
                  writes=[t_bfb])
            S.op("dve", lambda e: e.memset(Rs[:], 0.0), writes=[t_Rs])
            for tt in range(NT):
                ps = PS[6]
                tps = t_PS[6]
                for k in range(8):
                    S.op("pe", lambda e, k=k, tt=tt: e.matmul(ps[:, 0:16], lhsT=hT[:, k, tt * 128:(tt + 1) * 128],
                                                          rhs=WF[:, k, :], start=(k == 0), stop=(k == 7)),
                         reads=[t_hT[tt], t_WB], writes=[tps])
                S.op("dve", lambda e: e.tensor_tensor(out=zt[:], in0=ps[:, 0:16], in1=bfb[:], op=ALU.add),
                     reads=[t_bfb], writes=[tps, t_zt])
                S.op("act", lambda e: e.activation(out=zt[:], in_=zt[:], func=AF.Exp, scale=-1.0), writes=[t_zt])
                S.op("act", lambda e, tt=tt: e.activation(out=NL[:, tt, :], in_=zt[:], func=AF.Ln, bias=epsb[:, 1:2],
                                                         scale=1.0), reads=[t_eps], writes=[t_zt, t_NL[tt]])
                pc = PS[5]
                tpc = t_PS[5]
                S.op("pe", lambda e, tt=tt: e.matmul(pc[:, 0:16], lhsT=trif[:], rhs=NL[:, tt, :], start=True, stop=False),
                     reads=[t_NL[tt], t_c1], writes=[tpc])
                S.op("pe", lambda e: e.matmul(pc[:, 0:16], lhsT=onesf[:], rhs=Rs[:], start=False, stop=True),
                     reads=[t_Rs, t_cst], writes=[tpc])
                S.op("dve", lambda e, tt=tt: e.tensor_copy(out=Ct[:, tt, :], in_=pc[:, 0:16]), writes=[tpc, t_Ct[tt]])
                S.op("dve", lambda e, tt=tt: e.tensor_tensor(out=Rs[:], in0=Rs[:], in1=NL[:, tt, :], op=ALU.add),
                     reads=[t_NL[tt]], writes=[t_Rs])
            t_CS, t_r1, t_ctf = S.tok("CS"), S.tok("r1"), S.tok("ctf")
            for g4 in range(8):
                ps = PS[6]
                tps = t_PS[6]
                for ti in range(4):
                    tt = g4 * 4 + ti
                    S.op("pe", lambda e, tt=tt, ti=ti: e.matmul(ps[0:16, ti * 128:(ti + 1) * 128], lhsT=Ct[:, tt, :],
                                                            rhs=identf[:], start=True, stop=True),
                         reads=[t_Ct[tt], t_c1], writes=[tps])
                S.op("dve", lambda e, g4=g4: e.tensor_scalar(out=CTf[:, g4 * 512:(g4 + 1) * 512], in0=ps[0:16, :],
                                                          scalar1=-1.0, scalar2=None, op0=ALU.mult),
                     writes=[tps, t_ctf])
            tmpb = at("tmpb", [16, S_LEN], BF16, o_aot)
            t_tmpb = S.tok("tmpb")
            S.op("dve", lambda e: e.tensor_copy(out=CS[0:16, :], in_=CTf[:, :]), reads=[t_ctf], writes=[t_CS])
            S.op("dve", lambda e: e.tensor_tensor(out=r1[:, :], in0=CTf[:, :], in1=CS[0:16, :], op=ALU.subtract),
                 reads=[t_ctf, t_CS], writes=[t_r1])
            S.op("dve", lambda e: e.tensor_copy(out=tmpb[:, :], in_=r1[:, :]), reads=[t_r1], writes=[t_tmpb])
            S.op("dve", lambda e: e.tensor_copy(out=CS[32:48, :], in_=tmpb[:, :]), reads=[t_tmpb], writes=[t_CS])
            S.op("dve", lambda e: e.tensor_tensor(out=r1[:, :], in0=r1[:, :], in1=tmpb[:, :], op=ALU.subtract),
                 reads=[t_tmpb], writes=[t_r1])
            S.op("dve", lambda e: e.tensor_copy(out=CS[64:80, :], in_=r1[:, :]), reads=[t_r1], writes=[t_CS])
            S.barrier()
            checkpoint(7)
            if 'g' in DBG:
                S.op("pe", lambda e: e.matmul(PS[6][0:64, 0:16], lhsT=hT[:, 0, 0:64], rhs=hT[:, 0, 0:16],
                                              start=True, stop=True), reads=[t_hT[0]], writes=[t_PS[6]])
            if 'a' not in DBG:
                S.op("dve", lambda e: e.memset(KT[64:67, :], 1.0), writes=[t_KTaug])
            WBqk = WB[:, 0:1024].rearrange("p (c s n) -> p c s n", c=8, s=2)
            checkpoint(71)

            KT2 = at("KT2", [128, S_LEN], BF16, o_wa)
            KTb = [KT, KT2]
            t_KTb = [[S.tok("ktb0"), S.tok("ktb0aug")], [S.tok("ktb1"), S.tok("ktb1aug")]]
            S.op("dve", lambda e: e.memset(KT2[64:67, :], 1.0), writes=[t_KTb[1][1]])
            t_KTb[0][1] = t_KTaug
            WVb = at("WVb", [128, 1024], BF16, o_rc + 2048)
            t_WQb, t_WKb, t_WVb = S.tok("wqb"), S.tok("wkb"), S.tok("wvb")
            t_VAp = S.toks(2, "vap")
            S.op("dve", lambda e: e.memset(VA4[:, :, 0:2, 64:128], 1.0), writes=[t_VAp[0], t_VA])
            S.op("dve", lambda e: e.memset(VA4[:, :, 2:4, 64:128], 1.0), writes=[t_VAp[1], t_VA])
            rot = {"n": 0}

            def bank():
                rot["n"] += 1
                return 4 + rot["n"] % 2

            def load_wk(h):
                S.dma("pool", lambda e: e.dma_start(out=WBqk[:, :, 1, :],
                                                    in_=wview(o_w_in, K0 + h * 64, K0 + (h + 1) * 64)),
                      writes=[t_WKb])

            def load_wq(h):
                S.dma("pool", lambda e: e.dma_start(out=WBqk[:, :, 0, :],
                                                    in_=wview(o_w_in, Q0 + h * 64, Q0 + (h + 1) * 64)),
                      writes=[t_WQb])

            def load_wv(p):
                S.dma("pool", lambda e: e.dma_start(out=WVb[:, :].rearrange("p (c n) -> p c n", c=8),
                                                    in_=wview(o_w_in, V0 + p * 128, V0 + (p + 1) * 128)),
                      writes=[t_WVb])

            def k_block(h, b):
                pb = bank()
                proj_fm(PS[pb], t_PS[pb], lambda k: WB[:, k * 128 + 64:(k + 1) * 128], 8,
                        lambda k: hT[:, k, b * 512:(b + 1) * 512], t_hT[4 * b:4 * b + 4], [t_WKb], 64)
                S.op("dve", copy_op("dve", KTb[h % 2][0:64, b * 512:(b + 1) * 512], PS[pb][0:64, :]),
                     writes=[t_PS[pb], t_KTb[h % 2][0]])

            def q_block(h, b):
                pb = bank()
                both = h + 1 < 16
                M = 128 if both else 64
                proj_fm(PS[pb], t_PS[pb], lambda k: WB[:, k * 128:k * 128 + M], 8,
                        lambda k: hT[:, k, b * 512:(b + 1) * 512], t_hT[4 * b:4 * b + 4],
                        [t_WQb, t_WKb] if both else [t_WQb], M)
                S.op("dve", copy_op("dve", QT[0:64, b * 512:(b + 1) * 512], PS[pb][0:64, :], scale=0.125),
                     writes=[t_PS[pb], t_QT[b]])
                if both:
                    S.op("dve", copy_op("dve", KTb[(h + 1) % 2][0:64, b * 512:(b + 1) * 512], PS[pb][64:128, :]),
                         writes=[t_PS[pb], t_KTb[(h + 1) % 2][0]])

            def v_tiles(p, t4):
                pb = bank()
                ps = PS[pb]
                s0 = 2 * (p % 2)
                for ti in range(4):
                    tt = t4 * 4 + ti
                    for k in range(8):
                        S.op("pe", lambda e, k=k, tt=tt, ti=ti: e.matmul(
                            ps[:, ti * 128:(ti + 1) * 128], lhsT=hT[:, k, tt * 128:(tt + 1) * 128],
                            rhs=WVb[:, k * 128:(k + 1) * 128], start=(k == 0), stop=(k == 7)),
                            reads=[t_hT[tt], t_WVb], writes=[t_PS[pb]])
                for ti in range(4):
                    tt = t4 * 4 + ti
                    S.op("dve", copy_op("dve", VA4[:, tt, s0:s0 + 2, 0:64],
                                        ps[:, ti * 128:(ti + 1) * 128].rearrange("p (h d) -> p h d", h=2)),
                         writes=[t_PS[pb], t_VAp[p % 2]])

            load_wv(0)
            for t4 in range(8):
                v_tiles(0, t4)
            load_wk(0)
            for b in range(NB):
                k_block(0, b)
            for h in range(16):
                hh = h % 4
                load_wq(h)
                for r_ in range(3):
                    S.dma("sp", lambda e, h=h, r_=r_: e.dma_start(out=QT[64 + r_:65 + r_, :],
                                                                 in_=CS[32 * r_ + h:32 * r_ + h + 1, :]),
                          reads=[t_CS], writes=[t_QTaug])
                bgl = []
                if h + 1 < 16:
                    load_wk(h + 1)
                if h % 2 == 1 and h + 1 < 16:
                    load_wv((h + 1) // 2)
                    bgl += [(lambda h=h, t4=t4: v_tiles((h + 1) // 2, t4)) for t4 in range(8)]
                c = h // 2
                po = (h % 2) * 64

                def out_fn(g, c=c, po=po):
                    return AOT[po:po + 64, c, g * 512:(g + 1) * 512], [t_AOT[c][g]]
                attention(67, dense_groups(), lambda j, h=h: Ct[:, j, h:h + 1], 1.0,
                          lambda j, hh=hh: VA4[:, j, hh, :], out_fn, rd_extra=t_Ct,
                          KTt=(KTb[h % 2], t_KTb[h % 2]), pre_group=(lambda g, h=h: q_block(h, g)),
                          bg=bgl, va_tok=t_VAp[(h // 2) % 2], one_rc=True)

            checkpoint(8)
            S.barrier()
            gate_phase(o_w_in, G0)
            checkpoint(9)
            out_phase(o_w_out, x1_d, (t_x1 if mode == "full" else None), last_layer=True)
            S.wait_all("sp", t_out)
        else:
            S.wait_all("sp", t_x1)


    def checkpoint(k):
        if stop == k:
            raise _Stop()

    try:
        _layers()
    except _Stop:
        pass
    S.emit()
    return nc


_CACHE = {}


def _get(mode):
    if mode not in _CACHE:
        _CACHE[mode] = build_program(mode)
    return _CACHE[mode]


L0_KEYS = ["e_g_in", "e_w_in", "e_g_q_a", "e_w_q_up", "e_g_kv_a", "e_w_kv_up", "e_sinks", "e_w_out"]
L1_KEYS = ["o_g_in", "o_w_in", "o_b_f", "o_w_out"]


def _maps(inputs, n, mode, x1=None):
    consts = _constants()
    maps = []
    for b in range(n):
        m = dict(consts)
        if mode in ("full", "l0"):
            m["x"] = np.ascontiguousarray(inputs["x"][b])
            m["positions"] = np.ascontiguousarray(inputs["positions"][b]).astype(np.int32)
            for k in L0_KEYS:
                m[k] = np.ascontiguousarray(inputs[k][0])
        if mode in ("full", "l1"):
            for k in L1_KEYS:
                m[k] = np.ascontiguousarray(inputs[k][0])
            m["g_final"] = np.ascontiguousarray(inputs["g_final"])
        if mode == "l1":
            m["x1"] = np.ascontiguousarray(x1[b])
        maps.append(m)
    return maps


def kernel(**inputs):
    n = inputs["x"].shape[0]
    inputs = {k: np.asarray(v) for k, v in inputs.items()}
    nc = _get("full")
    res = run_bass_kernel_spmd(nc, _maps(inputs, n, "full"), core_ids=list(range(n)))
    return np.stack([np.asarray(r["out"]) for r in res.results], axis=0).astype(np.float32)
```

```python
import math
import os
DBG = os.environ.get('KDBG', '')
import numpy as np
import concourse.bass as bass
import concourse.mybir as mybir
from concourse.bass_utils import run_bass_kernel_spmd

F32 = mybir.dt.float32
BF16 = mybir.dt.bfloat16
I32 = mybir.dt.int32
AF = mybir.ActivationFunctionType
ALU = mybir.AluOpType

S_LEN = 4096
D = 1024
NT = 32
NB = 8
EPS = 1e-6
SEM_ROT = 12000


class Tok:
    __slots__ = ("name", "w", "r", "dsem")

    def __init__(self, name=""):
        self.name = name
        self.w = None
        self.r = {}
        self.dsem = None


class _Rec:
    def __init__(self):
        self.call = None

    def __getattr__(self, name):
        def f(*a, **k):
            assert self.call is None
            self.call = (name, a, k)
            return self
        return f


def _freeze(fn):
    r = _Rec()
    fn(r)
    assert r.call is not None
    return r.call


class Sched:
    ENGS = ("pe", "act", "dve", "pool", "sp")

    def __init__(self, nc):
        self.nc = nc
        self.sems = []
        self.semeng = {}
        self.ops = {e: [] for e in self.ENGS}
        self.esem = {}
        self.ecnt = {}
        self.seen = {e: {} for e in self.ENGS}
        self.semval = {}
        self.unsig = {e: False for e in self.ENGS}
        self.noself = {"pe"}
        for e in ("pe", "act", "dve", "pool"):
            self._new_esem(e)

    def _alloc(self, name, eng=None):
        h = self.nc.alloc_semaphore(name=name)
        self.sems.append(h)
        sid = len(self.sems) - 1
        self.semval[sid] = 0
        self.semeng[sid] = eng
        return sid

    def _new_esem(self, e):
        self.esem[e] = self._alloc(f"s_{e}_{len(self.sems)}", e)
        self.ecnt[e] = 0

    def tok(self, name=""):
        return Tok(name)

    def toks(self, n, name=""):
        return [Tok(f"{name}{i}") for i in range(n)]

    def _collect(self, e, reads, writes):
        need = {}

        def add(s, v):
            if need.get(s, 0) < v:
                need[s] = v
        for t in reads:
            if t.w is not None:
                add(*t.w)
        for t in writes:
            if t.w is not None:
                add(*t.w)
            for s, v in t.r.items():
                add(s, v)
        waits = []
        seen = self.seen[e]
        for s, v in need.items():
            if e in self.noself and self.semeng[s] == e:
                continue
            if seen.get(s, 0) >= v:
                continue
            seen[s] = v
            waits.append((s, v))
        return waits

    def _mark(self, s, val, reads, writes):
        for t in reads:
            if t.r.get(s, 0) < val:
                t.r[s] = val
        for t in writes:
            t.w = (s, val)
            t.r = {}

    def op(self, e, fn, reads=(), writes=(), sig=True):
        waits = self._collect(e, reads, writes)
        if sig and self.ecnt[e] >= SEM_ROT and not self.unsig[e]:
            self._new_esem(e)
        self.unsig[e] = not sig
        s = self.esem[e]
        val = self.ecnt[e] + 1
        if sig:
            self.ecnt[e] = val
            self.semval[s] = val
        self.ops[e].append((waits, _freeze(fn), (s, 1) if sig else None))
        self._mark(s, val, reads, writes)

    def barrier(self):
        cur = [(s, v) for s, v in self.semval.items() if v > 0]
        for e in self.ENGS:
            waits = []
            for s, v in cur:
                if (self.semeng[s] == e and e in self.noself) or self.seen[e].get(s, 0) >= v:
                    continue
                self.seen[e][s] = v
                waits.append((s, v))
            self.ops[e].append((waits, None, None))

    def dma(self, q, fn, reads=(), writes=(), owner=None):
        waits = self._collect(q, reads, writes)
        if owner is None:
            owner = writes[0] if writes else reads[0]
        if owner.dsem is None or self.semval[owner.dsem] >= SEM_ROT * 2:
            owner.dsem = self._alloc(f"d_{len(self.sems)}")
        s = owner.dsem
        self.semval[s] += 16
        val = self.semval[s]
        self.ops[q].append((waits, _freeze(fn), (s, 16)))
        self._mark(s, val, reads, writes)

    def wait_all(self, e, toks):
        waits = self._collect(e, [], toks)
        self.ops[e].append((waits, None, None))

    def emit(self):
        nc = self.nc
        sems = self.sems
        ops = self.ops

        def replay(eng, lst):
            for waits, fn, inc in lst:
                for s, v in waits:
                    eng.wait_ge(sems[s], v)
                if fn is None:
                    continue
                name, a, k = fn
                ins = getattr(eng, name)(*a, **k)
                if inc is not None:
                    ins.then_inc(sems[inc[0]], inc[1])

        with nc.Block() as block:
            @block.tensor
            def _(eng):
                replay(eng, ops["pe"])

            @block.scalar
            def _(eng):
                replay(eng, ops["act"])

            @block.vector
            def _(eng):
                replay(eng, ops["dve"])

            @block.gpsimd
            def _(eng):
                replay(eng, ops["pool"])

            @block.sync
            def _(eng):
                replay(eng, ops["sp"])


def _constants():
    c = {}
    c["c_ident"] = np.eye(128, dtype=np.float32)
    k = np.arange(128)[:, None]
    q = np.arange(128)[None, :]
    c["c_tri"] = (k <= q).astype(np.float32)
    c["c_ones"] = np.ones((128, 128), np.float32)
    qq = np.arange(512)[None, None, :]
    kk = np.arange(128)[:, None, None]
    rr = np.arange(4)[None, :, None]
    c["c_mask"] = ((rr * 128 + kk) <= qq).astype(np.float32)
    slopes = 2.0 ** (-8.0 * (np.arange(8, dtype=np.float64) + 1.0) / 8)
    t = np.zeros((128, 8, 2, 128), np.float64)
    kq = np.arange(128)[:, None]
    qv = np.arange(128)[None, :]
    for h in range(8):
        dist0 = 128 + qv - kq
        t[:, h, 1, :] = np.where(dist0 < 128, np.exp(-slopes[h] * dist0), 0.0)
        dist1 = qv - kq
        t[:, h, 0, :] = np.where(dist1 >= 0, np.exp(-slopes[h] * np.maximum(dist1, 0)), 0.0)
    c["c_swt"] = t.astype(np.float32)
    invf = 1.0 / (10000.0 ** (np.arange(0, 32, 2, dtype=np.float32) / 32))
    v = np.zeros((128, 2), np.float32)
    for p in range(64, 128):
        v[p, 0] = invf[(p - 64) % 16]
        v[p, 1] = (math.pi / 2) if p < 96 else 0.0
    c["c_rope"] = v
    return c


CONST_SHAPES = {"c_ident": [128, 128], "c_tri": [128, 128], "c_ones": [128, 128],
                "c_mask": [128, 4, 512], "c_swt": [128, 8, 2, 128], "c_rope": [128, 2]}


class _Stop(Exception):
    pass


def build_program(mode="full", stop=0):
    nc = bass.Bass("TRN2", target_bir_lowering=False)
    S = Sched(nc)
    do0 = mode in ("full", "l0")
    do1 = mode in ("full", "l1")

    def din(name, shape, dt=F32):
        return nc.dram_tensor(name, shape, dt, kind="ExternalInput").ap()

    cst = {k: din(k, v) for k, v in CONST_SHAPES.items()}
    if do0:
        x_d = din("x", [S_LEN, D])
        pos_d = din("positions", [S_LEN], I32)
        e_g_in = din("e_g_in", [D])
        e_w_in = din("e_w_in", [D, 2208])
        e_g_q_a = din("e_g_q_a", [256])
        e_w_q_up = din("e_w_q_up", [256, 768])
        e_g_kv_a = din("e_g_kv_a", [128])
        e_w_kv_up = din("e_w_kv_up", [128, 1024])
        e_sinks = din("e_sinks", [8])
        e_w_out = din("e_w_out", [D, D])
    if do1:
        o_g_in = din("o_g_in", [D])
        o_w_in = din("o_w_in", [D, 4112])
        o_b_f = din("o_b_f", [16])
        o_w_out = din("o_w_out", [D, D])
        g_final = din("g_final", [D])
        out_d = nc.dram_tensor("out", [S_LEN, D], F32, kind="ExternalOutput").ap()
    if mode == "full":
        x1_d = nc.dram_tensor("x1s", [S_LEN, D], F32).ap()
    elif mode == "l0":
        x1_d = nc.dram_tensor("x1", [S_LEN, D], F32, kind="ExternalOutput").ap()
    else:
        x1_d = din("x1", [S_LEN, D])
    t_x1 = S.toks(NT, "x1d")
    t_x1own = S.tok("x1own")
    t_outown = S.toks(2, "outown")
    t_out = S.toks(NT, "outd")

    def region(nbytes):
        st, _ = nc.bump_sbuf(nbytes)
        return st

    def at(name, shape, dt, off):
        return nc.alloc_sbuf_tensor_at(name, shape, dt, offset=off)

    def sb(name, shape, dt):
        return nc.alloc_sbuf_tensor(name, shape, dt)

    hT = sb("hT", [128, 8, S_LEN], BF16)
    t_hT = S.toks(NT, "hT")
    o_aot = region(65536)
    AOT = at("AOT", [128, 8, S_LEN], BF16, o_aot)
    t_AOT = [[S.tok(f"aot{c}_{b}") for b in range(NB)] for c in range(8)]
    o_big = region(32768)
    BIG = at("BIG", [128, 32 * 4 * 128], BF16, o_big)
    t_VA = S.tok("VA")
    t_LAT = S.toks(NB, "LAT")
    o_trig = region(8192)
    TRIG = at("TRIG", [128, S_LEN], BF16, o_trig)
    t_TRIG = S.tok("TRIG")
    o_pool = region(19456)
    QT = at("QT", [128, S_LEN], BF16, o_pool)
    t_QT = S.toks(NB, "QT")
    t_QTaug = S.tok("QTaug")
    KT = at("KT", [128, S_LEN], BF16, o_pool + 8192)
    t_KT = S.tok("KT")
    t_KTaug = S.tok("KTaug")
    NPT = 5
    PT = [at(f"pt{i}", [128, 512], BF16, o_pool + 16384 + 1024 * i) for i in range(3)]
    PT += [sb(f"ptx{i}", [128, 512], BF16) for i in range(NPT - 3)]
    t_PT = S.toks(NPT, "pt")
    XT = [at(f"xt{i}", [128, D], F32, o_pool + 4096 * i) for i in range(3)]
    t_XT = S.toks(3, "xt")
    XB = [at(f"xb{i}", [128, D], BF16, o_pool + 12288 + 2048 * i) for i in range(2)]
    t_XB = S.toks(2, "xb")
    ET = [at(f"et{i}", [128, 512], F32, o_pool + 2048 * i) for i in range(3)]
    t_ET = S.toks(3, "et")
    o_pf = region(1024)
    PF = [at(f"pf{i}", [128, 128], F32, o_pf + 512 * i) for i in range(2)]
    t_PF = S.toks(2, "pf")
    o_rc = region(4096)
    RC = [at(f"rc{i}", [128, 512], F32, o_rc + 2048 * i) for i in range(2)]
    t_RC = S.toks(2, "rc")
    GB = at("GB", [128, D], F32, o_rc)
    t_GB = S.tok("GB")
    o_wa = region(8192)
    WA = at("WA", [128, 3328], BF16, o_wa)
    t_WA = S.tok("WA")
    WB = sb("WB", [128, 1024], BF16)
    t_WB = S.tok("WB")
    MASK = sb("MASK", [128, 128], BF16)
    t_MASK = S.tok("MASK")
    identb = sb("identb", [128, 128], BF16)
    onesf = sb("onesf", [128, 128], F32)
    t_cst = S.tok("cst")
    stat = sb("stat", [128, 8], F32)
    t_statS = S.toks(2, "stat")
    epsb = sb("epsb", [128, 2], F32)
    t_eps = S.tok("eps")

    PS = [nc.alloc_psum_tensor(f"ps{i}", [128, 512], F32) for i in range(7)]
    t_PS = S.toks(7, "ps")
    PST = nc.alloc_psum_tensor("pst", [128, 8, 128], BF16)
    t_PST = S.tok("pst")

    S.dma("sp", lambda e: e.dma_start(out=onesf[:], in_=cst["c_ones"][:, :]), writes=[t_cst])
    t_cstb = S.tok("cstb")
    S.dma("pool", lambda e: e.dma_start(out=identb[:], in_=cst["c_ident"][:, :]), writes=[t_cstb])
    S.dma("pool", lambda e: e.dma_start(out=MASK[:], in_=cst["c_tri"][:, :]), writes=[t_MASK])
    S.op("dve", lambda e: e.memset(epsb[:, 0:1], EPS), writes=[t_eps])
    S.op("dve", lambda e: e.memset(epsb[:, 1:2], 1.0), writes=[t_eps])

    cnt = {"ev": 0}

    def evac_engine():
        cnt["ev"] += 1
        return "act" if cnt["ev"] % 2 else "dve"

    def copy_op(eng, out, in_, scale=None):
        if eng == "act":
            if scale is None:
                return lambda e: e.copy(out=out, in_=in_)
            return lambda e: e.mul(out=out, in_=in_, mul=scale)
        if scale is None:
            return lambda e: e.tensor_copy(out=out, in_=in_)
        return lambda e: e.tensor_scalar(out=out, in0=in_, scalar1=scale, scalar2=None, op0=ALU.mult)

    def load_gain(g_d):
        S.dma("sp", lambda e: e.dma_start(out=GB[:], in_=bass.AP(g_d.tensor, 0, [[0, 128], [1, D]])),
              writes=[t_GB])

    def rstd_from_ms(col, tst):
        S.op("act", lambda e: e.activation(out=stat[:, col + 1:col + 2], in_=stat[:, col:col + 1], func=AF.Ln,
                                           bias=epsb[:, 0:1], scale=1.0), reads=[t_eps], writes=[tst])
        S.op("act", lambda e: e.activation(out=stat[:, col + 1:col + 2], in_=stat[:, col + 1:col + 2],
                                           func=AF.Exp, scale=-0.5), writes=[tst])

    def norm_tile(xt, t_xt, out_ap, t_outs, junk_ap, t_junk, slot=0):
        c0 = 4 * slot
        tst = t_statS[slot]
        S.op("dve", lambda e: e.memset(stat[:, c0:c0 + 1], 0.0), writes=[tst])
        S.op("act", lambda e: e.activation(out=junk_ap, in_=xt[:], func=AF.Square, scale=1.0 / 32,
                                           accum_out=stat[:, c0:c0 + 1]), reads=[t_xt], writes=[t_junk, tst])
        rstd_from_ms(c0, tst)
        S.op("dve", lambda e: e.scalar_tensor_tensor(out=out_ap, in0=xt[:], scalar=stat[:, c0 + 1:c0 + 2], in1=GB[:],
                                                     op0=ALU.mult, op1=ALU.mult),
             reads=[t_xt, tst, t_GB], writes=list(t_outs))

    def to_hT(xt, t_xt, tt):
        i = tt % 2
        norm_tile(xt, t_xt, XB[i][:], [t_XB[i]], XB[i][:], t_XB[i], slot=i)
        for c in range(8):
            S.op("pe", lambda e, c=c: e.transpose(out=PST[:, c, :], in_=XB[i][:, c * 128:(c + 1) * 128],
                                                   identity=identb[:]),
                 reads=[t_XB[i], t_cstb], writes=[t_PST])
        eng = evac_engine()
        S.op(eng, copy_op(eng, hT[:, :, tt * 128:(tt + 1) * 128], PST[:, :, :]), writes=[t_PST, t_hT[tt]])

    def phase_A(src_d):
        for tt in range(NT):
            i = tt % 2
            S.dma("sp", lambda e, tt=tt, i=i: e.dma_start(out=XT[i][:], in_=src_d[tt * 128:(tt + 1) * 128, :]),
                  writes=[t_XT[i]])
            to_hT(XT[i], t_XT[i], tt)

    def wview(w_d, c0, c1):
        return w_d.rearrange("(c p) n -> p c n", p=128)[:, :, c0:c1]

    def proj_fm(ps, t_ps, w_ap_fn, nk, src_fn, src_toks, w_toks, M):
        for k in range(nk):
            S.op("pe", lambda e, k=k: e.matmul(ps[0:M, :], lhsT=w_ap_fn(k), rhs=src_fn(k),
                                               start=(k == 0), stop=(k == nk - 1)),
                 reads=list(src_toks) + list(w_toks), writes=[t_ps])

    def attention(KR, groups, bias_fn, scale, va_fn, out_fn, den_add=None, fp32_tables=False, rd_extra=(),
                  KTt=None, pre_group=None, bg=(), va_tok=None, one_rc=False, pf_bufs=None, skip_gc=False):
        steps = []
        for gi, (q0, QW, kbs, gidx) in enumerate(groups):
            for n, kb in enumerate(kbs):
                if len(kb) == 3:
                    j, c0, tbl = kb
                    c1, t0, t1 = QW, c0, c0 + 128
                else:
                    j, c0, c1, tbl = kb
                    t0, t1 = c0, c1
                steps.append((gi, q0, QW, j, c0, c1, tbl, t0, t1, n == 0, n == len(kbs) - 1, gidx))

        SB_ = (0, 1, 6)
        LA = 2
        KTx, t_KTx = (KT, [t_KT, t_KTaug]) if KTt is None else KTt
        t_VAx = t_VA if va_tok is None else va_tok
        PFx = PF if pf_bufs is None else pf_bufs
        bg = list(bg)
        nsteps = len(steps)
        bg_stride = max(1, nsteps // (len(bg) + 1)) if bg else 0

        def emit_qk(i):
            gi, q0, QW, j, c0, c1, tbl, t0, t1, first, last, gidx = steps[i]
            sp = PS[SB_[i % 3]]
            S.op("pe", lambda e: e.matmul(sp[:, c0:c1], lhsT=KTx[0:KR, j * 128:(j + 1) * 128],
                                          rhs=QT[0:KR, q0 + c0:q0 + c1], start=True, stop=True),
                 reads=list(t_KTx) + [t_QT[q0 // 512], t_QTaug], writes=[t_PS[SB_[i % 3]]])

        first_idx = {}
        for i_, st_ in enumerate(steps):
            first_idx.setdefault(st_[0], i_)
        if pre_group is not None:
            pre_group(groups[0][3])
        for i0 in range(min(LA, len(steps))):
            emit_qk(i0)
        for i, (gi, q0, QW, j, c0, c1, tbl, t0, t1, first, last, gidx) in enumerate(steps):
            if i + LA < len(steps):
                emit_qk(i + LA)
            if pre_group is not None and i == first_idx[gi] + 1 and gi + 1 < len(groups):
                pre_group(groups[gi + 1][3])
            if bg and i > 0 and i % bg_stride == 0:
                bg.pop(0)()
            sp = PS[SB_[i % 3]]
            tsp = t_PS[SB_[i % 3]]
            pt = PT[i % NPT]
            tpt = t_PT[i % NPT]
            kw = {"scale": scale}
            b = bias_fn(j) if bias_fn is not None else None
            if b is not None:
                kw["bias"] = b
            if tbl is not None and fp32_tables:
                pf = PFx[i % 2]
                w = c1 - c0
                S.op("act", lambda e: e.activation(out=pf[:, 0:w], in_=sp[:, c0:c1], func=AF.Exp, **kw),
                     reads=list(rd_extra), writes=[tsp, t_PF[i % 2]])
                S.op("dve", lambda e: e.tensor_tensor(out=pt[:, c0:c1], in0=pf[:, 0:w], in1=tbl, op=ALU.mult),
                     reads=[t_PF[i % 2]] + list(rd_extra), writes=[tpt])
            else:
                S.op("act", lambda e: e.activation(out=pt[:, c0:c1], in_=sp[:, c0:c1], func=AF.Exp, **kw),
                     reads=list(rd_extra), writes=[tsp, tpt])
                if tbl is not None:
                    S.op("dve", lambda e: e.tensor_tensor(out=pt[:, t0:t1], in0=pt[:, t0:t1], in1=tbl, op=ALU.mult),
                         reads=[t_MASK], writes=[tpt])
            op_ = PS[2 + gi % 2]
            top = t_PS[2 + gi % 2]
            mkw = {"skip_group_check": True} if skip_gc else {}
            S.op("pe", lambda e: e.matmul(op_[:, c0:c1], lhsT=va_fn(j), rhs=pt[:, c0:c1], start=first, stop=last,
                                          **mkw),
                 reads=[tpt, t_VAx], writes=[top])
            if last:
                rc = RC[0 if one_rc else gi % 2]
                trc = t_RC[0 if one_rc else gi % 2]
                if den_add is not None:
                    S.op("dve", lambda e: e.tensor_scalar(
                        out=rc[64:128, 0:QW], in0=op_[64:128, 0:QW], scalar1=den_add, scalar2=None, op0=ALU.add),
                        reads=list(rd_extra), writes=[top, trc])
                    S.op("act", lambda e: e.activation(out=rc[64:128, 0:QW], in_=rc[64:128, 0:QW], func=AF.Ln),
                         writes=[trc])
                    S.op("act", lambda e: e.activation(out=rc[64:128, 0:QW], in_=rc[64:128, 0:QW], func=AF.Exp,
                                                       scale=-1.0), writes=[trc])
                else:
                    S.op("dve", lambda e: e.reciprocal(out=rc[64:128, 0:QW], in_=op_[64:128, 0:QW]),
                         writes=[top, trc])
                o_ap, o_toks = out_fn(gidx)
                S.op("dve", lambda e: e.tensor_tensor(out=o_ap, in0=op_[0:64, 0:QW], in1=rc[64:128, 0:QW],
                                                      op=ALU.mult),
                     reads=[trc], writes=[top] + list(o_toks))
        while bg:
            bg.pop(0)()

    def _unused():
        pass

    def dense_groups():
        gs = []
        for g in range(NB):
            kbs = [(j, 0, None) for j in range(4 * g)] + [(4 * g + r, r * 128, MASK[:, :]) for r in range(4)]
            gs.append((g * 512, 512, kbs, g))
        return gs

    def gate_phase(w_in_d, goff):
        for c in range(8):
            S.dma("pool", lambda e, c=c: e.dma_start(
                out=WB[:, 0:1024].rearrange("p (c n) -> p c n", c=8),
                in_=wview(w_in_d, goff + c * 128, goff + (c + 1) * 128)), writes=[t_WB])
            for b in range(NB):
                ps = PS[4 + b % 2]
                tps = t_PS[4 + b % 2]
                proj_fm(ps, tps, lambda k: WB[:, k * 128:(k + 1) * 128], 8,
                        lambda k, b=b: hT[:, k, b * 512:(b + 1) * 512], t_hT[4 * b:4 * b + 4], [t_WB], 128)
                gt = PT[b % NPT]
                S.op("act", lambda e, gt=gt, ps=ps: e.activation(out=gt[:], in_=ps[:], func=AF.Silu),
                     writes=[tps, t_PT[b % NPT]])
                S.op("dve", lambda e, gt=gt, c=c, b=b: e.tensor_tensor(
                    out=AOT[:, c, b * 512:(b + 1) * 512], in0=AOT[:, c, b * 512:(b + 1) * 512], in1=gt[:],
                    op=ALU.mult), reads=[t_PT[b % NPT]], writes=[t_AOT[c][b]])

    def out_phase(w_out_d, res_d, t_res, last_layer, next_gain_d=None):
        S.barrier()
        WO = at("WO_%d" % int(last_layer), [128, 8 * 1024], BF16, o_big)
        t_WO = S.tok("WO")
        S.dma("pool", lambda e: e.dma_start(out=WO[:].rearrange("p (c n) -> p c n", c=8),
                                            in_=wview(w_out_d, 0, D)), writes=[t_WO])
        if last_layer:
            load_gain(g_final)
        elif next_gain_d is not None:
            load_gain(next_gain_d)
        t_x1o = S.toks(3, "x1own")
        t_outo = S.toks(3, "outown")
        def stage1(tt):
            i = tt % 3
            b = tt // 4
            pp = 2 * (tt % 2)
            S.dma("sp", lambda e: e.dma_start(out=XT[i][:], in_=res_d[tt * 128:(tt + 1) * 128, :]),
                  reads=[t_res[tt]] if t_res is not None else [], writes=[t_XT[i]])
            for half in range(2):
                ps = PS[pp + half]
                tps = t_PS[pp + half]
                for c in range(8):
                    S.op("pe", lambda e: e.matmul(
                        ps[:, :], lhsT=AOT[:, c, tt * 128:(tt + 1) * 128],
                        rhs=WO[:, c * 1024 + half * 512: c * 1024 + (half + 1) * 512],
                        start=(c == 0), stop=(c == 7)),
                        reads=[t_AOT[c][b], t_WO], writes=[tps])

        def stage1b(tt):
            i = tt % 3
            pp = 2 * (tt % 2)
            for half in range(2):
                ps = PS[pp + half]
                tps = t_PS[pp + half]
                S.op("dve", lambda e: e.tensor_tensor(
                    out=XT[i][:, half * 512:(half + 1) * 512], in0=ps[:, :], in1=XT[i][:, half * 512:(half + 1) * 512],
                    op=ALU.add), writes=[tps, t_XT[i]])

        def stage2(tt):
            i = tt % 3
            if last_layer:
                norm_tile(XT[i], t_XT[i], XT[i][:], [t_XT[i]], XB[tt % 2][:], t_XB[tt % 2], slot=tt % 2)
                S.dma("sp", lambda e: e.dma_start(out=out_d[tt * 128:(tt + 1) * 128, :], in_=XT[i][:]),
                      reads=[t_XT[i]], writes=[t_out[tt]], owner=t_outo[i])
            else:
                S.dma("sp", lambda e: e.dma_start(out=x1_d[tt * 128:(tt + 1) * 128, :], in_=XT[i][:]),
                      reads=[t_XT[i]], writes=[t_x1[tt]], owner=t_x1o[i])
                if mode == "full":
                    to_hT(XT[i], t_XT[i], tt)

        for tt in range(NT + 1):
            if tt < NT:
                stage1(tt)
            if tt >= 1:
                stage2(tt - 1)
            if tt < NT:
                stage1b(tt)
        S.barrier()

    def _layers():
        if do0:
            VA0 = at("VA0", [128, 32, 128], BF16, o_big)
            LATt = at("LATt", [128, 3, S_LEN], BF16, o_big + 8192)
            SWT = at("SWT", [128, 2048], F32, o_big + 8192)
            PFw = [at(f"pfw{i}", [128, 256], F32, o_big + 16384 + 1024 * i) for i in range(2)]
            posi = at("posi", [128, S_LEN], I32, o_aot)
            kf = at("kf", [128, S_LEN], F32, o_aot + 16384)
            ANG = at("ANG", [128, S_LEN], F32, o_aot + 32768)
            t_pos, t_kf, t_ang = S.tok("posi"), S.tok("kf"), S.tok("ang")
            load_gain(e_g_in)
            phase_A(x_d)

            gq = sb("gq", [128, 2], F32)
            gkv = sb("gkv", [128, 1], F32)
            esink = sb("esink", [128, 8], F32)
            ropec = sb("ropec", [128, 2], F32)
            t_sm = S.tok("small0")
            for c2 in range(2):
                S.dma("sp", lambda e, c2=c2: e.dma_start(
                    out=gq[:, c2:c2 + 1], in_=e_g_q_a[c2 * 128:(c2 + 1) * 128].rearrange("(p o) -> p o", o=1)),
                    writes=[t_sm])
            S.dma("sp", lambda e: e.dma_start(out=gkv[:], in_=e_g_kv_a.rearrange("(p o) -> p o", o=1)), writes=[t_sm])
            S.dma("sp", lambda e: e.dma_start(out=esink[:], in_=bass.AP(e_sinks.tensor, 0, [[0, 128], [1, 8]])),
                  writes=[t_sm])
            S.dma("sp", lambda e: e.dma_start(out=ropec[:], in_=cst["c_rope"][:, :]), writes=[t_sm])
            S.op("act", lambda e: e.activation(out=esink[:], in_=esink[:], func=AF.Exp), writes=[t_sm])

            S.dma("sp", lambda e: e.dma_start(out=posi[64:128, :], in_=bass.AP(pos_d.tensor, 0, [[0, 64], [1, S_LEN]])),
                  writes=[t_pos])
            P6 = slice(64, 128)
            S.op("dve", lambda e: e.tensor_copy(out=ANG[P6, :], in_=posi[P6, :]), reads=[t_pos], writes=[t_ang])
            S.op("dve", lambda e: e.tensor_scalar(out=ANG[P6, :], in0=ANG[P6, :], scalar1=ropec[P6, 0:1],
                                                  scalar2=ropec[P6, 1:2], op0=ALU.mult, op1=ALU.add),
                 reads=[t_sm], writes=[t_ang])
            S.op("dve", lambda e: e.tensor_scalar(out=kf[P6, :], in0=ANG[P6, :], scalar1=1.0 / (2 * math.pi),
                                                  scalar2=0.5, op0=ALU.mult, op1=ALU.add),
                 reads=[t_ang], writes=[t_kf])
            S.op("dve", lambda e: e.tensor_copy(out=posi[P6, :], in_=kf[P6, :]), reads=[t_kf], writes=[t_pos])
            S.op("dve", lambda e: e.tensor_copy(out=kf[P6, :], in_=posi[P6, :]), reads=[t_pos], writes=[t_kf])
            C1 = 6.28125
            C2 = 2 * math.pi - C1
            S.op("dve", lambda e: e.scalar_tensor_tensor(out=ANG[P6, :], in0=kf[P6, :], scalar=-C1, in1=ANG[P6, :],
                                                         op0=ALU.mult, op1=ALU.add), reads=[t_kf], writes=[t_ang])
            S.op("dve", lambda e: e.scalar_tensor_tensor(out=ANG[P6, :], in0=kf[P6, :], scalar=-C2, in1=ANG[P6, :],
                                                         op0=ALU.mult, op1=ALU.add), reads=[t_kf], writes=[t_ang])
            S.op("dve", lambda e: e.tensor_single_scalar(out=kf[P6, :], in_=ANG[P6, :], scalar=-math.pi, op=ALU.is_lt),
                 reads=[t_ang], writes=[t_kf])
            S.op("dve", lambda e: e.scalar_tensor_tensor(out=ANG[P6, :], in0=kf[P6, :], scalar=2 * math.pi,
                                                         in1=ANG[P6, :], op0=ALU.mult, op1=ALU.add),
                 reads=[t_kf], writes=[t_ang])
            S.op("dve", lambda e: e.tensor_scalar(out=ANG[P6, :], in0=ANG[P6, :], scalar1=-3.1415925, scalar2=3.1415925,
                                                  op0=ALU.max, op1=ALU.min), writes=[t_ang])
            S.op("act", lambda e: e.activation(out=TRIG[P6, :], in_=ANG[P6, :], func=AF.Sin), reads=[t_ang],
                 writes=[t_TRIG])
            S.barrier()
            checkpoint(1)

            S.dma("sp", lambda e: e.dma_start(out=SWT[:, :], in_=cst["c_swt"].rearrange("p h r q -> p (h r q)")),
                  writes=[t_kf])
            S.op("dve", lambda e: e.memset(VA0[:, :, 64:128], 1.0), writes=[t_VA])
            SWA_Q0, SWA_K0, SWA_V0 = 416, 928, 1056
            WBkv = WB[:, 0:1024].rearrange("p (c s n) -> p c s n", c=8, s=2)
            for h in range(8):
                kv = h // 4
                if h % 4 == 0:
                    S.dma("pool", lambda e, kv=kv: e.dma_start(
                        out=WBkv[:, :, 0, :], in_=wview(e_w_in, SWA_K0 + kv * 64, SWA_K0 + (kv + 1) * 64)), writes=[t_WB])
                    S.dma("pool", lambda e, kv=kv: e.dma_start(
                        out=WBkv[:, :, 1, :], in_=wview(e_w_in, SWA_V0 + kv * 64, SWA_V0 + (kv + 1) * 64)), writes=[t_WB])
                    for b in range(NB):
                        ps = PS[4 + b % 2]
                        tps = t_PS[4 + b % 2]
                        proj_fm(ps, tps, lambda k: WB[:, k * 128:k * 128 + 64], 8,
                                lambda k, b=b: hT[:, k, b * 512:(b + 1) * 512], t_hT[4 * b:4 * b + 4], [t_WB], 64)
                        eng = evac_engine()
                        S.op(eng, copy_op(eng, KT[0:64, b * 512:(b + 1) * 512], ps[0:64, :]), writes=[tps, t_KT])
                    for t8 in range(4):
                        ps = PS[4 + t8 % 2]
                        tps = t_PS[4 + t8 % 2]
                        for ti in range(8):
                            tt = t8 * 8 + ti
                            for k in range(8):
                                S.op("pe", lambda e, k=k, tt=tt, ti=ti, ps=ps: e.matmul(
                                    ps[:, ti * 64:(ti + 1) * 64], lhsT=hT[:, k, tt * 128:(tt + 1) * 128],
                                    rhs=WB[:, k * 128 + 64:k * 128 + 128], start=(k == 0), stop=(k == 7)),
                                    reads=[t_hT[tt], t_WB], writes=[tps])
                        eng = evac_engine()
                        S.op(eng, copy_op(eng, VA0[:, t8 * 8:(t8 + 1) * 8, 0:64],
                                          ps[:, :].rearrange("p (t d) -> p t d", t=8)), writes=[tps, t_VA])
                S.dma("pool", lambda e, h=h: e.dma_start(
                    out=WA[:, 0:512].rearrange("p (c n) -> p c n", c=8),
                    in_=wview(e_w_in, SWA_Q0 + h * 64, SWA_Q0 + (h + 1) * 64)), writes=[t_WA])
                for b in range(NB):
                    ps = PS[4 + b % 2]
                    tps = t_PS[4 + b % 2]
                    proj_fm(ps, tps, lambda k: WA[:, k * 64:(k + 1) * 64], 8,
                            lambda k, b=b: hT[:, k, b * 512:(b + 1) * 512], t_hT[4 * b:4 * b + 4], [t_WA], 64)
                    eng = evac_engine()
                    S.op(eng, copy_op(eng, QT[0:64, b * 512:(b + 1) * 512], ps[0:64, :], scale=0.125),
                         writes=[tps, t_QT[b]])
                groups = []
                for g4 in range(NB):
                    n0 = 4 * g4
                    kbs = []
                    for j in range(max(0, n0 - 1), n0 + 4):
                        qa, qb = max(j, n0), min(j + 1, n0 + 3)
                        ta = 0 if j >= n0 else 128
                        tb = 256 if j + 1 <= n0 + 3 else 128
                        kbs.append((j, (qa - n0) * 128, (qb - n0 + 1) * 128,
                                    SWT[:, h * 256 + ta:h * 256 + tb]))
                    groups.append((g4 * 512, 512, kbs, g4))
                c = 4 + h // 2
                po = (h % 2) * 64

                def out_fn(g, c=c, po=po):
                    return AOT[po:po + 64, c, g * 512:(g + 1) * 512], [t_AOT[c][g]]
                attention(64, groups, None, 1.0, lambda j: VA0[:, j, :], out_fn,
                          den_add=esink[64:128, h:h + 1], fp32_tables=True, rd_extra=[t_sm, t_kf],
                          pf_bufs=PFw, skip_gc=True)
            S.barrier()
            checkpoint(2)

            W416 = WA[:, 0:8 * 416].rearrange("p (c n) -> p c n", c=8)
            S.dma("pool", lambda e: e.dma_start(out=W416, in_=wview(e_w_in, 0, 416)), writes=[t_WA])
            WROT = WB[:, 0:8 * 96].rearrange("p (c n) -> p c n", c=8)
            S.op("dve", lambda e: e.memset(WROT[:, :, 0:64], 0.0), writes=[t_WB])
            S.op("dve", lambda e: e.tensor_scalar(out=WROT[:, :, 64:80], in0=W416[:, :, 400:416], scalar1=-1.0,
                                                  scalar2=None, op0=ALU.mult), reads=[t_WA], writes=[t_WB])
            S.op("dve", lambda e: e.tensor_copy(out=WROT[:, :, 80:96], in_=W416[:, :, 384:400]), reads=[t_WA],
                 writes=[t_WB])
            for b in range(NB):
                bs = slice(b * 512, (b + 1) * 512)
                hsrc = lambda k, b=b: hT[:, k, b * 512:(b + 1) * 512]
                ht = t_hT[4 * b:4 * b + 4]
                proj_fm(PS[0], t_PS[0], lambda k: W416[:, k, 0:128], 8, hsrc, ht, [t_WA], 128)
                proj_fm(PS[1], t_PS[1], lambda k: W416[:, k, 128:256], 8, hsrc, ht, [t_WA], 128)
                proj_fm(PS[2], t_PS[2], lambda k: W416[:, k, 256:384], 8, hsrc, ht, [t_WA], 128)
                proj_fm(PS[3], t_PS[3], lambda k: W416[:, k, 320:416], 8, hsrc, ht, [t_WA], 96)
                proj_fm(PS[4], t_PS[4], lambda k: WROT[:, k, :], 8, hsrc, ht, [t_WB], 96)
                for n in range(3):
                    S.op("act", lambda e, n=n: e.activation(out=ET[n][:], in_=PS[n][:], func=AF.Square),
                         writes=[t_PS[n], t_ET[n]])
                S.op("pe", lambda e: e.matmul(PS[5][:, :], lhsT=onesf[:], rhs=ET[0][:], start=True, stop=False),
                     reads=[t_ET[0], t_cst], writes=[t_PS[5]])
                S.op("pe", lambda e: e.matmul(PS[5][:, :], lhsT=onesf[:], rhs=ET[1][:], start=False, stop=True),
                     reads=[t_ET[1], t_cst], writes=[t_PS[5]])
                S.op("pe", lambda e: e.matmul(PS[6][:, :], lhsT=onesf[:], rhs=ET[2][:], start=True, stop=True),
                     reads=[t_ET[2], t_cst], writes=[t_PS[6]])
                for (pi, n_, n) in ((5, 256.0, 0), (6, 128.0, 1)):
                    S.op("act", lambda e, pi=pi, n_=n_, n=n: e.activation(out=ET[n][:], in_=PS[pi][:], func=AF.Ln,
                                                                    bias=epsb[:, 0:1], scale=1.0 / n_),
                         reads=[t_eps], writes=[t_PS[pi], t_ET[n]])
                    S.op("act", lambda e, n=n: e.activation(out=ET[n][:], in_=ET[n][:], func=AF.Exp, scale=-0.5),
                         writes=[t_ET[n]])
                for (pi, chunk, gsc, n) in ((0, 0, gq[:, 0:1], 0), (1, 1, gq[:, 1:2], 0), (2, 2, gkv[:, 0:1], 1)):
                    S.op("dve", lambda e, pi=pi, chunk=chunk, gsc=gsc, n=n, bs=bs: e.scalar_tensor_tensor(
                        out=LATt[:, chunk, bs], in0=PS[pi][:], scalar=gsc, in1=ET[n][:], op0=ALU.mult, op1=ALU.mult),
                        reads=[t_ET[n], t_sm], writes=[t_PS[pi], t_LAT[b]])
                S.op("dve", lambda e, bs=bs: e.tensor_tensor(out=RC[0][64:96, :], in0=PS[3][64:96, :], in1=TRIG[64:96, bs],
                                                         op=ALU.mult), reads=[t_TRIG], writes=[t_PS[3], t_RC[0]])
                S.op("dve", lambda e, bs=bs: e.tensor_tensor(out=RC[1][64:96, :], in0=PS[4][64:96, :], in1=TRIG[96:128, bs],
                                                         op=ALU.mult), reads=[t_TRIG], writes=[t_PS[4], t_RC[1]])
                S.op("dve", lambda e, bs=bs: e.tensor_tensor(out=KT[64:96, bs], in0=RC[0][64:96, :], in1=RC[1][64:96, :],
                                                         op=ALU.add), reads=[t_RC[0], t_RC[1]], writes=[t_KTaug])
            S.barrier()
            checkpoint(3)

            WQ = WA[:, 0:2 * 768].rearrange("p (c n) -> p c n", c=2)
            S.dma("pool", lambda e: e.dma_start(out=WQ, in_=wview(e_w_q_up, 0, 768)), writes=[t_WA])
            WQR = WA[:, 1536:1536 + 2 * 768].rearrange("p (c n) -> p c n", c=2)
            WKV = WB[:, 0:1024]
            S.dma("pool", lambda e: e.dma_start(out=WKV, in_=e_w_kv_up[:, :]), writes=[t_WB])
            S.op("dve", lambda e: e.memset(WA[:, 1536:1536 + 2 * 768], 0.0), writes=[t_WA])
            for c2 in range(2):
                src = WQ[:, c2, :].rearrange("p (h d) -> p h d", h=8)
                dst = WQR[:, c2, :].rearrange("p (h d) -> p h d", h=8)
                S.op("dve", lambda e, src=src, dst=dst: e.tensor_scalar(out=dst[:, :, 64:80], in0=src[:, :, 80:96],
                                                                  scalar1=-1.0, scalar2=None, op0=ALU.mult),
                     writes=[t_WA])
                S.op("dve", lambda e, src=src, dst=dst: e.tensor_copy(out=dst[:, :, 80:96], in_=src[:, :, 64:80]),
                     writes=[t_WA])
            mla_scale = 96.0 ** -0.5
            for h in range(8):
                for b in range(NB):
                    ps = PS[4 + b % 2]
                    tps = t_PS[4 + b % 2]
                    S.op("pe", lambda e, b=b, h=h, ps=ps: e.matmul(ps[0:64, :], lhsT=WKV[:, h * 128:h * 128 + 64],
                                                             rhs=LATt[:, 2, b * 512:(b + 1) * 512], start=True, stop=True),
                         reads=[t_LAT[b], t_WB], writes=[tps])
                    eng = evac_engine()
                    S.op(eng, copy_op(eng, KT[0:64, b * 512:(b + 1) * 512], ps[0:64, :]), writes=[tps, t_KT])
                for t8 in range(4):
                    ps = PS[4 + t8 % 2]
                    tps = t_PS[4 + t8 % 2]
                    for ti in range(8):
                        tt = t8 * 8 + ti
                        S.op("pe", lambda e, tt=tt, ti=ti, h=h, ps=ps: e.matmul(
                            ps[:, ti * 64:(ti + 1) * 64], lhsT=LATt[:, 2, tt * 128:(tt + 1) * 128],
                            rhs=WKV[:, h * 128 + 64:h * 128 + 128], start=True, stop=True),
                            reads=[t_LAT[tt // 4], t_WB], writes=[tps])
                    eng = evac_engine()
                    S.op(eng, copy_op(eng, VA0[:, t8 * 8:(t8 + 1) * 8, 0:64],
                                      ps[:, :].rearrange("p (t d) -> p t d", t=8)), writes=[tps, t_VA])
                for b in range(NB):
                    bs = slice(b * 512, (b + 1) * 512)
                    p1, tp1 = PS[4], t_PS[4]
                    p2, tp2 = PS[5], t_PS[5]
                    for k in range(2):
                        S.op("pe", lambda e, k=k, h=h, bs=bs: e.matmul(p1[0:96, :], lhsT=WQ[:, k, h * 96:(h + 1) * 96],
                                                                rhs=LATt[:, k, bs], start=(k == 0), stop=(k == 1)),
                             reads=[t_LAT[b], t_WA], writes=[tp1])
                    for k in range(2):
                        S.op("pe", lambda e, k=k, h=h, bs=bs: e.matmul(p2[0:96, :], lhsT=WQR[:, k, h * 96:(h + 1) * 96],
                                                                rhs=LATt[:, k, bs], start=(k == 0), stop=(k == 1)),
                             reads=[t_LAT[b], t_WA], writes=[tp2])
                    S.op("act", lambda e, bs=bs: e.copy(out=QT[0:64, bs], in_=p1[0:64, :]),
                         writes=[tp1, t_QT[b]])
                    S.op("dve", lambda e, bs=bs: e.tensor_tensor(out=RC[0][64:96, :], in0=p1[64:96, :], in1=TRIG[64:96, bs],
                                                             op=ALU.mult), reads=[t_TRIG], writes=[tp1, t_RC[0]])
                    S.op("dve", lambda e, bs=bs: e.tensor_tensor(out=RC[1][64:96, :], in0=p2[64:96, :], in1=TRIG[96:128, bs],
                                                             op=ALU.mult), reads=[t_TRIG], writes=[tp2, t_RC[1]])
                    S.op("dve", lambda e, bs=bs: e.tensor_tensor(out=QT[64:96, bs], in0=RC[0][64:96, :], in1=RC[1][64:96, :],
                                                             op=ALU.add), reads=[t_RC[0], t_RC[1]], writes=[t_QT[b]])
                c = h // 2
                po = (h % 2) * 64

                def out_fn(g, c=c, po=po):
                    return AOT[po:po + 64, c, g * 512:(g + 1) * 512], [t_AOT[c][g]]
                attention(96, dense_groups(), None, mla_scale, lambda j: VA0[:, j, :], out_fn)

            checkpoint(4)
            gate_phase(e_w_in, 1184)
            checkpoint(5)
            out_phase(e_w_out, x_d, None, last_layer=False, next_gain_d=(o_g_in if do1 else None))
            checkpoint(6)

        if do1:
            VA4 = at("VA4", [128, 32, 4, 128], BF16, o_big)
            if not do0:
                load_gain(o_g_in)
                phase_A(x1_d)
                S.barrier()
            Q0, K0, V0, F0, G0 = 0, 1024, 2048, 3072, 3088
            WF = WB[:, 0:128].rearrange("p (c n) -> p c n", c=8)
            S.dma("pool", lambda e: e.dma_start(out=WF, in_=wview(o_w_in, F0, F0 + 16)), writes=[t_WB])
            NL = at("NL", [128, 32, 16], F32, o_wa)
            trif = at("trif", [128, 128], F32, o_wa + 2048)
            identf = at("identf", [128, 128], F32, o_wa + 2560)
            CTf = at("CTf", [16, S_LEN], F32, o_big)
            r1 = at("r1", [16, S_LEN], F32, o_big + 16384)
            CS = at("CS", [128, S_LEN], BF16, o_trig)
            bfb = sb("bfb", [128, 16], F32)
            Ct = sb("Ct", [128, 32, 16], F32)
            Rs = sb("Rs", [128, 16], F32)
            zt = sb("zt", [128, 16], F32)
            t_bfb, t_NL, t_Ct, t_Rs, t_zt = S.tok("bfb"), S.toks(NT, "NL"), S.toks(NT, "Ct"), S.tok("Rs"), S.tok("zt")
            t_c1 = S.tok("cst1")
            S.dma("sp", lambda e: e.dma_start(out=trif[:], in_=cst["c_tri"][:, :]), writes=[t_c1])
            S.dma("sp", lambda e: e.dma_start(out=identf[:], in_=cst["c_ident"][:, :]), writes=[t_c1])
            S.dma("sp", lambda e: e.dma_start(out=bfb[:], in_=bass.AP(o_b_f.tensor, 0, [[0, 128], [1, 16]])),
                  writes=[t_bfb])
            S.op("dve", lambda e: e.memset(Rs[:], 0.0), writes=[t_Rs])
            for tt in range(NT):
                ps = PS[6]
                tps = t_PS[6]
                for k in range(8):
                    S.op("pe", lambda e, k=k, tt=tt: e.matmul(ps[:, 0:16], lhsT=hT[:, k, tt * 128:(tt + 1) * 128],
                                                          rhs=WF[:, k, :], start=(k == 0), stop=(k == 7)),
                         reads=[t_hT[tt], t_WB], writes=[tps])
                S.op("dve", lambda e: e.tensor_tensor(out=zt[:], in0=ps[:, 0:16], in1=bfb[:], op=ALU.add),
                     reads=[t_bfb], writes=[tps, t_zt])
                S.op("act", lambda e: e.activation(out=zt[:], in_=zt[:], func=AF.Exp, scale=-1.0), writes=[t_zt])
                S.op("act", lambda e, tt=tt: e.activation(out=NL[:, tt, :], in_=zt[:], func=AF.Ln, bias=epsb[:, 1:2],
                                                         scale=1.0), reads=[t_eps], writes=[t_zt, t_NL[tt]])
                pc = PS[5]
                tpc = t_PS[5]
                S.op("pe", lambda e, tt=tt: e.matmul(pc[:, 0:16], lhsT=trif[:], rhs=NL[:, tt, :], start=True, stop=False),
                     reads=[t_NL[tt], t_c1], writes=[tpc])
                S.op("pe", lambda e: e.matmul(pc[:, 0:16], lhsT=onesf[:], rhs=Rs[:], start=False, stop=True),
                     reads=[t_Rs, t_cst], writes=[tpc])
                S.op("dve", lambda e, tt=tt: e.tensor_copy(out=Ct[:, tt, :], in_=pc[:, 0:16]), writes=[tpc, t_Ct[tt]])
                S.op("dve", lambda e, tt=tt: e.tensor_tensor(out=Rs[:], in0=Rs[:], in1=NL[:, tt, :], op=ALU.add),
                     reads=[t_NL[tt]], writes=[t_Rs])
            t_CS, t_r1, t_ctf = S.tok("CS"), S.tok("r1"), S.tok("ctf")
            for g4 in range(8):
                ps = PS[6]
                tps = t_PS[6]
                for ti in range(4):
                    tt = g4 * 4 + ti
                    S.op("pe", lambda e, tt=tt, ti=ti: e.matmul(ps[0:16, ti * 128:(ti + 1) * 128], lhsT=Ct[:, tt, :],
                                                            rhs=identf[:], start=True, stop=True),
                         reads=[t_Ct[tt], t_c1], writes=[tps])
                S.op("dve", lambda e, g4=g4: e.tensor_scalar(out=CTf[:, g4 * 512:(g4 + 1) * 512], in0=ps[0:16, :],
                                                          scalar1=-1.0, scalar2=None, op0=ALU.mult),
                     writes=[tps, t_ctf])
            tmpb = at("tmpb", [16, S_LEN], BF16, o_aot)
            t_tmpb = S.tok("tmpb")
            S.op("dve", lambda e: e.tensor_copy(out=CS[0:16, :], in_=CTf[:, :]), reads=[t_ctf], writes=[t_CS])
            S.op("dve", lambda e: e.tensor_tensor(out=r1[:, :], in0=CTf[:, :], in1=CS[0:16, :], op=ALU.subtract),
                 reads=[t_ctf, t_CS], writes=[t_r1])
            S.op("dve", lambda e: e.tensor_copy(out=tmpb[:, :], in_=r1[:, :]), reads=[t_r1], writes=[t_tmpb])
            S.op("dve", lambda e: e.tensor_copy(out=CS[32:48, :], in_=tmpb[:, :]), reads=[t_tmpb], writes=[t_CS])
            S.op("dve", lambda e: e.tensor_tensor(out=r1[:, :], in0=r1[:, :], in1=tmpb[:, :], op=ALU.subtract),
                 reads=[t_tmpb], writes=[t_r1])
            S.op("dve", lambda e: e.tensor_copy(out=CS[64:80, :], in_=r1[:, :]), reads=[t_r1], writes=[t_CS])
            S.barrier()
            checkpoint(7)
            if 'g' in DBG:
                S.op("pe", lambda e: e.matmul(PS[6][0:64, 0:16], lhsT=hT[:, 0, 0:64], rhs=hT[:, 0, 0:16],
                                              start=True, stop=True), reads=[t_hT[0]], writes=[t_PS[6]])
            if 'a' not in DBG:
                S.op("dve", lambda e: e.memset(KT[64:67, :], 1.0), writes=[t_KTaug])
            WBqk = WB[:, 0:1024].rearrange("p (c s n) -> p c s n", c=8, s=2)
            checkpoint(71)

            KT2 = at("KT2", [128, S_LEN], BF16, o_wa)
            KTb = [KT, KT2]
            t_KTb = [[S.tok("ktb0"), S.tok("ktb0aug")], [S.tok("ktb1"), S.tok("ktb1aug")]]
            S.op("dve", lambda e: e.memset(KT2[64:67, :], 1.0), writes=[t_KTb[1][1]])
            t_KTb[0][1] = t_KTaug
            WVb = at("WVb", [128, 1024], BF16, o_rc + 2048)
            t_WQb, t_WKb, t_WVb = S.tok("wqb"), S.tok("wkb"), S.tok("wvb")
            t_VAp = S.toks(2, "vap")
            S.op("dve", lambda e: e.memset(VA4[:, :, 0:2, 64:128], 1.0), writes=[t_VAp[0], t_VA])
            S.op("dve", lambda e: e.memset(VA4[:, :, 2:4, 64:128], 1.0), writes=[t_VAp[1], t_VA])
            rot = {"n": 0}

            def bank():
                rot["n"] += 1
                return 4 + rot["n"] % 2

            def load_wk(h):
                S.dma("pool", lambda e: e.dma_start(out=WBqk[:, :, 1, :],
                                                    in_=wview(o_w_in, K0 + h * 64, K0 + (h + 1) * 64)),
                      writes=[t_WKb])

            def load_wq(h):
                S.dma("pool", lambda e: e.dma_start(out=WBqk[:, :, 0, :],
                                                    in_=wview(o_w_in, Q0 + h * 64, Q0 + (h + 1) * 64)),
                      writes=[t_WQb])

            def load_wv(p):
                S.dma("pool", lambda e: e.dma_start(out=WVb[:, :].rearrange("p (c n) -> p c n", c=8),
                                                    in_=wview(o_w_in, V0 + p * 128, V0 + (p + 1) * 128)),
                      writes=[t_WVb])

            def k_block(h, b):
                pb = bank()
                proj_fm(PS[pb], t_PS[pb], lambda k: WB[:, k * 128 + 64:(k + 1) * 128], 8,
                        lambda k: hT[:, k, b * 512:(b + 1) * 512], t_hT[4 * b:4 * b + 4], [t_WKb], 64)
                S.op("dve", copy_op("dve", KTb[h % 2][0:64, b * 512:(b + 1) * 512], PS[pb][0:64, :]),
                     writes=[t_PS[pb], t_KTb[h % 2][0]])

            def q_block(h, b):
                pb = bank()
                both = h + 1 < 16
                M = 128 if both else 64
                proj_fm(PS[pb], t_PS[pb], lambda k: WB[:, k * 128:k * 128 + M], 8,
                        lambda k: hT[:, k, b * 512:(b + 1) * 512], t_hT[4 * b:4 * b + 4],
                        [t_WQb, t_WKb] if both else [t_WQb], M)
                S.op("dve", copy_op("dve", QT[0:64, b * 512:(b + 1) * 512], PS[pb][0:64, :], scale=0.125),
                     writes=[t_PS[pb], t_QT[b]])
                if both:
                    S.op("dve", copy_op("dve", KTb[(h + 1) % 2][0:64, b * 512:(b + 1) * 512], PS[pb][64:128, :]),
                         writes=[t_PS[pb], t_KTb[(h + 1) % 2][0]])

            def v_tiles(p, t4):
                pb = bank()
                ps = PS[pb]
                s0 = 2 * (p % 2)
                for ti in range(4):
                    tt = t4 * 4 + ti
                    for k in range(8):
                        S.op("pe", lambda e, k=k, tt=tt, ti=ti: e.matmul(
                            ps[:, ti * 128:(ti + 1) * 128], lhsT=hT[:, k, tt * 128:(tt + 1) * 128],
                            rhs=WVb[:, k * 128:(k + 1) * 128], start=(k == 0), stop=(k == 7)),
                            reads=[t_hT[tt], t_WVb], writes=[t_PS[pb]])
                for ti in range(4):
                    tt = t4 * 4 + ti
                    S.op("dve", copy_op("dve", VA4[:, tt, s0:s0 + 2, 0:64],
                                        ps[:, ti * 128:(ti + 1) * 128].rearrange("p (h d) -> p h d", h=2)),
                         writes=[t_PS[pb], t_VAp[p % 2]])

            load_wv(0)
            for t4 in range(8):
                v_tiles(0, t4)
            load_wk(0)
            for b in range(NB):
                k_block(0, b)
            for h in range(16):
                hh = h % 4
                load_wq(h)
                for r_ in range(3):
                    S.dma("sp", lambda e, h=h, r_=r_: e.dma_start(out=QT[64 + r_:65 + r_, :],
                                                                 in_=CS[32 * r_ + h:32 * r_ + h + 1, :]),
                          reads=[t_CS], writes=[t_QTaug])
                bgl = []
                if h + 1 < 16:
                    load_wk(h + 1)
                if h % 2 == 1 and h + 1 < 16:
                    load_wv((h + 1) // 2)
                    bgl += [(lambda h=h, t4=t4: v_tiles((h + 1) // 2, t4)) for t4 in range(8)]
                c = h // 2
                po = (h % 2) * 64

                def out_fn(g, c=c, po=po):
                    return AOT[po:po + 64, c, g * 512:(g + 1) * 512], [t_AOT[c][g]]
                attention(67, dense_groups(), lambda j, h=h: Ct[:, j, h:h + 1], 1.0,
                          lambda j, hh=hh: VA4[:, j, hh, :], out_fn, rd_extra=t_Ct,
                          KTt=(KTb[h % 2], t_KTb[h % 2]), pre_group=(lambda g, h=h: q_block(h, g)),
                          bg=bgl, va_tok=t_VAp[(h // 2) % 2], one_rc=True)

            checkpoint(8)
            S.barrier()
            gate_phase(o_w_in, G0)
            checkpoint(9)
            out_phase(o_w_out, x1_d, (t_x1 if mode == "full" else None), last_layer=True)
            S.wait_all("sp", t_out)
        else:
            S.wait_all("sp", t_x1)


    def checkpoint(k):
        if stop == k:
            raise _Stop()

    try:
        _layers()
    except _Stop:
        pass
    S.emit()
    return nc


_CACHE = {}


def _get(mode):
    if mode not in _CACHE:
        _CACHE[mode] = build_program(mode)
    return _CACHE[mode]


L0_KEYS = ["e_g_in", "e_w_in", "e_g_q_a", "e_w_q_up", "e_g_kv_a", "e_w_kv_up", "e_sinks", "e_w_out"]
L1_KEYS = ["o_g_in", "o_w_in", "o_b_f", "o_w_out"]


def _maps(inputs, n, mode, x1=None):
    consts = _constants()
    maps = []
    for b in range(n):
        m = dict(consts)
        if mode in ("full", "l0"):
            m["x"] = np.ascontiguousarray(inputs["x"][b])
            m["positions"] = np.ascontiguousarray(inputs["positions"][b]).astype(np.int32)
            for k in L0_KEYS:
                m[k] = np.ascontiguousarray(inputs[k][0])
        if mode in ("full", "l1"):
            for k in L1_KEYS:
                m[k] = np.ascontiguousarray(inputs[k][0])
            m["g_final"] = np.ascontiguousarray(inputs["g_final"])
        if mode == "l1":
            m["x1"] = np.ascontiguousarray(x1[b])
        maps.append(m)
    return maps


def kernel(**inputs):
    n = inputs["x"].shape[0]
    inputs = {k: np.asarray(v) for k, v in inputs.items()}
    nc = _get("full")
    res = run_bass_kernel_spmd(nc, _maps(inputs, n, "full"), core_ids=list(range(n)))
    return np.stack([np.asarray(r["out"]) for r in res.results], axis=0).astype(np.float32)
```

```python
import math
import os
DBG = os.environ.get('KDBG', '')
import numpy as np
import concourse.bass as bass
import concourse.mybir as mybir
from concourse.bass_utils import run_bass_kernel_spmd

F32 = mybir.dt.float32
BF16 = mybir.dt.bfloat16
I32 = mybir.dt.int32
AF = mybir.ActivationFunctionType
ALU = mybir.AluOpType

S_LEN = 4096
D = 1024
NT = 32
NB = 8
EPS = 1e-6
SEM_ROT = 12000


class Tok:
    __slots__ = ("name", "w", "r", "dsem")

    def __init__(self, name=""):
        self.name = name
        self.w = None
        self.r = {}
        self.dsem = None


class _Rec:
    def __init__(self):
        self.call = None

    def __getattr__(self, name):
        def f(*a, **k):
            assert self.call is None
            self.call = (name, a, k)
            return self
        return f


def _freeze(fn):
    r = _Rec()
    fn(r)
    assert r.call is not None
    return r.call


class Sched:
    ENGS = ("pe", "act", "dve", "pool", "sp")

    def __init__(self, nc):
        self.nc = nc
        self.sems = []
        self.semeng = {}
        self.ops = {e: [] for e in self.ENGS}
        self.esem = {}
        self.ecnt = {}
        self.seen = {e: {} for e in self.ENGS}
        self.semval = {}
        self.unsig = {e: False for e in self.ENGS}
        self.noself = {"pe"}
        for e in ("pe", "act", "dve", "pool"):
            self._new_esem(e)

    def _alloc(self, name, eng=None):
        h = self.nc.alloc_semaphore(name=name)
        self.sems.append(h)
        sid = len(self.sems) - 1
        self.semval[sid] = 0
        self.semeng[sid] = eng
        return sid

    def _new_esem(self, e):
        self.esem[e] = self._alloc(f"s_{e}_{len(self.sems)}", e)
        self.ecnt[e] = 0

    def tok(self, name=""):
        return Tok(name)

    def toks(self, n, name=""):
        return [Tok(f"{name}{i}") for i in range(n)]

    def _collect(self, e, reads, writes):
        need = {}

        def add(s, v):
            if need.get(s, 0) < v:
                need[s] = v
        for t in reads:
            if t.w is not None:
                add(*t.w)
        for t in writes:
            if t.w is not None:
                add(*t.w)
            for s, v in t.r.items():
                add(s, v)
        waits = []
        seen = self.seen[e]
        for s, v in need.items():
            if e in self.noself and self.semeng[s] == e:
                continue
            if seen.get(s, 0) >= v:
                continue
            seen[s] = v
            waits.append((s, v))
        return waits

    def _mark(self, s, val, reads, writes):
        for t in reads:
            if t.r.get(s, 0) < val:
                t.r[s] = val
        for t in writes:
            t.w = (s, val)
            t.r = {}

    def op(self, e, fn, reads=(), writes=(), sig=True):
        waits = self._collect(e, reads, writes)
        if sig and self.ecnt[e] >= SEM_ROT and not self.unsig[e]:
            self._new_esem(e)
        self.unsig[e] = not sig
        s = self.esem[e]
        val = self.ecnt[e] + 1
        if sig:
            self.ecnt[e] = val
            self.semval[s] = val
        self.ops[e].append((waits, _freeze(fn), (s, 1) if sig else None))
        self._mark(s, val, reads, writes)

    def barrier(self):
        cur = [(s, v) for s, v in self.semval.items() if v > 0]
        for e in self.ENGS:
            waits = []
            for s, v in cur:
                if (self.semeng[s] == e and e in self.noself) or self.seen[e].get(s, 0) >= v:
                    continue
                self.seen[e][s] = v
                waits.append((s, v))
            self.ops[e].append((waits, None, None))

    def dma(self, q, fn, reads=(), writes=(), owner=None):
        waits = self._collect(q, reads, writes)
        if owner is None:
            owner = writes[0] if writes else reads[0]
        if owner.dsem is None or self.semval[owner.dsem] >= SEM_ROT * 2:
            owner.dsem = self._alloc(f"d_{len(self.sems)}")
        s = owner.dsem
        self.semval[s] += 16
        val = self.semval[s]
        self.ops[q].append((waits, _freeze(fn), (s, 16)))
        self._mark(s, val, reads, writes)

    def wait_all(self, e, toks):
        waits = self._collect(e, [], toks)
        self.ops[e].append((waits, None, None))

    def emit(self):
        nc = self.nc
        sems = self.sems
        ops = self.ops

        def replay(eng, lst):
            for waits, fn, inc in lst:
                for s, v in waits:
                    eng.wait_ge(sems[s], v)
                if fn is None:
                    continue
                name, a, k = fn
                ins = getattr(eng, name)(*a, **k)
                if inc is not None:
                    ins.then_inc(sems[inc[0]], inc[1])

        with nc.Block() as block:
            @block.tensor
            def _(eng):
                replay(eng, ops["pe"])

            @block.scalar
            def _(eng):
                replay(eng, ops["act"])

            @block.vector
            def _(eng):
                replay(eng, ops["dve"])

            @block.gpsimd
            def _(eng):
                replay(eng, ops["pool"])

            @block.sync
            def _(eng):
                replay(eng, ops["sp"])


def _constants():
    c = {}
    c["c_ident"] = np.eye(128, dtype=np.float32)
    k = np.arange(128)[:, None]
    q = np.arange(128)[None, :]
    c["c_tri"] = (k <= q).astype(np.float32)
    c["c_ones"] = np.ones((128, 128), np.float32)
    qq = np.arange(512)[None, None, :]
    kk = np.arange(128)[:, None, None]
    rr = np.arange(4)[None, :, None]
    c["c_mask"] = ((rr * 128 + kk) <= qq).astype(np.float32)
    slopes = 2.0 ** (-8.0 * (np.arange(8, dtype=np.float64) + 1.0) / 8)
    t = np.zeros((128, 8, 2, 128), np.float64)
    kq = np.arange(128)[:, None]
    qv = np.arange(128)[None, :]
    for h in range(8):
        dist0 = 128 + qv - kq
        t[:, h, 1, :] = np.where(dist0 < 128, np.exp(-slopes[h] * dist0), 0.0)
        dist1 = qv - kq
        t[:, h, 0, :] = np.where(dist1 >= 0, np.exp(-slopes[h] * np.maximum(dist1, 0)), 0.0)
    c["c_swt"] = t.astype(np.float32)
    invf = 1.0 / (10000.0 ** (np.arange(0, 32, 2, dtype=np.float32) / 32))
    v = np.zeros((128, 2), np.float32)
    for p in range(64, 128):
        v[p, 0] = invf[(p - 64) % 16]
        v[p, 1] = (math.pi / 2) if p < 96 else 0.0
    c["c_rope"] = v
    return c


CONST_SHAPES = {"c_ident": [128, 128], "c_tri": [128, 128], "c_ones": [128, 128],
                "c_mask": [128, 4, 512], "c_swt": [128, 8, 2, 128], "c_rope": [128, 2]}


class _Stop(Exception):
    pass


def build_program(mode="full", stop=0):
    nc = bass.Bass("TRN2", target_bir_lowering=False)
    S = Sched(nc)
    do0 = mode in ("full", "l0")
    do1 = mode in ("full", "l1")

    def din(name, shape, dt=F32):
        return nc.dram_tensor(name, shape, dt, kind="ExternalInput").ap()

    cst = {k: din(k, v) for k, v in CONST_SHAPES.items()}
    if do0:
        x_d = din("x", [S_LEN, D])
        pos_d = din("positions", [S_LEN], I32)
        e_g_in = din("e_g_in", [D])
        e_w_in = din("e_w_in", [D, 2208])
        e_g_q_a = din("e_g_q_a", [256])
        e_w_q_up = din("e_w_q_up", [256, 768])
        e_g_kv_a = din("e_g_kv_a", [128])
        e_w_kv_up = din("e_w_kv_up", [128, 1024])
        e_sinks = din("e_sinks", [8])
        e_w_out = din("e_w_out", [D, D])
    if do1:
        o_g_in = din("o_g_in", [D])
        o_w_in = din("o_w_in", [D, 4112])
        o_b_f = din("o_b_f", [16])
        o_w_out = din("o_w_out", [D, D])
        g_final = din("g_final", [D])
        out_d = nc.dram_tensor("out", [S_LEN, D], F32, kind="ExternalOutput").ap()
    if mode == "full":
        x1_d = nc.dram_tensor("x1s", [S_LEN, D], F32).ap()
    elif mode == "l0":
        x1_d = nc.dram_tensor("x1", [S_LEN, D], F32, kind="ExternalOutput").ap()
    else:
        x1_d = din("x1", [S_LEN, D])
    t_x1 = S.toks(NT, "x1d")
    t_x1own = S.tok("x1own")
    t_outown = S.toks(2, "outown")
    t_out = S.toks(NT, "outd")

    def region(nbytes):
        st, _ = nc.bump_sbuf(nbytes)
        return st

    def at(name, shape, dt, off):
        return nc.alloc_sbuf_tensor_at(name, shape, dt, offset=off)

    def sb(name, shape, dt):
        return nc.alloc_sbuf_tensor(name, shape, dt)

    hT = sb("hT", [128, 8, S_LEN], BF16)
    t_hT = S.toks(NT, "hT")
    o_aot = region(65536)
    AOT = at("AOT", [128, 8, S_LEN], BF16, o_aot)
    t_AOT = [[S.tok(f"aot{c}_{b}") for b in range(NB)] for c in range(8)]
    o_big = region(32768)
    BIG = at("BIG", [128, 32 * 4 * 128], BF16, o_big)
    t_VA = S.tok("VA")
    t_LAT = S.toks(NB, "LAT")
    o_trig = region(8192)
    TRIG = at("TRIG", [128, S_LEN], BF16, o_trig)
    t_TRIG = S.tok("TRIG")
    o_pool = region(19456)
    QT = at("QT", [128, S_LEN], BF16, o_pool)
    t_QT = S.toks(NB, "QT")
    t_QTaug = S.tok("QTaug")
    KT = at("KT", [128, S_LEN], BF16, o_pool + 8192)
    t_KT = S.tok("KT")
    t_KTaug = S.tok("KTaug")
    NPT = 5
    PT = [at(f"pt{i}", [128, 512], BF16, o_pool + 16384 + 1024 * i) for i in range(3)]
    PT += [sb(f"ptx{i}", [128, 512], BF16) for i in range(NPT - 3)]
    t_PT = S.toks(NPT, "pt")
    XT = [at(f"xt{i}", [128, D], F32, o_pool + 4096 * i) for i in range(3)]
    t_XT = S.toks(3, "xt")
    XB = [at(f"xb{i}", [128, D], BF16, o_pool + 12288 + 2048 * i) for i in range(2)]
    t_XB = S.toks(2, "xb")
    ET = [at(f"et{i}", [128, 512], F32, o_pool + 2048 * i) for i in range(3)]
    t_ET = S.toks(3, "et")
    o_pf = region(1024)
    PF = [at(f"pf{i}", [128, 128], F32, o_pf + 512 * i) for i in range(2)]
    t_PF = S.toks(2, "pf")
    o_rc = region(4096)
    RC = [at(f"rc{i}", [128, 512], F32, o_rc + 2048 * i) for i in range(2)]
    t_RC = S.toks(2, "rc")
    GB = at("GB", [128, D], F32, o_rc)
    t_GB = S.tok("GB")
    o_wa = region(8192)
    WA = at("WA", [128, 3328], BF16, o_wa)
    t_WA = S.tok("WA")
    WB = sb("WB", [128, 1024], BF16)
    t_WB = S.tok("WB")
    MASK = sb("MASK", [128, 128], BF16)
    t_MASK = S.tok("MASK")
    identb = sb("identb", [128, 128], BF16)
    onesf = sb("onesf", [128, 128], F32)
    t_cst = S.tok("cst")
    stat = sb("stat", [128, 8], F32)
    t_statS = S.toks(2, "stat")
    epsb = sb("epsb", [128, 2], F32)
    t_eps = S.tok("eps")

    PS = [nc.alloc_psum_tensor(f"ps{i}", [128, 512], F32) for i in range(7)]
    t_PS = S.toks(7, "ps")
    PST = nc.alloc_psum_tensor("pst", [128, 8, 128], BF16)
    t_PST = S.tok("pst")

    S.dma("sp", lambda e: e.dma_start(out=onesf[:], in_=cst["c_ones"][:, :]), writes=[t_cst])
    t_cstb = S.tok("cstb")
    S.dma("pool", lambda e: e.dma_start(out=identb[:], in_=cst["c_ident"][:, :]), writes=[t_cstb])
    S.dma("pool", lambda e: e.dma_start(out=MASK[:], in_=cst["c_tri"][:, :]), writes=[t_MASK])
    S.op("dve", lambda e: e.memset(epsb[:, 0:1], EPS), writes=[t_eps])
    S.op("dve", lambda e: e.memset(epsb[:, 1:2], 1.0), writes=[t_eps])

    cnt = {"ev": 0}

    def evac_engine():
        cnt["ev"] += 1
        return "act" if cnt["ev"] % 2 else "dve"

    def copy_op(eng, out, in_, scale=None):
        if eng == "act":
            if scale is None:
                return lambda e: e.copy(out=out, in_=in_)
            return lambda e: e.mul(out=out, in_=in_, mul=scale)
        if scale is None:
            return lambda e: e.tensor_copy(out=out, in_=in_)
        return lambda e: e.tensor_scalar(out=out, in0=in_, scalar1=scale, scalar2=None, op0=ALU.mult)

    def load_gain(g_d):
        S.dma("sp", lambda e: e.dma_start(out=GB[:], in_=bass.AP(g_d.tensor, 0, [[0, 128], [1, D]])),
              writes=[t_GB])

    def rstd_from_ms(col, tst):
        S.op("act", lambda e: e.activation(out=stat[:, col + 1:col + 2], in_=stat[:, col:col + 1], func=AF.Ln,
                                           bias=epsb[:, 0:1], scale=1.0), reads=[t_eps], writes=[tst])
        S.op("act", lambda e: e.activation(out=stat[:, col + 1:col + 2], in_=stat[:, col + 1:col + 2],
                                           func=AF.Exp, scale=-0.5), writes=[tst])

    def norm_tile(xt, t_xt, out_ap, t_outs, junk_ap, t_junk, slot=0):
        c0 = 4 * slot
        tst = t_statS[slot]
        S.op("dve", lambda e: e.memset(stat[:, c0:c0 + 1], 0.0), writes=[tst])
        S.op("act", lambda e: e.activation(out=junk_ap, in_=xt[:], func=AF.Square, scale=1.0 / 32,
                                           accum_out=stat[:, c0:c0 + 1]), reads=[t_xt], writes=[t_junk, tst])
        rstd_from_ms(c0, tst)
        S.op("dve", lambda e: e.scalar_tensor_tensor(out=out_ap, in0=xt[:], scalar=stat[:, c0 + 1:c0 + 2], in1=GB[:],
                                                     op0=ALU.mult, op1=ALU.mult),
             reads=[t_xt, tst, t_GB], writes=list(t_outs))

    def to_hT_norm(xt, t_xt, tt):
        i = tt % 2
        norm_tile(xt, t_xt, XB[i][:], [t_XB[i]], XB[i][:], t_XB[i], slot=i)

    def to_hT_tr(tt):
        i = tt % 2
        for c in range(8):
            S.op("pe", lambda e, c=c: e.transpose(out=PST[:, c, :], in_=XB[i][:, c * 128:(c + 1) * 128],
                                                   identity=identb[:]),
                 reads=[t_XB[i], t_cstb], writes=[t_PST])
        S.op("dve", copy_op("dve", hT[:, :, tt * 128:(tt + 1) * 128], PST[:, :, :]), writes=[t_PST, t_hT[tt]])

    def phase_A(src_d):
        for tt in range(NT + 1):
            if tt < NT:
                i = tt % 3
                S.dma("sp", lambda e, tt=tt, i=i: e.dma_start(out=XT[i][:], in_=src_d[tt * 128:(tt + 1) * 128, :]),
                      writes=[t_XT[i]])
                to_hT_norm(XT[i], t_XT[i], tt)
            if tt >= 1:
                to_hT_tr(tt - 1)

    def wview(w_d, c0, c1):
        return w_d.rearrange("(c p) n -> p c n", p=128)[:, :, c0:c1]

    def proj_fm(ps, t_ps, w_ap_fn, nk, src_fn, src_toks, w_toks, M):
        for k in range(nk):
            S.op("pe", lambda e, k=k: e.matmul(ps[0:M, :], lhsT=w_ap_fn(k), rhs=src_fn(k),
                                               start=(k == 0), stop=(k == nk - 1)),
                 reads=list(src_toks) + list(w_toks), writes=[t_ps])

    def attention(KR, groups, bias_fn, scale, va_fn, out_fn, den_add=None, fp32_tables=False, rd_extra=(),
                  KTt=None, pre_group=None, bg=(), va_tok=None, one_rc=False, pf_bufs=None, skip_gc=False):
        steps = []
        for gi, (q0, QW, kbs, gidx) in enumerate(groups):
            for n, kb in enumerate(kbs):
                if len(kb) == 3:
                    j, c0, tbl = kb
                    c1, t0, t1 = QW, c0, c0 + 128
                else:
                    j, c0, c1, tbl = kb
                    t0, t1 = c0, c1
                steps.append((gi, q0, QW, j, c0, c1, tbl, t0, t1, n == 0, n == len(kbs) - 1, gidx))

        SB_ = (0, 1, 6)
        LA = 2
        KTx, t_KTx = (KT, [t_KT, t_KTaug]) if KTt is None else KTt
        t_VAx = t_VA if va_tok is None else va_tok
        PFx = PF if pf_bufs is None else pf_bufs
        bg = list(bg)
        nsteps = len(steps)
        bg_stride = max(1, nsteps // (len(bg) + 1)) if bg else 0

        def emit_qk(i):
            gi, q0, QW, j, c0, c1, tbl, t0, t1, first, last, gidx = steps[i]
            sp = PS[SB_[i % 3]]
            S.op("pe", lambda e: e.matmul(sp[:, c0:c1], lhsT=KTx[0:KR, j * 128:(j + 1) * 128],
                                          rhs=QT[0:KR, q0 + c0:q0 + c1], start=True, stop=True),
                 reads=list(t_KTx) + [t_QT[q0 // 512], t_QTaug], writes=[t_PS[SB_[i % 3]]])

        first_idx = {}
        for i_, st_ in enumerate(steps):
            first_idx.setdefault(st_[0], i_)
        if pre_group is not None:
            pre_group(groups[0][3])
        for i0 in range(min(LA, len(steps))):
            emit_qk(i0)
        for i, (gi, q0, QW, j, c0, c1, tbl, t0, t1, first, last, gidx) in enumerate(steps):
            if i + LA < len(steps):
                emit_qk(i + LA)
            if pre_group is not None and i == first_idx[gi] + 1 and gi + 1 < len(groups):
                pre_group(groups[gi + 1][3])
            if bg and i > 0 and i % bg_stride == 0:
                bg.pop(0)()
            sp = PS[SB_[i % 3]]
            tsp = t_PS[SB_[i % 3]]
            pt = PT[i % NPT]
            tpt = t_PT[i % NPT]
            kw = {"scale": scale}
            b = bias_fn(j) if bias_fn is not None else None
            if b is not None:
                kw["bias"] = b
            if tbl is not None and fp32_tables:
                pf = PFx[i % 2]
                w = c1 - c0
                S.op("act", lambda e: e.activation(out=pf[:, 0:w], in_=sp[:, c0:c1], func=AF.Exp, **kw),
                     reads=list(rd_extra), writes=[tsp, t_PF[i % 2]])
                S.op("dve", lambda e: e.tensor_tensor(out=pt[:, c0:c1], in0=pf[:, 0:w], in1=tbl, op=ALU.mult),
                     reads=[t_PF[i % 2]] + list(rd_extra), writes=[tpt])
            else:
                S.op("act", lambda e: e.activation(out=pt[:, c0:c1], in_=sp[:, c0:c1], func=AF.Exp, **kw),
                     reads=list(rd_extra), writes=[tsp, tpt])
                if tbl is not None:
                    S.op("dve", lambda e: e.tensor_tensor(out=pt[:, t0:t1], in0=pt[:, t0:t1], in1=tbl, op=ALU.mult),
                         reads=[t_MASK], writes=[tpt])
            op_ = PS[2 + gi % 2]
            top = t_PS[2 + gi % 2]
            mkw = {"skip_group_check": True} if skip_gc else {}
            S.op("pe", lambda e: e.matmul(op_[:, c0:c1], lhsT=va_fn(j), rhs=pt[:, c0:c1], start=first, stop=last,
                                          **mkw),
                 reads=[tpt, t_VAx], writes=[top])
            if last:
                rc = RC[0 if one_rc else gi % 2]
                trc = t_RC[0 if one_rc else gi % 2]
                if den_add is not None:
                    S.op("dve", lambda e: e.tensor_scalar(
                        out=rc[64:128, 0:QW], in0=op_[64:128, 0:QW], scalar1=den_add, scalar2=None, op0=ALU.add),
                        reads=list(rd_extra), writes=[top, trc])
                    S.op("act", lambda e: e.activation(out=rc[64:128, 0:QW], in_=rc[64:128, 0:QW], func=AF.Ln),
                         writes=[trc])
                    S.op("act", lambda e: e.activation(out=rc[64:128, 0:QW], in_=rc[64:128, 0:QW], func=AF.Exp,
                                                       scale=-1.0), writes=[trc])
                else:
                    S.op("dve", lambda e: e.reciprocal(out=rc[64:128, 0:QW], in_=op_[64:128, 0:QW]),
                         writes=[top, trc])
                o_ap, o_toks = out_fn(gidx)
                S.op("dve", lambda e: e.tensor_tensor(out=o_ap, in0=op_[0:64, 0:QW], in1=rc[64:128, 0:QW],
                                                      op=ALU.mult),
                     reads=[trc], writes=[top] + list(o_toks))
        while bg:
            bg.pop(0)()

    def _unused():
        pass

    def dense_groups():
        gs = []
        for g in range(NB):
            kbs = [(j, 0, None) for j in range(4 * g)] + [(4 * g + r, r * 128, MASK[:, :]) for r in range(4)]
            gs.append((g * 512, 512, kbs, g))
        return gs

    def gate_phase(w_in_d, goff):
        for c in range(8):
            S.dma("pool", lambda e, c=c: e.dma_start(
                out=WB[:, 0:1024].rearrange("p (c n) -> p c n", c=8),
                in_=wview(w_in_d, goff + c * 128, goff + (c + 1) * 128)), writes=[t_WB])
            for b in range(NB):
                ps = PS[4 + b % 2]
                tps = t_PS[4 + b % 2]
                proj_fm(ps, tps, lambda k: WB[:, k * 128:(k + 1) * 128], 8,
                        lambda k, b=b: hT[:, k, b * 512:(b + 1) * 512], t_hT[4 * b:4 * b + 4], [t_WB], 128)
                gt = PT[b % NPT]
                S.op("act", lambda e, gt=gt, ps=ps: e.activation(out=gt[:], in_=ps[:], func=AF.Silu),
                     writes=[tps, t_PT[b % NPT]])
                S.op("dve", lambda e, gt=gt, c=c, b=b: e.tensor_tensor(
                    out=AOT[:, c, b * 512:(b + 1) * 512], in0=AOT[:, c, b * 512:(b + 1) * 512], in1=gt[:],
                    op=ALU.mult), reads=[t_PT[b % NPT]], writes=[t_AOT[c][b]])

    def out_phase(w_out_d, res_d, t_res, last_layer, next_gain_d=None, gate=None):
        S.barrier()
        WO = at("WO_%d" % int(last_layer), [128, 8 * 1024], BF16, o_big)
        t_WO = S.tok("WO")
        WG = at("WG_%d" % int(last_layer), [128, 8 * 1024], BF16, o_big + 16384)
        t_WG = S.tok("WG")
        gw_d, goff = gate
        for c in range(8):
            S.dma("pool", lambda e, c=c: e.dma_start(
                out=WG[:, c * 1024:(c + 1) * 1024].rearrange("p (k n) -> p k n", k=8),
                in_=wview(gw_d, goff + c * 128, goff + (c + 1) * 128)), writes=[t_WG])
        S.dma("pool", lambda e: e.dma_start(out=WO[:].rearrange("p (c n) -> p c n", c=8),
                                            in_=wview(w_out_d, 0, D)), writes=[t_WO])
        gcnt = {"n": 0}

        def gate_chunk(c, b):
            gcnt["n"] += 1
            pb = 4 + gcnt["n"] % 2
            proj_fm(PS[pb], t_PS[pb], lambda k: WG[:, c * 1024 + k * 128:c * 1024 + (k + 1) * 128], 8,
                    lambda k: hT[:, k, b * 512:(b + 1) * 512], t_hT[4 * b:4 * b + 4], [t_WG], 128)
            gi_ = gcnt["n"] % NPT
            gt = PT[gi_]
            S.op("act", lambda e: e.activation(out=gt[:], in_=PS[pb][:], func=AF.Silu),
                 writes=[t_PS[pb], t_PT[gi_]])
            S.op("dve", lambda e: e.tensor_tensor(
                out=AOT[:, c, b * 512:(b + 1) * 512], in0=AOT[:, c, b * 512:(b + 1) * 512], in1=gt[:],
                op=ALU.mult), reads=[t_PT[gi_]], writes=[t_AOT[c][b]])

        for c in range(8):
            gate_chunk(c, 0)
        if last_layer:
            load_gain(g_final)
        elif next_gain_d is not None:
            load_gain(next_gain_d)
        t_x1o = S.toks(3, "x1own")
        t_outo = S.toks(3, "outown")
        def stage1(tt):
            i = tt % 3
            b = tt // 4
            pp = 2 * (tt % 2)
            S.dma("sp", lambda e: e.dma_start(out=XT[i][:], in_=res_d[tt * 128:(tt + 1) * 128, :]),
                  reads=[t_res[tt]] if t_res is not None else [], writes=[t_XT[i]])
            for half in range(2):
                ps = PS[pp + half]
                tps = t_PS[pp + half]
                for c in range(8):
                    S.op("pe", lambda e: e.matmul(
                        ps[:, :], lhsT=AOT[:, c, tt * 128:(tt + 1) * 128],
                        rhs=WO[:, c * 1024 + half * 512: c * 1024 + (half + 1) * 512],
                        start=(c == 0), stop=(c == 7)),
                        reads=[t_AOT[c][b], t_WO], writes=[tps])

        def stage1b(tt):
            i = tt % 3
            pp = 2 * (tt % 2)
            for half in range(2):
                ps = PS[pp + half]
                tps = t_PS[pp + half]
                S.op("dve", lambda e: e.tensor_tensor(
                    out=XT[i][:, half * 512:(half + 1) * 512], in0=ps[:, :], in1=XT[i][:, half * 512:(half + 1) * 512],
                    op=ALU.add), writes=[tps, t_XT[i]])

        def stage2a(tt):
            i = tt % 3
            if last_layer:
                norm_tile(XT[i], t_XT[i], XT[i][:], [t_XT[i]], XB[tt % 2][:], t_XB[tt % 2], slot=tt % 2)
                S.dma("sp", lambda e: e.dma_start(out=out_d[tt * 128:(tt + 1) * 128, :], in_=XT[i][:]),
                      reads=[t_XT[i]], writes=[t_out[tt]], owner=t_outo[i])
            else:
                S.dma("sp", lambda e: e.dma_start(out=x1_d[tt * 128:(tt + 1) * 128, :], in_=XT[i][:]),
                      reads=[t_XT[i]], writes=[t_x1[tt]], owner=t_x1o[i])
                if mode == "full":
                    to_hT_norm(XT[i], t_XT[i], tt)

        for tt in range(NT + 2):
            if tt < NT:
                stage1(tt)
            if 1 <= tt <= NT:
                stage2a(tt - 1)
            if tt < NT:
                stage1b(tt)
            if tt < NT and tt // 4 + 1 < NB:
                for c in (2 * (tt % 4), 2 * (tt % 4) + 1):
                    gate_chunk(c, tt // 4 + 1)
            if tt >= 2 and (not last_layer) and mode == "full":
                to_hT_tr(tt - 2)
        S.barrier()

    def _layers():
        if do0:
            VA0 = at("VA0", [128, 32, 128], BF16, o_big)
            LATt = at("LATt", [128, 3, S_LEN], BF16, o_big + 8192)
            SWT = at("SWT", [128, 2048], F32, o_big + 8192)
            PFw = [at(f"pfw{i}", [128, 256], F32, o_big + 16384 + 1024 * i) for i in range(2)]
            posi = at("posi", [128, S_LEN], I32, o_aot)
            kf = at("kf", [128, S_LEN], F32, o_aot + 16384)
            ANG = at("ANG", [128, S_LEN], F32, o_aot + 32768)
            t_pos, t_kf, t_ang = S.tok("posi"), S.tok("kf"), S.tok("ang")
            load_gain(e_g_in)
            phase_A(x_d)

            gq = sb("gq", [128, 2], F32)
            gkv = sb("gkv", [128, 1], F32)
            esink = sb("esink", [128, 8], F32)
            ropec = sb("ropec", [128, 2], F32)
            t_sm = S.tok("small0")
            for c2 in range(2):
                S.dma("sp", lambda e, c2=c2: e.dma_start(
                    out=gq[:, c2:c2 + 1], in_=e_g_q_a[c2 * 128:(c2 + 1) * 128].rearrange("(p o) -> p o", o=1)),
                    writes=[t_sm])
            S.dma("sp", lambda e: e.dma_start(out=gkv[:], in_=e_g_kv_a.rearrange("(p o) -> p o", o=1)), writes=[t_sm])
            S.dma("sp", lambda e: e.dma_start(out=esink[:], in_=bass.AP(e_sinks.tensor, 0, [[0, 128], [1, 8]])),
                  writes=[t_sm])
            S.dma("sp", lambda e: e.dma_start(out=ropec[:], in_=cst["c_rope"][:, :]), writes=[t_sm])
            S.op("act", lambda e: e.activation(out=esink[:], in_=esink[:], func=AF.Exp), writes=[t_sm])

            S.dma("sp", lambda e: e.dma_start(out=posi[64:128, :], in_=bass.AP(pos_d.tensor, 0, [[0, 64], [1, S_LEN]])),
                  writes=[t_pos])
            P6 = slice(64, 128)
            S.op("dve", lambda e: e.tensor_copy(out=ANG[P6, :], in_=posi[P6, :]), reads=[t_pos], writes=[t_ang])
            S.op("dve", lambda e: e.tensor_scalar(out=ANG[P6, :], in0=ANG[P6, :], scalar1=ropec[P6, 0:1],
                                                  scalar2=ropec[P6, 1:2], op0=ALU.mult, op1=ALU.add),
                 reads=[t_sm], writes=[t_ang])
            S.op("dve", lambda e: e.tensor_scalar(out=kf[P6, :], in0=ANG[P6, :], scalar1=1.0 / (2 * math.pi),
                                                  scalar2=0.5, op0=ALU.mult, op1=ALU.add),
                 reads=[t_ang], writes=[t_kf])
            S.op("dve", lambda e: e.tensor_copy(out=posi[P6, :], in_=kf[P6, :]), reads=[t_kf], writes=[t_pos])
            S.op("dve", lambda e: e.tensor_copy(out=kf[P6, :], in_=posi[P6, :]), reads=[t_pos], writes=[t_kf])
            C1 = 6.28125
            C2 = 2 * math.pi - C1
            S.op("dve", lambda e: e.scalar_tensor_tensor(out=ANG[P6, :], in0=kf[P6, :], scalar=-C1, in1=ANG[P6, :],
                                                         op0=ALU.mult, op1=ALU.add), reads=[t_kf], writes=[t_ang])
            S.op("dve", lambda e: e.scalar_tensor_tensor(out=ANG[P6, :], in0=kf[P6, :], scalar=-C2, in1=ANG[P6, :],
                                                         op0=ALU.mult, op1=ALU.add), reads=[t_kf], writes=[t_ang])
            S.op("dve", lambda e: e.tensor_single_scalar(out=kf[P6, :], in_=ANG[P6, :], scalar=-math.pi, op=ALU.is_lt),
                 reads=[t_ang], writes=[t_kf])
            S.op("dve", lambda e: e.scalar_tensor_tensor(out=ANG[P6, :], in0=kf[P6, :], scalar=2 * math.pi,
                                                         in1=ANG[P6, :], op0=ALU.mult, op1=ALU.add),
                 reads=[t_kf], writes=[t_ang])
            S.op("dve", lambda e: e.tensor_scalar(out=ANG[P6, :], in0=ANG[P6, :], scalar1=-3.1415925, scalar2=3.1415925,
                                                  op0=ALU.max, op1=ALU.min), writes=[t_ang])
            S.op("act", lambda e: e.activation(out=TRIG[P6, :], in_=ANG[P6, :], func=AF.Sin), reads=[t_ang],
                 writes=[t_TRIG])
            S.barrier()
            checkpoint(1)

            S.dma("sp", lambda e: e.dma_start(out=SWT[:, :], in_=cst["c_swt"].rearrange("p h r q -> p (h r q)")),
                  writes=[t_kf])
            S.op("dve", lambda e: e.memset(VA0[:, :, 64:128], 1.0), writes=[t_VA])
            SWA_Q0, SWA_K0, SWA_V0 = 416, 928, 1056
            WBkv = WB[:, 0:1024].rearrange("p (c s n) -> p c s n", c=8, s=2)
            for h in range(8):
                kv = h // 4
                if h % 4 == 0:
                    S.dma("pool", lambda e, kv=kv: e.dma_start(
                        out=WBkv[:, :, 0, :], in_=wview(e_w_in, SWA_K0 + kv * 64, SWA_K0 + (kv + 1) * 64)), writes=[t_WB])
                    S.dma("pool", lambda e, kv=kv: e.dma_start(
                        out=WBkv[:, :, 1, :], in_=wview(e_w_in, SWA_V0 + kv * 64, SWA_V0 + (kv + 1) * 64)), writes=[t_WB])
                    for b in range(NB):
                        ps = PS[4 + b % 2]
                        tps = t_PS[4 + b % 2]
                        proj_fm(ps, tps, lambda k: WB[:, k * 128:k * 128 + 64], 8,
                                lambda k, b=b: hT[:, k, b * 512:(b + 1) * 512], t_hT[4 * b:4 * b + 4], [t_WB], 64)
                        eng = evac_engine()
                        S.op(eng, copy_op(eng, KT[0:64, b * 512:(b + 1) * 512], ps[0:64, :]), writes=[tps, t_KT])
                    for t8 in range(4):
                        ps = PS[4 + t8 % 2]
                        tps = t_PS[4 + t8 % 2]
                        for ti in range(8):
                            tt = t8 * 8 + ti
                            for k in range(8):
                                S.op("pe", lambda e, k=k, tt=tt, ti=ti, ps=ps: e.matmul(
                                    ps[:, ti * 64:(ti + 1) * 64], lhsT=hT[:, k, tt * 128:(tt + 1) * 128],
                                    rhs=WB[:, k * 128 + 64:k * 128 + 128], start=(k == 0), stop=(k == 7)),
                                    reads=[t_hT[tt], t_WB], writes=[tps])
                        eng = evac_engine()
                        S.op(eng, copy_op(eng, VA0[:, t8 * 8:(t8 + 1) * 8, 0:64],
                                          ps[:, :].rearrange("p (t d) -> p t d", t=8)), writes=[tps, t_VA])
                S.dma("pool", lambda e, h=h: e.dma_start(
                    out=WA[:, 0:512].rearrange("p (c n) -> p c n", c=8),
                    in_=wview(e_w_in, SWA_Q0 + h * 64, SWA_Q0 + (h + 1) * 64)), writes=[t_WA])
                for b in range(NB):
                    ps = PS[4 + b % 2]
                    tps = t_PS[4 + b % 2]
                    proj_fm(ps, tps, lambda k: WA[:, k * 64:(k + 1) * 64], 8,
                            lambda k, b=b: hT[:, k, b * 512:(b + 1) * 512], t_hT[4 * b:4 * b + 4], [t_WA], 64)
                    eng = evac_engine()
                    S.op(eng, copy_op(eng, QT[0:64, b * 512:(b + 1) * 512], ps[0:64, :], scale=0.125),
                         writes=[tps, t_QT[b]])
                groups = []
                for g4 in range(NB):
                    n0 = 4 * g4
                    kbs = []
                    for j in range(max(0, n0 - 1), n0 + 4):
                        qa, qb = max(j, n0), min(j + 1, n0 + 3)
                        ta = 0 if j >= n0 else 128
                        tb = 256 if j + 1 <= n0 + 3 else 128
                        kbs.append((j, (qa - n0) * 128, (qb - n0 + 1) * 128,
                                    SWT[:, h * 256 + ta:h * 256 + tb]))
                    groups.append((g4 * 512, 512, kbs, g4))
                c = 4 + h // 2
                po = (h % 2) * 64

                def out_fn(g, c=c, po=po):
                    return AOT[po:po + 64, c, g * 512:(g + 1) * 512], [t_AOT[c][g]]
                attention(64, groups, None, 1.0, lambda j: VA0[:, j, :], out_fn,
                          den_add=esink[64:128, h:h + 1], fp32_tables=True, rd_extra=[t_sm, t_kf],
                          pf_bufs=PFw, skip_gc=True)
            S.barrier()
            checkpoint(2)

            W416 = WA[:, 0:8 * 416].rearrange("p (c n) -> p c n", c=8)
            S.dma("pool", lambda e: e.dma_start(out=W416, in_=wview(e_w_in, 0, 416)), writes=[t_WA])
            WROT = WB[:, 0:8 * 96].rearrange("p (c n) -> p c n", c=8)
            S.op("dve", lambda e: e.memset(WROT[:, :, 0:64], 0.0), writes=[t_WB])
            S.op("dve", lambda e: e.tensor_scalar(out=WROT[:, :, 64:80], in0=W416[:, :, 400:416], scalar1=-1.0,
                                                  scalar2=None, op0=ALU.mult), reads=[t_WA], writes=[t_WB])
            S.op("dve", lambda e: e.tensor_copy(out=WROT[:, :, 80:96], in_=W416[:, :, 384:400]), reads=[t_WA],
                 writes=[t_WB])
            for b in range(NB):
                bs = slice(b * 512, (b + 1) * 512)
                hsrc = lambda k, b=b: hT[:, k, b * 512:(b + 1) * 512]
                ht = t_hT[4 * b:4 * b + 4]
                proj_fm(PS[0], t_PS[0], lambda k: W416[:, k, 0:128], 8, hsrc, ht, [t_WA], 128)
                proj_fm(PS[1], t_PS[1], lambda k: W416[:, k, 128:256], 8, hsrc, ht, [t_WA], 128)
                proj_fm(PS[2], t_PS[2], lambda k: W416[:, k, 256:384], 8, hsrc, ht, [t_WA], 128)
                proj_fm(PS[3], t_PS[3], lambda k: W416[:, k, 320:416], 8, hsrc, ht, [t_WA], 96)
                proj_fm(PS[4], t_PS[4], lambda k: WROT[:, k, :], 8, hsrc, ht, [t_WB], 96)
                for n in range(3):
                    S.op("act", lambda e, n=n: e.activation(out=ET[n][:], in_=PS[n][:], func=AF.Square),
                         writes=[t_PS[n], t_ET[n]])
                S.op("pe", lambda e: e.matmul(PS[5][:, :], lhsT=onesf[:], rhs=ET[0][:], start=True, stop=False),
                     reads=[t_ET[0], t_cst], writes=[t_PS[5]])
                S.op("pe", lambda e: e.matmul(PS[5][:, :], lhsT=onesf[:], rhs=ET[1][:], start=False, stop=True),
                     reads=[t_ET[1], t_cst], writes=[t_PS[5]])
                S.op("pe", lambda e: e.matmul(PS[6][:, :], lhsT=onesf[:], rhs=ET[2][:], start=True, stop=True),
                     reads=[t_ET[2], t_cst], writes=[t_PS[6]])
                for (pi, n_, n) in ((5, 256.0, 0), (6, 128.0, 1)):
                    S.op("act", lambda e, pi=pi, n_=n_, n=n: e.activation(out=ET[n][:], in_=PS[pi][:], func=AF.Ln,
                                                                    bias=epsb[:, 0:1], scale=1.0 / n_),
                         reads=[t_eps], writes=[t_PS[pi], t_ET[n]])
                    S.op("act", lambda e, n=n: e.activation(out=ET[n][:], in_=ET[n][:], func=AF.Exp, scale=-0.5),
                         writes=[t_ET[n]])
                for (pi, chunk, gsc, n) in ((0, 0, gq[:, 0:1], 0), (1, 1, gq[:, 1:2], 0), (2, 2, gkv[:, 0:1], 1)):
                    S.op("dve", lambda e, pi=pi, chunk=chunk, gsc=gsc, n=n, bs=bs: e.scalar_tensor_tensor(
                        out=LATt[:, chunk, bs], in0=PS[pi][:], scalar=gsc, in1=ET[n][:], op0=ALU.mult, op1=ALU.mult),
                        reads=[t_ET[n], t_sm], writes=[t_PS[pi], t_LAT[b]])
                S.op("dve", lambda e, bs=bs: e.tensor_tensor(out=RC[0][64:96, :], in0=PS[3][64:96, :], in1=TRIG[64:96, bs],
                                                         op=ALU.mult), reads=[t_TRIG], writes=[t_PS[3], t_RC[0]])
                S.op("dve", lambda e, bs=bs: e.tensor_tensor(out=RC[1][64:96, :], in0=PS[4][64:96, :], in1=TRIG[96:128, bs],
                                                         op=ALU.mult), reads=[t_TRIG], writes=[t_PS[4], t_RC[1]])
                S.op("dve", lambda e, bs=bs: e.tensor_tensor(out=KT[64:96, bs], in0=RC[0][64:96, :], in1=RC[1][64:96, :],
                                                         op=ALU.add), reads=[t_RC[0], t_RC[1]], writes=[t_KTaug])
            S.barrier()
            checkpoint(3)

            WQ = WA[:, 0:2 * 768].rearrange("p (c n) -> p c n", c=2)
            S.dma("pool", lambda e: e.dma_start(out=WQ, in_=wview(e_w_q_up, 0, 768)), writes=[t_WA])
            WQR = WA[:, 1536:1536 + 2 * 768].rearrange("p (c n) -> p c n", c=2)
            WKV = WB[:, 0:1024]
            S.dma("pool", lambda e: e.dma_start(out=WKV, in_=e_w_kv_up[:, :]), writes=[t_WB])
            S.op("dve", lambda e: e.memset(WA[:, 1536:1536 + 2 * 768], 0.0), writes=[t_WA])
            for c2 in range(2):
                src = WQ[:, c2, :].rearrange("p (h d) -> p h d", h=8)
                dst = WQR[:, c2, :].rearrange("p (h d) -> p h d", h=8)
                S.op("dve", lambda e, src=src, dst=dst: e.tensor_scalar(out=dst[:, :, 64:80], in0=src[:, :, 80:96],
                                                                  scalar1=-1.0, scalar2=None, op0=ALU.mult),
                     writes=[t_WA])
                S.op("dve", lambda e, src=src, dst=dst: e.tensor_copy(out=dst[:, :, 80:96], in_=src[:, :, 64:80]),
                     writes=[t_WA])
            mla_scale = 96.0 ** -0.5
            for h in range(8):
                for b in range(NB):
                    ps = PS[4 + b % 2]
                    tps = t_PS[4 + b % 2]
                    S.op("pe", lambda e, b=b, h=h, ps=ps: e.matmul(ps[0:64, :], lhsT=WKV[:, h * 128:h * 128 + 64],
                                                             rhs=LATt[:, 2, b * 512:(b + 1) * 512], start=True, stop=True),
                         reads=[t_LAT[b], t_WB], writes=[tps])
                    eng = evac_engine()
                    S.op(eng, copy_op(eng, KT[0:64, b * 512:(b + 1) * 512], ps[0:64, :]), writes=[tps, t_KT])
                for t8 in range(4):
                    ps = PS[4 + t8 % 2]
                    tps = t_PS[4 + t8 % 2]
                    for ti in range(8):
                        tt = t8 * 8 + ti
                        S.op("pe", lambda e, tt=tt, ti=ti, h=h, ps=ps: e.matmul(
                            ps[:, ti * 64:(ti + 1) * 64], lhsT=LATt[:, 2, tt * 128:(tt + 1) * 128],
                            rhs=WKV[:, h * 128 + 64:h * 128 + 128], start=True, stop=True),
                            reads=[t_LAT[tt // 4], t_WB], writes=[tps])
                    eng = evac_engine()
                    S.op(eng, copy_op(eng, VA0[:, t8 * 8:(t8 + 1) * 8, 0:64],
                                      ps[:, :].rearrange("p (t d) -> p t d", t=8)), writes=[tps, t_VA])
                for b in range(NB):
                    bs = slice(b * 512, (b + 1) * 512)
                    p1, tp1 = PS[4], t_PS[4]
                    p2, tp2 = PS[5], t_PS[5]
                    for k in range(2):
                        S.op("pe", lambda e, k=k, h=h, bs=bs: e.matmul(p1[0:96, :], lhsT=WQ[:, k, h * 96:(h + 1) * 96],
                                                                rhs=LATt[:, k, bs], start=(k == 0), stop=(k == 1)),
                             reads=[t_LAT[b], t_WA], writes=[tp1])
                    for k in range(2):
                        S.op("pe", lambda e, k=k, h=h, bs=bs: e.matmul(p2[0:96, :], lhsT=WQR[:, k, h * 96:(h + 1) * 96],
                                                                rhs=LATt[:, k, bs], start=(k == 0), stop=(k == 1)),
                             reads=[t_LAT[b], t_WA], writes=[tp2])
                    S.op("act", lambda e, bs=bs: e.copy(out=QT[0:64, bs], in_=p1[0:64, :]),
                         writes=[tp1, t_QT[b]])
                    S.op("dve", lambda e, bs=bs: e.tensor_tensor(out=RC[0][64:96, :], in0=p1[64:96, :], in1=TRIG[64:96, bs],
                                                             op=ALU.mult), reads=[t_TRIG], writes=[tp1, t_RC[0]])
                    S.op("dve", lambda e, bs=bs: e.tensor_tensor(out=RC[1][64:96, :], in0=p2[64:96, :], in1=TRIG[96:128, bs],
                                                             op=ALU.mult), reads=[t_TRIG], writes=[tp2, t_RC[1]])
                    S.op("dve", lambda e, bs=bs: e.tensor_tensor(out=QT[64:96, bs], in0=RC[0][64:96, :], in1=RC[1][64:96, :],
                                                             op=ALU.add), reads=[t_RC[0], t_RC[1]], writes=[t_QT[b]])
                c = h // 2
                po = (h % 2) * 64

                def out_fn(g, c=c, po=po):
                    return AOT[po:po + 64, c, g * 512:(g + 1) * 512], [t_AOT[c][g]]
                attention(96, dense_groups(), None, mla_scale, lambda j: VA0[:, j, :], out_fn)

            checkpoint(4)
            checkpoint(5)
            out_phase(e_w_out, x_d, None, last_layer=False, next_gain_d=(o_g_in if do1 else None),
                      gate=(e_w_in, 1184))
            checkpoint(6)

        if do1:
            VA4 = at("VA4", [128, 32, 4, 128], BF16, o_big)
            if not do0:
                load_gain(o_g_in)
                phase_A(x1_d)
                S.barrier()
            Q0, K0, V0, F0, G0 = 0, 1024, 2048, 3072, 3088
            WF = WB[:, 0:128].rearrange("p (c n) -> p c n", c=8)
            S.dma("pool", lambda e: e.dma_start(out=WF, in_=wview(o_w_in, F0, F0 + 16)), writes=[t_WB])
            NL = at("NL", [128, 32, 16], F32, o_wa)
            trif = at("trif", [128, 128], F32, o_wa + 2048)
            identf = at("identf", [128, 128], F32, o_wa + 2560)
            CTf = at("CTf", [16, S_LEN], F32, o_big)
            r1 = at("r1", [16, S_LEN], F32, o_big + 16384)
            CS = at("CS", [128, S_LEN], BF16, o_trig)
            bfb = sb("bfb", [128, 16], F32)
            Ct = sb("Ct", [128, 32, 16], F32)
            Rs = sb("Rs", [128, 16], F32)
            zt = sb("zt", [128, 16], F32)
            t_bfb, t_NL, t_Ct, t_Rs, t_zt = S.tok("bfb"), S.toks(NT, "NL"), S.toks(NT, "Ct"), S.tok("Rs"), S.tok("zt")
            t_c1 = S.tok("cst1")
            S.dma("sp", lambda e: e.dma_start(out=trif[:], in_=cst["c_tri"][:, :]), writes=[t_c1])
            S.dma("sp", lambda e: e.dma_start(out=identf[:], in_=cst["c_ident"][:, :]), writes=[t_c1])
            S.dma("sp", lambda e: e.dma_start(out=bfb[:], in_=bass.AP(o_b_f.tensor, 0, [[0, 128], [1, 16]])),
                  writes=[t_bfb])
            S.op("dve", lambda e: e.memset(Rs[:], 0.0), writes=[t_Rs])
            for tt in range(NT):
                ps = PS[6]
                tps = t_PS[6]
                for k in range(8):
                    S.op("pe", lambda e, k=k, tt=tt: e.matmul(ps[:, 0:16], lhsT=hT[:, k, tt * 128:(tt + 1) * 128],
                                                          rhs=WF[:, k, :], start=(k == 0), stop=(k == 7)),
                         reads=[t_hT[tt], t_WB], writes=[tps])
                S.op("dve", lambda e: e.tensor_tensor(out=zt[:], in0=ps[:, 0:16], in1=bfb[:], op=ALU.add),
                     reads=[t_bfb], writes=[tps, t_zt])
                S.op("act", lambda e: e.activation(out=zt[:], in_=zt[:], func=AF.Exp, scale=-1.0), writes=[t_zt])
                S.op("act", lambda e, tt=tt: e.activation(out=NL[:, tt, :], in_=zt[:], func=AF.Ln, bias=epsb[:, 1:2],
                                                         scale=1.0), reads=[t_eps], writes=[t_zt, t_NL[tt]])
                pc = PS[5]
                tpc = t_PS[5]
                S.op("pe", lambda e, tt=tt: e.matmul(pc[:, 0:16], lhsT=trif[:], rhs=NL[:, tt, :], start=True, stop=False),
                     reads=[t_NL[tt], t_c1], writes=[tpc])
                S.op("pe", lambda e: e.matmul(pc[:, 0:16], lhsT=onesf[:], rhs=Rs[:], start=False, stop=True),
                     reads=[t_Rs, t_cst], writes=[tpc])
                S.op("dve", lambda e, tt=tt: e.tensor_copy(out=Ct[:, tt, :], in_=pc[:, 0:16]), writes=[tpc, t_Ct[tt]])
                S.op("dve", lambda e, tt=tt: e.tensor_tensor(out=Rs[:], in0=Rs[:], in1=NL[:, tt, :], op=ALU.add),
                     reads=[t_NL[tt]], writes=[t_Rs])
            t_CS, t_r1, t_ctf = S.tok("CS"), S.tok("r1"), S.tok("ctf")
            for g4 in range(8):
                ps = PS[6]
                tps = t_PS[6]
                for ti in range(4):
                    tt = g4 * 4 + ti
                    S.op("pe", lambda e, tt=tt, ti=ti: e.matmul(ps[0:16, ti * 128:(ti + 1) * 128], lhsT=Ct[:, tt, :],
                                                            rhs=identf[:], start=True, stop=True),
                         reads=[t_Ct[tt], t_c1], writes=[tps])
                S.op("dve", lambda e, g4=g4: e.tensor_scalar(out=CTf[:, g4 * 512:(g4 + 1) * 512], in0=ps[0:16, :],
                                                          scalar1=-1.0, scalar2=None, op0=ALU.mult),
                     writes=[tps, t_ctf])
            tmpb = at("tmpb", [16, S_LEN], BF16, o_aot)
            t_tmpb = S.tok("tmpb")
            S.op("dve", lambda e: e.tensor_copy(out=CS[0:16, :], in_=CTf[:, :]), reads=[t_ctf], writes=[t_CS])
            S.op("dve", lambda e: e.tensor_tensor(out=r1[:, :], in0=CTf[:, :], in1=CS[0:16, :], op=ALU.subtract),
                 reads=[t_ctf, t_CS], writes=[t_r1])
            S.op("dve", lambda e: e.tensor_copy(out=tmpb[:, :], in_=r1[:, :]), reads=[t_r1], writes=[t_tmpb])
            S.op("dve", lambda e: e.tensor_copy(out=CS[32:48, :], in_=tmpb[:, :]), reads=[t_tmpb], writes=[t_CS])
            S.op("dve", lambda e: e.tensor_tensor(out=r1[:, :], in0=r1[:, :], in1=tmpb[:, :], op=ALU.subtract),
                 reads=[t_tmpb], writes=[t_r1])
            S.op("dve", lambda e: e.tensor_copy(out=CS[64:80, :], in_=r1[:, :]), reads=[t_r1], writes=[t_CS])
            S.barrier()
            checkpoint(7)
            if 'g' in DBG:
                S.op("pe", lambda e: e.matmul(PS[6][0:64, 0:16], lhsT=hT[:, 0, 0:64], rhs=hT[:, 0, 0:16],
                                              start=True, stop=True), reads=[t_hT[0]], writes=[t_PS[6]])
            if 'a' not in DBG:
                S.op("dve", lambda e: e.memset(KT[64:67, :], 1.0), writes=[t_KTaug])
            WBqk = WB[:, 0:1024].rearrange("p (c s n) -> p c s n", c=8, s=2)
            checkpoint(71)

            KT2 = at("KT2", [128, S_LEN], BF16, o_wa)
            KTb = [KT, KT2]
            t_KTb = [[S.tok("ktb0"), S.tok("ktb0aug")], [S.tok("ktb1"), S.tok("ktb1aug")]]
            S.op("dve", lambda e: e.memset(KT2[64:67, :], 1.0), writes=[t_KTb[1][1]])
            t_KTb[0][1] = t_KTaug
            WVb = at("WVb", [128, 1024], BF16, o_rc + 2048)
            t_WQb, t_WKb, t_WVb = S.tok("wqb"), S.tok("wkb"), S.tok("wvb")
            t_VAp = S.toks(2, "vap")
            S.op("dve", lambda e: e.memset(VA4[:, :, 0:2, 64:128], 1.0), writes=[t_VAp[0], t_VA])
            S.op("dve", lambda e: e.memset(VA4[:, :, 2:4, 64:128], 1.0), writes=[t_VAp[1], t_VA])
            rot = {"n": 0}

            def bank():
                rot["n"] += 1
                return 4 + rot["n"] % 2

            def load_wk(h):
                S.dma("pool", lambda e: e.dma_start(out=WBqk[:, :, 1, :],
                                                    in_=wview(o_w_in, K0 + h * 64, K0 + (h + 1) * 64)),
                      writes=[t_WKb])

            def load_wq(h):
                S.dma("pool", lambda e: e.dma_start(out=WBqk[:, :, 0, :],
                                                    in_=wview(o_w_in, Q0 + h * 64, Q0 + (h + 1) * 64)),
                      writes=[t_WQb])

            def load_wv(p):
                S.dma("pool", lambda e: e.dma_start(out=WVb[:, :].rearrange("p (c n) -> p c n", c=8),
                                                    in_=wview(o_w_in, V0 + p * 128, V0 + (p + 1) * 128)),
                      writes=[t_WVb])

            def k_block(h, b):
                pb = bank()
                proj_fm(PS[pb], t_PS[pb], lambda k: WB[:, k * 128 + 64:(k + 1) * 128], 8,
                        lambda k: hT[:, k, b * 512:(b + 1) * 512], t_hT[4 * b:4 * b + 4], [t_WKb], 64)
                S.op("dve", copy_op("dve", KTb[h % 2][0:64, b * 512:(b + 1) * 512], PS[pb][0:64, :]),
                     writes=[t_PS[pb], t_KTb[h % 2][0]])

            def q_block(h, b):
                pb = bank()
                both = h + 1 < 16
                M = 128 if both else 64
                proj_fm(PS[pb], t_PS[pb], lambda k: WB[:, k * 128:k * 128 + M], 8,
                        lambda k: hT[:, k, b * 512:(b + 1) * 512], t_hT[4 * b:4 * b + 4],
                        [t_WQb, t_WKb] if both else [t_WQb], M)
                S.op("dve", copy_op("dve", QT[0:64, b * 512:(b + 1) * 512], PS[pb][0:64, :], scale=0.125),
                     writes=[t_PS[pb], t_QT[b]])
                if both:
                    S.op("dve", copy_op("dve", KTb[(h + 1) % 2][0:64, b * 512:(b + 1) * 512], PS[pb][64:128, :]),
                         writes=[t_PS[pb], t_KTb[(h + 1) % 2][0]])

            def v_tiles(p, t4):
                pb = bank()
                ps = PS[pb]
                s0 = 2 * (p % 2)
                for ti in range(4):
                    tt = t4 * 4 + ti
                    for k in range(8):
                        S.op("pe", lambda e, k=k, tt=tt, ti=ti: e.matmul(
                            ps[:, ti * 128:(ti + 1) * 128], lhsT=hT[:, k, tt * 128:(tt + 1) * 128],
                            rhs=WVb[:, k * 128:(k + 1) * 128], start=(k == 0), stop=(k == 7)),
                            reads=[t_hT[tt], t_WVb], writes=[t_PS[pb]])
                for ti in range(4):
                    tt = t4 * 4 + ti
                    S.op("dve", copy_op("dve", VA4[:, tt, s0:s0 + 2, 0:64],
                                        ps[:, ti * 128:(ti + 1) * 128].rearrange("p (h d) -> p h d", h=2)),
                         writes=[t_PS[pb], t_VAp[p % 2]])

            load_wv(0)
            for t4 in range(8):
                v_tiles(0, t4)
            load_wk(0)
            for b in range(NB):
                k_block(0, b)
            for h in range(16):
                hh = h % 4
                load_wq(h)
                for r_ in range(3):
                    S.dma("sp", lambda e, h=h, r_=r_: e.dma_start(out=QT[64 + r_:65 + r_, :],
                                                                 in_=CS[32 * r_ + h:32 * r_ + h + 1, :]),
                          reads=[t_CS], writes=[t_QTaug])
                bgl = []
                if h + 1 < 16:
                    load_wk(h + 1)
                if h % 2 == 1 and h + 1 < 16:
                    load_wv((h + 1) // 2)
                    bgl += [(lambda h=h, t4=t4: v_tiles((h + 1) // 2, t4)) for t4 in range(8)]
                c = h // 2
                po = (h % 2) * 64

                def out_fn(g, c=c, po=po):
                    return AOT[po:po + 64, c, g * 512:(g + 1) * 512], [t_AOT[c][g]]
                attention(67, dense_groups(), lambda j, h=h: Ct[:, j, h:h + 1], 1.0,
                          lambda j, hh=hh: VA4[:, j, hh, :], out_fn, rd_extra=t_Ct,
                          KTt=(KTb[h % 2], t_KTb[h % 2]), pre_group=(lambda g, h=h: q_block(h, g)),
                          bg=bgl, va_tok=t_VAp[(h // 2) % 2], one_rc=True)

            checkpoint(8)
            S.barrier()
            checkpoint(9)
            out_phase(o_w_out, x1_d, (t_x1 if mode == "full" else None), last_layer=True, gate=(o_w_in, G0))
            S.wait_all("sp", t_out)
        else:
            S.wait_all("sp", t_x1)


    def checkpoint(k):
        if stop == k:
            raise _Stop()

    try:
        _layers()
    except _Stop:
        pass
    S.emit()
    return nc


_CACHE = {}


def _get(mode):
    if mode not in _CACHE:
        _CACHE[mode] = build_program(mode)
    return _CACHE[mode]


L0_KEYS = ["e_g_in", "e_w_in", "e_g_q_a", "e_w_q_up", "e_g_kv_a", "e_w_kv_up", "e_sinks", "e_w_out"]
L1_KEYS = ["o_g_in", "o_w_in", "o_b_f", "o_w_out"]


def _maps(inputs, n, mode, x1=None):
    consts = _constants()
    maps = []
    for b in range(n):
        m = dict(consts)
        if mode in ("full", "l0"):
            m["x"] = np.ascontiguousarray(inputs["x"][b])
            m["positions"] = np.ascontiguousarray(inputs["positions"][b]).astype(np.int32)
            for k in L0_KEYS:
                m[k] = np.ascontiguousarray(inputs[k][0])
        if mode in ("full", "l1"):
            for k in L1_KEYS:
                m[k] = np.ascontiguousarray(inputs[k][0])
            m["g_final"] = np.ascontiguousarray(inputs["g_final"])
        if mode == "l1":
            m["x1"] = np.ascontiguousarray(x1[b])
        maps.append(m)
    return maps


def kernel(**inputs):
    n = inputs["x"].shape[0]
    inputs = {k: np.asarray(v) for k, v in inputs.items()}
    nc = _get("full")
    res = run_bass_kernel_spmd(nc, _maps(inputs, n, "full"), core_ids=list(range(n)))
    return np.stack([np.asarray(r["out"]) for r in res.results], axis=0).astype(np.float32)
```

```python
import math
import os
DBG = os.environ.get('KDBG', '')
import numpy as np
import concourse.bass as bass
import concourse.mybir as mybir
from concourse.bass_utils import run_bass_kernel_spmd

F32 = mybir.dt.float32
BF16 = mybir.dt.bfloat16
I32 = mybir.dt.int32
AF = mybir.ActivationFunctionType
ALU = mybir.AluOpType

S_LEN = 4096
D = 1024
NT = 32
NB = 8
EPS = 1e-6
SEM_ROT = 12000


class Tok:
    __slots__ = ("name", "w", "r", "dsem")

    def __init__(self, name=""):
        self.name = name
        self.w = None
        self.r = {}
        self.dsem = None


class _Rec:
    def __init__(self):
        self.call = None

    def __getattr__(self, name):
        def f(*a, **k):
            assert self.call is None
            self.call = (name, a, k)
            return self
        return f


def _freeze(fn):
    r = _Rec()
    fn(r)
    assert r.call is not None
    return r.call


class Sched:
    ENGS = ("pe", "act", "dve", "pool", "sp")

    def __init__(self, nc):
        self.nc = nc
        self.sems = []
        self.semeng = {}
        self.ops = {e: [] for e in self.ENGS}
        self.esem = {}
        self.ecnt = {}
        self.seen = {e: {} for e in self.ENGS}
        self.semval = {}
        self.unsig = {e: False for e in self.ENGS}
        self.noself = {"pe"}
        for e in ("pe", "act", "dve", "pool"):
            self._new_esem(e)

    def _alloc(self, name, eng=None):
        h = self.nc.alloc_semaphore(name=name)
        self.sems.append(h)
        sid = len(self.sems) - 1
        self.semval[sid] = 0
        self.semeng[sid] = eng
        return sid

    def _new_esem(self, e):
        self.esem[e] = self._alloc(f"s_{e}_{len(self.sems)}", e)
        self.ecnt[e] = 0

    def tok(self, name=""):
        return Tok(name)

    def toks(self, n, name=""):
        return [Tok(f"{name}{i}") for i in range(n)]

    def _collect(self, e, reads, writes):
        need = {}

        def add(s, v):
            if need.get(s, 0) < v:
                need[s] = v
        for t in reads:
            if t.w is not None:
                add(*t.w)
        for t in writes:
            if t.w is not None:
                add(*t.w)
            for s, v in t.r.items():
                add(s, v)
        waits = []
        seen = self.seen[e]
        for s, v in need.items():
            if e in self.noself and self.semeng[s] == e:
                continue
            if seen.get(s, 0) >= v:
                continue
            seen[s] = v
            waits.append((s, v))
        return waits

    def _mark(self, s, val, reads, writes):
        for t in reads:
            if t.r.get(s, 0) < val:
                t.r[s] = val
        for t in writes:
            t.w = (s, val)
            t.r = {}

    def op(self, e, fn, reads=(), writes=(), sig=True):
        waits = self._collect(e, reads, writes)
        if sig and self.ecnt[e] >= SEM_ROT and not self.unsig[e]:
            self._new_esem(e)
        self.unsig[e] = not sig
        s = self.esem[e]
        val = self.ecnt[e] + 1
        if sig:
            self.ecnt[e] = val
            self.semval[s] = val
        self.ops[e].append((waits, _freeze(fn), (s, 1) if sig else None))
        self._mark(s, val, reads, writes)

    def barrier(self):
        cur = [(s, v) for s, v in self.semval.items() if v > 0]
        for e in self.ENGS:
            waits = []
            for s, v in cur:
                if (self.semeng[s] == e and e in self.noself) or self.seen[e].get(s, 0) >= v:
                    continue
                self.seen[e][s] = v
                waits.append((s, v))
            self.ops[e].append((waits, None, None))

    def dma(self, q, fn, reads=(), writes=(), owner=None):
        waits = self._collect(q, reads, writes)
        if owner is None:
            owner = writes[0] if writes else reads[0]
        if owner.dsem is None or self.semval[owner.dsem] >= SEM_ROT * 2:
            owner.dsem = self._alloc(f"d_{len(self.sems)}")
        s = owner.dsem
        self.semval[s] += 16
        val = self.semval[s]
        self.ops[q].append((waits, _freeze(fn), (s, 16)))
        self._mark(s, val, reads, writes)

    def wait_all(self, e, toks):
        waits = self._collect(e, [], toks)
        self.ops[e].append((waits, None, None))

    def emit(self):
        nc = self.nc
        sems = self.sems
        ops = self.ops

        def replay(eng, lst):
            for waits, fn, inc in lst:
                for s, v in waits:
                    eng.wait_ge(sems[s], v)
                if fn is None:
                    continue
                name, a, k = fn
                ins = getattr(eng, name)(*a, **k)
                if inc is not None:
                    ins.then_inc(sems[inc[0]], inc[1])

        with nc.Block() as block:
            @block.tensor
            def _(eng):
                replay(eng, ops["pe"])

            @block.scalar
            def _(eng):
                replay(eng, ops["act"])

            @block.vector
            def _(eng):
                replay(eng, ops["dve"])

            @block.gpsimd
            def _(eng):
                replay(eng, ops["pool"])

            @block.sync
            def _(eng):
                replay(eng, ops["sp"])


def _constants():
    c = {}
    c["c_ident"] = np.eye(128, dtype=np.float32)
    k = np.arange(128)[:, None]
    q = np.arange(128)[None, :]
    c["c_tri"] = (k <= q).astype(np.float32)
    c["c_ones"] = np.ones((128, 128), np.float32)
    qq = np.arange(512)[None, None, :]
    kk = np.arange(128)[:, None, None]
    rr = np.arange(4)[None, :, None]
    c["c_mask"] = ((rr * 128 + kk) <= qq).astype(np.float32)
    slopes = 2.0 ** (-8.0 * (np.arange(8, dtype=np.float64) + 1.0) / 8)
    t = np.zeros((128, 8, 2, 128), np.float64)
    kq = np.arange(128)[:, None]
    qv = np.arange(128)[None, :]
    for h in range(8):
        dist0 = 128 + qv - kq
        t[:, h, 1, :] = np.where(dist0 < 128, np.exp(-slopes[h] * dist0), 0.0)
        dist1 = qv - kq
        t[:, h, 0, :] = np.where(dist1 >= 0, np.exp(-slopes[h] * np.maximum(dist1, 0)), 0.0)
    c["c_swt"] = t.astype(np.float32)
    invf = 1.0 / (10000.0 ** (np.arange(0, 32, 2, dtype=np.float32) / 32))
    v = np.zeros((128, 2), np.float32)
    for p in range(64, 128):
        v[p, 0] = invf[(p - 64) % 16]
        v[p, 1] = (math.pi / 2) if p < 96 else 0.0
    c["c_rope"] = v
    return c


CONST_SHAPES = {"c_ident": [128, 128], "c_tri": [128, 128], "c_ones": [128, 128],
                "c_mask": [128, 4, 512], "c_swt": [128, 8, 2, 128], "c_rope": [128, 2]}


class _Stop(Exception):
    pass


def build_program(mode="full", stop=0):
    nc = bass.Bass("TRN2", target_bir_lowering=False)
    S = Sched(nc)
    do0 = mode in ("full", "l0")
    do1 = mode in ("full", "l1")

    def din(name, shape, dt=F32):
        return nc.dram_tensor(name, shape, dt, kind="ExternalInput").ap()

    cst = {k: din(k, v) for k, v in CONST_SHAPES.items()}
    if do0:
        x_d = din("x", [S_LEN, D])
        pos_d = din("positions", [S_LEN], I32)
        e_g_in = din("e_g_in", [D])
        e_w_in = din("e_w_in", [D, 2208])
        e_g_q_a = din("e_g_q_a", [256])
        e_w_q_up = din("e_w_q_up", [256, 768])
        e_g_kv_a = din("e_g_kv_a", [128])
        e_w_kv_up = din("e_w_kv_up", [128, 1024])
        e_sinks = din("e_sinks", [8])
        e_w_out = din("e_w_out", [D, D])
    if do1:
        o_g_in = din("o_g_in", [D])
        o_w_in = din("o_w_in", [D, 4112])
        o_b_f = din("o_b_f", [16])
        o_w_out = din("o_w_out", [D, D])
        g_final = din("g_final", [D])
        out_d = nc.dram_tensor("out", [S_LEN, D], F32, kind="ExternalOutput").ap()
    if mode == "full":
        x1_d = nc.dram_tensor("x1s", [S_LEN, D], F32).ap()
    elif mode == "l0":
        x1_d = nc.dram_tensor("x1", [S_LEN, D], F32, kind="ExternalOutput").ap()
    else:
        x1_d = din("x1", [S_LEN, D])
    t_x1 = S.toks(NT, "x1d")
    t_x1own = S.tok("x1own")
    t_outown = S.toks(2, "outown")
    t_out = S.toks(NT, "outd")

    def region(nbytes):
        st, _ = nc.bump_sbuf(nbytes)
        return st

    def at(name, shape, dt, off):
        return nc.alloc_sbuf_tensor_at(name, shape, dt, offset=off)

    def sb(name, shape, dt):
        return nc.alloc_sbuf_tensor(name, shape, dt)

    hT = sb("hT", [128, 8, S_LEN], BF16)
    t_hT = S.toks(NT, "hT")
    o_aot = region(65536)
    AOT = at("AOT", [128, 8, S_LEN], BF16, o_aot)
    t_AOT = [[S.tok(f"aot{c}_{b}") for b in range(NB)] for c in range(8)]
    o_big = region(32768)
    BIG = at("BIG", [128, 32 * 4 * 128], BF16, o_big)
    t_VA = S.tok("VA")
    t_LAT = S.toks(NB, "LAT")
    o_trig = region(8192)
    TRIG = at("TRIG", [128, S_LEN], BF16, o_trig)
    t_TRIG = S.tok("TRIG")
    o_pool = region(19456)
    QT = at("QT", [128, S_LEN], BF16, o_pool)
    t_QT = S.toks(NB, "QT")
    t_QTaug = S.tok("QTaug")
    KT = at("KT", [128, S_LEN], BF16, o_pool + 8192)
    t_KT = S.tok("KT")
    t_KTaug = S.tok("KTaug")
    NPT = 5
    PT = [at(f"pt{i}", [128, 512], BF16, o_pool + 16384 + 1024 * i) for i in range(3)]
    PT += [sb(f"ptx{i}", [128, 512], BF16) for i in range(NPT - 3)]
    t_PT = S.toks(NPT, "pt")
    XT = [at(f"xt{i}", [128, D], F32, o_pool + 4096 * i) for i in range(3)]
    t_XT = S.toks(3, "xt")
    XB = [at(f"xb{i}", [128, D], BF16, o_pool + 12288 + 2048 * i) for i in range(2)]
    t_XB = S.toks(2, "xb")
    ET = [at(f"et{i}", [128, 512], F32, o_pool + 2048 * i) for i in range(3)]
    t_ET = S.toks(3, "et")
    o_pf = region(1024)
    PF = [at(f"pf{i}", [128, 128], F32, o_pf + 512 * i) for i in range(2)]
    t_PF = S.toks(2, "pf")
    o_rc = region(4096)
    RC = [at(f"rc{i}", [128, 512], F32, o_rc + 2048 * i) for i in range(2)]
    t_RC = S.toks(2, "rc")
    GB = at("GB", [128, D], F32, o_rc)
    t_GB = S.tok("GB")
    o_wa = region(8192)
    WA = at("WA", [128, 3328], BF16, o_wa)
    t_WA = S.tok("WA")
    WB = sb("WB", [128, 1024], BF16)
    t_WB = S.tok("WB")
    MASK = sb("MASK", [128, 128], BF16)
    t_MASK = S.tok("MASK")
    identb = sb("identb", [128, 128], BF16)
    onesf = sb("onesf", [128, 128], F32)
    t_cst = S.tok("cst")
    stat = sb("stat", [128, 8], F32)
    t_statS = S.toks(2, "stat")
    epsb = sb("epsb", [128, 2], F32)
    t_eps = S.tok("eps")

    PS = [nc.alloc_psum_tensor(f"ps{i}", [128, 512], F32) for i in range(7)]
    t_PS = S.toks(7, "ps")
    PST = nc.alloc_psum_tensor("pst", [128, 8, 128], BF16)
    t_PST = S.tok("pst")

    S.dma("sp", lambda e: e.dma_start(out=onesf[:], in_=cst["c_ones"][:, :]), writes=[t_cst])
    t_cstb = S.tok("cstb")
    S.dma("pool", lambda e: e.dma_start(out=identb[:], in_=cst["c_ident"][:, :]), writes=[t_cstb])
    S.dma("pool", lambda e: e.dma_start(out=MASK[:], in_=cst["c_tri"][:, :]), writes=[t_MASK])
    S.op("dve", lambda e: e.memset(epsb[:, 0:1], EPS), writes=[t_eps])
    S.op("dve", lambda e: e.memset(epsb[:, 1:2], 1.0), writes=[t_eps])

    cnt = {"ev": 0}

    def evac_engine():
        cnt["ev"] += 1
        return "act" if cnt["ev"] % 2 else "dve"

    def copy_op(eng, out, in_, scale=None):
        if eng == "act":
            if scale is None:
                return lambda e: e.copy(out=out, in_=in_)
            return lambda e: e.mul(out=out, in_=in_, mul=scale)
        if scale is None:
            return lambda e: e.tensor_copy(out=out, in_=in_)
        return lambda e: e.tensor_scalar(out=out, in0=in_, scalar1=scale, scalar2=None, op0=ALU.mult)

    def load_gain(g_d):
        S.dma("sp", lambda e: e.dma_start(out=GB[:], in_=bass.AP(g_d.tensor, 0, [[0, 128], [1, D]])),
              writes=[t_GB])

    def rstd_from_ms(col, tst):
        S.op("act", lambda e: e.activation(out=stat[:, col + 1:col + 2], in_=stat[:, col:col + 1], func=AF.Ln,
                                           bias=epsb[:, 0:1], scale=1.0), reads=[t_eps], writes=[tst])
        S.op("act", lambda e: e.activation(out=stat[:, col + 1:col + 2], in_=stat[:, col + 1:col + 2],
                                           func=AF.Exp, scale=-0.5), writes=[tst])

    def norm_tile(xt, t_xt, out_ap, t_outs, junk_ap, t_junk, slot=0):
        c0 = 4 * slot
        tst = t_statS[slot]
        S.op("dve", lambda e: e.memset(stat[:, c0:c0 + 1], 0.0), writes=[tst])
        S.op("act", lambda e: e.activation(out=junk_ap, in_=xt[:], func=AF.Square, scale=1.0 / 32,
                                           accum_out=stat[:, c0:c0 + 1]), reads=[t_xt], writes=[t_junk, tst])
        rstd_from_ms(c0, tst)
        S.op("dve", lambda e: e.scalar_tensor_tensor(out=out_ap, in0=xt[:], scalar=stat[:, c0 + 1:c0 + 2], in1=GB[:],
                                                     op0=ALU.mult, op1=ALU.mult),
             reads=[t_xt, tst, t_GB], writes=list(t_outs))

    def to_hT_norm(xt, t_xt, tt):
        i = tt % 2
        norm_tile(xt, t_xt, XB[i][:], [t_XB[i]], XB[i][:], t_XB[i], slot=i)

    def to_hT_tr(tt):
        i = tt % 2
        for c in range(8):
            S.op("pe", lambda e, c=c: e.transpose(out=PST[:, c, :], in_=XB[i][:, c * 128:(c + 1) * 128],
                                                   identity=identb[:]),
                 reads=[t_XB[i], t_cstb], writes=[t_PST])
        S.op("dve", copy_op("dve", hT[:, :, tt * 128:(tt + 1) * 128], PST[:, :, :]), writes=[t_PST, t_hT[tt]])

    def phase_A(src_d):
        for tt in range(NT + 1):
            if tt < NT:
                i = tt % 3
                S.dma("sp", lambda e, tt=tt, i=i: e.dma_start(out=XT[i][:], in_=src_d[tt * 128:(tt + 1) * 128, :]),
                      writes=[t_XT[i]])
                to_hT_norm(XT[i], t_XT[i], tt)
            if tt >= 1:
                to_hT_tr(tt - 1)

    def wview(w_d, c0, c1):
        return w_d.rearrange("(c p) n -> p c n", p=128)[:, :, c0:c1]

    def proj_fm(ps, t_ps, w_ap_fn, nk, src_fn, src_toks, w_toks, M):
        for k in range(nk):
            S.op("pe", lambda e, k=k: e.matmul(ps[0:M, :], lhsT=w_ap_fn(k), rhs=src_fn(k),
                                               start=(k == 0), stop=(k == nk - 1)),
                 reads=list(src_toks) + list(w_toks), writes=[t_ps])

    def attention(KR, groups, bias_fn, scale, va_fn, out_fn, den_add=None, fp32_tables=False, rd_extra=(),
                  KTt=None, pre_group=None, bg=(), va_tok=None, one_rc=False, pf_bufs=None, skip_gc=False):
        steps = []
        for gi, (q0, QW, kbs, gidx) in enumerate(groups):
            for n, kb in enumerate(kbs):
                if len(kb) == 3:
                    j, c0, tbl = kb
                    c1, t0, t1 = QW, c0, c0 + 128
                else:
                    j, c0, c1, tbl = kb
                    t0, t1 = c0, c1
                steps.append((gi, q0, QW, j, c0, c1, tbl, t0, t1, n == 0, n == len(kbs) - 1, gidx))

        SB_ = (0, 1, 6)
        LA = 2
        KTx, t_KTx = (KT, [t_KT, t_KTaug]) if KTt is None else KTt
        t_VAx = t_VA if va_tok is None else va_tok
        PFx = PF if pf_bufs is None else pf_bufs
        bg = list(bg)
        nsteps = len(steps)
        bg_stride = max(1, nsteps // (len(bg) + 1)) if bg else 0

        def emit_qk(i):
            gi, q0, QW, j, c0, c1, tbl, t0, t1, first, last, gidx = steps[i]
            sp = PS[SB_[i % 3]]
            S.op("pe", lambda e: e.matmul(sp[:, c0:c1], lhsT=KTx[0:KR, j * 128:(j + 1) * 128],
                                          rhs=QT[0:KR, q0 + c0:q0 + c1], start=True, stop=True),
                 reads=list(t_KTx) + [t_QT[q0 // 512], t_QTaug], writes=[t_PS[SB_[i % 3]]])

        first_idx = {}
        for i_, st_ in enumerate(steps):
            first_idx.setdefault(st_[0], i_)
        if pre_group is not None:
            pre_group(groups[0][3])
        for i0 in range(min(LA, len(steps))):
            emit_qk(i0)
        for i, (gi, q0, QW, j, c0, c1, tbl, t0, t1, first, last, gidx) in enumerate(steps):
            if i + LA < len(steps):
                emit_qk(i + LA)
            if pre_group is not None and i == first_idx[gi] + 1 and gi + 1 < len(groups):
                pre_group(groups[gi + 1][3])
            if bg and i > 0 and i % bg_stride == 0:
                bg.pop(0)()
            sp = PS[SB_[i % 3]]
            tsp = t_PS[SB_[i % 3]]
            pt = PT[i % NPT]
            tpt = t_PT[i % NPT]
            kw = {"scale": scale}
            b = bias_fn(j) if bias_fn is not None else None
            if b is not None:
                kw["bias"] = b
            if tbl is not None and fp32_tables:
                pf = PFx[i % 2]
                w = c1 - c0
                S.op("act", lambda e: e.activation(out=pf[:, 0:w], in_=sp[:, c0:c1], func=AF.Exp, **kw),
                     reads=list(rd_extra), writes=[tsp, t_PF[i % 2]])
                S.op("dve", lambda e: e.tensor_tensor(out=pt[:, c0:c1], in0=pf[:, 0:w], in1=tbl, op=ALU.mult),
                     reads=[t_PF[i % 2]] + list(rd_extra), writes=[tpt])
            else:
                S.op("act", lambda e: e.activation(out=pt[:, c0:c1], in_=sp[:, c0:c1], func=AF.Exp, **kw),
                     reads=list(rd_extra), writes=[tsp, tpt])
                if tbl is not None:
                    S.op("dve", lambda e: e.tensor_tensor(out=pt[:, t0:t1], in0=pt[:, t0:t1], in1=tbl, op=ALU.mult),
                         reads=[t_MASK], writes=[tpt])
            op_ = PS[2 + gi % 2]
            top = t_PS[2 + gi % 2]
            mkw = {"skip_group_check": True} if skip_gc else {}
            S.op("pe", lambda e: e.matmul(op_[:, c0:c1], lhsT=va_fn(j), rhs=pt[:, c0:c1], start=first, stop=last,
                                          **mkw),
                 reads=[tpt, t_VAx], writes=[top])
            if last:
                rc = RC[0 if one_rc else gi % 2]
                trc = t_RC[0 if one_rc else gi % 2]
                if den_add is not None:
                    S.op("dve", lambda e: e.tensor_scalar(
                        out=rc[64:128, 0:QW], in0=op_[64:128, 0:QW], scalar1=den_add, scalar2=None, op0=ALU.add),
                        reads=list(rd_extra), writes=[top, trc])
                    S.op("act", lambda e: e.activation(out=rc[64:128, 0:QW], in_=rc[64:128, 0:QW], func=AF.Ln),
                         writes=[trc])
                    S.op("act", lambda e: e.activation(out=rc[64:128, 0:QW], in_=rc[64:128, 0:QW], func=AF.Exp,
                                                       scale=-1.0), writes=[trc])
                else:
                    S.op("dve", lambda e: e.reciprocal(out=rc[64:128, 0:QW], in_=op_[64:128, 0:QW]),
                         writes=[top, trc])
                o_ap, o_toks = out_fn(gidx)
                S.op("dve", lambda e: e.tensor_tensor(out=o_ap, in0=op_[0:64, 0:QW], in1=rc[64:128, 0:QW],
                                                      op=ALU.mult),
                     reads=[trc], writes=[top] + list(o_toks))
        while bg:
            bg.pop(0)()

    def _unused():
        pass

    def dense_groups():
        gs = []
        for g in range(NB):
            kbs = [(j, 0, None) for j in range(4 * g)] + [(4 * g + r, r * 128, MASK[:, :]) for r in range(4)]
            gs.append((g * 512, 512, kbs, g))
        return gs

    def gate_phase(w_in_d, goff):
        for c in range(8):
            S.dma("pool", lambda e, c=c: e.dma_start(
                out=WB[:, 0:1024].rearrange("p (c n) -> p c n", c=8),
                in_=wview(w_in_d, goff + c * 128, goff + (c + 1) * 128)), writes=[t_WB])
            for b in range(NB):
                ps = PS[4 + b % 2]
                tps = t_PS[4 + b % 2]
                proj_fm(ps, tps, lambda k: WB[:, k * 128:(k + 1) * 128], 8,
                        lambda k, b=b: hT[:, k, b * 512:(b + 1) * 512], t_hT[4 * b:4 * b + 4], [t_WB], 128)
                gt = PT[b % NPT]
                S.op("act", lambda e, gt=gt, ps=ps: e.activation(out=gt[:], in_=ps[:], func=AF.Silu),
                     writes=[tps, t_PT[b % NPT]])
                S.op("dve", lambda e, gt=gt, c=c, b=b: e.tensor_tensor(
                    out=AOT[:, c, b * 512:(b + 1) * 512], in0=AOT[:, c, b * 512:(b + 1) * 512], in1=gt[:],
                    op=ALU.mult), reads=[t_PT[b % NPT]], writes=[t_AOT[c][b]])

    def out_phase(w_out_d, res_d, t_res, last_layer, next_gain_d=None, gate=None):
        S.barrier()
        WO = at("WO_%d" % int(last_layer), [128, 8 * 1024], BF16, o_big)
        t_WO = S.tok("WO")
        WG = at("WG_%d" % int(last_layer), [128, 8 * 1024], BF16, o_big + 16384)
        t_WG = S.tok("WG")
        gw_d, goff = gate
        for c in range(8):
            S.dma("pool", lambda e, c=c: e.dma_start(
                out=WG[:, c * 1024:(c + 1) * 1024].rearrange("p (k n) -> p k n", k=8),
                in_=wview(gw_d, goff + c * 128, goff + (c + 1) * 128)), writes=[t_WG])
        S.dma("pool", lambda e: e.dma_start(out=WO[:].rearrange("p (c n) -> p c n", c=8),
                                            in_=wview(w_out_d, 0, D)), writes=[t_WO])
        gcnt = {"n": 0}

        def gate_chunk(c, b):
            gcnt["n"] += 1
            pb = 4 + gcnt["n"] % 2
            proj_fm(PS[pb], t_PS[pb], lambda k: WG[:, c * 1024 + k * 128:c * 1024 + (k + 1) * 128], 8,
                    lambda k: hT[:, k, b * 512:(b + 1) * 512], t_hT[4 * b:4 * b + 4], [t_WG], 128)
            gi_ = gcnt["n"] % NPT
            gt = PT[gi_]
            S.op("act", lambda e: e.activation(out=gt[:], in_=PS[pb][:], func=AF.Silu),
                 writes=[t_PS[pb], t_PT[gi_]])
            S.op("dve", lambda e: e.tensor_tensor(
                out=AOT[:, c, b * 512:(b + 1) * 512], in0=AOT[:, c, b * 512:(b + 1) * 512], in1=gt[:],
                op=ALU.mult), reads=[t_PT[gi_]], writes=[t_AOT[c][b]])

        for c in range(8):
            gate_chunk(c, 0)
        if last_layer:
            load_gain(g_final)
        elif next_gain_d is not None:
            load_gain(next_gain_d)
        t_x1o = S.toks(3, "x1own")
        t_outo = S.toks(3, "outown")
        def stage1(tt):
            i = tt % 3
            b = tt // 4
            pp = 2 * (tt % 2)
            S.dma("sp", lambda e: e.dma_start(out=XT[i][:], in_=res_d[tt * 128:(tt + 1) * 128, :]),
                  reads=[t_res[tt]] if t_res is not None else [], writes=[t_XT[i]])
            for half in range(2):
                ps = PS[pp + half]
                tps = t_PS[pp + half]
                for c in range(8):
                    S.op("pe", lambda e: e.matmul(
                        ps[:, :], lhsT=AOT[:, c, tt * 128:(tt + 1) * 128],
                        rhs=WO[:, c * 1024 + half * 512: c * 1024 + (half + 1) * 512],
                        start=(c == 0), stop=(c == 7)),
                        reads=[t_AOT[c][b], t_WO], writes=[tps])

        def stage1b(tt):
            i = tt % 3
            pp = 2 * (tt % 2)
            for half in range(2):
                ps = PS[pp + half]
                tps = t_PS[pp + half]
                S.op("dve", lambda e: e.tensor_tensor(
                    out=XT[i][:, half * 512:(half + 1) * 512], in0=ps[:, :], in1=XT[i][:, half * 512:(half + 1) * 512],
                    op=ALU.add), writes=[tps, t_XT[i]])

        def stage2a(tt):
            i = tt % 3
            if last_layer:
                norm_tile(XT[i], t_XT[i], XT[i][:], [t_XT[i]], XB[tt % 2][:], t_XB[tt % 2], slot=tt % 2)
                S.dma("sp", lambda e: e.dma_start(out=out_d[tt * 128:(tt + 1) * 128, :], in_=XT[i][:]),
                      reads=[t_XT[i]], writes=[t_out[tt]], owner=t_outo[i])
            else:
                S.dma("sp", lambda e: e.dma_start(out=x1_d[tt * 128:(tt + 1) * 128, :], in_=XT[i][:]),
                      reads=[t_XT[i]], writes=[t_x1[tt]], owner=t_x1o[i])
                if mode == "full":
                    to_hT_norm(XT[i], t_XT[i], tt)

        for tt in range(NT + 2):
            if tt < NT:
                stage1(tt)
            if 1 <= tt <= NT:
                stage2a(tt - 1)
            if tt < NT:
                stage1b(tt)
            if tt < NT and tt // 4 + 1 < NB:
                for c in (2 * (tt % 4), 2 * (tt % 4) + 1):
                    gate_chunk(c, tt // 4 + 1)
            if tt >= 2 and (not last_layer) and mode == "full":
                to_hT_tr(tt - 2)
        S.barrier()

    def _layers():
        if do0:
            VA0 = at("VA0", [128, 32, 128], BF16, o_big)
            LATt = at("LATt", [128, 3, S_LEN], BF16, o_big + 8192)
            SWT = at("SWT", [128, 2048], F32, o_big + 8192)
            PFw = [at(f"pfw{i}", [128, 256], F32, o_big + 16384 + 1024 * i) for i in range(2)]
            posi = at("posi", [128, S_LEN], I32, o_aot)
            kf = at("kf", [128, S_LEN], F32, o_aot + 16384)
            ANG = at("ANG", [128, S_LEN], F32, o_aot + 32768)
            t_pos, t_kf, t_ang = S.tok("posi"), S.tok("kf"), S.tok("ang")
            load_gain(e_g_in)
            phase_A(x_d)

            gq = sb("gq", [128, 2], F32)
            gkv = sb("gkv", [128, 1], F32)
            esink = sb("esink", [128, 8], F32)
            ropec = sb("ropec", [128, 2], F32)
            t_sm = S.tok("small0")
            for c2 in range(2):
                S.dma("sp", lambda e, c2=c2: e.dma_start(
                    out=gq[:, c2:c2 + 1], in_=e_g_q_a[c2 * 128:(c2 + 1) * 128].rearrange("(p o) -> p o", o=1)),
                    writes=[t_sm])
            S.dma("sp", lambda e: e.dma_start(out=gkv[:], in_=e_g_kv_a.rearrange("(p o) -> p o", o=1)), writes=[t_sm])
            S.dma("sp", lambda e: e.dma_start(out=esink[:], in_=bass.AP(e_sinks.tensor, 0, [[0, 128], [1, 8]])),
                  writes=[t_sm])
            S.dma("sp", lambda e: e.dma_start(out=ropec[:], in_=cst["c_rope"][:, :]), writes=[t_sm])
            S.op("act", lambda e: e.activation(out=esink[:], in_=esink[:], func=AF.Exp), writes=[t_sm])

            S.dma("sp", lambda e: e.dma_start(out=posi[64:128, :], in_=bass.AP(pos_d.tensor, 0, [[0, 64], [1, S_LEN]])),
                  writes=[t_pos])
            P6 = slice(64, 128)
            S.op("dve", lambda e: e.tensor_copy(out=ANG[P6, :], in_=posi[P6, :]), reads=[t_pos], writes=[t_ang])
            S.op("dve", lambda e: e.tensor_scalar(out=ANG[P6, :], in0=ANG[P6, :], scalar1=ropec[P6, 0:1],
                                                  scalar2=ropec[P6, 1:2], op0=ALU.mult, op1=ALU.add),
                 reads=[t_sm], writes=[t_ang])
            S.op("dve", lambda e: e.tensor_scalar(out=kf[P6, :], in0=ANG[P6, :], scalar1=1.0 / (2 * math.pi),
                                                  scalar2=0.5, op0=ALU.mult, op1=ALU.add),
                 reads=[t_ang], writes=[t_kf])
            S.op("dve", lambda e: e.tensor_copy(out=posi[P6, :], in_=kf[P6, :]), reads=[t_kf], writes=[t_pos])
            S.op("dve", lambda e: e.tensor_copy(out=kf[P6, :], in_=posi[P6, :]), reads=[t_pos], writes=[t_kf])
            C1 = 6.28125
            C2 = 2 * math.pi - C1
            S.op("dve", lambda e: e.scalar_tensor_tensor(out=ANG[P6, :], in0=kf[P6, :], scalar=-C1, in1=ANG[P6, :],
                                                         op0=ALU.mult, op1=ALU.add), reads=[t_kf], writes=[t_ang])
            S.op("dve", lambda e: e.scalar_tensor_tensor(out=ANG[P6, :], in0=kf[P6, :], scalar=-C2, in1=ANG[P6, :],
                                                         op0=ALU.mult, op1=ALU.add), reads=[t_kf], writes=[t_ang])
            S.op("dve", lambda e: e.tensor_single_scalar(out=kf[P6, :], in_=ANG[P6, :], scalar=-math.pi, op=ALU.is_lt),
                 reads=[t_ang], writes=[t_kf])
            S.op("dve", lambda e: e.scalar_tensor_tensor(out=ANG[P6, :], in0=kf[P6, :], scalar=2 * math.pi,
                                                         in1=ANG[P6, :], op0=ALU.mult, op1=ALU.add),
                 reads=[t_kf], writes=[t_ang])
            S.op("dve", lambda e: e.tensor_scalar(out=ANG[P6, :], in0=ANG[P6, :], scalar1=-3.1415925, scalar2=3.1415925,
                                                  op0=ALU.max, op1=ALU.min), writes=[t_ang])
            S.op("act", lambda e: e.activation(out=TRIG[P6, :], in_=ANG[P6, :], func=AF.Sin), reads=[t_ang],
                 writes=[t_TRIG])
            S.barrier()
            checkpoint(1)

            S.dma("sp", lambda e: e.dma_start(out=SWT[:, :], in_=cst["c_swt"].rearrange("p h r q -> p (h r q)")),
                  writes=[t_kf])
            S.op("dve", lambda e: e.memset(VA0[:, :, 64:128], 1.0), writes=[t_VA])
            SWA_Q0, SWA_K0, SWA_V0 = 416, 928, 1056
            WBkv = WB[:, 0:1024].rearrange("p (c s n) -> p c s n", c=8, s=2)
            for h in range(8):
                kv = h // 4
                if h % 4 == 0:
                    S.dma("pool", lambda e, kv=kv: e.dma_start(
                        out=WBkv[:, :, 0, :], in_=wview(e_w_in, SWA_K0 + kv * 64, SWA_K0 + (kv + 1) * 64)), writes=[t_WB])
                    S.dma("pool", lambda e, kv=kv: e.dma_start(
                        out=WBkv[:, :, 1, :], in_=wview(e_w_in, SWA_V0 + kv * 64, SWA_V0 + (kv + 1) * 64)), writes=[t_WB])
                    for b in range(NB):
                        ps = PS[4 + b % 2]
                        tps = t_PS[4 + b % 2]
                        proj_fm(ps, tps, lambda k: WB[:, k * 128:k * 128 + 64], 8,
                                lambda k, b=b: hT[:, k, b * 512:(b + 1) * 512], t_hT[4 * b:4 * b + 4], [t_WB], 64)
                        eng = evac_engine()
                        S.op(eng, copy_op(eng, KT[0:64, b * 512:(b + 1) * 512], ps[0:64, :]), writes=[tps, t_KT])
                    for t8 in range(4):
                        ps = PS[4 + t8 % 2]
                        tps = t_PS[4 + t8 % 2]
                        for ti in range(8):
                            tt = t8 * 8 + ti
                            for k in range(8):
                                S.op("pe", lambda e, k=k, tt=tt, ti=ti, ps=ps: e.matmul(
                                    ps[:, ti * 64:(ti + 1) * 64], lhsT=hT[:, k, tt * 128:(tt + 1) * 128],
                                    rhs=WB[:, k * 128 + 64:k * 128 + 128], start=(k == 0), stop=(k == 7)),
                                    reads=[t_hT[tt], t_WB], writes=[tps])
                        eng = evac_engine()
                        S.op(eng, copy_op(eng, VA0[:, t8 * 8:(t8 + 1) * 8, 0:64],
                                          ps[:, :].rearrange("p (t d) -> p t d", t=8)), writes=[tps, t_VA])
                S.dma("pool", lambda e, h=h: e.dma_start(
                    out=WA[:, 0:512].rearrange("p (c n) -> p c n", c=8),
                    in_=wview(e_w_in, SWA_Q0 + h * 64, SWA_Q0 + (h + 1) * 64)), writes=[t_WA])
                for b in range(NB):
                    ps = PS[4 + b % 2]
                    tps = t_PS[4 + b % 2]
                    proj_fm(ps, tps, lambda k: WA[:, k * 64:(k + 1) * 64], 8,
                            lambda k, b=b: hT[:, k, b * 512:(b + 1) * 512], t_hT[4 * b:4 * b + 4], [t_WA], 64)
                    eng = evac_engine()
                    S.op(eng, copy_op(eng, QT[0:64, b * 512:(b + 1) * 512], ps[0:64, :], scale=0.125),
                         writes=[tps, t_QT[b]])
                groups = []
                for g4 in range(NB):
                    n0 = 4 * g4
                    kbs = []
                    for j in range(max(0, n0 - 1), n0 + 4):
                        qa, qb = max(j, n0), min(j + 1, n0 + 3)
                        ta = 0 if j >= n0 else 128
                        tb = 256 if j + 1 <= n0 + 3 else 128
                        kbs.append((j, (qa - n0) * 128, (qb - n0 + 1) * 128,
                                    SWT[:, h * 256 + ta:h * 256 + tb]))
                    groups.append((g4 * 512, 512, kbs, g4))
                c = 4 + h // 2
                po = (h % 2) * 64

                def out_fn(g, c=c, po=po):
                    return AOT[po:po + 64, c, g * 512:(g + 1) * 512], [t_AOT[c][g]]
                attention(64, groups, None, 1.0, lambda j: VA0[:, j, :], out_fn,
                          den_add=esink[64:128, h:h + 1], fp32_tables=True, rd_extra=[t_sm, t_kf],
                          pf_bufs=PFw, skip_gc=True)
            S.barrier()
            checkpoint(2)

            W416 = WA[:, 0:8 * 416].rearrange("p (c n) -> p c n", c=8)
            S.dma("pool", lambda e: e.dma_start(out=W416, in_=wview(e_w_in, 0, 416)), writes=[t_WA])
            WROT = WB[:, 0:8 * 96].rearrange("p (c n) -> p c n", c=8)
            S.op("dve", lambda e: e.memset(WROT[:, :, 0:64], 0.0), writes=[t_WB])
            S.op("dve", lambda e: e.tensor_scalar(out=WROT[:, :, 64:80], in0=W416[:, :, 400:416], scalar1=-1.0,
                                                  scalar2=None, op0=ALU.mult), reads=[t_WA], writes=[t_WB])
            S.op("dve", lambda e: e.tensor_copy(out=WROT[:, :, 80:96], in_=W416[:, :, 384:400]), reads=[t_WA],
                 writes=[t_WB])
            for b in range(NB):
                bs = slice(b * 512, (b + 1) * 512)
                hsrc = lambda k, b=b: hT[:, k, b * 512:(b + 1) * 512]
                ht = t_hT[4 * b:4 * b + 4]
                proj_fm(PS[0], t_PS[0], lambda k: W416[:, k, 0:128], 8, hsrc, ht, [t_WA], 128)
                proj_fm(PS[1], t_PS[1], lambda k: W416[:, k, 128:256], 8, hsrc, ht, [t_WA], 128)
                proj_fm(PS[2], t_PS[2], lambda k: W416[:, k, 256:384], 8, hsrc, ht, [t_WA], 128)
                proj_fm(PS[3], t_PS[3], lambda k: W416[:, k, 320:416], 8, hsrc, ht, [t_WA], 96)
                proj_fm(PS[4], t_PS[4], lambda k: WROT[:, k, :], 8, hsrc, ht, [t_WB], 96)
                for n in range(3):
                    S.op("act", lambda e, n=n: e.activation(out=ET[n][:], in_=PS[n][:], func=AF.Square),
                         writes=[t_PS[n], t_ET[n]])
                S.op("pe", lambda e: e.matmul(PS[5][:, :], lhsT=onesf[:], rhs=ET[0][:], start=True, stop=False),
                     reads=[t_ET[0], t_cst], writes=[t_PS[5]])
                S.op("pe", lambda e: e.matmul(PS[5][:, :], lhsT=onesf[:], rhs=ET[1][:], start=False, stop=True),
                     reads=[t_ET[1], t_cst], writes=[t_PS[5]])
                S.op("pe", lambda e: e.matmul(PS[6][:, :], lhsT=onesf[:], rhs=ET[2][:], start=True, stop=True),
                     reads=[t_ET[2], t_cst], writes=[t_PS[6]])
                for (pi, n_, n) in ((5, 256.0, 0), (6, 128.0, 1)):
                    S.op("act", lambda e, pi=pi, n_=n_, n=n: e.activation(out=ET[n][:], in_=PS[pi][:], func=AF.Ln,
                                                                    bias=epsb[:, 0:1], scale=1.0 / n_),
                         reads=[t_eps], writes=[t_PS[pi], t_ET[n]])
                    S.op("act", lambda e, n=n: e.activation(out=ET[n][:], in_=ET[n][:], func=AF.Exp, scale=-0.5),
                         writes=[t_ET[n]])
                for (pi, chunk, gsc, n) in ((0, 0, gq[:, 0:1], 0), (1, 1, gq[:, 1:2], 0), (2, 2, gkv[:, 0:1], 1)):
                    S.op("dve", lambda e, pi=pi, chunk=chunk, gsc=gsc, n=n, bs=bs: e.scalar_tensor_tensor(
                        out=LATt[:, chunk, bs], in0=PS[pi][:], scalar=gsc, in1=ET[n][:], op0=ALU.mult, op1=ALU.mult),
                        reads=[t_ET[n], t_sm], writes=[t_PS[pi], t_LAT[b]])
                S.op("dve", lambda e, bs=bs: e.tensor_tensor(out=RC[0][64:96, :], in0=PS[3][64:96, :], in1=TRIG[64:96, bs],
                                                         op=ALU.mult), reads=[t_TRIG], writes=[t_PS[3], t_RC[0]])
                S.op("dve", lambda e, bs=bs: e.tensor_tensor(out=RC[1][64:96, :], in0=PS[4][64:96, :], in1=TRIG[96:128, bs],
                                                         op=ALU.mult), reads=[t_TRIG], writes=[t_PS[4], t_RC[1]])
                S.op("dve", lambda e, bs=bs: e.tensor_tensor(out=KT[64:96, bs], in0=RC[0][64:96, :], in1=RC[1][64:96, :],
                                                         op=ALU.add), reads=[t_RC[0], t_RC[1]], writes=[t_KTaug])
            S.barrier()
            checkpoint(3)

            WQ = WA[:, 0:2 * 768].rearrange("p (c n) -> p c n", c=2)
            S.dma("pool", lambda e: e.dma_start(out=WQ, in_=wview(e_w_q_up, 0, 768)), writes=[t_WA])
            WQR = WA[:, 1536:1536 + 2 * 768].rearrange("p (c n) -> p c n", c=2)
            WKV = WB[:, 0:1024]
            S.dma("pool", lambda e: e.dma_start(out=WKV, in_=e_w_kv_up[:, :]), writes=[t_WB])
            S.op("dve", lambda e: e.memset(WA[:, 1536:1536 + 2 * 768], 0.0), writes=[t_WA])
            for c2 in range(2):
                src = WQ[:, c2, :].rearrange("p (h d) -> p h d", h=8)
                dst = WQR[:, c2, :].rearrange("p (h d) -> p h d", h=8)
                S.op("dve", lambda e, src=src, dst=dst: e.tensor_scalar(out=dst[:, :, 64:80], in0=src[:, :, 80:96],
                                                                  scalar1=-1.0, scalar2=None, op0=ALU.mult),
                     writes=[t_WA])
                S.op("dve", lambda e, src=src, dst=dst: e.tensor_copy(out=dst[:, :, 80:96], in_=src[:, :, 64:80]),
                     writes=[t_WA])
            mla_scale = 96.0 ** -0.5
            for h in range(8):
                for b in range(NB):
                    ps = PS[4 + b % 2]
                    tps = t_PS[4 + b % 2]
                    S.op("pe", lambda e, b=b, h=h, ps=ps: e.matmul(ps[0:64, :], lhsT=WKV[:, h * 128:h * 128 + 64],
                                                             rhs=LATt[:, 2, b * 512:(b + 1) * 512], start=True, stop=True),
                         reads=[t_LAT[b], t_WB], writes=[tps])
                    eng = evac_engine()
                    S.op(eng, copy_op(eng, KT[0:64, b * 512:(b + 1) * 512], ps[0:64, :]), writes=[tps, t_KT])
                for t8 in range(4):
                    ps = PS[4 + t8 % 2]
                    tps = t_PS[4 + t8 % 2]
                    for ti in range(8):
                        tt = t8 * 8 + ti
                        S.op("pe", lambda e, tt=tt, ti=ti, h=h, ps=ps: e.matmul(
                            ps[:, ti * 64:(ti + 1) * 64], lhsT=LATt[:, 2, tt * 128:(tt + 1) * 128],
                            rhs=WKV[:, h * 128 + 64:h * 128 + 128], start=True, stop=True),
                            reads=[t_LAT[tt // 4], t_WB], writes=[tps])
                    eng = evac_engine()
                    S.op(eng, copy_op(eng, VA0[:, t8 * 8:(t8 + 1) * 8, 0:64],
                                      ps[:, :].rearrange("p (t d) -> p t d", t=8)), writes=[tps, t_VA])
                for b in range(NB):
                    bs = slice(b * 512, (b + 1) * 512)
                    p1, tp1 = PS[4], t_PS[4]
                    p2, tp2 = PS[5], t_PS[5]
                    for k in range(2):
                        S.op("pe", lambda e, k=k, h=h, bs=bs: e.matmul(p1[0:96, :], lhsT=WQ[:, k, h * 96:(h + 1) * 96],
                                                                rhs=LATt[:, k, bs], start=(k == 0), stop=(k == 1)),
                             reads=[t_LAT[b], t_WA], writes=[tp1])
                    for k in range(2):
                        S.op("pe", lambda e, k=k, h=h, bs=bs: e.matmul(p2[0:96, :], lhsT=WQR[:, k, h * 96:(h + 1) * 96],
                                                                rhs=LATt[:, k, bs], start=(k == 0), stop=(k == 1)),
                             reads=[t_LAT[b], t_WA], writes=[tp2])
                    S.op("act", lambda e, bs=bs: e.copy(out=QT[0:64, bs], in_=p1[0:64, :]),
                         writes=[tp1, t_QT[b]])
                    S.op("dve", lambda e, bs=bs: e.tensor_tensor(out=RC[0][64:96, :], in0=p1[64:96, :], in1=TRIG[64:96, bs],
                                                             op=ALU.mult), reads=[t_TRIG], writes=[tp1, t_RC[0]])
                    S.op("dve", lambda e, bs=bs: e.tensor_tensor(out=RC[1][64:96, :], in0=p2[64:96, :], in1=TRIG[96:128, bs],
                                                             op=ALU.mult), reads=[t_TRIG], writes=[tp2, t_RC[1]])
                    S.op("dve", lambda e, bs=bs: e.tensor_tensor(out=QT[64:96, bs], in0=RC[0][64:96, :], in1=RC[1][64:96, :],
                                                             op=ALU.add), reads=[t_RC[0], t_RC[1]], writes=[t_QT[b]])
                c = h // 2
                po = (h % 2) * 64

                def out_fn(g, c=c, po=po):
                    return AOT[po:po + 64, c, g * 512:(g + 1) * 512], [t_AOT[c][g]]
                attention(96, dense_groups(), None, mla_scale, lambda j: VA0[:, j, :], out_fn)

            checkpoint(4)
            checkpoint(5)
            out_phase(e_w_out, x_d, None, last_layer=False, next_gain_d=(o_g_in if do1 else None),
                      gate=(e_w_in, 1184))
            checkpoint(6)

        if do1:
            VA4 = at("VA4", [128, 32, 4, 128], BF16, o_big)
            if not do0:
                load_gain(o_g_in)
                phase_A(x1_d)
                S.barrier()
            Q0, K0, V0, F0, G0 = 0, 1024, 2048, 3072, 3088
            WF = WB[:, 0:128].rearrange("p (c n) -> p c n", c=8)
            S.dma("pool", lambda e: e.dma_start(out=WF, in_=wview(o_w_in, F0, F0 + 16)), writes=[t_WB])
            NL = at("NL", [128, 32, 16], F32, o_wa)
            trif = at("trif", [128, 128], F32, o_wa + 2048)
            identf = at("identf", [128, 128], F32, o_wa + 2560)
            CTf = at("CTf", [16, S_LEN], F32, o_big)
            r1 = at("r1", [16, S_LEN], F32, o_big + 16384)
            CS = at("CS", [128, S_LEN], BF16, o_trig)
            bfb = sb("bfb", [128, 16], F32)
            Ct = sb("Ct", [128, 32, 16], F32)
            Rs = sb("Rs", [128, 16], F32)
            zt = sb("zt", [128, 16], F32)
            t_bfb, t_NL, t_Ct, t_Rs, t_zt = S.tok("bfb"), S.toks(NT, "NL"), S.toks(NT, "Ct"), S.tok("Rs"), S.tok("zt")
            t_c1 = S.tok("cst1")
            S.dma("sp", lambda e: e.dma_start(out=trif[:], in_=cst["c_tri"][:, :]), writes=[t_c1])
            S.dma("sp", lambda e: e.dma_start(out=identf[:], in_=cst["c_ident"][:, :]), writes=[t_c1])
            S.dma("sp", lambda e: e.dma_start(out=bfb[:], in_=bass.AP(o_b_f.tensor, 0, [[0, 128], [1, 16]])),
                  writes=[t_bfb])
            Tt = at("Tt", [128, 32, 16], F32, o_wa + 3072)
            Pfx = at("Pfx", [128, 32, 16], F32, o_wa + 5120)
            t_NLa, t_Tt, t_Pfx = S.tok("NLa"), S.tok("Tt"), S.tok("Pfx")
            pf_, tpf_ = PS[4], t_PS[4]
            for tt in range(NT):
                for k in range(8):
                    S.op("pe", lambda e, k=k, tt=tt: e.matmul(pf_[:, tt * 16:(tt + 1) * 16],
                                                          lhsT=hT[:, k, tt * 128:(tt + 1) * 128],
                                                          rhs=WF[:, k, :], start=(k == 0), stop=(k == 7)),
                         reads=[t_hT[tt], t_WB], writes=[tpf_])
            S.op("dve", lambda e: e.tensor_tensor(out=NL[:, :, :], in0=pf_[:, :].rearrange("p (t h) -> p t h", t=32),
                                                  in1=bass.AP(bfb, 0, [[16, 128], [0, 32], [1, 16]]), op=ALU.add),
                 reads=[t_bfb], writes=[tpf_, t_NLa])
            S.op("act", lambda e: e.activation(out=NL[:, :, :], in_=NL[:, :, :], func=AF.Exp, scale=-1.0),
                 writes=[t_NLa])
            S.op("act", lambda e: e.activation(out=NL[:, :, :], in_=NL[:, :, :], func=AF.Ln, bias=epsb[:, 1:2],
                                               scale=1.0), reads=[t_eps], writes=[t_NLa])
            pT_, tpT_ = PS[5], t_PS[5]
            pC_, tpC_ = PS[6], t_PS[6]
            for tt in range(NT):
                S.op("pe", lambda e, tt=tt: e.matmul(pT_[:, tt * 16:(tt + 1) * 16], lhsT=onesf[:], rhs=NL[:, tt, :],
                                                     start=True, stop=True),
                     reads=[t_NLa, t_cst], writes=[tpT_])
            for tt in range(NT):
                S.op("pe", lambda e, tt=tt: e.matmul(pC_[:, tt * 16:(tt + 1) * 16], lhsT=trif[:], rhs=NL[:, tt, :],
                                                     start=True, stop=True),
                     reads=[t_NLa, t_c1], writes=[tpC_])
            S.op("dve", lambda e: e.tensor_copy(out=Tt[:, :, :], in_=pT_[:, :].rearrange("p (t h) -> p t h", t=32)),
                 writes=[tpT_, t_Tt])
            S.op("dve", lambda e: e.memset(Pfx[:, 0, :], 0.0), writes=[t_Pfx])
            for tt in range(1, NT):
                S.op("dve", lambda e, tt=tt: e.tensor_tensor(out=Pfx[:, tt, :], in0=Pfx[:, tt - 1, :],
                                                           in1=Tt[:, tt - 1, :], op=ALU.add),
                     reads=[t_Tt], writes=[t_Pfx])
            S.op("dve", lambda e: e.tensor_tensor(out=Ct[:, :, :], in0=pC_[:, :].rearrange("p (t h) -> p t h", t=32),
                                                  in1=Pfx[:, :, :], op=ALU.add),
                 reads=[t_Pfx], writes=[tpC_] + list(t_Ct))
            t_CS, t_r1, t_ctf = S.tok("CS"), S.tok("r1"), S.tok("ctf")
            for g4 in range(8):
                ps = PS[6]
                tps = t_PS[6]
                for ti in range(4):
                    tt = g4 * 4 + ti
                    S.op("pe", lambda e, tt=tt, ti=ti: e.matmul(ps[0:16, ti * 128:(ti + 1) * 128], lhsT=Ct[:, tt, :],
                                                            rhs=identf[:], start=True, stop=True),
                         reads=[t_Ct[tt], t_c1], writes=[tps])
                S.op("dve", lambda e, g4=g4: e.tensor_scalar(out=CTf[:, g4 * 512:(g4 + 1) * 512], in0=ps[0:16, :],
                                                          scalar1=-1.0, scalar2=None, op0=ALU.mult),
                     writes=[tps, t_ctf])
            tmpb = at("tmpb", [16, S_LEN], BF16, o_aot)
            t_tmpb = S.tok("tmpb")
            S.op("dve", lambda e: e.tensor_copy(out=CS[0:16, :], in_=CTf[:, :]), reads=[t_ctf], writes=[t_CS])
            S.op("dve", lambda e: e.tensor_tensor(out=r1[:, :], in0=CTf[:, :], in1=CS[0:16, :], op=ALU.subtract),
                 reads=[t_ctf, t_CS], writes=[t_r1])
            S.op("dve", lambda e: e.tensor_copy(out=tmpb[:, :], in_=r1[:, :]), reads=[t_r1], writes=[t_tmpb])
            S.op("dve", lambda e: e.tensor_copy(out=CS[32:48, :], in_=tmpb[:, :]), reads=[t_tmpb], writes=[t_CS])
            S.op("dve", lambda e: e.tensor_tensor(out=r1[:, :], in0=r1[:, :], in1=tmpb[:, :], op=ALU.subtract),
                 reads=[t_tmpb], writes=[t_r1])
            S.op("dve", lambda e: e.tensor_copy(out=CS[64:80, :], in_=r1[:, :]), reads=[t_r1], writes=[t_CS])
            S.barrier()
            checkpoint(7)
            if 'g' in DBG:
                S.op("pe", lambda e: e.matmul(PS[6][0:64, 0:16], lhsT=hT[:, 0, 0:64], rhs=hT[:, 0, 0:16],
                                              start=True, stop=True), reads=[t_hT[0]], writes=[t_PS[6]])
            if 'a' not in DBG:
                S.op("dve", lambda e: e.memset(KT[64:67, :], 1.0), writes=[t_KTaug])
            WBqk = WB[:, 0:1024].rearrange("p (c s n) -> p c s n", c=8, s=2)
            checkpoint(71)

            KT2 = at("KT2", [128, S_LEN], BF16, o_wa)
            KTb = [KT, KT2]
            t_KTb = [[S.tok("ktb0"), S.tok("ktb0aug")], [S.tok("ktb1"), S.tok("ktb1aug")]]
            S.op("dve", lambda e: e.memset(KT2[64:67, :], 1.0), writes=[t_KTb[1][1]])
            t_KTb[0][1] = t_KTaug
            WVb = at("WVb", [128, 1024], BF16, o_rc + 2048)
            t_WQb, t_WKb, t_WVb = S.tok("wqb"), S.tok("wkb"), S.tok("wvb")
            t_VAp = S.toks(2, "vap")
            S.op("dve", lambda e: e.memset(VA4[:, :, 0:2, 64:128], 1.0), writes=[t_VAp[0], t_VA])
            S.op("dve", lambda e: e.memset(VA4[:, :, 2:4, 64:128], 1.0), writes=[t_VAp[1], t_VA])
            rot = {"n": 0}

            def bank():
                rot["n"] += 1
                return 4 + rot["n"] % 2

            def load_wk(h):
                S.dma("pool", lambda e: e.dma_start(out=WBqk[:, :, 1, :],
                                                    in_=wview(o_w_in, K0 + h * 64, K0 + (h + 1) * 64)),
                      writes=[t_WKb])

            def load_wq(h):
                S.dma("pool", lambda e: e.dma_start(out=WBqk[:, :, 0, :],
                                                    in_=wview(o_w_in, Q0 + h * 64, Q0 + (h + 1) * 64)),
                      writes=[t_WQb])

            def load_wv(p):
                S.dma("pool", lambda e: e.dma_start(out=WVb[:, :].rearrange("p (c n) -> p c n", c=8),
                                                    in_=wview(o_w_in, V0 + p * 128, V0 + (p + 1) * 128)),
                      writes=[t_WVb])

            def k_block(h, b):
                pb = bank()
                proj_fm(PS[pb], t_PS[pb], lambda k: WB[:, k * 128 + 64:(k + 1) * 128], 8,
                        lambda k: hT[:, k, b * 512:(b + 1) * 512], t_hT[4 * b:4 * b + 4], [t_WKb], 64)
                S.op("dve", copy_op("dve", KTb[h % 2][0:64, b * 512:(b + 1) * 512], PS[pb][0:64, :]),
                     writes=[t_PS[pb], t_KTb[h % 2][0]])

            def q_block(h, b):
                pb = bank()
                both = h + 1 < 16
                M = 128 if both else 64
                proj_fm(PS[pb], t_PS[pb], lambda k: WB[:, k * 128:k * 128 + M], 8,
                        lambda k: hT[:, k, b * 512:(b + 1) * 512], t_hT[4 * b:4 * b + 4],
                        [t_WQb, t_WKb] if both else [t_WQb], M)
                S.op("dve", copy_op("dve", QT[0:64, b * 512:(b + 1) * 512], PS[pb][0:64, :], scale=0.125),
                     writes=[t_PS[pb], t_QT[b]])
                if both:
                    S.op("dve", copy_op("dve", KTb[(h + 1) % 2][0:64, b * 512:(b + 1) * 512], PS[pb][64:128, :]),
                         writes=[t_PS[pb], t_KTb[(h + 1) % 2][0]])

            def v_tiles(p, t4):
                pb = bank()
                ps = PS[pb]
                s0 = 2 * (p % 2)
                for ti in range(4):
                    tt = t4 * 4 + ti
                    for k in range(8):
                        S.op("pe", lambda e, k=k, tt=tt, ti=ti: e.matmul(
                            ps[:, ti * 128:(ti + 1) * 128], lhsT=hT[:, k, tt * 128:(tt + 1) * 128],
                            rhs=WVb[:, k * 128:(k + 1) * 128], start=(k == 0), stop=(k == 7)),
                            reads=[t_hT[tt], t_WVb], writes=[t_PS[pb]])
                for ti in range(4):
                    tt = t4 * 4 + ti
                    S.op("dve", copy_op("dve", VA4[:, tt, s0:s0 + 2, 0:64],
                                        ps[:, ti * 128:(ti + 1) * 128].rearrange("p (h d) -> p h d", h=2)),
                         writes=[t_PS[pb], t_VAp[p % 2]])

            load_wv(0)
            for t4 in range(8):
                v_tiles(0, t4)
            load_wk(0)
            for b in range(NB):
                k_block(0, b)
            for h in range(16):
                hh = h % 4
                load_wq(h)
                for r_ in range(3):
                    S.dma("sp", lambda e, h=h, r_=r_: e.dma_start(out=QT[64 + r_:65 + r_, :],
                                                                 in_=CS[32 * r_ + h:32 * r_ + h + 1, :]),
                          reads=[t_CS], writes=[t_QTaug])
                bgl = []
                if h + 1 < 16:
                    load_wk(h + 1)
                if h % 2 == 1 and h + 1 < 16:
                    load_wv((h + 1) // 2)
                    bgl += [(lambda h=h, t4=t4: v_tiles((h + 1) // 2, t4)) for t4 in range(8)]
                c = h // 2
                po = (h % 2) * 64

                def out_fn(g, c=c, po=po):
                    return AOT[po:po + 64, c, g * 512:(g + 1) * 512], [t_AOT[c][g]]
                attention(67, dense_groups(), lambda j, h=h: Ct[:, j, h:h + 1], 1.0,
                          lambda j, hh=hh: VA4[:, j, hh, :], out_fn, rd_extra=t_Ct,
                          KTt=(KTb[h % 2], t_KTb[h % 2]), pre_group=(lambda g, h=h: q_block(h, g)),
                          bg=bgl, va_tok=t_VAp[(h // 2) % 2], one_rc=True)

            checkpoint(8)
            S.barrier()
            checkpoint(9)
            out_phase(o_w_out, x1_d, (t_x1 if mode == "full" else None), last_layer=True, gate=(o_w_in, G0))
            S.wait_all("sp", t_out)
        else:
            S.wait_all("sp", t_x1)


    def checkpoint(k):
        if stop == k:
            raise _Stop()

    try:
        _layers()
    except _Stop:
        pass
    S.emit()
    return nc


_CACHE = {}


def _get(mode):
    if mode not in _CACHE:
        _CACHE[mode] = build_program(mode)
    return _CACHE[mode]


L0_KEYS = ["e_g_in", "e_w_in", "e_g_q_a", "e_w_q_up", "e_g_kv_a", "e_w_kv_up", "e_sinks", "e_w_out"]
L1_KEYS = ["o_g_in", "o_w_in", "o_b_f", "o_w_out"]


def _maps(inputs, n, mode, x1=None):
    consts = _constants()
    maps = []
    for b in range(n):
        m = dict(consts)
        if mode in ("full", "l0"):
            m["x"] = np.ascontiguousarray(inputs["x"][b])
            m["positions"] = np.ascontiguousarray(inputs["positions"][b]).astype(np.int32)
            for k in L0_KEYS:
                m[k] = np.ascontiguousarray(inputs[k][0])
        if mode in ("full", "l1"):
            for k in L1_KEYS:
                m[k] = np.ascontiguousarray(inputs[k][0])
            m["g_final"] = np.ascontiguousarray(inputs["g_final"])
        if mode == "l1":
            m["x1"] = np.ascontiguousarray(x1[b])
        maps.append(m)
    return maps


def kernel(**inputs):
    n = inputs["x"].shape[0]
    inputs = {k: np.asarray(v) for k, v in inputs.items()}
    nc = _get("full")
    res = run_bass_kernel_spmd(nc, _maps(inputs, n, "full"), core_ids=list(range(n)))
    return np.stack([np.asarray(r["out"]) for r in res.results], axis=0).astype(np.float32)
```

```python
import math
import os
DBG = os.environ.get('KDBG', '')
import numpy as np
import concourse.bass as bass
import concourse.mybir as mybir
from concourse.bass_utils import run_bass_kernel_spmd

F32 = mybir.dt.float32
BF16 = mybir.dt.bfloat16
I32 = mybir.dt.int32
AF = mybir.ActivationFunctionType
ALU = mybir.AluOpType

S_LEN = 4096
D = 1024
NT = 32
NB = 8
EPS = 1e-6
SEM_ROT = 12000


class Tok:
    __slots__ = ("name", "w", "r", "dsem")

    def __init__(self, name=""):
        self.name = name
        self.w = None
        self.r = {}
        self.dsem = None


class _Rec:
    def __init__(self):
        self.call = None

    def __getattr__(self, name):
        def f(*a, **k):
            assert self.call is None
            self.call = (name, a, k)
            return self
        return f


def _freeze(fn):
    r = _Rec()
    fn(r)
    assert r.call is not None
    return r.call


class Sched:
    ENGS = ("pe", "act", "dve", "pool", "sp")

    def __init__(self, nc):
        self.nc = nc
        self.sems = []
        self.semeng = {}
        self.ops = {e: [] for e in self.ENGS}
        self.esem = {}
        self.ecnt = {}
        self.seen = {e: {} for e in self.ENGS}
        self.semval = {}
        self.unsig = {e: False for e in self.ENGS}
        self.noself = {"pe"}
        for e in ("pe", "act", "dve", "pool"):
            self._new_esem(e)

    def _alloc(self, name, eng=None):
        h = self.nc.alloc_semaphore(name=name)
        self.sems.append(h)
        sid = len(self.sems) - 1
        self.semval[sid] = 0
        self.semeng[sid] = eng
        return sid

    def _new_esem(self, e):
        self.esem[e] = self._alloc(f"s_{e}_{len(self.sems)}", e)
        self.ecnt[e] = 0

    def tok(self, name=""):
        return Tok(name)

    def toks(self, n, name=""):
        return [Tok(f"{name}{i}") for i in range(n)]

    def _collect(self, e, reads, writes):
        need = {}

        def add(s, v):
            if need.get(s, 0) < v:
                need[s] = v
        for t in reads:
            if t.w is not None:
                add(*t.w)
        for t in writes:
            if t.w is not None:
                add(*t.w)
            for s, v in t.r.items():
                add(s, v)
        waits = []
        seen = self.seen[e]
        for s, v in need.items():
            if e in self.noself and self.semeng[s] == e:
                continue
            if seen.get(s, 0) >= v:
                continue
            seen[s] = v
            waits.append((s, v))
        return waits

    def _mark(self, s, val, reads, writes):
        for t in reads:
            if t.r.get(s, 0) < val:
                t.r[s] = val
        for t in writes:
            t.w = (s, val)
            t.r = {}

    def op(self, e, fn, reads=(), writes=(), sig=True):
        waits = self._collect(e, reads, writes)
        if sig and self.ecnt[e] >= SEM_ROT and not self.unsig[e]:
            self._new_esem(e)
        self.unsig[e] = not sig
        s = self.esem[e]
        val = self.ecnt[e] + 1
        if sig:
            self.ecnt[e] = val
            self.semval[s] = val
        self.ops[e].append((waits, _freeze(fn), (s, 1) if sig else None))
        self._mark(s, val, reads, writes)

    def barrier(self):
        cur = [(s, v) for s, v in self.semval.items() if v > 0]
        for e in self.ENGS:
            waits = []
            for s, v in cur:
                if (self.semeng[s] == e and e in self.noself) or self.seen[e].get(s, 0) >= v:
                    continue
                self.seen[e][s] = v
                waits.append((s, v))
            self.ops[e].append((waits, None, None))

    def dma(self, q, fn, reads=(), writes=(), owner=None):
        waits = self._collect(q, reads, writes)
        if owner is None:
            owner = writes[0] if writes else reads[0]
        if owner.dsem is None or self.semval[owner.dsem] >= SEM_ROT * 2:
            owner.dsem = self._alloc(f"d_{len(self.sems)}")
        s = owner.dsem
        self.semval[s] += 16
        val = self.semval[s]
        self.ops[q].append((waits, _freeze(fn), (s, 16)))
        self._mark(s, val, reads, writes)

    def wait_all(self, e, toks):
        waits = self._collect(e, [], toks)
        self.ops[e].append((waits, None, None))

    def emit(self):
        nc = self.nc
        sems = self.sems
        ops = self.ops

        def replay(eng, lst):
            for waits, fn, inc in lst:
                for s, v in waits:
                    eng.wait_ge(sems[s], v)
                if fn is None:
                    continue
                name, a, k = fn
                ins = getattr(eng, name)(*a, **k)
                if inc is not None:
                    ins.then_inc(sems[inc[0]], inc[1])

        with nc.Block() as block:
            @block.tensor
            def _(eng):
                replay(eng, ops["pe"])

            @block.scalar
            def _(eng):
                replay(eng, ops["act"])

            @block.vector
            def _(eng):
                replay(eng, ops["dve"])

            @block.gpsimd
            def _(eng):
                replay(eng, ops["pool"])

            @block.sync
            def _(eng):
                replay(eng, ops["sp"])


def _constants():
    c = {}
    c["c_ident"] = np.eye(128, dtype=np.float32)
    k = np.arange(128)[:, None]
    q = np.arange(128)[None, :]
    c["c_tri"] = (k <= q).astype(np.float32)
    c["c_ones"] = np.ones((128, 128), np.float32)
    qq = np.arange(512)[None, None, :]
    kk = np.arange(128)[:, None, None]
    rr = np.arange(4)[None, :, None]
    c["c_mask"] = ((rr * 128 + kk) <= qq).astype(np.float32)
    slopes = 2.0 ** (-8.0 * (np.arange(8, dtype=np.float64) + 1.0) / 8)
    t = np.zeros((128, 8, 2, 128), np.float64)
    kq = np.arange(128)[:, None]
    qv = np.arange(128)[None, :]
    for h in range(8):
        dist0 = 128 + qv - kq
        t[:, h, 1, :] = np.where(dist0 < 128, np.exp(-slopes[h] * dist0), 0.0)
        dist1 = qv - kq
        t[:, h, 0, :] = np.where(dist1 >= 0, np.exp(-slopes[h] * np.maximum(dist1, 0)), 0.0)
    c["c_swt"] = t.astype(np.float32)
    invf = 1.0 / (10000.0 ** (np.arange(0, 32, 2, dtype=np.float32) / 32))
    v = np.zeros((128, 2), np.float32)
    for p in range(64, 128):
        v[p, 0] = invf[(p - 64) % 16]
        v[p, 1] = (math.pi / 2) if p < 96 else 0.0
    c["c_rope"] = v
    return c


CONST_SHAPES = {"c_ident": [128, 128], "c_tri": [128, 128], "c_ones": [128, 128],
                "c_mask": [128, 4, 512], "c_swt": [128, 8, 2, 128], "c_rope": [128, 2]}


class _Stop(Exception):
    pass


def build_program(mode="full", stop=0):
    nc = bass.Bass("TRN2", target_bir_lowering=False)
    S = Sched(nc)
    do0 = mode in ("full", "l0")
    do1 = mode in ("full", "l1")

    def din(name, shape, dt=F32):
        return nc.dram_tensor(name, shape, dt, kind="ExternalInput").ap()

    cst = {k: din(k, v) for k, v in CONST_SHAPES.items()}
    if do0:
        x_d = din("x", [S_LEN, D])
        pos_d = din("positions", [S_LEN], I32)
        e_g_in = din("e_g_in", [D])
        e_w_in = din("e_w_in", [D, 2208])
        e_g_q_a = din("e_g_q_a", [256])
        e_w_q_up = din("e_w_q_up", [256, 768])
        e_g_kv_a = din("e_g_kv_a", [128])
        e_w_kv_up = din("e_w_kv_up", [128, 1024])
        e_sinks = din("e_sinks", [8])
        e_w_out = din("e_w_out", [D, D])
    if do1:
        o_g_in = din("o_g_in", [D])
        o_w_in = din("o_w_in", [D, 4112])
        o_b_f = din("o_b_f", [16])
        o_w_out = din("o_w_out", [D, D])
        g_final = din("g_final", [D])
        out_d = nc.dram_tensor("out", [S_LEN, D], F32, kind="ExternalOutput").ap()
    if mode == "full":
        x1_d = nc.dram_tensor("x1s", [S_LEN, D], F32).ap()
    elif mode == "l0":
        x1_d = nc.dram_tensor("x1", [S_LEN, D], F32, kind="ExternalOutput").ap()
    else:
        x1_d = din("x1", [S_LEN, D])
    t_x1 = S.toks(NT, "x1d")
    t_x1own = S.tok("x1own")
    t_outown = S.toks(2, "outown")
    t_out = S.toks(NT, "outd")

    def region(nbytes):
        st, _ = nc.bump_sbuf(nbytes)
        return st

    def at(name, shape, dt, off):
        return nc.alloc_sbuf_tensor_at(name, shape, dt, offset=off)

    def sb(name, shape, dt):
        return nc.alloc_sbuf_tensor(name, shape, dt)

    hT = sb("hT", [128, 8, S_LEN], BF16)
    t_hT = S.toks(NT, "hT")
    o_aot = region(65536)
    AOT = at("AOT", [128, 8, S_LEN], BF16, o_aot)
    t_AOT = [[S.tok(f"aot{c}_{b}") for b in range(NB)] for c in range(8)]
    o_big = region(32768)
    BIG = at("BIG", [128, 32 * 4 * 128], BF16, o_big)
    t_VA = S.tok("VA")
    t_LAT = S.toks(NB, "LAT")
    o_trig = region(8192)
    TRIG = at("TRIG", [128, S_LEN], BF16, o_trig)
    t_TRIG = S.tok("TRIG")
    o_pool = region(19456)
    QT = at("QT", [128, S_LEN], BF16, o_pool)
    t_QT = S.toks(NB, "QT")
    t_QTaug = S.tok("QTaug")
    KT = at("KT", [128, S_LEN], BF16, o_pool + 8192)
    t_KT = S.tok("KT")
    t_KTaug = S.tok("KTaug")
    NPT = 5
    PT = [at(f"pt{i}", [128, 512], BF16, o_pool + 16384 + 1024 * i) for i in range(3)]
    PT += [sb(f"ptx{i}", [128, 512], BF16) for i in range(NPT - 3)]
    t_PT = S.toks(NPT, "pt")
    XT = [at(f"xt{i}", [128, D], F32, o_pool + 4096 * i) for i in range(3)]
    t_XT = S.toks(3, "xt")
    XB = [at(f"xb{i}", [128, D], BF16, o_pool + 12288 + 2048 * i) for i in range(2)]
    t_XB = S.toks(2, "xb")
    ET = [at(f"et{i}", [128, 512], F32, o_pool + 2048 * i) for i in range(3)]
    t_ET = S.toks(3, "et")
    o_pf = region(1024)
    PF = [at(f"pf{i}", [128, 128], F32, o_pf + 512 * i) for i in range(2)]
    t_PF = S.toks(2, "pf")
    o_rc = region(4096)
    RC = [at(f"rc{i}", [128, 512], F32, o_rc + 2048 * i) for i in range(2)]
    t_RC = S.toks(2, "rc")
    GB = at("GB", [128, D], F32, o_rc)
    t_GB = S.tok("GB")
    o_wa = region(8192)
    WA = at("WA", [128, 3328], BF16, o_wa)
    t_WA = S.tok("WA")
    WB = sb("WB", [128, 1024], BF16)
    t_WB = S.tok("WB")
    MASK = sb("MASK", [128, 128], BF16)
    t_MASK = S.tok("MASK")
    identb = sb("identb", [128, 128], BF16)
    onesf = sb("onesf", [128, 128], F32)
    t_cst = S.tok("cst")
    stat = sb("stat", [128, 8], F32)
    t_statS = S.toks(2, "stat")
    epsb = sb("epsb", [128, 2], F32)
    t_eps = S.tok("eps")

    PS = [nc.alloc_psum_tensor(f"ps{i}", [128, 512], F32) for i in range(7)]
    t_PS = S.toks(7, "ps")
    PST = nc.alloc_psum_tensor("pst", [128, 8, 128], BF16)
    t_PST = S.tok("pst")

    S.dma("sp", lambda e: e.dma_start(out=onesf[:], in_=cst["c_ones"][:, :]), writes=[t_cst])
    t_cstb = S.tok("cstb")
    S.dma("pool", lambda e: e.dma_start(out=identb[:], in_=cst["c_ident"][:, :]), writes=[t_cstb])
    S.dma("pool", lambda e: e.dma_start(out=MASK[:], in_=cst["c_tri"][:, :]), writes=[t_MASK])
    S.op("dve", lambda e: e.memset(epsb[:, 0:1], EPS), writes=[t_eps])
    S.op("dve", lambda e: e.memset(epsb[:, 1:2], 1.0), writes=[t_eps])

    cnt = {"ev": 0}

    def evac_engine():
        cnt["ev"] += 1
        return "act" if cnt["ev"] % 2 else "dve"

    def copy_op(eng, out, in_, scale=None):
        if eng == "act":
            if scale is None:
                return lambda e: e.copy(out=out, in_=in_)
            return lambda e: e.mul(out=out, in_=in_, mul=scale)
        if scale is None:
            return lambda e: e.tensor_copy(out=out, in_=in_)
        return lambda e: e.tensor_scalar(out=out, in0=in_, scalar1=scale, scalar2=None, op0=ALU.mult)

    def load_gain(g_d):
        S.dma("sp", lambda e: e.dma_start(out=GB[:], in_=bass.AP(g_d.tensor, 0, [[0, 128], [1, D]])),
              writes=[t_GB])

    def rstd_from_ms(col, tst):
        S.op("act", lambda e: e.activation(out=stat[:, col + 1:col + 2], in_=stat[:, col:col + 1], func=AF.Ln,
                                           bias=epsb[:, 0:1], scale=1.0), reads=[t_eps], writes=[tst])
        S.op("act", lambda e: e.activation(out=stat[:, col + 1:col + 2], in_=stat[:, col + 1:col + 2],
                                           func=AF.Exp, scale=-0.5), writes=[tst])

    def norm_tile(xt, t_xt, out_ap, t_outs, junk_ap, t_junk, slot=0):
        c0 = 4 * slot
        tst = t_statS[slot]
        S.op("dve", lambda e: e.memset(stat[:, c0:c0 + 1], 0.0), writes=[tst])
        S.op("act", lambda e: e.activation(out=junk_ap, in_=xt[:], func=AF.Square, scale=1.0 / 32,
                                           accum_out=stat[:, c0:c0 + 1]), reads=[t_xt], writes=[t_junk, tst])
        rstd_from_ms(c0, tst)
        S.op("dve", lambda e: e.scalar_tensor_tensor(out=out_ap, in0=xt[:], scalar=stat[:, c0 + 1:c0 + 2], in1=GB[:],
                                                     op0=ALU.mult, op1=ALU.mult),
             reads=[t_xt, tst, t_GB], writes=list(t_outs))

    def to_hT_norm(xt, t_xt, tt):
        i = tt % 2
        norm_tile(xt, t_xt, XB[i][:], [t_XB[i]], XB[i][:], t_XB[i], slot=i)

    def to_hT_tr(tt):
        i = tt % 2
        for c in range(8):
            S.op("pe", lambda e, c=c: e.transpose(out=PST[:, c, :], in_=XB[i][:, c * 128:(c + 1) * 128],
                                                   identity=identb[:]),
                 reads=[t_XB[i], t_cstb], writes=[t_PST])
        S.op("dve", copy_op("dve", hT[:, :, tt * 128:(tt + 1) * 128], PST[:, :, :]), writes=[t_PST, t_hT[tt]])

    def phase_A(src_d):
        for tt in range(NT + 1):
            if tt < NT:
                i = tt % 3
                S.dma("sp", lambda e, tt=tt, i=i: e.dma_start(out=XT[i][:], in_=src_d[tt * 128:(tt + 1) * 128, :]),
                      writes=[t_XT[i]])
                to_hT_norm(XT[i], t_XT[i], tt)
            if tt >= 1:
                to_hT_tr(tt - 1)

    def wview(w_d, c0, c1):
        return w_d.rearrange("(c p) n -> p c n", p=128)[:, :, c0:c1]

    def proj_fm(ps, t_ps, w_ap_fn, nk, src_fn, src_toks, w_toks, M):
        for k in range(nk):
            S.op("pe", lambda e, k=k: e.matmul(ps[0:M, :], lhsT=w_ap_fn(k), rhs=src_fn(k),
                                               start=(k == 0), stop=(k == nk - 1)),
                 reads=list(src_toks) + list(w_toks), writes=[t_ps])

    def attention(KR, groups, bias_fn, scale, va_fn, out_fn, den_add=None, fp32_tables=False, rd_extra=(),
                  KTt=None, pre_group=None, bg=(), va_tok=None, one_rc=False, pf_bufs=None, skip_gc=False, r0=0):
        steps = []
        for gi, (q0, QW, kbs, gidx) in enumerate(groups):
            for n, kb in enumerate(kbs):
                if len(kb) == 3:
                    j, c0, tbl = kb
                    c1, t0, t1 = QW, c0, c0 + 128
                else:
                    j, c0, c1, tbl = kb
                    t0, t1 = c0, c1
                steps.append((gi, q0, QW, j, c0, c1, tbl, t0, t1, n == 0, n == len(kbs) - 1, gidx))

        SB_ = (0, 1, 6)
        LA = 2
        KTx, t_KTx = (KT, [t_KT, t_KTaug]) if KTt is None else KTt
        t_VAx = t_VA if va_tok is None else va_tok
        PFx = PF if pf_bufs is None else pf_bufs
        bg = list(bg)
        nsteps = len(steps)
        bg_stride = max(1, nsteps // (len(bg) + 1)) if bg else 0

        def emit_qk(i):
            gi, q0, QW, j, c0, c1, tbl, t0, t1, first, last, gidx = steps[i]
            sp = PS[SB_[i % 3]]
            S.op("pe", lambda e: e.matmul(sp[:, c0:c1], lhsT=KTx[r0:r0 + KR, j * 128:(j + 1) * 128],
                                          rhs=QT[r0:r0 + KR, q0 + c0:q0 + c1], start=True, stop=True),
                 reads=list(t_KTx) + [t_QT[q0 // 512], t_QTaug], writes=[t_PS[SB_[i % 3]]])

        first_idx = {}
        for i_, st_ in enumerate(steps):
            first_idx.setdefault(st_[0], i_)
        if pre_group is not None:
            pre_group(groups[0][3])
        for i0 in range(min(LA, len(steps))):
            emit_qk(i0)
        for i, (gi, q0, QW, j, c0, c1, tbl, t0, t1, first, last, gidx) in enumerate(steps):
            if i + LA < len(steps):
                emit_qk(i + LA)
            if pre_group is not None and i == first_idx[gi] + 1 and gi + 1 < len(groups):
                pre_group(groups[gi + 1][3])
            if bg and i > 0 and i % bg_stride == 0:
                bg.pop(0)()
            sp = PS[SB_[i % 3]]
            tsp = t_PS[SB_[i % 3]]
            pt = PT[i % NPT]
            tpt = t_PT[i % NPT]
            kw = {"scale": scale}
            b = bias_fn(j) if bias_fn is not None else None
            if b is not None:
                kw["bias"] = b
            if tbl is not None and fp32_tables:
                pf = PFx[i % 2]
                w = c1 - c0
                S.op("act", lambda e: e.activation(out=pf[:, 0:w], in_=sp[:, c0:c1], func=AF.Exp, **kw),
                     reads=list(rd_extra), writes=[tsp, t_PF[i % 2]])
                S.op("dve", lambda e: e.tensor_tensor(out=pt[:, c0:c1], in0=pf[:, 0:w], in1=tbl, op=ALU.mult),
                     reads=[t_PF[i % 2]] + list(rd_extra), writes=[tpt])
            else:
                S.op("act", lambda e: e.activation(out=pt[:, c0:c1], in_=sp[:, c0:c1], func=AF.Exp, **kw),
                     reads=list(rd_extra), writes=[tsp, tpt])
                if tbl is not None:
                    S.op("dve", lambda e: e.tensor_tensor(out=pt[:, t0:t1], in0=pt[:, t0:t1], in1=tbl, op=ALU.mult),
                         reads=[t_MASK], writes=[tpt])
            op_ = PS[2 + gi % 2]
            top = t_PS[2 + gi % 2]
            mkw = {"skip_group_check": True} if skip_gc else {}
            S.op("pe", lambda e: e.matmul(op_[:, c0:c1], lhsT=va_fn(j), rhs=pt[:, c0:c1], start=first, stop=last,
                                          **mkw),
                 reads=[tpt, t_VAx], writes=[top])
            if last:
                rc = RC[0 if one_rc else gi % 2]
                trc = t_RC[0 if one_rc else gi % 2]
                if den_add is not None:
                    S.op("dve", lambda e: e.tensor_scalar(
                        out=rc[64:128, 0:QW], in0=op_[64:128, 0:QW], scalar1=den_add, scalar2=None, op0=ALU.add),
                        reads=list(rd_extra), writes=[top, trc])
                    S.op("act", lambda e: e.activation(out=rc[64:128, 0:QW], in_=rc[64:128, 0:QW], func=AF.Ln),
                         writes=[trc])
                    S.op("act", lambda e: e.activation(out=rc[64:128, 0:QW], in_=rc[64:128, 0:QW], func=AF.Exp,
                                                       scale=-1.0), writes=[trc])
                else:
                    S.op("dve", lambda e: e.reciprocal(out=rc[64:128, 0:QW], in_=op_[64:128, 0:QW]),
                         writes=[top, trc])
                o_ap, o_toks = out_fn(gidx)
                S.op("dve", lambda e: e.tensor_tensor(out=o_ap, in0=op_[0:64, 0:QW], in1=rc[64:128, 0:QW],
                                                      op=ALU.mult),
                     reads=[trc], writes=[top] + list(o_toks))
        while bg:
            bg.pop(0)()

    def _unused():
        pass

    def dense_groups():
        gs = []
        for g in range(NB):
            kbs = [(j, 0, None) for j in range(4 * g)] + [(4 * g + r, r * 128, MASK[:, :]) for r in range(4)]
            gs.append((g * 512, 512, kbs, g))
        return gs

    def gate_phase(w_in_d, goff):
        for c in range(8):
            S.dma("pool", lambda e, c=c: e.dma_start(
                out=WB[:, 0:1024].rearrange("p (c n) -> p c n", c=8),
                in_=wview(w_in_d, goff + c * 128, goff + (c + 1) * 128)), writes=[t_WB])
            for b in range(NB):
                ps = PS[4 + b % 2]
                tps = t_PS[4 + b % 2]
                proj_fm(ps, tps, lambda k: WB[:, k * 128:(k + 1) * 128], 8,
                        lambda k, b=b: hT[:, k, b * 512:(b + 1) * 512], t_hT[4 * b:4 * b + 4], [t_WB], 128)
                gt = PT[b % NPT]
                S.op("act", lambda e, gt=gt, ps=ps: e.activation(out=gt[:], in_=ps[:], func=AF.Silu),
                     writes=[tps, t_PT[b % NPT]])
                S.op("dve", lambda e, gt=gt, c=c, b=b: e.tensor_tensor(
                    out=AOT[:, c, b * 512:(b + 1) * 512], in0=AOT[:, c, b * 512:(b + 1) * 512], in1=gt[:],
                    op=ALU.mult), reads=[t_PT[b % NPT]], writes=[t_AOT[c][b]])

    def out_phase(w_out_d, res_d, t_res, last_layer, next_gain_d=None, gate=None):
        S.barrier()
        WO = at("WO_%d" % int(last_layer), [128, 8 * 1024], BF16, o_big)
        t_WO = S.tok("WO")
        WG = at("WG_%d" % int(last_layer), [128, 8 * 1024], BF16, o_big + 16384)
        t_WG = S.tok("WG")
        gw_d, goff = gate
        for c in range(8):
            S.dma("pool", lambda e, c=c: e.dma_start(
                out=WG[:, c * 1024:(c + 1) * 1024].rearrange("p (k n) -> p k n", k=8),
                in_=wview(gw_d, goff + c * 128, goff + (c + 1) * 128)), writes=[t_WG])
        S.dma("pool", lambda e: e.dma_start(out=WO[:].rearrange("p (c n) -> p c n", c=8),
                                            in_=wview(w_out_d, 0, D)), writes=[t_WO])
        gcnt = {"n": 0}

        def gate_chunk(c, b):
            gcnt["n"] += 1
            pb = 4 + gcnt["n"] % 2
            proj_fm(PS[pb], t_PS[pb], lambda k: WG[:, c * 1024 + k * 128:c * 1024 + (k + 1) * 128], 8,
                    lambda k: hT[:, k, b * 512:(b + 1) * 512], t_hT[4 * b:4 * b + 4], [t_WG], 128)
            gi_ = gcnt["n"] % NPT
            gt = PT[gi_]
            S.op("act", lambda e: e.activation(out=gt[:], in_=PS[pb][:], func=AF.Silu),
                 writes=[t_PS[pb], t_PT[gi_]])
            S.op("dve", lambda e: e.tensor_tensor(
                out=AOT[:, c, b * 512:(b + 1) * 512], in0=AOT[:, c, b * 512:(b + 1) * 512], in1=gt[:],
                op=ALU.mult), reads=[t_PT[gi_]], writes=[t_AOT[c][b]])

        for c in range(8):
            gate_chunk(c, 0)
        if last_layer:
            load_gain(g_final)
        elif next_gain_d is not None:
            load_gain(next_gain_d)
        t_x1o = S.toks(3, "x1own")
        t_outo = S.toks(3, "outown")
        def stage1(tt):
            i = tt % 3
            b = tt // 4
            pp = 2 * (tt % 2)
            S.dma("sp", lambda e: e.dma_start(out=XT[i][:], in_=res_d[tt * 128:(tt + 1) * 128, :]),
                  reads=[t_res[tt]] if t_res is not None else [], writes=[t_XT[i]])
            for half in range(2):
                ps = PS[pp + half]
                tps = t_PS[pp + half]
                for c in range(8):
                    S.op("pe", lambda e: e.matmul(
                        ps[:, :], lhsT=AOT[:, c, tt * 128:(tt + 1) * 128],
                        rhs=WO[:, c * 1024 + half * 512: c * 1024 + (half + 1) * 512],
                        start=(c == 0), stop=(c == 7)),
                        reads=[t_AOT[c][b], t_WO], writes=[tps])

        def stage1b(tt):
            i = tt % 3
            pp = 2 * (tt % 2)
            for half in range(2):
                ps = PS[pp + half]
                tps = t_PS[pp + half]
                S.op("dve", lambda e: e.tensor_tensor(
                    out=XT[i][:, half * 512:(half + 1) * 512], in0=ps[:, :], in1=XT[i][:, half * 512:(half + 1) * 512],
                    op=ALU.add), writes=[tps, t_XT[i]])

        def stage2a(tt):
            i = tt % 3
            if last_layer:
                norm_tile(XT[i], t_XT[i], XT[i][:], [t_XT[i]], XB[tt % 2][:], t_XB[tt % 2], slot=tt % 2)
                S.dma("sp", lambda e: e.dma_start(out=out_d[tt * 128:(tt + 1) * 128, :], in_=XT[i][:]),
                      reads=[t_XT[i]], writes=[t_out[tt]], owner=t_outo[i])
            else:
                S.dma("sp", lambda e: e.dma_start(out=x1_d[tt * 128:(tt + 1) * 128, :], in_=XT[i][:]),
                      reads=[t_XT[i]], writes=[t_x1[tt]], owner=t_x1o[i])
                if mode == "full":
                    to_hT_norm(XT[i], t_XT[i], tt)

        for tt in range(NT + 2):
            if tt < NT:
                stage1(tt)
            if 1 <= tt <= NT:
                stage2a(tt - 1)
            if tt < NT:
                stage1b(tt)
            if tt < NT and tt // 4 + 1 < NB:
                for c in (2 * (tt % 4), 2 * (tt % 4) + 1):
                    gate_chunk(c, tt // 4 + 1)
            if tt >= 2 and (not last_layer) and mode == "full":
                to_hT_tr(tt - 2)
        S.barrier()

    def _layers():
        if do0:
            VA0 = at("VA0", [128, 32, 128], BF16, o_big)
            LATt = at("LATt", [128, 3, S_LEN], BF16, o_big + 8192)
            SWT = at("SWT", [128, 2048], F32, o_big + 8192)
            PFw = [at(f"pfw{i}", [128, 256], F32, o_big + 16384 + 1024 * i) for i in range(2)]
            posi = at("posi", [128, S_LEN], I32, o_aot)
            kf = at("kf", [128, S_LEN], F32, o_aot + 16384)
            ANG = at("ANG", [128, S_LEN], F32, o_aot + 32768)
            t_pos, t_kf, t_ang = S.tok("posi"), S.tok("kf"), S.tok("ang")
            load_gain(e_g_in)
            phase_A(x_d)

            gq = sb("gq", [128, 2], F32)
            gkv = sb("gkv", [128, 1], F32)
            esink = sb("esink", [128, 8], F32)
            ropec = sb("ropec", [128, 2], F32)
            t_sm = S.tok("small0")
            for c2 in range(2):
                S.dma("sp", lambda e, c2=c2: e.dma_start(
                    out=gq[:, c2:c2 + 1], in_=e_g_q_a[c2 * 128:(c2 + 1) * 128].rearrange("(p o) -> p o", o=1)),
                    writes=[t_sm])
            S.dma("sp", lambda e: e.dma_start(out=gkv[:], in_=e_g_kv_a.rearrange("(p o) -> p o", o=1)), writes=[t_sm])
            S.dma("sp", lambda e: e.dma_start(out=esink[:], in_=bass.AP(e_sinks.tensor, 0, [[0, 128], [1, 8]])),
                  writes=[t_sm])
            S.dma("sp", lambda e: e.dma_start(out=ropec[:], in_=cst["c_rope"][:, :]), writes=[t_sm])
            S.op("act", lambda e: e.activation(out=esink[:], in_=esink[:], func=AF.Exp), writes=[t_sm])

            S.dma("sp", lambda e: e.dma_start(out=posi[64:128, :], in_=bass.AP(pos_d.tensor, 0, [[0, 64], [1, S_LEN]])),
                  writes=[t_pos])
            P6 = slice(64, 128)
            S.op("dve", lambda e: e.tensor_copy(out=ANG[P6, :], in_=posi[P6, :]), reads=[t_pos], writes=[t_ang])
            S.op("dve", lambda e: e.tensor_scalar(out=ANG[P6, :], in0=ANG[P6, :], scalar1=ropec[P6, 0:1],
                                                  scalar2=ropec[P6, 1:2], op0=ALU.mult, op1=ALU.add),
                 reads=[t_sm], writes=[t_ang])
            S.op("dve", lambda e: e.tensor_scalar(out=kf[P6, :], in0=ANG[P6, :], scalar1=1.0 / (2 * math.pi),
                                                  scalar2=0.5, op0=ALU.mult, op1=ALU.add),
                 reads=[t_ang], writes=[t_kf])
            S.op("dve", lambda e: e.tensor_copy(out=posi[P6, :], in_=kf[P6, :]), reads=[t_kf], writes=[t_pos])
            S.op("dve", lambda e: e.tensor_copy(out=kf[P6, :], in_=posi[P6, :]), reads=[t_pos], writes=[t_kf])
            C1 = 6.28125
            C2 = 2 * math.pi - C1
            S.op("dve", lambda e: e.scalar_tensor_tensor(out=ANG[P6, :], in0=kf[P6, :], scalar=-C1, in1=ANG[P6, :],
                                                         op0=ALU.mult, op1=ALU.add), reads=[t_kf], writes=[t_ang])
            S.op("dve", lambda e: e.scalar_tensor_tensor(out=ANG[P6, :], in0=kf[P6, :], scalar=-C2, in1=ANG[P6, :],
                                                         op0=ALU.mult, op1=ALU.add), reads=[t_kf], writes=[t_ang])
            S.op("dve", lambda e: e.tensor_single_scalar(out=kf[P6, :], in_=ANG[P6, :], scalar=-math.pi, op=ALU.is_lt),
                 reads=[t_ang], writes=[t_kf])
            S.op("dve", lambda e: e.scalar_tensor_tensor(out=ANG[P6, :], in0=kf[P6, :], scalar=2 * math.pi,
                                                         in1=ANG[P6, :], op0=ALU.mult, op1=ALU.add),
                 reads=[t_kf], writes=[t_ang])
            S.op("dve", lambda e: e.tensor_scalar(out=ANG[P6, :], in0=ANG[P6, :], scalar1=-3.1415925, scalar2=3.1415925,
                                                  op0=ALU.max, op1=ALU.min), writes=[t_ang])
            S.op("act", lambda e: e.activation(out=TRIG[P6, :], in_=ANG[P6, :], func=AF.Sin), reads=[t_ang],
                 writes=[t_TRIG])
            S.barrier()
            checkpoint(1)

            S.dma("sp", lambda e: e.dma_start(out=SWT[:, :], in_=cst["c_swt"].rearrange("p h r q -> p (h r q)")),
                  writes=[t_kf])
            S.op("dve", lambda e: e.memset(VA0[:, :, 64:128], 1.0), writes=[t_VA])
            SWA_Q0, SWA_K0, SWA_V0 = 416, 928, 1056
            WBkv = WB[:, 0:1024].rearrange("p (c s n) -> p c s n", c=8, s=2)
            for h in range(8):
                kv = h // 4
                if h % 4 == 0:
                    S.dma("pool", lambda e, kv=kv: e.dma_start(
                        out=WBkv[:, :, 0, :], in_=wview(e_w_in, SWA_K0 + kv * 64, SWA_K0 + (kv + 1) * 64)), writes=[t_WB])
                    S.dma("pool", lambda e, kv=kv: e.dma_start(
                        out=WBkv[:, :, 1, :], in_=wview(e_w_in, SWA_V0 + kv * 64, SWA_V0 + (kv + 1) * 64)), writes=[t_WB])
                    for b in range(NB):
                        ps = PS[4 + b % 2]
                        tps = t_PS[4 + b % 2]
                        proj_fm(ps, tps, lambda k: WB[:, k * 128:k * 128 + 64], 8,
                                lambda k, b=b: hT[:, k, b * 512:(b + 1) * 512], t_hT[4 * b:4 * b + 4], [t_WB], 64)
                        S.op("act", copy_op("act", KT[0:64, b * 512:(b + 1) * 512], ps[0:64, :]), writes=[tps, t_KT])
                        S.op("dve", copy_op("dve", KT[64:128, b * 512:(b + 1) * 512], ps[0:64, :]),
                             writes=[tps, t_KT])
                    for t8 in range(4):
                        ps = PS[4 + t8 % 2]
                        tps = t_PS[4 + t8 % 2]
                        for ti in range(8):
                            tt = t8 * 8 + ti
                            for k in range(8):
                                S.op("pe", lambda e, k=k, tt=tt, ti=ti, ps=ps: e.matmul(
                                    ps[:, ti * 64:(ti + 1) * 64], lhsT=hT[:, k, tt * 128:(tt + 1) * 128],
                                    rhs=WB[:, k * 128 + 64:k * 128 + 128], start=(k == 0), stop=(k == 7)),
                                    reads=[t_hT[tt], t_WB], writes=[tps])
                        eng = evac_engine()
                        S.op(eng, copy_op(eng, VA0[:, t8 * 8:(t8 + 1) * 8, 0:64],
                                          ps[:, :].rearrange("p (t d) -> p t d", t=8)), writes=[tps, t_VA])
                if h % 2 == 0:
                    S.dma("pool", lambda e, h=h: e.dma_start(
                        out=WA[:, 0:1024].rearrange("p (c n) -> p c n", c=8),
                        in_=wview(e_w_in, SWA_Q0 + h * 64, SWA_Q0 + (h + 2) * 64)), writes=[t_WA])
                    for b in range(NB):
                        ps = PS[4 + b % 2]
                        tps = t_PS[4 + b % 2]
                        proj_fm(ps, tps, lambda k: WA[:, k * 128:(k + 1) * 128], 8,
                                lambda k, b=b: hT[:, k, b * 512:(b + 1) * 512], t_hT[4 * b:4 * b + 4], [t_WA], 128)
                        eng = evac_engine()
                        S.op(eng, copy_op(eng, QT[0:128, b * 512:(b + 1) * 512], ps[0:128, :], scale=0.125),
                             writes=[tps, t_QT[b]])
                groups = []
                for g4 in range(NB):
                    n0 = 4 * g4
                    kbs = []
                    for j in range(max(0, n0 - 1), n0 + 4):
                        qa, qb = max(j, n0), min(j + 1, n0 + 3)
                        ta = 0 if j >= n0 else 128
                        tb = 256 if j + 1 <= n0 + 3 else 128
                        kbs.append((j, (qa - n0) * 128, (qb - n0 + 1) * 128,
                                    SWT[:, h * 256 + ta:h * 256 + tb]))
                    groups.append((g4 * 512, 512, kbs, g4))
                c = 4 + h // 2
                po = (h % 2) * 64

                def out_fn(g, c=c, po=po):
                    return AOT[po:po + 64, c, g * 512:(g + 1) * 512], [t_AOT[c][g]]
                attention(64, groups, None, 1.0, lambda j: VA0[:, j, :], out_fn,
                          den_add=esink[64:128, h:h + 1], fp32_tables=True, rd_extra=[t_sm, t_kf],
                          pf_bufs=PFw, skip_gc=True, r0=64 * (h % 2))
            S.barrier()
            checkpoint(2)

            W416 = WA[:, 0:8 * 416].rearrange("p (c n) -> p c n", c=8)
            S.dma("pool", lambda e: e.dma_start(out=W416, in_=wview(e_w_in, 0, 416)), writes=[t_WA])
            WROT = WB[:, 0:8 * 96].rearrange("p (c n) -> p c n", c=8)
            S.op("dve", lambda e: e.memset(WROT[:, :, 0:64], 0.0), writes=[t_WB])
            S.op("dve", lambda e: e.tensor_scalar(out=WROT[:, :, 64:80], in0=W416[:, :, 400:416], scalar1=-1.0,
                                                  scalar2=None, op0=ALU.mult), reads=[t_WA], writes=[t_WB])
            S.op("dve", lambda e: e.tensor_copy(out=WROT[:, :, 80:96], in_=W416[:, :, 384:400]), reads=[t_WA],
                 writes=[t_WB])
            for b in range(NB):
                bs = slice(b * 512, (b + 1) * 512)
                hsrc = lambda k, b=b: hT[:, k, b * 512:(b + 1) * 512]
                ht = t_hT[4 * b:4 * b + 4]
                proj_fm(PS[0], t_PS[0], lambda k: W416[:, k, 0:128], 8, hsrc, ht, [t_WA], 128)
                proj_fm(PS[1], t_PS[1], lambda k: W416[:, k, 128:256], 8, hsrc, ht, [t_WA], 128)
                proj_fm(PS[2], t_PS[2], lambda k: W416[:, k, 256:384], 8, hsrc, ht, [t_WA], 128)
                proj_fm(PS[3], t_PS[3], lambda k: W416[:, k, 320:416], 8, hsrc, ht, [t_WA], 96)
                proj_fm(PS[4], t_PS[4], lambda k: WROT[:, k, :], 8, hsrc, ht, [t_WB], 96)
                for n in range(3):
                    S.op("act", lambda e, n=n: e.activation(out=ET[n][:], in_=PS[n][:], func=AF.Square),
                         writes=[t_PS[n], t_ET[n]])
                S.op("pe", lambda e: e.matmul(PS[5][:, :], lhsT=onesf[:], rhs=ET[0][:], start=True, stop=False),
                     reads=[t_ET[0], t_cst], writes=[t_PS[5]])
                S.op("pe", lambda e: e.matmul(PS[5][:, :], lhsT=onesf[:], rhs=ET[1][:], start=False, stop=True),
                     reads=[t_ET[1], t_cst], writes=[t_PS[5]])
                S.op("pe", lambda e: e.matmul(PS[6][:, :], lhsT=onesf[:], rhs=ET[2][:], start=True, stop=True),
                     reads=[t_ET[2], t_cst], writes=[t_PS[6]])
                for (pi, n_, n) in ((5, 256.0, 0), (6, 128.0, 1)):
                    S.op("act", lambda e, pi=pi, n_=n_, n=n: e.activation(out=ET[n][:], in_=PS[pi][:], func=AF.Ln,
                                                                    bias=epsb[:, 0:1], scale=1.0 / n_),
                         reads=[t_eps], writes=[t_PS[pi], t_ET[n]])
                    S.op("act", lambda e, n=n: e.activation(out=ET[n][:], in_=ET[n][:], func=AF.Exp, scale=-0.5),
                         writes=[t_ET[n]])
                for (pi, chunk, gsc, n) in ((0, 0, gq[:, 0:1], 0), (1, 1, gq[:, 1:2], 0), (2, 2, gkv[:, 0:1], 1)):
                    S.op("dve", lambda e, pi=pi, chunk=chunk, gsc=gsc, n=n, bs=bs: e.scalar_tensor_tensor(
                        out=LATt[:, chunk, bs], in0=PS[pi][:], scalar=gsc, in1=ET[n][:], op0=ALU.mult, op1=ALU.mult),
                        reads=[t_ET[n], t_sm], writes=[t_PS[pi], t_LAT[b]])
                S.op("dve", lambda e, bs=bs: e.tensor_tensor(out=RC[0][64:96, :], in0=PS[3][64:96, :], in1=TRIG[64:96, bs],
                                                         op=ALU.mult), reads=[t_TRIG], writes=[t_PS[3], t_RC[0]])
                S.op("dve", lambda e, bs=bs: e.tensor_tensor(out=RC[1][64:96, :], in0=PS[4][64:96, :], in1=TRIG[96:128, bs],
                                                         op=ALU.mult), reads=[t_TRIG], writes=[t_PS[4], t_RC[1]])
                S.op("dve", lambda e, bs=bs: e.tensor_tensor(out=KT[64:96, bs], in0=RC[0][64:96, :], in1=RC[1][64:96, :],
                                                         op=ALU.add), reads=[t_RC[0], t_RC[1]], writes=[t_KTaug])
            S.barrier()
            checkpoint(3)

            WQ = WA[:, 0:2 * 768].rearrange("p (c n) -> p c n", c=2)
            S.dma("pool", lambda e: e.dma_start(out=WQ, in_=wview(e_w_q_up, 0, 768)), writes=[t_WA])
            WQR = WA[:, 1536:1536 + 2 * 768].rearrange("p (c n) -> p c n", c=2)
            WKV = WB[:, 0:1024]
            S.dma("pool", lambda e: e.dma_start(out=WKV, in_=e_w_kv_up[:, :]), writes=[t_WB])
            S.op("dve", lambda e: e.memset(WA[:, 1536:1536 + 2 * 768], 0.0), writes=[t_WA])
            for c2 in range(2):
                src = WQ[:, c2, :].rearrange("p (h d) -> p h d", h=8)
                dst = WQR[:, c2, :].rearrange("p (h d) -> p h d", h=8)
                S.op("dve", lambda e, src=src, dst=dst: e.tensor_scalar(out=dst[:, :, 64:80], in0=src[:, :, 80:96],
                                                                  scalar1=-1.0, scalar2=None, op0=ALU.mult),
                     writes=[t_WA])
                S.op("dve", lambda e, src=src, dst=dst: e.tensor_copy(out=dst[:, :, 80:96], in_=src[:, :, 64:80]),
                     writes=[t_WA])
            mla_scale = 96.0 ** -0.5
            for h in range(8):
                for b in range(NB):
                    ps = PS[4 + b % 2]
                    tps = t_PS[4 + b % 2]
                    S.op("pe", lambda e, b=b, h=h, ps=ps: e.matmul(ps[0:64, :], lhsT=WKV[:, h * 128:h * 128 + 64],
                                                             rhs=LATt[:, 2, b * 512:(b + 1) * 512], start=True, stop=True),
                         reads=[t_LAT[b], t_WB], writes=[tps])
                    eng = evac_engine()
                    S.op(eng, copy_op(eng, KT[0:64, b * 512:(b + 1) * 512], ps[0:64, :]), writes=[tps, t_KT])
                for t8 in range(4):
                    ps = PS[4 + t8 % 2]
                    tps = t_PS[4 + t8 % 2]
                    for ti in range(8):
                        tt = t8 * 8 + ti
                        S.op("pe", lambda e, tt=tt, ti=ti, h=h, ps=ps: e.matmul(
                            ps[:, ti * 64:(ti + 1) * 64], lhsT=LATt[:, 2, tt * 128:(tt + 1) * 128],
                            rhs=WKV[:, h * 128 + 64:h * 128 + 128], start=True, stop=True),
                            reads=[t_LAT[tt // 4], t_WB], writes=[tps])
                    eng = evac_engine()
                    S.op(eng, copy_op(eng, VA0[:, t8 * 8:(t8 + 1) * 8, 0:64],
                                      ps[:, :].rearrange("p (t d) -> p t d", t=8)), writes=[tps, t_VA])
                for b in range(NB):
                    bs = slice(b * 512, (b + 1) * 512)
                    p1, tp1 = PS[4], t_PS[4]
                    p2, tp2 = PS[5], t_PS[5]
                    for k in range(2):
                        S.op("pe", lambda e, k=k, h=h, bs=bs: e.matmul(p1[0:96, :], lhsT=WQ[:, k, h * 96:(h + 1) * 96],
                                                                rhs=LATt[:, k, bs], start=(k == 0), stop=(k == 1)),
                             reads=[t_LAT[b], t_WA], writes=[tp1])
                    for k in range(2):
                        S.op("pe", lambda e, k=k, h=h, bs=bs: e.matmul(p2[0:96, :], lhsT=WQR[:, k, h * 96:(h + 1) * 96],
                                                                rhs=LATt[:, k, bs], start=(k == 0), stop=(k == 1)),
                             reads=[t_LAT[b], t_WA], writes=[tp2])
                    S.op("act", lambda e, bs=bs: e.copy(out=QT[0:64, bs], in_=p1[0:64, :]),
                         writes=[tp1, t_QT[b]])
                    S.op("dve", lambda e, bs=bs: e.tensor_tensor(out=RC[0][64:96, :], in0=p1[64:96, :], in1=TRIG[64:96, bs],
                                                             op=ALU.mult), reads=[t_TRIG], writes=[tp1, t_RC[0]])
                    S.op("dve", lambda e, bs=bs: e.tensor_tensor(out=RC[1][64:96, :], in0=p2[64:96, :], in1=TRIG[96:128, bs],
                                                             op=ALU.mult), reads=[t_TRIG], writes=[tp2, t_RC[1]])
                    S.op("dve", lambda e, bs=bs: e.tensor_tensor(out=QT[64:96, bs], in0=RC[0][64:96, :], in1=RC[1][64:96, :],
                                                             op=ALU.add), reads=[t_RC[0], t_RC[1]], writes=[t_QT[b]])
                c = h // 2
                po = (h % 2) * 64

                def out_fn(g, c=c, po=po):
                    return AOT[po:po + 64, c, g * 512:(g + 1) * 512], [t_AOT[c][g]]
                attention(96, dense_groups(), None, mla_scale, lambda j: VA0[:, j, :], out_fn)

            checkpoint(4)
            checkpoint(5)
            out_phase(e_w_out, x_d, None, last_layer=False, next_gain_d=(o_g_in if do1 else None),
                      gate=(e_w_in, 1184))
            checkpoint(6)

        if do1:
            VA4 = at("VA4", [128, 32, 4, 128], BF16, o_big)
            if not do0:
                load_gain(o_g_in)
                phase_A(x1_d)
                S.barrier()
            Q0, K0, V0, F0, G0 = 0, 1024, 2048, 3072, 3088
            WF = WB[:, 0:128].rearrange("p (c n) -> p c n", c=8)
            S.dma("pool", lambda e: e.dma_start(out=WF, in_=wview(o_w_in, F0, F0 + 16)), writes=[t_WB])
            NL = at("NL", [128, 32, 16], F32, o_wa)
            trif = at("trif", [128, 128], F32, o_wa + 2048)
            identf = at("identf", [128, 128], F32, o_wa + 2560)
            CTf = at("CTf", [16, S_LEN], F32, o_big)
            r1 = at("r1", [16, S_LEN], F32, o_big + 16384)
            CS = at("CS", [128, S_LEN], BF16, o_trig)
            bfb = sb("bfb", [128, 16], F32)
            Ct = sb("Ct", [128, 32, 16], F32)
            Rs = sb("Rs", [128, 16], F32)
            zt = sb("zt", [128, 16], F32)
            t_bfb, t_NL, t_Ct, t_Rs, t_zt = S.tok("bfb"), S.toks(NT, "NL"), S.toks(NT, "Ct"), S.tok("Rs"), S.tok("zt")
            t_c1 = S.tok("cst1")
            S.dma("sp", lambda e: e.dma_start(out=trif[:], in_=cst["c_tri"][:, :]), writes=[t_c1])
            S.dma("sp", lambda e: e.dma_start(out=identf[:], in_=cst["c_ident"][:, :]), writes=[t_c1])
            S.dma("sp", lambda e: e.dma_start(out=bfb[:], in_=bass.AP(o_b_f.tensor, 0, [[0, 128], [1, 16]])),
                  writes=[t_bfb])
            Tt = at("Tt", [128, 32, 16], F32, o_wa + 3072)
            Pfx = at("Pfx", [128, 32, 16], F32, o_wa + 5120)
            t_NLa, t_Tt, t_Pfx = S.tok("NLa"), S.tok("Tt"), S.tok("Pfx")
            pf_, tpf_ = PS[4], t_PS[4]
            for tt in range(NT):
                for k in range(8):
                    S.op("pe", lambda e, k=k, tt=tt: e.matmul(pf_[:, tt * 16:(tt + 1) * 16],
                                                          lhsT=hT[:, k, tt * 128:(tt + 1) * 128],
                                                          rhs=WF[:, k, :], start=(k == 0), stop=(k == 7)),
                         reads=[t_hT[tt], t_WB], writes=[tpf_])
            S.op("dve", lambda e: e.tensor_tensor(out=NL[:, :, :], in0=pf_[:, :].rearrange("p (t h) -> p t h", t=32),
                                                  in1=bass.AP(bfb, 0, [[16, 128], [0, 32], [1, 16]]), op=ALU.add),
                 reads=[t_bfb], writes=[tpf_, t_NLa])
            S.op("act", lambda e: e.activation(out=NL[:, :, :], in_=NL[:, :, :], func=AF.Exp, scale=-1.0),
                 writes=[t_NLa])
            S.op("act", lambda e: e.activation(out=NL[:, :, :], in_=NL[:, :, :], func=AF.Ln, bias=epsb[:, 1:2],
                                               scale=1.0), reads=[t_eps], writes=[t_NLa])
            pT_, tpT_ = PS[5], t_PS[5]
            pC_, tpC_ = PS[6], t_PS[6]
            for tt in range(NT):
                S.op("pe", lambda e, tt=tt: e.matmul(pT_[:, tt * 16:(tt + 1) * 16], lhsT=onesf[:], rhs=NL[:, tt, :],
                                                     start=True, stop=True),
                     reads=[t_NLa, t_cst], writes=[tpT_])
            for tt in range(NT):
                S.op("pe", lambda e, tt=tt: e.matmul(pC_[:, tt * 16:(tt + 1) * 16], lhsT=trif[:], rhs=NL[:, tt, :],
                                                     start=True, stop=True),
                     reads=[t_NLa, t_c1], writes=[tpC_])
            S.op("dve", lambda e: e.tensor_copy(out=Tt[:, :, :], in_=pT_[:, :].rearrange("p (t h) -> p t h", t=32)),
                 writes=[tpT_, t_Tt])
            S.op("dve", lambda e: e.memset(Pfx[:, 0, :], 0.0), writes=[t_Pfx])
            for tt in range(1, NT):
                S.op("dve", lambda e, tt=tt: e.tensor_tensor(out=Pfx[:, tt, :], in0=Pfx[:, tt - 1, :],
                                                           in1=Tt[:, tt - 1, :], op=ALU.add),
                     reads=[t_Tt], writes=[t_Pfx])
            S.op("dve", lambda e: e.tensor_tensor(out=Ct[:, :, :], in0=pC_[:, :].rearrange("p (t h) -> p t h", t=32),
                                                  in1=Pfx[:, :, :], op=ALU.add),
                 reads=[t_Pfx], writes=[tpC_] + list(t_Ct))
            t_CS, t_r1, t_ctf = S.tok("CS"), S.tok("r1"), S.tok("ctf")
            for g4 in range(8):
                ps = PS[6]
                tps = t_PS[6]
                for ti in range(4):
                    tt = g4 * 4 + ti
                    S.op("pe", lambda e, tt=tt, ti=ti: e.matmul(ps[0:16, ti * 128:(ti + 1) * 128], lhsT=Ct[:, tt, :],
                                                            rhs=identf[:], start=True, stop=True),
                         reads=[t_Ct[tt], t_c1], writes=[tps])
                S.op("dve", lambda e, g4=g4: e.tensor_scalar(out=CTf[:, g4 * 512:(g4 + 1) * 512], in0=ps[0:16, :],
                                                          scalar1=-1.0, scalar2=None, op0=ALU.mult),
                     writes=[tps, t_ctf])
            tmpb = at("tmpb", [16, S_LEN], BF16, o_aot)
            t_tmpb = S.tok("tmpb")
            S.op("dve", lambda e: e.tensor_copy(out=CS[0:16, :], in_=CTf[:, :]), reads=[t_ctf], writes=[t_CS])
            S.op("dve", lambda e: e.tensor_tensor(out=r1[:, :], in0=CTf[:, :], in1=CS[0:16, :], op=ALU.subtract),
                 reads=[t_ctf, t_CS], writes=[t_r1])
            S.op("dve", lambda e: e.tensor_copy(out=tmpb[:, :], in_=r1[:, :]), reads=[t_r1], writes=[t_tmpb])
            S.op("dve", lambda e: e.tensor_copy(out=CS[32:48, :], in_=tmpb[:, :]), reads=[t_tmpb], writes=[t_CS])
            S.op("dve", lambda e: e.tensor_tensor(out=r1[:, :], in0=r1[:, :], in1=tmpb[:, :], op=ALU.subtract),
                 reads=[t_tmpb], writes=[t_r1])
            S.op("dve", lambda e: e.tensor_copy(out=CS[64:80, :], in_=r1[:, :]), reads=[t_r1], writes=[t_CS])
            S.barrier()
            checkpoint(7)
            if 'g' in DBG:
                S.op("pe", lambda e: e.matmul(PS[6][0:64, 0:16], lhsT=hT[:, 0, 0:64], rhs=hT[:, 0, 0:16],
                                              start=True, stop=True), reads=[t_hT[0]], writes=[t_PS[6]])
            if 'a' not in DBG:
                S.op("dve", lambda e: e.memset(KT[64:67, :], 1.0), writes=[t_KTaug])
            WBqk = WB[:, 0:1024].rearrange("p (c s n) -> p c s n", c=8, s=2)
            checkpoint(71)

            KT2 = at("KT2", [128, S_LEN], BF16, o_wa)
            KTb = [KT, KT2]
            t_KTb = [[S.tok("ktb0"), S.tok("ktb0aug")], [S.tok("ktb1"), S.tok("ktb1aug")]]
            S.op("dve", lambda e: e.memset(KT2[64:67, :], 1.0), writes=[t_KTb[1][1]])
            t_KTb[0][1] = t_KTaug
            WVb = at("WVb", [128, 1024], BF16, o_rc + 2048)
            t_WQb, t_WKb, t_WVb = S.tok("wqb"), S.tok("wkb"), S.tok("wvb")
            t_VAp = S.toks(2, "vap")
            S.op("dve", lambda e: e.memset(VA4[:, :, 0:2, 64:128], 1.0), writes=[t_VAp[0], t_VA])
            S.op("dve", lambda e: e.memset(VA4[:, :, 2:4, 64:128], 1.0), writes=[t_VAp[1], t_VA])
            rot = {"n": 0}

            def bank():
                rot["n"] += 1
                return 4 + rot["n"] % 2

            def load_wk(h):
                S.dma("pool", lambda e: e.dma_start(out=WBqk[:, :, 1, :],
                                                    in_=wview(o_w_in, K0 + h * 64, K0 + (h + 1) * 64)),
                      writes=[t_WKb])

            def load_wq(h):
                S.dma("pool", lambda e: e.dma_start(out=WBqk[:, :, 0, :],
                                                    in_=wview(o_w_in, Q0 + h * 64, Q0 + (h + 1) * 64)),
                      writes=[t_WQb])

            def load_wv(p):
                S.dma("pool", lambda e: e.dma_start(out=WVb[:, :].rearrange("p (c n) -> p c n", c=8),
                                                    in_=wview(o_w_in, V0 + p * 128, V0 + (p + 1) * 128)),
                      writes=[t_WVb])

            def k_block(h, b):
                pb = bank()
                proj_fm(PS[pb], t_PS[pb], lambda k: WB[:, k * 128 + 64:(k + 1) * 128], 8,
                        lambda k: hT[:, k, b * 512:(b + 1) * 512], t_hT[4 * b:4 * b + 4], [t_WKb], 64)
                S.op("dve", copy_op("dve", KTb[h % 2][0:64, b * 512:(b + 1) * 512], PS[pb][0:64, :]),
                     writes=[t_PS[pb], t_KTb[h % 2][0]])

            def q_block(h, b):
                pb = bank()
                both = h + 1 < 16
                M = 128 if both else 64
                proj_fm(PS[pb], t_PS[pb], lambda k: WB[:, k * 128:k * 128 + M], 8,
                        lambda k: hT[:, k, b * 512:(b + 1) * 512], t_hT[4 * b:4 * b + 4],
                        [t_WQb, t_WKb] if both else [t_WQb], M)
                S.op("dve", copy_op("dve", QT[0:64, b * 512:(b + 1) * 512], PS[pb][0:64, :], scale=0.125),
                     writes=[t_PS[pb], t_QT[b]])
                if both:
                    S.op("dve", copy_op("dve", KTb[(h + 1) % 2][0:64, b * 512:(b + 1) * 512], PS[pb][64:128, :]),
                         writes=[t_PS[pb], t_KTb[(h + 1) % 2][0]])

            def v_tiles(p, t4):
                pb = bank()
                ps = PS[pb]
                s0 = 2 * (p % 2)
                for ti in range(4):
                    tt = t4 * 4 + ti
                    for k in range(8):
                        S.op("pe", lambda e, k=k, tt=tt, ti=ti: e.matmul(
                            ps[:, ti * 128:(ti + 1) * 128], lhsT=hT[:, k, tt * 128:(tt + 1) * 128],
                            rhs=WVb[:, k * 128:(k + 1) * 128], start=(k == 0), stop=(k == 7)),
                            reads=[t_hT[tt], t_WVb], writes=[t_PS[pb]])
                for ti in range(4):
                    tt = t4 * 4 + ti
                    S.op("dve", copy_op("dve", VA4[:, tt, s0:s0 + 2, 0:64],
                                        ps[:, ti * 128:(ti + 1) * 128].rearrange("p (h d) -> p h d", h=2)),
                         writes=[t_PS[pb], t_VAp[p % 2]])

            load_wv(0)
            for t4 in range(8):
                v_tiles(0, t4)
            load_wk(0)
            for b in range(NB):
                k_block(0, b)
            for h in range(16):
                hh = h % 4
                load_wq(h)
                for r_ in range(3):
                    S.dma("sp", lambda e, h=h, r_=r_: e.dma_start(out=QT[64 + r_:65 + r_, :],
                                                                 in_=CS[32 * r_ + h:32 * r_ + h + 1, :]),
                          reads=[t_CS], writes=[t_QTaug])
                bgl = []
                if h + 1 < 16:
                    load_wk(h + 1)
                if h % 2 == 1 and h + 1 < 16:
                    load_wv((h + 1) // 2)
                    bgl += [(lambda h=h, t4=t4: v_tiles((h + 1) // 2, t4)) for t4 in range(8)]
                c = h // 2
                po = (h % 2) * 64

                def out_fn(g, c=c, po=po):
                    return AOT[po:po + 64, c, g * 512:(g + 1) * 512], [t_AOT[c][g]]
                attention(67, dense_groups(), lambda j, h=h: Ct[:, j, h:h + 1], 1.0,
                          lambda j, hh=hh: VA4[:, j, hh, :], out_fn, rd_extra=t_Ct,
                          KTt=(KTb[h % 2], t_KTb[h % 2]), pre_group=(lambda g, h=h: q_block(h, g)),
                          bg=bgl, va_tok=t_VAp[(h // 2) % 2], one_rc=True)

            checkpoint(8)
            S.barrier()
            checkpoint(9)
            out_phase(o_w_out, x1_d, (t_x1 if mode == "full" else None), last_layer=True, gate=(o_w_in, G0))
            S.wait_all("sp", t_out)
        else:
            S.wait_all("sp", t_x1)


    def checkpoint(k):
        if stop == k:
            raise _Stop()

    try:
        _layers()
    except _Stop:
        pass
    S.emit()
    return nc


_CACHE = {}


def _get(mode):
    if mode not in _CACHE:
        _CACHE[mode] = build_program(mode)
    return _CACHE[mode]


L0_KEYS = ["e_g_in", "e_w_in", "e_g_q_a", "e_w_q_up", "e_g_kv_a", "e_w_kv_up", "e_sinks", "e_w_out"]
L1_KEYS = ["o_g_in", "o_w_in", "o_b_f", "o_w_out"]


def _maps(inputs, n, mode, x1=None):
    consts = _constants()
    maps = []
    for b in range(n):
        m = dict(consts)
        if mode in ("full", "l0"):
            m["x"] = np.ascontiguousarray(inputs["x"][b])
            m["positions"] = np.ascontiguousarray(inputs["positions"][b]).astype(np.int32)
            for k in L0_KEYS:
                m[k] = np.ascontiguousarray(inputs[k][0])
        if mode in ("full", "l1"):
            for k in L1_KEYS:
                m[k] = np.ascontiguousarray(inputs[k][0])
            m["g_final"] = np.ascontiguousarray(inputs["g_final"])
        if mode == "l1":
            m["x1"] = np.ascontiguousarray(x1[b])
        maps.append(m)
    return maps


def kernel(**inputs):
    n = inputs["x"].shape[0]
    inputs = {k: np.asarray(v) for k, v in inputs.items()}
    nc = _get("full")
    res = run_bass_kernel_spmd(nc, _maps(inputs, n, "full"), core_ids=list(range(n)))
    return np.stack([np.asarray(r["out"]) for r in res.results], axis=0).astype(np.float32)
```

```python
import math
import os
DBG = os.environ.get('KDBG', '')
import numpy as np
import concourse.bass as bass
import concourse.mybir as mybir
from concourse.bass_utils import run_bass_kernel_spmd

F32 = mybir.dt.float32
BF16 = mybir.dt.bfloat16
I32 = mybir.dt.int32
AF = mybir.ActivationFunctionType
ALU = mybir.AluOpType

S_LEN = 4096
D = 1024
NT = 32
NB = 8
EPS = 1e-6
SEM_ROT = 12000


class Tok:
    __slots__ = ("name", "w", "r", "dsem")

    def __init__(self, name=""):
        self.name = name
        self.w = None
        self.r = {}
        self.dsem = None


class _Rec:
    def __init__(self):
        self.call = None

    def __getattr__(self, name):
        def f(*a, **k):
            assert self.call is None
            self.call = (name, a, k)
            return self
        return f


def _freeze(fn):
    r = _Rec()
    fn(r)
    assert r.call is not None
    return r.call


class Sched:
    ENGS = ("pe", "act", "dve", "pool", "sp")

    def __init__(self, nc):
        self.nc = nc
        self.sems = []
        self.semeng = {}
        self.ops = {e: [] for e in self.ENGS}
        self.esem = {}
        self.ecnt = {}
        self.seen = {e: {} for e in self.ENGS}
        self.semval = {}
        self.unsig = {e: False for e in self.ENGS}
        self.noself = {"pe"}
        for e in ("pe", "act", "dve", "pool"):
            self._new_esem(e)

    def _alloc(self, name, eng=None):
        h = self.nc.alloc_semaphore(name=name)
        self.sems.append(h)
        sid = len(self.sems) - 1
        self.semval[sid] = 0
        self.semeng[sid] = eng
        return sid

    def _new_esem(self, e):
        self.esem[e] = self._alloc(f"s_{e}_{len(self.sems)}", e)
        self.ecnt[e] = 0

    def tok(self, name=""):
        return Tok(name)

    def toks(self, n, name=""):
        return [Tok(f"{name}{i}") for i in range(n)]

    def _collect(self, e, reads, writes):
        need = {}

        def add(s, v):
            if need.get(s, 0) < v:
                need[s] = v
        for t in reads:
            if t.w is not None:
                add(*t.w)
        for t in writes:
            if t.w is not None:
                add(*t.w)
            for s, v in t.r.items():
                add(s, v)
        waits = []
        seen = self.seen[e]
        for s, v in need.items():
            if e in self.noself and self.semeng[s] == e:
                continue
            if seen.get(s, 0) >= v:
                continue
            seen[s] = v
            waits.append((s, v))
        return waits

    def _mark(self, s, val, reads, writes):
        for t in reads:
            if t.r.get(s, 0) < val:
                t.r[s] = val
        for t in writes:
            t.w = (s, val)
            t.r = {}

    def op(self, e, fn, reads=(), writes=(), sig=True):
        waits = self._collect(e, reads, writes)
        if sig and self.ecnt[e] >= SEM_ROT and not self.unsig[e]:
            self._new_esem(e)
        self.unsig[e] = not sig
        s = self.esem[e]
        val = self.ecnt[e] + 1
        if sig:
            self.ecnt[e] = val
            self.semval[s] = val
        self.ops[e].append((waits, _freeze(fn), (s, 1) if sig else None))
        self._mark(s, val, reads, writes)

    def barrier(self):
        cur = [(s, v) for s, v in self.semval.items() if v > 0]
        for e in self.ENGS:
            waits = []
            for s, v in cur:
                if (self.semeng[s] == e and e in self.noself) or self.seen[e].get(s, 0) >= v:
                    continue
                self.seen[e][s] = v
                waits.append((s, v))
            self.ops[e].append((waits, None, None))

    def dma(self, q, fn, reads=(), writes=(), owner=None):
        waits = self._collect(q, reads, writes)
        if owner is None:
            owner = writes[0] if writes else reads[0]
        if owner.dsem is None or self.semval[owner.dsem] >= SEM_ROT * 2:
            owner.dsem = self._alloc(f"d_{len(self.sems)}")
        s = owner.dsem
        self.semval[s] += 16
        val = self.semval[s]
        self.ops[q].append((waits, _freeze(fn), (s, 16)))
        self._mark(s, val, reads, writes)

    def wait_all(self, e, toks):
        waits = self._collect(e, [], toks)
        self.ops[e].append((waits, None, None))

    def emit(self):
        nc = self.nc
        sems = self.sems
        ops = self.ops

        def replay(eng, lst):
            for waits, fn, inc in lst:
                for s, v in waits:
                    eng.wait_ge(sems[s], v)
                if fn is None:
                    continue
                name, a, k = fn
                ins = getattr(eng, name)(*a, **k)
                if inc is not None:
                    ins.then_inc(sems[inc[0]], inc[1])

        with nc.Block() as block:
            @block.tensor
            def _(eng):
                replay(eng, ops["pe"])

            @block.scalar
            def _(eng):
                replay(eng, ops["act"])

            @block.vector
            def _(eng):
                replay(eng, ops["dve"])

            @block.gpsimd
            def _(eng):
                replay(eng, ops["pool"])

            @block.sync
            def _(eng):
                replay(eng, ops["sp"])


def _constants():
    c = {}
    c["c_ident"] = np.eye(128, dtype=np.float32)
    k = np.arange(128)[:, None]
    q = np.arange(128)[None, :]
    c["c_tri"] = (k <= q).astype(np.float32)
    c["c_ones"] = np.ones((128, 128), np.float32)
    qq = np.arange(512)[None, None, :]
    kk = np.arange(128)[:, None, None]
    rr = np.arange(4)[None, :, None]
    c["c_mask"] = ((rr * 128 + kk) <= qq).astype(np.float32)
    slopes = 2.0 ** (-8.0 * (np.arange(8, dtype=np.float64) + 1.0) / 8)
    t = np.zeros((128, 8, 2, 128), np.float64)
    kq = np.arange(128)[:, None]
    qv = np.arange(128)[None, :]
    for h in range(8):
        dist0 = 128 + qv - kq
        t[:, h, 1, :] = np.where(dist0 < 128, np.exp(-slopes[h] * dist0), 0.0)
        dist1 = qv - kq
        t[:, h, 0, :] = np.where(dist1 >= 0, np.exp(-slopes[h] * np.maximum(dist1, 0)), 0.0)
    c["c_swt"] = t.astype(np.float32)
    invf = 1.0 / (10000.0 ** (np.arange(0, 32, 2, dtype=np.float32) / 32))
    v = np.zeros((128, 2), np.float32)
    for p in range(64, 128):
        v[p, 0] = invf[(p - 64) % 16]
        v[p, 1] = (math.pi / 2) if p < 96 else 0.0
    c["c_rope"] = v
    return c


CONST_SHAPES = {"c_ident": [128, 128], "c_tri": [128, 128], "c_ones": [128, 128],
                "c_mask": [128, 4, 512], "c_swt": [128, 8, 2, 128], "c_rope": [128, 2]}


class _Stop(Exception):
    pass


def build_program(mode="full", stop=0):
    nc = bass.Bass("TRN2", target_bir_lowering=False)
    S = Sched(nc)
    do0 = mode in ("full", "l0")
    do1 = mode in ("full", "l1")

    def din(name, shape, dt=F32):
        return nc.dram_tensor(name, shape, dt, kind="ExternalInput").ap()

    cst = {k: din(k, v) for k, v in CONST_SHAPES.items()}
    if do0:
        x_d = din("x", [S_LEN, D])
        pos_d = din("positions", [S_LEN], I32)
        e_g_in = din("e_g_in", [D])
        e_w_in = din("e_w_in", [D, 2208])
        e_g_q_a = din("e_g_q_a", [256])
        e_w_q_up = din("e_w_q_up", [256, 768])
        e_g_kv_a = din("e_g_kv_a", [128])
        e_w_kv_up = din("e_w_kv_up", [128, 1024])
        e_sinks = din("e_sinks", [8])
        e_w_out = din("e_w_out", [D, D])
    if do1:
        o_g_in = din("o_g_in", [D])
        o_w_in = din("o_w_in", [D, 4112])
        o_b_f = din("o_b_f", [16])
        o_w_out = din("o_w_out", [D, D])
        g_final = din("g_final", [D])
        out_d = nc.dram_tensor("out", [S_LEN, D], F32, kind="ExternalOutput").ap()
    if mode == "full":
        x1_d = nc.dram_tensor("x1s", [S_LEN, D], F32).ap()
    elif mode == "l0":
        x1_d = nc.dram_tensor("x1", [S_LEN, D], F32, kind="ExternalOutput").ap()
    else:
        x1_d = din("x1", [S_LEN, D])
    t_x1 = S.toks(NT, "x1d")
    t_x1own = S.tok("x1own")
    t_outown = S.toks(2, "outown")
    t_out = S.toks(NT, "outd")

    def region(nbytes):
        st, _ = nc.bump_sbuf(nbytes)
        return st

    def at(name, shape, dt, off):
        return nc.alloc_sbuf_tensor_at(name, shape, dt, offset=off)

    def sb(name, shape, dt):
        return nc.alloc_sbuf_tensor(name, shape, dt)

    hT = sb("hT", [128, 8, S_LEN], BF16)
    t_hT = S.toks(NT, "hT")
    o_aot = region(65536)
    AOT = at("AOT", [128, 8, S_LEN], BF16, o_aot)
    t_AOT = [[S.tok(f"aot{c}_{b}") for b in range(NB)] for c in range(8)]
    o_big = region(32768)
    BIG = at("BIG", [128, 32 * 4 * 128], BF16, o_big)
    t_VA = S.tok("VA")
    t_LAT = S.toks(NB, "LAT")
    o_trig = region(8192)
    TRIG = at("TRIG", [128, S_LEN], BF16, o_trig)
    t_TRIG = S.tok("TRIG")
    o_pool = region(19456)
    QT = at("QT", [128, S_LEN], BF16, o_pool)
    t_QT = S.toks(NB, "QT")
    t_QTaug = S.tok("QTaug")
    KT = at("KT", [128, S_LEN], BF16, o_pool + 8192)
    t_KT = S.tok("KT")
    t_KTaug = S.tok("KTaug")
    NPT = 5
    PT = [at(f"pt{i}", [128, 512], BF16, o_pool + 16384 + 1024 * i) for i in range(3)]
    PT += [sb(f"ptx{i}", [128, 512], BF16) for i in range(NPT - 3)]
    t_PT = S.toks(NPT, "pt")
    XT = [at(f"xt{i}", [128, D], F32, o_pool + 4096 * i) for i in range(3)]
    t_XT = S.toks(3, "xt")
    XB = [at(f"xb{i}", [128, D], BF16, o_pool + 12288 + 2048 * i) for i in range(2)]
    t_XB = S.toks(2, "xb")
    ET = [at(f"et{i}", [128, 512], F32, o_pool + 2048 * i) for i in range(3)]
    t_ET = S.toks(3, "et")
    o_pf = region(1024)
    PF = [at(f"pf{i}", [128, 128], F32, o_pf + 512 * i) for i in range(2)]
    t_PF = S.toks(2, "pf")
    o_rc = region(4096)
    RC = [at(f"rc{i}", [128, 512], F32, o_rc + 2048 * i) for i in range(2)]
    t_RC = S.toks(2, "rc")
    GB = at("GB", [128, D], F32, o_rc)
    t_GB = S.tok("GB")
    o_wa = region(8192)
    WA = at("WA", [128, 3328], BF16, o_wa)
    t_WA = S.tok("WA")
    WB = sb("WB", [128, 1024], BF16)
    t_WB = S.tok("WB")
    MASK = sb("MASK", [128, 128], BF16)
    t_MASK = S.tok("MASK")
    identb = sb("identb", [128, 128], BF16)
    onesf = sb("onesf", [128, 128], F32)
    t_cst = S.tok("cst")
    stat = sb("stat", [128, 8], F32)
    t_statS = S.toks(2, "stat")
    epsb = sb("epsb", [128, 2], F32)
    t_eps = S.tok("eps")

    PS = [nc.alloc_psum_tensor(f"ps{i}", [128, 512], F32) for i in range(7)]
    t_PS = S.toks(7, "ps")
    PST = nc.alloc_psum_tensor("pst", [128, 8, 128], BF16)
    t_PST = S.tok("pst")

    S.dma("sp", lambda e: e.dma_start(out=onesf[:], in_=cst["c_ones"][:, :]), writes=[t_cst])
    t_cstb = S.tok("cstb")
    S.dma("pool", lambda e: e.dma_start(out=identb[:], in_=cst["c_ident"][:, :]), writes=[t_cstb])
    S.dma("pool", lambda e: e.dma_start(out=MASK[:], in_=cst["c_tri"][:, :]), writes=[t_MASK])
    S.op("dve", lambda e: e.memset(epsb[:, 0:1], EPS), writes=[t_eps])
    S.op("dve", lambda e: e.memset(epsb[:, 1:2], 1.0), writes=[t_eps])

    cnt = {"ev": 0}

    def evac_engine():
        cnt["ev"] += 1
        return "act" if cnt["ev"] % 2 else "dve"

    def copy_op(eng, out, in_, scale=None):
        if eng == "act":
            if scale is None:
                return lambda e: e.copy(out=out, in_=in_)
            return lambda e: e.mul(out=out, in_=in_, mul=scale)
        if scale is None:
            return lambda e: e.tensor_copy(out=out, in_=in_)
        return lambda e: e.tensor_scalar(out=out, in0=in_, scalar1=scale, scalar2=None, op0=ALU.mult)

    def load_gain(g_d):
        S.dma("sp", lambda e: e.dma_start(out=GB[:], in_=bass.AP(g_d.tensor, 0, [[0, 128], [1, D]])),
              writes=[t_GB])

    def rstd_from_ms(col, tst):
        S.op("act", lambda e: e.activation(out=stat[:, col + 1:col + 2], in_=stat[:, col:col + 1], func=AF.Ln,
                                           bias=epsb[:, 0:1], scale=1.0), reads=[t_eps], writes=[tst])
        S.op("act", lambda e: e.activation(out=stat[:, col + 1:col + 2], in_=stat[:, col + 1:col + 2],
                                           func=AF.Exp, scale=-0.5), writes=[tst])

    def norm_tile(xt, t_xt, out_ap, t_outs, junk_ap, t_junk, slot=0):
        c0 = 4 * slot
        tst = t_statS[slot]
        S.op("dve", lambda e: e.memset(stat[:, c0:c0 + 1], 0.0), writes=[tst])
        S.op("act", lambda e: e.activation(out=junk_ap, in_=xt[:], func=AF.Square, scale=1.0 / 32,
                                           accum_out=stat[:, c0:c0 + 1]), reads=[t_xt], writes=[t_junk, tst])
        rstd_from_ms(c0, tst)
        S.op("dve", lambda e: e.scalar_tensor_tensor(out=out_ap, in0=xt[:], scalar=stat[:, c0 + 1:c0 + 2], in1=GB[:],
                                                     op0=ALU.mult, op1=ALU.mult),
             reads=[t_xt, tst, t_GB], writes=list(t_outs))

    def to_hT_norm(xt, t_xt, tt):
        i = tt % 2
        norm_tile(xt, t_xt, XB[i][:], [t_XB[i]], XB[i][:], t_XB[i], slot=i)

    def to_hT_tr(tt):
        i = tt % 2
        for c in range(8):
            S.op("pe", lambda e, c=c: e.transpose(out=PST[:, c, :], in_=XB[i][:, c * 128:(c + 1) * 128],
                                                   identity=identb[:]),
                 reads=[t_XB[i], t_cstb], writes=[t_PST])
        S.op("dve", copy_op("dve", hT[:, :, tt * 128:(tt + 1) * 128], PST[:, :, :]), writes=[t_PST, t_hT[tt]])

    def phase_A(src_d):
        for tt in range(NT + 1):
            if tt < NT:
                i = tt % 3
                S.dma("sp", lambda e, tt=tt, i=i: e.dma_start(out=XT[i][:], in_=src_d[tt * 128:(tt + 1) * 128, :]),
                      writes=[t_XT[i]])
                to_hT_norm(XT[i], t_XT[i], tt)
            if tt >= 1:
                to_hT_tr(tt - 1)

    def wview(w_d, c0, c1):
        return w_d.rearrange("(c p) n -> p c n", p=128)[:, :, c0:c1]

    def proj_fm(ps, t_ps, w_ap_fn, nk, src_fn, src_toks, w_toks, M):
        for k in range(nk):
            S.op("pe", lambda e, k=k: e.matmul(ps[0:M, :], lhsT=w_ap_fn(k), rhs=src_fn(k),
                                               start=(k == 0), stop=(k == nk - 1)),
                 reads=list(src_toks) + list(w_toks), writes=[t_ps])

    def attention(KR, groups, bias_fn, scale, va_fn, out_fn, den_add=None, fp32_tables=False, rd_extra=(),
                  KTt=None, pre_group=None, bg=(), va_tok=None, one_rc=False, pf_bufs=None, skip_gc=False, r0=0):
        steps = []
        for gi, (q0, QW, kbs, gidx) in enumerate(groups):
            for n, kb in enumerate(kbs):
                if len(kb) == 3:
                    j, c0, tbl = kb
                    c1, t0, t1 = QW, c0, c0 + 128
                else:
                    j, c0, c1, tbl = kb
                    t0, t1 = c0, c1
                steps.append((gi, q0, QW, j, c0, c1, tbl, t0, t1, n == 0, n == len(kbs) - 1, gidx))

        SB_ = (0, 1, 6)
        LA = 2
        KTx, t_KTx = (KT, [t_KT, t_KTaug]) if KTt is None else KTt
        t_VAx = t_VA if va_tok is None else va_tok
        PFx = PF if pf_bufs is None else pf_bufs
        bg = list(bg)
        nsteps = len(steps)
        bg_stride = max(1, nsteps // (len(bg) + 1)) if bg else 0

        def emit_qk(i):
            gi, q0, QW, j, c0, c1, tbl, t0, t1, first, last, gidx = steps[i]
            sp = PS[SB_[i % 3]]
            S.op("pe", lambda e: e.matmul(sp[:, c0:c1], lhsT=KTx[r0:r0 + KR, j * 128:(j + 1) * 128],
                                          rhs=QT[r0:r0 + KR, q0 + c0:q0 + c1], start=True, stop=True),
                 reads=list(t_KTx) + [t_QT[q0 // 512], t_QTaug], writes=[t_PS[SB_[i % 3]]])

        first_idx = {}
        for i_, st_ in enumerate(steps):
            first_idx.setdefault(st_[0], i_)
        if pre_group is not None:
            pre_group(groups[0][3])
        for i0 in range(min(LA, len(steps))):
            emit_qk(i0)
        for i, (gi, q0, QW, j, c0, c1, tbl, t0, t1, first, last, gidx) in enumerate(steps):
            if i + LA < len(steps):
                emit_qk(i + LA)
            if pre_group is not None and i == first_idx[gi] + 1 and gi + 1 < len(groups):
                pre_group(groups[gi + 1][3])
            if bg and i > 0 and i % bg_stride == 0:
                bg.pop(0)()
            sp = PS[SB_[i % 3]]
            tsp = t_PS[SB_[i % 3]]
            pt = PT[i % NPT]
            tpt = t_PT[i % NPT]
            kw = {"scale": scale}
            b = bias_fn(j) if bias_fn is not None else None
            if b is not None:
                kw["bias"] = b
            if tbl is not None and fp32_tables:
                pf = PFx[i % 2]
                w = c1 - c0
                S.op("act", lambda e: e.activation(out=pf[:, 0:w], in_=sp[:, c0:c1], func=AF.Exp, **kw),
                     reads=list(rd_extra), writes=[tsp, t_PF[i % 2]])
                S.op("dve", lambda e: e.tensor_tensor(out=pt[:, c0:c1], in0=pf[:, 0:w], in1=tbl, op=ALU.mult),
                     reads=[t_PF[i % 2]] + list(rd_extra), writes=[tpt])
            else:
                S.op("act", lambda e: e.activation(out=pt[:, c0:c1], in_=sp[:, c0:c1], func=AF.Exp, **kw),
                     reads=list(rd_extra), writes=[tsp, tpt])
                if tbl is not None:
                    S.op("dve", lambda e: e.tensor_tensor(out=pt[:, t0:t1], in0=pt[:, t0:t1], in1=tbl, op=ALU.mult),
                         reads=[t_MASK], writes=[tpt])
            op_ = PS[2 + gi % 2]
            top = t_PS[2 + gi % 2]
            mkw = {"skip_group_check": True} if skip_gc else {}
            S.op("pe", lambda e: e.matmul(op_[:, c0:c1], lhsT=va_fn(j), rhs=pt[:, c0:c1], start=first, stop=last,
                                          **mkw),
                 reads=[tpt, t_VAx], writes=[top])
            if last:
                rc = RC[0 if one_rc else gi % 2]
                trc = t_RC[0 if one_rc else gi % 2]
                if den_add is not None:
                    S.op("dve", lambda e: e.tensor_scalar(
                        out=rc[64:128, 0:QW], in0=op_[64:128, 0:QW], scalar1=den_add, scalar2=None, op0=ALU.add),
                        reads=list(rd_extra), writes=[top, trc])
                    S.op("act", lambda e: e.activation(out=rc[64:128, 0:QW], in_=rc[64:128, 0:QW], func=AF.Ln),
                         writes=[trc])
                    S.op("act", lambda e: e.activation(out=rc[64:128, 0:QW], in_=rc[64:128, 0:QW], func=AF.Exp,
                                                       scale=-1.0), writes=[trc])
                else:
                    S.op("dve", lambda e: e.reciprocal(out=rc[64:128, 0:QW], in_=op_[64:128, 0:QW]),
                         writes=[top, trc])
                o_ap, o_toks = out_fn(gidx)
                S.op("dve", lambda e: e.tensor_tensor(out=o_ap, in0=op_[0:64, 0:QW], in1=rc[64:128, 0:QW],
                                                      op=ALU.mult),
                     reads=[trc], writes=[top] + list(o_toks))
        while bg:
            bg.pop(0)()

    def _unused():
        pass

    def dense_groups():
        gs = []
        for g in range(NB):
            kbs = [(j, 0, None) for j in range(4 * g)] + [(4 * g + r, r * 128, MASK[:, :]) for r in range(4)]
            gs.append((g * 512, 512, kbs, g))
        return gs

    def gate_phase(w_in_d, goff):
        for c in range(8):
            S.dma("pool", lambda e, c=c: e.dma_start(
                out=WB[:, 0:1024].rearrange("p (c n) -> p c n", c=8),
                in_=wview(w_in_d, goff + c * 128, goff + (c + 1) * 128)), writes=[t_WB])
            for b in range(NB):
                ps = PS[4 + b % 2]
                tps = t_PS[4 + b % 2]
                proj_fm(ps, tps, lambda k: WB[:, k * 128:(k + 1) * 128], 8,
                        lambda k, b=b: hT[:, k, b * 512:(b + 1) * 512], t_hT[4 * b:4 * b + 4], [t_WB], 128)
                gt = PT[b % NPT]
                S.op("act", lambda e, gt=gt, ps=ps: e.activation(out=gt[:], in_=ps[:], func=AF.Silu),
                     writes=[tps, t_PT[b % NPT]])
                S.op("dve", lambda e, gt=gt, c=c, b=b: e.tensor_tensor(
                    out=AOT[:, c, b * 512:(b + 1) * 512], in0=AOT[:, c, b * 512:(b + 1) * 512], in1=gt[:],
                    op=ALU.mult), reads=[t_PT[b % NPT]], writes=[t_AOT[c][b]])

    def out_phase(w_out_d, res_d, t_res, last_layer, next_gain_d=None, gate=None):
        S.barrier()
        WO = at("WO_%d" % int(last_layer), [128, 8 * 1024], BF16, o_big)
        t_WO = S.tok("WO")
        WG = at("WG_%d" % int(last_layer), [128, 8 * 1024], BF16, o_big + 16384)
        t_WG = S.tok("WG")
        gw_d, goff = gate
        for c in range(8):
            S.dma("pool", lambda e, c=c: e.dma_start(
                out=WG[:, c * 1024:(c + 1) * 1024].rearrange("p (k n) -> p k n", k=8),
                in_=wview(gw_d, goff + c * 128, goff + (c + 1) * 128)), writes=[t_WG])
        S.dma("pool", lambda e: e.dma_start(out=WO[:].rearrange("p (c n) -> p c n", c=8),
                                            in_=wview(w_out_d, 0, D)), writes=[t_WO])
        gcnt = {"n": 0}

        def gate_chunk(c, b):
            gcnt["n"] += 1
            pb = 4 + gcnt["n"] % 2
            proj_fm(PS[pb], t_PS[pb], lambda k: WG[:, c * 1024 + k * 128:c * 1024 + (k + 1) * 128], 8,
                    lambda k: hT[:, k, b * 512:(b + 1) * 512], t_hT[4 * b:4 * b + 4], [t_WG], 128)
            gi_ = gcnt["n"] % NPT
            gt = PT[gi_]
            S.op("act", lambda e: e.activation(out=gt[:], in_=PS[pb][:], func=AF.Silu),
                 writes=[t_PS[pb], t_PT[gi_]])
            S.op("dve", lambda e: e.tensor_tensor(
                out=AOT[:, c, b * 512:(b + 1) * 512], in0=AOT[:, c, b * 512:(b + 1) * 512], in1=gt[:],
                op=ALU.mult), reads=[t_PT[gi_]], writes=[t_AOT[c][b]])

        for c in range(8):
            gate_chunk(c, 0)
        if last_layer:
            load_gain(g_final)
        elif next_gain_d is not None:
            load_gain(next_gain_d)
        t_x1o = S.toks(3, "x1own")
        t_outo = S.toks(3, "outown")
        def stage1(tt):
            i = tt % 3
            b = tt // 4
            pp = 2 * (tt % 2)
            S.dma("sp", lambda e: e.dma_start(out=XT[i][:], in_=res_d[tt * 128:(tt + 1) * 128, :]),
                  reads=[t_res[tt]] if t_res is not None else [], writes=[t_XT[i]])
            for half in range(2):
                ps = PS[pp + half]
                tps = t_PS[pp + half]
                for c in range(8):
                    S.op("pe", lambda e: e.matmul(
                        ps[:, :], lhsT=AOT[:, c, tt * 128:(tt + 1) * 128],
                        rhs=WO[:, c * 1024 + half * 512: c * 1024 + (half + 1) * 512],
                        start=(c == 0), stop=(c == 7)),
                        reads=[t_AOT[c][b], t_WO], writes=[tps])

        def stage1b(tt):
            i = tt % 3
            pp = 2 * (tt % 2)
            for half in range(2):
                ps = PS[pp + half]
                tps = t_PS[pp + half]
                S.op("dve", lambda e: e.tensor_tensor(
                    out=XT[i][:, half * 512:(half + 1) * 512], in0=ps[:, :], in1=XT[i][:, half * 512:(half + 1) * 512],
                    op=ALU.add), writes=[tps, t_XT[i]])

        def stage2a(tt):
            i = tt % 3
            if last_layer:
                norm_tile(XT[i], t_XT[i], XT[i][:], [t_XT[i]], XB[tt % 2][:], t_XB[tt % 2], slot=tt % 2)
                S.dma("sp", lambda e: e.dma_start(out=out_d[tt * 128:(tt + 1) * 128, :], in_=XT[i][:]),
                      reads=[t_XT[i]], writes=[t_out[tt]], owner=t_outo[i])
            else:
                S.dma("sp", lambda e: e.dma_start(out=x1_d[tt * 128:(tt + 1) * 128, :], in_=XT[i][:]),
                      reads=[t_XT[i]], writes=[t_x1[tt]], owner=t_x1o[i])
                if mode == "full":
                    to_hT_norm(XT[i], t_XT[i], tt)

        for tt in range(NT + 2):
            if tt < NT:
                stage1(tt)
            if 1 <= tt <= NT:
                stage2a(tt - 1)
            if tt < NT:
                stage1b(tt)
            if tt < NT and tt // 4 + 1 < NB:
                for c in (2 * (tt % 4), 2 * (tt % 4) + 1):
                    gate_chunk(c, tt // 4 + 1)
            if tt >= 2 and (not last_layer) and mode == "full":
                to_hT_tr(tt - 2)
        S.barrier()

    def _layers():
        if do0:
            VA0 = at("VA0", [128, 32, 128], BF16, o_big)
            LATt = at("LATt", [128, 3, S_LEN], BF16, o_big + 8192)
            SWT = at("SWT", [128, 2048], F32, o_big + 8192)
            PFw = [at(f"pfw{i}", [128, 256], F32, o_big + 16384 + 1024 * i) for i in range(2)]
            posi = at("posi", [128, S_LEN], I32, o_aot)
            kf = at("kf", [128, S_LEN], F32, o_aot + 16384)
            ANG = at("ANG", [128, S_LEN], F32, o_aot + 32768)
            t_pos, t_kf, t_ang = S.tok("posi"), S.tok("kf"), S.tok("ang")
            load_gain(e_g_in)
            phase_A(x_d)

            gq = sb("gq", [128, 2], F32)
            gkv = sb("gkv", [128, 1], F32)
            esink = sb("esink", [128, 8], F32)
            ropec = sb("ropec", [128, 2], F32)
            t_sm = S.tok("small0")
            for c2 in range(2):
                S.dma("sp", lambda e, c2=c2: e.dma_start(
                    out=gq[:, c2:c2 + 1], in_=e_g_q_a[c2 * 128:(c2 + 1) * 128].rearrange("(p o) -> p o", o=1)),
                    writes=[t_sm])
            S.dma("sp", lambda e: e.dma_start(out=gkv[:], in_=e_g_kv_a.rearrange("(p o) -> p o", o=1)), writes=[t_sm])
            S.dma("sp", lambda e: e.dma_start(out=esink[:], in_=bass.AP(e_sinks.tensor, 0, [[0, 128], [1, 8]])),
                  writes=[t_sm])
            S.dma("sp", lambda e: e.dma_start(out=ropec[:], in_=cst["c_rope"][:, :]), writes=[t_sm])
            S.op("act", lambda e: e.activation(out=esink[:], in_=esink[:], func=AF.Exp), writes=[t_sm])

            S.dma("sp", lambda e: e.dma_start(out=posi[64:128, :], in_=bass.AP(pos_d.tensor, 0, [[0, 64], [1, S_LEN]])),
                  writes=[t_pos])
            P6 = slice(64, 128)
            S.op("dve", lambda e: e.tensor_copy(out=ANG[P6, :], in_=posi[P6, :]), reads=[t_pos], writes=[t_ang])
            S.op("dve", lambda e: e.tensor_scalar(out=ANG[P6, :], in0=ANG[P6, :], scalar1=ropec[P6, 0:1],
                                                  scalar2=ropec[P6, 1:2], op0=ALU.mult, op1=ALU.add),
                 reads=[t_sm], writes=[t_ang])
            S.op("dve", lambda e: e.tensor_scalar(out=kf[P6, :], in0=ANG[P6, :], scalar1=1.0 / (2 * math.pi),
                                                  scalar2=0.5, op0=ALU.mult, op1=ALU.add),
                 reads=[t_ang], writes=[t_kf])
            S.op("dve", lambda e: e.tensor_copy(out=posi[P6, :], in_=kf[P6, :]), reads=[t_kf], writes=[t_pos])
            S.op("dve", lambda e: e.tensor_copy(out=kf[P6, :], in_=posi[P6, :]), reads=[t_pos], writes=[t_kf])
            C1 = 6.28125
            C2 = 2 * math.pi - C1
            S.op("dve", lambda e: e.scalar_tensor_tensor(out=ANG[P6, :], in0=kf[P6, :], scalar=-C1, in1=ANG[P6, :],
                                                         op0=ALU.mult, op1=ALU.add), reads=[t_kf], writes=[t_ang])
            S.op("dve", lambda e: e.scalar_tensor_tensor(out=ANG[P6, :], in0=kf[P6, :], scalar=-C2, in1=ANG[P6, :],
                                                         op0=ALU.mult, op1=ALU.add), reads=[t_kf], writes=[t_ang])
            S.op("dve", lambda e: e.tensor_single_scalar(out=kf[P6, :], in_=ANG[P6, :], scalar=-math.pi, op=ALU.is_lt),
                 reads=[t_ang], writes=[t_kf])
            S.op("dve", lambda e: e.scalar_tensor_tensor(out=ANG[P6, :], in0=kf[P6, :], scalar=2 * math.pi,
                                                         in1=ANG[P6, :], op0=ALU.mult, op1=ALU.add),
                 reads=[t_kf], writes=[t_ang])
            S.op("dve", lambda e: e.tensor_scalar(out=ANG[P6, :], in0=ANG[P6, :], scalar1=-3.1415925, scalar2=3.1415925,
                                                  op0=ALU.max, op1=ALU.min), writes=[t_ang])
            S.op("act", lambda e: e.activation(out=TRIG[P6, :], in_=ANG[P6, :], func=AF.Sin), reads=[t_ang],
                 writes=[t_TRIG])
            S.barrier()
            checkpoint(1)

            S.dma("sp", lambda e: e.dma_start(out=SWT[:, :], in_=cst["c_swt"].rearrange("p h r q -> p (h r q)")),
                  writes=[t_kf])
            S.op("dve", lambda e: e.memset(VA0[:, :, 64:128], 1.0), writes=[t_VA])
            SWA_Q0, SWA_K0, SWA_V0 = 416, 928, 1056
            WBkv = WB[:, 0:1024].rearrange("p (c s n) -> p c s n", c=8, s=2)
            for h in range(8):
                kv = h // 4
                if h % 4 == 0:
                    S.dma("pool", lambda e, kv=kv: e.dma_start(
                        out=WBkv[:, :, 0, :], in_=wview(e_w_in, SWA_K0 + kv * 64, SWA_K0 + (kv + 1) * 64)), writes=[t_WB])
                    S.dma("pool", lambda e, kv=kv: e.dma_start(
                        out=WBkv[:, :, 1, :], in_=wview(e_w_in, SWA_V0 + kv * 64, SWA_V0 + (kv + 1) * 64)), writes=[t_WB])
                    for b in range(NB):
                        ps = PS[4 + b % 2]
                        tps = t_PS[4 + b % 2]
                        proj_fm(ps, tps, lambda k: WB[:, k * 128:k * 128 + 64], 8,
                                lambda k, b=b: hT[:, k, b * 512:(b + 1) * 512], t_hT[4 * b:4 * b + 4], [t_WB], 64)
                        S.op("act", copy_op("act", KT[0:64, b * 512:(b + 1) * 512], ps[0:64, :]), writes=[tps, t_KT])
                        S.op("dve", copy_op("dve", KT[64:128, b * 512:(b + 1) * 512], ps[0:64, :]),
                             writes=[tps, t_KT])
                    for t8 in range(4):
                        ps = PS[4 + t8 % 2]
                        tps = t_PS[4 + t8 % 2]
                        for ti in range(8):
                            tt = t8 * 8 + ti
                            for k in range(8):
                                S.op("pe", lambda e, k=k, tt=tt, ti=ti, ps=ps: e.matmul(
                                    ps[:, ti * 64:(ti + 1) * 64], lhsT=hT[:, k, tt * 128:(tt + 1) * 128],
                                    rhs=WB[:, k * 128 + 64:k * 128 + 128], start=(k == 0), stop=(k == 7)),
                                    reads=[t_hT[tt], t_WB], writes=[tps])
                        eng = evac_engine()
                        S.op(eng, copy_op(eng, VA0[:, t8 * 8:(t8 + 1) * 8, 0:64],
                                          ps[:, :].rearrange("p (t d) -> p t d", t=8)), writes=[tps, t_VA])
                if h % 2 == 0:
                    S.dma("pool", lambda e, h=h: e.dma_start(
                        out=WA[:, 0:1024].rearrange("p (c n) -> p c n", c=8),
                        in_=wview(e_w_in, SWA_Q0 + h * 64, SWA_Q0 + (h + 2) * 64)), writes=[t_WA])
                    for b in range(NB):
                        ps = PS[4 + b % 2]
                        tps = t_PS[4 + b % 2]
                        proj_fm(ps, tps, lambda k: WA[:, k * 128:(k + 1) * 128], 8,
                                lambda k, b=b: hT[:, k, b * 512:(b + 1) * 512], t_hT[4 * b:4 * b + 4], [t_WA], 128)
                        eng = evac_engine()
                        S.op(eng, copy_op(eng, QT[0:128, b * 512:(b + 1) * 512], ps[0:128, :], scale=0.125),
                             writes=[tps, t_QT[b]])
                groups = []
                for g4 in range(NB):
                    n0 = 4 * g4
                    kbs = []
                    for j in range(max(0, n0 - 1), n0 + 4):
                        qa, qb = max(j, n0), min(j + 1, n0 + 3)
                        ta = 0 if j >= n0 else 128
                        tb = 256 if j + 1 <= n0 + 3 else 128
                        kbs.append((j, (qa - n0) * 128, (qb - n0 + 1) * 128,
                                    SWT[:, h * 256 + ta:h * 256 + tb]))
                    groups.append((g4 * 512, 512, kbs, g4))
                c = 4 + h // 2
                po = (h % 2) * 64

                def out_fn(g, c=c, po=po):
                    return AOT[po:po + 64, c, g * 512:(g + 1) * 512], [t_AOT[c][g]]
                attention(64, groups, None, 1.0, lambda j: VA0[:, j, :], out_fn,
                          den_add=esink[64:128, h:h + 1], fp32_tables=True, rd_extra=[t_sm, t_kf],
                          pf_bufs=PFw, skip_gc=True, r0=64 * (h % 2))
            S.barrier()
            checkpoint(2)

            W416 = WA[:, 0:8 * 416].rearrange("p (c n) -> p c n", c=8)
            S.dma("pool", lambda e: e.dma_start(out=W416, in_=wview(e_w_in, 0, 416)), writes=[t_WA])
            WROT = WB[:, 0:8 * 96].rearrange("p (c n) -> p c n", c=8)
            S.op("dve", lambda e: e.memset(WROT[:, :, 0:64], 0.0), writes=[t_WB])
            S.op("dve", lambda e: e.tensor_scalar(out=WROT[:, :, 64:80], in0=W416[:, :, 400:416], scalar1=-1.0,
                                                  scalar2=None, op0=ALU.mult), reads=[t_WA], writes=[t_WB])
            S.op("dve", lambda e: e.tensor_copy(out=WROT[:, :, 80:96], in_=W416[:, :, 384:400]), reads=[t_WA],
                 writes=[t_WB])
            for b in range(NB):
                bs = slice(b * 512, (b + 1) * 512)
                hsrc = lambda k, b=b: hT[:, k, b * 512:(b + 1) * 512]
                ht = t_hT[4 * b:4 * b + 4]
                proj_fm(PS[0], t_PS[0], lambda k: W416[:, k, 0:128], 8, hsrc, ht, [t_WA], 128)
                proj_fm(PS[1], t_PS[1], lambda k: W416[:, k, 128:256], 8, hsrc, ht, [t_WA], 128)
                proj_fm(PS[2], t_PS[2], lambda k: W416[:, k, 256:384], 8, hsrc, ht, [t_WA], 128)
                proj_fm(PS[3], t_PS[3], lambda k: W416[:, k, 320:416], 8, hsrc, ht, [t_WA], 96)
                proj_fm(PS[4], t_PS[4], lambda k: WROT[:, k, :], 8, hsrc, ht, [t_WB], 96)
                for n in range(3):
                    S.op("act", lambda e, n=n: e.activation(out=ET[n][:], in_=PS[n][:], func=AF.Square),
                         writes=[t_PS[n], t_ET[n]])
                S.op("pe", lambda e: e.matmul(PS[5][:, :], lhsT=onesf[:], rhs=ET[0][:], start=True, stop=False),
                     reads=[t_ET[0], t_cst], writes=[t_PS[5]])
                S.op("pe", lambda e: e.matmul(PS[5][:, :], lhsT=onesf[:], rhs=ET[1][:], start=False, stop=True),
                     reads=[t_ET[1], t_cst], writes=[t_PS[5]])
                S.op("pe", lambda e: e.matmul(PS[6][:, :], lhsT=onesf[:], rhs=ET[2][:], start=True, stop=True),
                     reads=[t_ET[2], t_cst], writes=[t_PS[6]])
                for (pi, n_, n) in ((5, 256.0, 0), (6, 128.0, 1)):
                    S.op("act", lambda e, pi=pi, n_=n_, n=n: e.activation(out=ET[n][:], in_=PS[pi][:], func=AF.Ln,
                                                                    bias=epsb[:, 0:1], scale=1.0 / n_),
                         reads=[t_eps], writes=[t_PS[pi], t_ET[n]])
                    S.op("act", lambda e, n=n: e.activation(out=ET[n][:], in_=ET[n][:], func=AF.Exp, scale=-0.5),
                         writes=[t_ET[n]])
                for (pi, chunk, gsc, n) in ((0, 0, gq[:, 0:1], 0), (1, 1, gq[:, 1:2], 0), (2, 2, gkv[:, 0:1], 1)):
                    S.op("dve", lambda e, pi=pi, chunk=chunk, gsc=gsc, n=n, bs=bs: e.scalar_tensor_tensor(
                        out=LATt[:, chunk, bs], in0=PS[pi][:], scalar=gsc, in1=ET[n][:], op0=ALU.mult, op1=ALU.mult),
                        reads=[t_ET[n], t_sm], writes=[t_PS[pi], t_LAT[b]])
                S.op("dve", lambda e, bs=bs: e.tensor_tensor(out=RC[0][64:96, :], in0=PS[3][64:96, :], in1=TRIG[64:96, bs],
                                                         op=ALU.mult), reads=[t_TRIG], writes=[t_PS[3], t_RC[0]])
                S.op("dve", lambda e, bs=bs: e.tensor_tensor(out=RC[1][64:96, :], in0=PS[4][64:96, :], in1=TRIG[96:128, bs],
                                                         op=ALU.mult), reads=[t_TRIG], writes=[t_PS[4], t_RC[1]])
                S.op("dve", lambda e, bs=bs: e.tensor_tensor(out=KT[64:96, bs], in0=RC[0][64:96, :], in1=RC[1][64:96, :],
                                                         op=ALU.add), reads=[t_RC[0], t_RC[1]], writes=[t_KTaug])
            S.barrier()
            checkpoint(3)

            WQ = WA[:, 0:2 * 768].rearrange("p (c n) -> p c n", c=2)
            S.dma("pool", lambda e: e.dma_start(out=WQ, in_=wview(e_w_q_up, 0, 768)), writes=[t_WA])
            WQR = WA[:, 1536:1536 + 2 * 768].rearrange("p (c n) -> p c n", c=2)
            WKV = WB[:, 0:1024]
            S.dma("pool", lambda e: e.dma_start(out=WKV, in_=e_w_kv_up[:, :]), writes=[t_WB])
            S.op("dve", lambda e: e.memset(WA[:, 1536:1536 + 2 * 768], 0.0), writes=[t_WA])
            for c2 in range(2):
                src = WQ[:, c2, :].rearrange("p (h d) -> p h d", h=8)
                dst = WQR[:, c2, :].rearrange("p (h d) -> p h d", h=8)
                S.op("dve", lambda e, src=src, dst=dst: e.tensor_scalar(out=dst[:, :, 64:80], in0=src[:, :, 80:96],
                                                                  scalar1=-1.0, scalar2=None, op0=ALU.mult),
                     writes=[t_WA])
                S.op("dve", lambda e, src=src, dst=dst: e.tensor_copy(out=dst[:, :, 80:96], in_=src[:, :, 64:80]),
                     writes=[t_WA])
            mla_scale = 96.0 ** -0.5
            for h in range(8):
                for b in range(NB):
                    ps = PS[4 + b % 2]
                    tps = t_PS[4 + b % 2]
                    S.op("pe", lambda e, b=b, h=h, ps=ps: e.matmul(ps[0:64, :], lhsT=WKV[:, h * 128:h * 128 + 64],
                                                             rhs=LATt[:, 2, b * 512:(b + 1) * 512], start=True, stop=True),
                         reads=[t_LAT[b], t_WB], writes=[tps])
                    eng = evac_engine()
                    S.op(eng, copy_op(eng, KT[0:64, b * 512:(b + 1) * 512], ps[0:64, :]), writes=[tps, t_KT])
                for t8 in range(4):
                    ps = PS[4 + t8 % 2]
                    tps = t_PS[4 + t8 % 2]
                    for ti in range(8):
                        tt = t8 * 8 + ti
                        S.op("pe", lambda e, tt=tt, ti=ti, h=h, ps=ps: e.matmul(
                            ps[:, ti * 64:(ti + 1) * 64], lhsT=LATt[:, 2, tt * 128:(tt + 1) * 128],
                            rhs=WKV[:, h * 128 + 64:h * 128 + 128], start=True, stop=True),
                            reads=[t_LAT[tt // 4], t_WB], writes=[tps])
                    eng = evac_engine()
                    S.op(eng, copy_op(eng, VA0[:, t8 * 8:(t8 + 1) * 8, 0:64],
                                      ps[:, :].rearrange("p (t d) -> p t d", t=8)), writes=[tps, t_VA])
                for b in range(NB):
                    bs = slice(b * 512, (b + 1) * 512)
                    p1, tp1 = PS[4], t_PS[4]
                    p2, tp2 = PS[5], t_PS[5]
                    for k in range(2):
                        S.op("pe", lambda e, k=k, h=h, bs=bs: e.matmul(p1[0:96, :], lhsT=WQ[:, k, h * 96:(h + 1) * 96],
                                                                rhs=LATt[:, k, bs], start=(k == 0), stop=(k == 1)),
                             reads=[t_LAT[b], t_WA], writes=[tp1])
                    for k in range(2):
                        S.op("pe", lambda e, k=k, h=h, bs=bs: e.matmul(p2[0:96, :], lhsT=WQR[:, k, h * 96:(h + 1) * 96],
                                                                rhs=LATt[:, k, bs], start=(k == 0), stop=(k == 1)),
                             reads=[t_LAT[b], t_WA], writes=[tp2])
                    S.op("act", lambda e, bs=bs: e.copy(out=QT[0:64, bs], in_=p1[0:64, :]),
                         writes=[tp1, t_QT[b]])
                    S.op("dve", lambda e, bs=bs: e.tensor_tensor(out=RC[0][64:96, :], in0=p1[64:96, :], in1=TRIG[64:96, bs],
                                                             op=ALU.mult), reads=[t_TRIG], writes=[tp1, t_RC[0]])
                    S.op("dve", lambda e, bs=bs: e.tensor_tensor(out=RC[1][64:96, :], in0=p2[64:96, :], in1=TRIG[96:128, bs],
                                                             op=ALU.mult), reads=[t_TRIG], writes=[tp2, t_RC[1]])
                    S.op("dve", lambda e, bs=bs: e.tensor_tensor(out=QT[64:96, bs], in0=RC[0][64:96, :], in1=RC[1][64:96, :],
                                                             op=ALU.add), reads=[t_RC[0], t_RC[1]], writes=[t_QT[b]])
                c = h // 2
                po = (h % 2) * 64

                def out_fn(g, c=c, po=po):
                    return AOT[po:po + 64, c, g * 512:(g + 1) * 512], [t_AOT[c][g]]
                attention(96, dense_groups(), None, mla_scale, lambda j: VA0[:, j, :], out_fn)

            checkpoint(4)
            checkpoint(5)
            out_phase(e_w_out, x_d, None, last_layer=False, next_gain_d=(o_g_in if do1 else None),
                      gate=(e_w_in, 1184))
            checkpoint(6)

        if do1:
            VA4 = at("VA4", [128, 32, 4, 128], BF16, o_big)
            if not do0:
                load_gain(o_g_in)
                phase_A(x1_d)
                S.barrier()
            Q0, K0, V0, F0, G0 = 0, 1024, 2048, 3072, 3088
            WF = WB[:, 0:128].rearrange("p (c n) -> p c n", c=8)
            S.dma("pool", lambda e: e.dma_start(out=WF, in_=wview(o_w_in, F0, F0 + 16)), writes=[t_WB])
            NL = at("NL", [128, 32, 16], F32, o_wa)
            trif = at("trif", [128, 128], F32, o_wa + 2048)
            identf = at("identf", [128, 128], F32, o_wa + 2560)
            CTf = at("CTf", [16, S_LEN], F32, o_big)
            r1 = at("r1", [16, S_LEN], F32, o_big + 16384)
            CS = at("CS", [128, S_LEN], BF16, o_trig)
            bfb = sb("bfb", [128, 16], F32)
            Ct = sb("Ct", [128, 32, 16], F32)
            Rs = sb("Rs", [128, 16], F32)
            zt = sb("zt", [128, 16], F32)
            t_bfb, t_NL, t_Ct, t_Rs, t_zt = S.tok("bfb"), S.toks(NT, "NL"), S.toks(NT, "Ct"), S.tok("Rs"), S.tok("zt")
            t_c1 = S.tok("cst1")
            S.dma("sp", lambda e: e.dma_start(out=trif[:], in_=cst["c_tri"][:, :]), writes=[t_c1])
            S.dma("sp", lambda e: e.dma_start(out=identf[:], in_=cst["c_ident"][:, :]), writes=[t_c1])
            S.dma("sp", lambda e: e.dma_start(out=bfb[:], in_=bass.AP(o_b_f.tensor, 0, [[0, 128], [1, 16]])),
                  writes=[t_bfb])
            Tt = at("Tt", [128, 32, 16], F32, o_wa + 3072)
            Pfx = at("Pfx", [128, 32, 16], F32, o_wa + 5120)
            t_NLa, t_Tt, t_Pfx = S.tok("NLa"), S.tok("Tt"), S.tok("Pfx")
            pf_, tpf_ = PS[4], t_PS[4]
            for tt in range(NT):
                for k in range(8):
                    S.op("pe", lambda e, k=k, tt=tt: e.matmul(pf_[:, tt * 16:(tt + 1) * 16],
                                                          lhsT=hT[:, k, tt * 128:(tt + 1) * 128],
                                                          rhs=WF[:, k, :], start=(k == 0), stop=(k == 7)),
                         reads=[t_hT[tt], t_WB], writes=[tpf_])
            S.op("dve", lambda e: e.tensor_tensor(out=NL[:, :, :], in0=pf_[:, :].rearrange("p (t h) -> p t h", t=32),
                                                  in1=bass.AP(bfb, 0, [[16, 128], [0, 32], [1, 16]]), op=ALU.add),
                 reads=[t_bfb], writes=[tpf_, t_NLa])
            S.op("act", lambda e: e.activation(out=NL[:, :, :], in_=NL[:, :, :], func=AF.Exp, scale=-1.0),
                 writes=[t_NLa])
            S.op("act", lambda e: e.activation(out=NL[:, :, :], in_=NL[:, :, :], func=AF.Ln, bias=epsb[:, 1:2],
                                               scale=1.0), reads=[t_eps], writes=[t_NLa])
            pT_, tpT_ = PS[5], t_PS[5]
            pC_, tpC_ = PS[6], t_PS[6]
            for tt in range(NT):
                S.op("pe", lambda e, tt=tt: e.matmul(pT_[:, tt * 16:(tt + 1) * 16], lhsT=onesf[:], rhs=NL[:, tt, :],
                                                     start=True, stop=True),
                     reads=[t_NLa, t_cst], writes=[tpT_])
            for tt in range(NT):
                S.op("pe", lambda e, tt=tt: e.matmul(pC_[:, tt * 16:(tt + 1) * 16], lhsT=trif[:], rhs=NL[:, tt, :],
                                                     start=True, stop=True),
                     reads=[t_NLa, t_c1], writes=[tpC_])
            S.op("dve", lambda e: e.tensor_copy(out=Tt[:, :, :], in_=pT_[:, :].rearrange("p (t h) -> p t h", t=32)),
                 writes=[tpT_, t_Tt])
            S.op("dve", lambda e: e.memset(Pfx[:, 0, :], 0.0), writes=[t_Pfx])
            for tt in range(1, NT):
                S.op("dve", lambda e, tt=tt: e.tensor_tensor(out=Pfx[:, tt, :], in0=Pfx[:, tt - 1, :],
                                                           in1=Tt[:, tt - 1, :], op=ALU.add),
                     reads=[t_Tt], writes=[t_Pfx])
            S.op("dve", lambda e: e.tensor_tensor(out=Ct[:, :, :], in0=pC_[:, :].rearrange("p (t h) -> p t h", t=32),
                                                  in1=Pfx[:, :, :], op=ALU.add),
                 reads=[t_Pfx], writes=[tpC_] + list(t_Ct))
            t_CS, t_r1, t_ctf = S.tok("CS"), S.tok("r1"), S.tok("ctf")
            for g4 in range(8):
                ps = PS[6]
                tps = t_PS[6]
                for ti in range(4):
                    tt = g4 * 4 + ti
                    S.op("pe", lambda e, tt=tt, ti=ti: e.matmul(ps[0:16, ti * 128:(ti + 1) * 128], lhsT=Ct[:, tt, :],
                                                            rhs=identf[:], start=True, stop=True),
                         reads=[t_Ct[tt], t_c1], writes=[tps])
                S.op("dve", lambda e, g4=g4: e.tensor_scalar(out=CTf[:, g4 * 512:(g4 + 1) * 512], in0=ps[0:16, :],
                                                          scalar1=-1.0, scalar2=None, op0=ALU.mult),
                     writes=[tps, t_ctf])
            tmpb = at("tmpb", [16, S_LEN], BF16, o_aot)
            t_tmpb = S.tok("tmpb")
            S.op("dve", lambda e: e.tensor_copy(out=CS[0:16, :], in_=CTf[:, :]), reads=[t_ctf], writes=[t_CS])
            S.op("dve", lambda e: e.tensor_tensor(out=r1[:, :], in0=CTf[:, :], in1=CS[0:16, :], op=ALU.subtract),
                 reads=[t_ctf, t_CS], writes=[t_r1])
            S.op("dve", lambda e: e.tensor_copy(out=tmpb[:, :], in_=r1[:, :]), reads=[t_r1], writes=[t_tmpb])
            S.op("dve", lambda e: e.tensor_copy(out=CS[32:48, :], in_=tmpb[:, :]), reads=[t_tmpb], writes=[t_CS])
            S.op("dve", lambda e: e.tensor_tensor(out=r1[:, :], in0=r1[:, :], in1=tmpb[:, :], op=ALU.subtract),
                 reads=[t_tmpb], writes=[t_r1])
            S.op("dve", lambda e: e.tensor_copy(out=CS[64:80, :], in_=r1[:, :]), reads=[t_r1], writes=[t_CS])
            S.barrier()
            checkpoint(7)
            if 'g' in DBG:
                S.op("pe", lambda e: e.matmul(PS[6][0:64, 0:16], lhsT=hT[:, 0, 0:64], rhs=hT[:, 0, 0:16],
                                              start=True, stop=True), reads=[t_hT[0]], writes=[t_PS[6]])
            if 'a' not in DBG:
                S.op("dve", lambda e: e.memset(KT[64:67, :], 1.0), writes=[t_KTaug])
            WBqk = WB[:, 0:1024].rearrange("p (c s n) -> p c s n", c=8, s=2)
            checkpoint(71)

            KT2 = at("KT2", [128, S_LEN], BF16, o_wa)
            KTb = [KT, KT2]
            t_KTb = [[S.tok("ktb0"), S.tok("ktb0aug")], [S.tok("ktb1"), S.tok("ktb1aug")]]
            S.op("dve", lambda e: e.memset(KT2[64:67, :], 1.0), writes=[t_KTb[1][1]])
            t_KTb[0][1] = t_KTaug
            WVb = at("WVb", [128, 1024], BF16, o_rc + 2048)
            t_WQb, t_WKb, t_WVb = S.tok("wqb"), S.tok("wkb"), S.tok("wvb")
            t_VAp = S.toks(2, "vap")
            S.op("dve", lambda e: e.memset(VA4[:, :, 0:2, 64:128], 1.0), writes=[t_VAp[0], t_VA])
            S.op("dve", lambda e: e.memset(VA4[:, :, 2:4, 64:128], 1.0), writes=[t_VAp[1], t_VA])
            rot = {"n": 0}

            def bank():
                rot["n"] += 1
                return 4 + rot["n"] % 2

            def load_wk(h):
                S.dma("pool", lambda e: e.dma_start(out=WBqk[:, :, 1, :],
                                                    in_=wview(o_w_in, K0 + h * 64, K0 + (h + 1) * 64)),
                      writes=[t_WKb])

            def load_wq(h):
                S.dma("pool", lambda e: e.dma_start(out=WBqk[:, :, 0, :],
                                                    in_=wview(o_w_in, Q0 + h * 64, Q0 + (h + 1) * 64)),
                      writes=[t_WQb])

            def load_wv(p):
                S.dma("pool", lambda e: e.dma_start(out=WVb[:, :].rearrange("p (c n) -> p c n", c=8),
                                                    in_=wview(o_w_in, V0 + p * 128, V0 + (p + 1) * 128)),
                      writes=[t_WVb])

            def k_block(h, b):
                pb = bank()
                proj_fm(PS[pb], t_PS[pb], lambda k: WB[:, k * 128 + 64:(k + 1) * 128], 8,
                        lambda k: hT[:, k, b * 512:(b + 1) * 512], t_hT[4 * b:4 * b + 4], [t_WKb], 64)
                S.op("dve", copy_op("dve", KTb[h % 2][0:64, b * 512:(b + 1) * 512], PS[pb][0:64, :]),
                     writes=[t_PS[pb], t_KTb[h % 2][0]])

            def q_block(h, b):
                pb = bank()
                both = h + 1 < 16
                M = 128 if both else 64
                proj_fm(PS[pb], t_PS[pb], lambda k: WB[:, k * 128:k * 128 + M], 8,
                        lambda k: hT[:, k, b * 512:(b + 1) * 512], t_hT[4 * b:4 * b + 4],
                        [t_WQb, t_WKb] if both else [t_WQb], M)
                S.op("dve", copy_op("dve", QT[0:64, b * 512:(b + 1) * 512], PS[pb][0:64, :], scale=0.125),
                     writes=[t_PS[pb], t_QT[b]])
                if both:
                    S.op("dve", copy_op("dve", KTb[(h + 1) % 2][0:64, b * 512:(b + 1) * 512], PS[pb][64:128, :]),
                         writes=[t_PS[pb], t_KTb[(h + 1) % 2][0]])

            def v_tiles(p, t2):
                pb = bank()
                ps = PS[pb]
                s0 = 2 * (p % 2)
                for ti in range(2):
                    tt = t2 * 2 + ti
                    for k in range(8):
                        S.op("pe", lambda e, k=k, tt=tt, ti=ti: e.matmul(
                            ps[:, ti * 128:(ti + 1) * 128], lhsT=hT[:, k, tt * 128:(tt + 1) * 128],
                            rhs=WVb[:, k * 128:(k + 1) * 128], start=(k == 0), stop=(k == 7)),
                            reads=[t_hT[tt], t_WVb], writes=[t_PS[pb]])
                for ti in range(2):
                    tt = t2 * 2 + ti
                    S.op("dve", copy_op("dve", VA4[:, tt, s0:s0 + 2, 0:64],
                                        ps[:, ti * 128:(ti + 1) * 128].rearrange("p (h d) -> p h d", h=2)),
                         writes=[t_PS[pb], t_VAp[p % 2]])

            load_wv(0)
            for t2 in range(16):
                v_tiles(0, t2)
            load_wk(0)
            for b in range(NB):
                k_block(0, b)
            for h in range(16):
                hh = h % 4
                load_wq(h)
                for r_ in range(3):
                    S.dma("sp", lambda e, h=h, r_=r_: e.dma_start(out=QT[64 + r_:65 + r_, :],
                                                                 in_=CS[32 * r_ + h:32 * r_ + h + 1, :]),
                          reads=[t_CS], writes=[t_QTaug])
                bgl = []
                if h + 1 < 16:
                    load_wk(h + 1)
                if h % 2 == 1 and h + 1 < 16:
                    load_wv((h + 1) // 2)
                    bgl += [(lambda h=h, t2=t2: v_tiles((h + 1) // 2, t2)) for t2 in range(16)]
                c = h // 2
                po = (h % 2) * 64

                def out_fn(g, c=c, po=po):
                    return AOT[po:po + 64, c, g * 512:(g + 1) * 512], [t_AOT[c][g]]
                attention(67, dense_groups(), lambda j, h=h: Ct[:, j, h:h + 1], 1.0,
                          lambda j, hh=hh: VA4[:, j, hh, :], out_fn, rd_extra=t_Ct,
                          KTt=(KTb[h % 2], t_KTb[h % 2]), pre_group=(lambda g, h=h: q_block(h, g)),
                          bg=bgl, va_tok=t_VAp[(h // 2) % 2], one_rc=True)

            checkpoint(8)
            S.barrier()
            checkpoint(9)
            out_phase(o_w_out, x1_d, (t_x1 if mode == "full" else None), last_layer=True, gate=(o_w_in, G0))
            S.wait_all("sp", t_out)
        else:
            S.wait_all("sp", t_x1)


    def checkpoint(k):
        if stop == k:
            raise _Stop()

    try:
        _layers()
    except _Stop:
        pass
    S.emit()
    return nc


_CACHE = {}


def _get(mode):
    if mode not in _CACHE:
        _CACHE[mode] = build_program(mode)
    return _CACHE[mode]


L0_KEYS = ["e_g_in", "e_w_in", "e_g_q_a", "e_w_q_up", "e_g_kv_a", "e_w_kv_up", "e_sinks", "e_w_out"]
L1_KEYS = ["o_g_in", "o_w_in", "o_b_f", "o_w_out"]


def _maps(inputs, n, mode, x1=None):
    consts = _constants()
    maps = []
    for b in range(n):
        m = dict(consts)
        if mode in ("full", "l0"):
            m["x"] = np.ascontiguousarray(inputs["x"][b])
            m["positions"] = np.ascontiguousarray(inputs["positions"][b]).astype(np.int32)
            for k in L0_KEYS:
                m[k] = np.ascontiguousarray(inputs[k][0])
        if mode in ("full", "l1"):
            for k in L1_KEYS:
                m[k] = np.ascontiguousarray(inputs[k][0])
            m["g_final"] = np.ascontiguousarray(inputs["g_final"])
        if mode == "l1":
            m["x1"] = np.ascontiguousarray(x1[b])
        maps.append(m)
    return maps


def kernel(**inputs):
    n = inputs["x"].shape[0]
    inputs = {k: np.asarray(v) for k, v in inputs.items()}
    nc = _get("full")
    res = run_bass_kernel_spmd(nc, _maps(inputs, n, "full"), core_ids=list(range(n)))
    return np.stack([np.asarray(r["out"]) for r in res.results], axis=0).astype(np.float32)
```

```python
import math
import os
DBG = os.environ.get('KDBG', '')
import numpy as np
import concourse.bass as bass
import concourse.mybir as mybir
from concourse.bass_utils import run_bass_kernel_spmd

F32 = mybir.dt.float32
BF16 = mybir.dt.bfloat16
I32 = mybir.dt.int32
AF = mybir.ActivationFunctionType
ALU = mybir.AluOpType

S_LEN = 4096
D = 1024
NT = 32
NB = 8
EPS = 1e-6
SEM_ROT = 12000


class Tok:
    __slots__ = ("name", "w", "r", "dsem")

    def __init__(self, name=""):
        self.name = name
        self.w = None
        self.r = {}
        self.dsem = None


class _Rec:
    def __init__(self):
        self.call = None

    def __getattr__(self, name):
        def f(*a, **k):
            assert self.call is None
            self.call = (name, a, k)
            return self
        return f


def _freeze(fn):
    r = _Rec()
    fn(r)
    assert r.call is not None
    return r.call


class Sched:
    ENGS = ("pe", "act", "dve", "pool", "sp")

    def __init__(self, nc):
        self.nc = nc
        self.sems = []
        self.semeng = {}
        self.ops = {e: [] for e in self.ENGS}
        self.esem = {}
        self.ecnt = {}
        self.seen = {e: {} for e in self.ENGS}
        self.semval = {}
        self.unsig = {e: False for e in self.ENGS}
        self.noself = {"pe"}
        for e in ("pe", "act", "dve", "pool"):
            self._new_esem(e)

    def _alloc(self, name, eng=None):
        h = self.nc.alloc_semaphore(name=name)
        self.sems.append(h)
        sid = len(self.sems) - 1
        self.semval[sid] = 0
        self.semeng[sid] = eng
        return sid

    def _new_esem(self, e):
        self.esem[e] = self._alloc(f"s_{e}_{len(self.sems)}", e)
        self.ecnt[e] = 0

    def tok(self, name=""):
        return Tok(name)

    def toks(self, n, name=""):
        return [Tok(f"{name}{i}") for i in range(n)]

    def _collect(self, e, reads, writes):
        need = {}

        def add(s, v):
            if need.get(s, 0) < v:
                need[s] = v
        for t in reads:
            if t.w is not None:
                add(*t.w)
        for t in writes:
            if t.w is not None:
                add(*t.w)
            for s, v in t.r.items():
                add(s, v)
        waits = []
        seen = self.seen[e]
        for s, v in need.items():
            if e in self.noself and self.semeng[s] == e:
                continue
            if seen.get(s, 0) >= v:
                continue
            seen[s] = v
            waits.append((s, v))
        return waits

    def _mark(self, s, val, reads, writes):
        for t in reads:
            if t.r.get(s, 0) < val:
                t.r[s] = val
        for t in writes:
            t.w = (s, val)
            t.r = {}

    def op(self, e, fn, reads=(), writes=(), sig=True):
        waits = self._collect(e, reads, writes)
        if sig and self.ecnt[e] >= SEM_ROT and not self.unsig[e]:
            self._new_esem(e)
        self.unsig[e] = not sig
        s = self.esem[e]
        val = self.ecnt[e] + 1
        if sig:
            self.ecnt[e] = val
            self.semval[s] = val
        self.ops[e].append((waits, _freeze(fn), (s, 1) if sig else None))
        self._mark(s, val, reads, writes)

    def barrier(self):
        cur = [(s, v) for s, v in self.semval.items() if v > 0]
        for e in self.ENGS:
            waits = []
            for s, v in cur:
                if (self.semeng[s] == e and e in self.noself) or self.seen[e].get(s, 0) >= v:
                    continue
                self.seen[e][s] = v
                waits.append((s, v))
            self.ops[e].append((waits, None, None))

    def dma(self, q, fn, reads=(), writes=(), owner=None):
        waits = self._collect(q, reads, writes)
        if owner is None:
            owner = writes[0] if writes else reads[0]
        if owner.dsem is None or self.semval[owner.dsem] >= SEM_ROT * 2:
            owner.dsem = self._alloc(f"d_{len(self.sems)}")
        s = owner.dsem
        self.semval[s] += 16
        val = self.semval[s]
        self.ops[q].append((waits, _freeze(fn), (s, 16)))
        self._mark(s, val, reads, writes)

    def wait_all(self, e, toks):
        waits = self._collect(e, [], toks)
        self.ops[e].append((waits, None, None))

    def emit(self):
        nc = self.nc
        sems = self.sems
        ops = self.ops

        def replay(eng, lst):
            for waits, fn, inc in lst:
                for s, v in waits:
                    eng.wait_ge(sems[s], v)
                if fn is None:
                    continue
                name, a, k = fn
                ins = getattr(eng, name)(*a, **k)
                if inc is not None:
                    ins.then_inc(sems[inc[0]], inc[1])

        with nc.Block() as block:
            @block.tensor
            def _(eng):
                replay(eng, ops["pe"])

            @block.scalar
            def _(eng):
                replay(eng, ops["act"])

            @block.vector
            def _(eng):
                replay(eng, ops["dve"])

            @block.gpsimd
            def _(eng):
                replay(eng, ops["pool"])

            @block.sync
            def _(eng):
                replay(eng, ops["sp"])


def _constants():
    c = {}
    c["c_ident"] = np.eye(128, dtype=np.float32)
    k = np.arange(128)[:, None]
    q = np.arange(128)[None, :]
    c["c_tri"] = (k <= q).astype(np.float32)
    c["c_ones"] = np.ones((128, 128), np.float32)
    qq = np.arange(512)[None, None, :]
    kk = np.arange(128)[:, None, None]
    rr = np.arange(4)[None, :, None]
    c["c_mask"] = ((rr * 128 + kk) <= qq).astype(np.float32)
    slopes = 2.0 ** (-8.0 * (np.arange(8, dtype=np.float64) + 1.0) / 8)
    t = np.zeros((128, 8, 2, 128), np.float64)
    kq = np.arange(128)[:, None]
    qv = np.arange(128)[None, :]
    for h in range(8):
        dist0 = 128 + qv - kq
        t[:, h, 1, :] = np.where(dist0 < 128, np.exp(-slopes[h] * dist0), 0.0)
        dist1 = qv - kq
        t[:, h, 0, :] = np.where(dist1 >= 0, np.exp(-slopes[h] * np.maximum(dist1, 0)), 0.0)
    c["c_swt"] = t.astype(np.float32)
    invf = 1.0 / (10000.0 ** (np.arange(0, 32, 2, dtype=np.float32) / 32))
    v = np.zeros((128, 2), np.float32)
    for p in range(64, 128):
        v[p, 0] = invf[(p - 64) % 16]
        v[p, 1] = (math.pi / 2) if p < 96 else 0.0
    c["c_rope"] = v
    return c


CONST_SHAPES = {"c_ident": [128, 128], "c_tri": [128, 128], "c_ones": [128, 128],
                "c_mask": [128, 4, 512], "c_swt": [128, 8, 2, 128], "c_rope": [128, 2]}


class _Stop(Exception):
    pass


def build_program(mode="full", stop=0):
    nc = bass.Bass("TRN2", target_bir_lowering=False)
    S = Sched(nc)
    do0 = mode in ("full", "l0")
    do1 = mode in ("full", "l1")

    def din(name, shape, dt=F32):
        return nc.dram_tensor(name, shape, dt, kind="ExternalInput").ap()

    cst = {k: din(k, v) for k, v in CONST_SHAPES.items()}
    if do0:
        x_d = din("x", [S_LEN, D])
        pos_d = din("positions", [S_LEN], I32)
        e_g_in = din("e_g_in", [D])
        e_w_in = din("e_w_in", [D, 2208])
        e_g_q_a = din("e_g_q_a", [256])
        e_w_q_up = din("e_w_q_up", [256, 768])
        e_g_kv_a = din("e_g_kv_a", [128])
        e_w_kv_up = din("e_w_kv_up", [128, 1024])
        e_sinks = din("e_sinks", [8])
        e_w_out = din("e_w_out", [D, D])
    if do1:
        o_g_in = din("o_g_in", [D])
        o_w_in = din("o_w_in", [D, 4112])
        o_b_f = din("o_b_f", [16])
        o_w_out = din("o_w_out", [D, D])
        g_final = din("g_final", [D])
        out_d = nc.dram_tensor("out", [S_LEN, D], F32, kind="ExternalOutput").ap()
    if mode == "full":
        x1_d = nc.dram_tensor("x1s", [S_LEN, D], F32).ap()
    elif mode == "l0":
        x1_d = nc.dram_tensor("x1", [S_LEN, D], F32, kind="ExternalOutput").ap()
    else:
        x1_d = din("x1", [S_LEN, D])
    t_x1 = S.toks(NT, "x1d")
    t_x1own = S.tok("x1own")
    t_outown = S.toks(2, "outown")
    t_out = S.toks(NT, "outd")

    def region(nbytes):
        st, _ = nc.bump_sbuf(nbytes)
        return st

    def at(name, shape, dt, off):
        return nc.alloc_sbuf_tensor_at(name, shape, dt, offset=off)

    def sb(name, shape, dt):
        return nc.alloc_sbuf_tensor(name, shape, dt)

    hT = sb("hT", [128, 8, S_LEN], BF16)
    t_hT = S.toks(NT, "hT")
    o_aot = region(65536)
    AOT = at("AOT", [128, 8, S_LEN], BF16, o_aot)
    t_AOT = [[S.tok(f"aot{c}_{b}") for b in range(NB)] for c in range(8)]
    o_big = region(32768)
    BIG = at("BIG", [128, 32 * 4 * 128], BF16, o_big)
    t_VA = S.tok("VA")
    t_LAT = S.toks(NB, "LAT")
    o_trig = region(8192)
    TRIG = at("TRIG", [128, S_LEN], BF16, o_trig)
    t_TRIG = S.tok("TRIG")
    o_pool = region(19456)
    QT = at("QT", [128, S_LEN], BF16, o_pool)
    t_QT = S.toks(NB, "QT")
    t_QTaug = S.tok("QTaug")
    KT = at("KT", [128, S_LEN], BF16, o_pool + 8192)
    t_KT = S.tok("KT")
    t_KTaug = S.tok("KTaug")
    NPT = 5
    PT = [at(f"pt{i}", [128, 512], BF16, o_pool + 16384 + 1024 * i) for i in range(3)]
    PT += [sb(f"ptx{i}", [128, 512], BF16) for i in range(NPT - 3)]
    t_PT = S.toks(NPT, "pt")
    XT = [at(f"xt{i}", [128, D], F32, o_pool + 4096 * i) for i in range(3)]
    t_XT = S.toks(3, "xt")
    XB = [at(f"xb{i}", [128, D], BF16, o_pool + 12288 + 2048 * i) for i in range(2)]
    t_XB = S.toks(2, "xb")
    ET = [at(f"et{i}", [128, 512], F32, o_pool + 2048 * i) for i in range(3)]
    t_ET = S.toks(3, "et")
    o_pf = region(1024)
    PF = [at(f"pf{i}", [128, 128], F32, o_pf + 512 * i) for i in range(2)]
    t_PF = S.toks(2, "pf")
    o_rc = region(4096)
    RC = [at(f"rc{i}", [128, 512], F32, o_rc + 2048 * i) for i in range(2)]
    t_RC = S.toks(2, "rc")
    GB = at("GB", [128, D], F32, o_rc)
    t_GB = S.tok("GB")
    o_wa = region(8192)
    WA = at("WA", [128, 3328], BF16, o_wa)
    t_WA = S.tok("WA")
    WB = sb("WB", [128, 1024], BF16)
    t_WB = S.tok("WB")
    MASK = sb("MASK", [128, 128], BF16)
    t_MASK = S.tok("MASK")
    identb = sb("identb", [128, 128], BF16)
    onesf = sb("onesf", [128, 128], F32)
    t_cst = S.tok("cst")
    stat = sb("stat", [128, 8], F32)
    t_statS = S.toks(2, "stat")
    epsb = sb("epsb", [128, 2], F32)
    t_eps = S.tok("eps")

    PS = [nc.alloc_psum_tensor(f"ps{i}", [128, 512], F32) for i in range(7)]
    t_PS = S.toks(7, "ps")
    PST = nc.alloc_psum_tensor("pst", [128, 8, 128], BF16)
    t_PST = S.tok("pst")

    S.dma("sp", lambda e: e.dma_start(out=onesf[:], in_=cst["c_ones"][:, :]), writes=[t_cst])
    t_cstb = S.tok("cstb")
    S.dma("pool", lambda e: e.dma_start(out=identb[:], in_=cst["c_ident"][:, :]), writes=[t_cstb])
    S.dma("pool", lambda e: e.dma_start(out=MASK[:], in_=cst["c_tri"][:, :]), writes=[t_MASK])
    S.op("dve", lambda e: e.memset(epsb[:, 0:1], EPS), writes=[t_eps])
    S.op("dve", lambda e: e.memset(epsb[:, 1:2], 1.0), writes=[t_eps])

    cnt = {"ev": 0}

    def evac_engine():
        cnt["ev"] += 1
        return "act" if cnt["ev"] % 2 else "dve"

    def copy_op(eng, out, in_, scale=None):
        if eng == "act":
            if scale is None:
                return lambda e: e.copy(out=out, in_=in_)
            return lambda e: e.mul(out=out, in_=in_, mul=scale)
        if scale is None:
            return lambda e: e.tensor_copy(out=out, in_=in_)
        return lambda e: e.tensor_scalar(out=out, in0=in_, scalar1=scale, scalar2=None, op0=ALU.mult)

    def load_gain(g_d):
        S.dma("sp", lambda e: e.dma_start(out=GB[:], in_=bass.AP(g_d.tensor, 0, [[0, 128], [1, D]])),
              writes=[t_GB])

    def rstd_from_ms(col, tst):
        S.op("act", lambda e: e.activation(out=stat[:, col + 1:col + 2], in_=stat[:, col:col + 1], func=AF.Ln,
                                           bias=epsb[:, 0:1], scale=1.0), reads=[t_eps], writes=[tst])
        S.op("act", lambda e: e.activation(out=stat[:, col + 1:col + 2], in_=stat[:, col + 1:col + 2],
                                           func=AF.Exp, scale=-0.5), writes=[tst])

    def norm_tile(xt, t_xt, out_ap, t_outs, junk_ap, t_junk, slot=0):
        c0 = 4 * slot
        tst = t_statS[slot]
        S.op("dve", lambda e: e.memset(stat[:, c0:c0 + 1], 0.0), writes=[tst])
        S.op("act", lambda e: e.activation(out=junk_ap, in_=xt[:], func=AF.Square, scale=1.0 / 32,
                                           accum_out=stat[:, c0:c0 + 1]), reads=[t_xt], writes=[t_junk, tst])
        rstd_from_ms(c0, tst)
        S.op("dve", lambda e: e.scalar_tensor_tensor(out=out_ap, in0=xt[:], scalar=stat[:, c0 + 1:c0 + 2], in1=GB[:],
                                                     op0=ALU.mult, op1=ALU.mult),
             reads=[t_xt, tst, t_GB], writes=list(t_outs))

    def to_hT_norm(xt, t_xt, tt):
        i = tt % 2
        norm_tile(xt, t_xt, XB[i][:], [t_XB[i]], XB[i][:], t_XB[i], slot=i)

    def to_hT_tr(tt):
        i = tt % 2
        for c in range(8):
            S.op("pe", lambda e, c=c: e.transpose(out=PST[:, c, :], in_=XB[i][:, c * 128:(c + 1) * 128],
                                                   identity=identb[:]),
                 reads=[t_XB[i], t_cstb], writes=[t_PST])
        S.op("dve", copy_op("dve", hT[:, :, tt * 128:(tt + 1) * 128], PST[:, :, :]), writes=[t_PST, t_hT[tt]])

    def phase_A(src_d):
        for tt in range(NT + 1):
            if tt < NT:
                i = tt % 3
                S.dma("sp", lambda e, tt=tt, i=i: e.dma_start(out=XT[i][:], in_=src_d[tt * 128:(tt + 1) * 128, :]),
                      writes=[t_XT[i]])
                to_hT_norm(XT[i], t_XT[i], tt)
            if tt >= 1:
                to_hT_tr(tt - 1)

    def wview(w_d, c0, c1):
        return w_d.rearrange("(c p) n -> p c n", p=128)[:, :, c0:c1]

    def proj_fm(ps, t_ps, w_ap_fn, nk, src_fn, src_toks, w_toks, M):
        for k in range(nk):
            S.op("pe", lambda e, k=k: e.matmul(ps[0:M, :], lhsT=w_ap_fn(k), rhs=src_fn(k),
                                               start=(k == 0), stop=(k == nk - 1)),
                 reads=list(src_toks) + list(w_toks), writes=[t_ps])

    def attention(KR, groups, bias_fn, scale, va_fn, out_fn, den_add=None, fp32_tables=False, rd_extra=(),
                  KTt=None, pre_group=None, bg=(), va_tok=None, one_rc=False, pf_bufs=None, skip_gc=False, r0=0):
        steps = []
        for gi, (q0, QW, kbs, gidx) in enumerate(groups):
            for n, kb in enumerate(kbs):
                if len(kb) == 3:
                    j, c0, tbl = kb
                    c1, t0, t1 = QW, c0, c0 + 128
                else:
                    j, c0, c1, tbl = kb
                    t0, t1 = c0, c1
                steps.append((gi, q0, QW, j, c0, c1, tbl, t0, t1, n == 0, n == len(kbs) - 1, gidx))

        SB_ = (0, 1, 6)
        LA = 2
        KTx, t_KTx = (KT, [t_KT, t_KTaug]) if KTt is None else KTt
        t_VAx = t_VA if va_tok is None else va_tok
        PFx = PF if pf_bufs is None else pf_bufs
        bg = list(bg)
        pending = []
        nsteps = len(steps)
        bg_stride = max(1, nsteps // (len(bg) + 1)) if bg else 0

        def emit_qk(i):
            gi, q0, QW, j, c0, c1, tbl, t0, t1, first, last, gidx = steps[i]
            sp = PS[SB_[i % 3]]
            S.op("pe", lambda e: e.matmul(sp[:, c0:c1], lhsT=KTx[r0:r0 + KR, j * 128:(j + 1) * 128],
                                          rhs=QT[r0:r0 + KR, q0 + c0:q0 + c1], start=True, stop=True),
                 reads=list(t_KTx) + [t_QT[q0 // 512], t_QTaug], writes=[t_PS[SB_[i % 3]]])

        first_idx = {}
        for i_, st_ in enumerate(steps):
            first_idx.setdefault(st_[0], i_)
        if pre_group is not None:
            pre_group(groups[0][3])
        for i0 in range(min(LA, len(steps))):
            emit_qk(i0)
        for i, (gi, q0, QW, j, c0, c1, tbl, t0, t1, first, last, gidx) in enumerate(steps):
            if i + LA < len(steps):
                emit_qk(i + LA)
            if pre_group is not None and i == first_idx[gi] + 1 and gi + 1 < len(groups):
                pre_group(groups[gi + 1][3])
            if bg and i > 0 and i % bg_stride == 0:
                bg.pop(0)()
            sp = PS[SB_[i % 3]]
            tsp = t_PS[SB_[i % 3]]
            pt = PT[i % NPT]
            tpt = t_PT[i % NPT]
            kw = {"scale": scale}
            b = bias_fn(j) if bias_fn is not None else None
            if b is not None:
                kw["bias"] = b
            if tbl is not None and fp32_tables:
                pf = PFx[i % 2]
                w = c1 - c0
                S.op("act", lambda e: e.activation(out=pf[:, 0:w], in_=sp[:, c0:c1], func=AF.Exp, **kw),
                     reads=list(rd_extra), writes=[tsp, t_PF[i % 2]])
                S.op("dve", lambda e: e.tensor_tensor(out=pt[:, c0:c1], in0=pf[:, 0:w], in1=tbl, op=ALU.mult),
                     reads=[t_PF[i % 2]] + list(rd_extra), writes=[tpt])
            else:
                S.op("act", lambda e: e.activation(out=pt[:, c0:c1], in_=sp[:, c0:c1], func=AF.Exp, **kw),
                     reads=list(rd_extra), writes=[tsp, tpt])
                if tbl is not None:
                    S.op("dve", lambda e: e.tensor_tensor(out=pt[:, t0:t1], in0=pt[:, t0:t1], in1=tbl, op=ALU.mult),
                         reads=[t_MASK], writes=[tpt])
            op_ = PS[2 + gi % 2]
            top = t_PS[2 + gi % 2]
            mkw = {"skip_group_check": True} if skip_gc else {}
            S.op("pe", lambda e: e.matmul(op_[:, c0:c1], lhsT=va_fn(j), rhs=pt[:, c0:c1], start=first, stop=last,
                                          **mkw),
                 reads=[tpt, t_VAx], writes=[top])
            if last:
                rc = RC[0 if one_rc else gi % 2]
                trc = t_RC[0 if one_rc else gi % 2]
                o_ap, o_toks = out_fn(gidx)
                if den_add is not None:
                    S.op("dve", lambda e: e.tensor_scalar(
                        out=rc[64:128, 0:QW], in0=op_[64:128, 0:QW], scalar1=den_add, scalar2=None, op0=ALU.add),
                        reads=list(rd_extra), writes=[top, trc])

                    def fin(rc=rc, trc=trc, op_=op_, top=top, QW=QW, o_ap=o_ap, o_toks=o_toks):
                        S.op("act", lambda e: e.activation(out=rc[64:128, 0:QW], in_=rc[64:128, 0:QW], func=AF.Ln),
                             writes=[trc])
                        S.op("act", lambda e: e.activation(out=rc[64:128, 0:QW], in_=rc[64:128, 0:QW], func=AF.Exp,
                                                           scale=-1.0), writes=[trc])
                        S.op("dve", lambda e: e.tensor_tensor(out=o_ap, in0=op_[0:64, 0:QW], in1=rc[64:128, 0:QW],
                                                              op=ALU.mult),
                             reads=[trc], writes=[top] + list(o_toks))
                    pending.append((i + 2, fin))
                else:
                    S.op("dve", lambda e: e.reciprocal(out=rc[64:128, 0:QW], in_=op_[64:128, 0:QW]),
                         writes=[top, trc])
                    S.op("dve", lambda e: e.tensor_tensor(out=o_ap, in0=op_[0:64, 0:QW], in1=rc[64:128, 0:QW],
                                                          op=ALU.mult),
                         reads=[trc], writes=[top] + list(o_toks))
            while pending and pending[0][0] <= i:
                pending.pop(0)[1]()
        while pending:
            pending.pop(0)[1]()
        while bg:
            bg.pop(0)()

    def _unused():
        pass

    def dense_groups():
        gs = []
        for g in range(NB):
            kbs = [(j, 0, None) for j in range(4 * g)] + [(4 * g + r, r * 128, MASK[:, :]) for r in range(4)]
            gs.append((g * 512, 512, kbs, g))
        return gs

    def gate_phase(w_in_d, goff):
        for c in range(8):
            S.dma("pool", lambda e, c=c: e.dma_start(
                out=WB[:, 0:1024].rearrange("p (c n) -> p c n", c=8),
                in_=wview(w_in_d, goff + c * 128, goff + (c + 1) * 128)), writes=[t_WB])
            for b in range(NB):
                ps = PS[4 + b % 2]
                tps = t_PS[4 + b % 2]
                proj_fm(ps, tps, lambda k: WB[:, k * 128:(k + 1) * 128], 8,
                        lambda k, b=b: hT[:, k, b * 512:(b + 1) * 512], t_hT[4 * b:4 * b + 4], [t_WB], 128)
                gt = PT[b % NPT]
                S.op("act", lambda e, gt=gt, ps=ps: e.activation(out=gt[:], in_=ps[:], func=AF.Silu),
                     writes=[tps, t_PT[b % NPT]])
                S.op("dve", lambda e, gt=gt, c=c, b=b: e.tensor_tensor(
                    out=AOT[:, c, b * 512:(b + 1) * 512], in0=AOT[:, c, b * 512:(b + 1) * 512], in1=gt[:],
                    op=ALU.mult), reads=[t_PT[b % NPT]], writes=[t_AOT[c][b]])

    def out_phase(w_out_d, res_d, t_res, last_layer, next_gain_d=None, gate=None):
        S.barrier()
        WO = at("WO_%d" % int(last_layer), [128, 8 * 1024], BF16, o_big)
        t_WO = S.tok("WO")
        WG = at("WG_%d" % int(last_layer), [128, 8 * 1024], BF16, o_big + 16384)
        t_WG = S.tok("WG")
        gw_d, goff = gate
        for c in range(8):
            S.dma("pool", lambda e, c=c: e.dma_start(
                out=WG[:, c * 1024:(c + 1) * 1024].rearrange("p (k n) -> p k n", k=8),
                in_=wview(gw_d, goff + c * 128, goff + (c + 1) * 128)), writes=[t_WG])
        S.dma("pool", lambda e: e.dma_start(out=WO[:].rearrange("p (c n) -> p c n", c=8),
                                            in_=wview(w_out_d, 0, D)), writes=[t_WO])
        gcnt = {"n": 0}

        def gate_chunk(c, b):
            gcnt["n"] += 1
            pb = 4 + gcnt["n"] % 2
            proj_fm(PS[pb], t_PS[pb], lambda k: WG[:, c * 1024 + k * 128:c * 1024 + (k + 1) * 128], 8,
                    lambda k: hT[:, k, b * 512:(b + 1) * 512], t_hT[4 * b:4 * b + 4], [t_WG], 128)
            gi_ = gcnt["n"] % NPT
            gt = PT[gi_]
            S.op("act", lambda e: e.activation(out=gt[:], in_=PS[pb][:], func=AF.Silu),
                 writes=[t_PS[pb], t_PT[gi_]])
            S.op("dve", lambda e: e.tensor_tensor(
                out=AOT[:, c, b * 512:(b + 1) * 512], in0=AOT[:, c, b * 512:(b + 1) * 512], in1=gt[:],
                op=ALU.mult), reads=[t_PT[gi_]], writes=[t_AOT[c][b]])

        for c in range(8):
            gate_chunk(c, 0)
        if last_layer:
            load_gain(g_final)
        elif next_gain_d is not None:
            load_gain(next_gain_d)
        t_x1o = S.toks(3, "x1own")
        t_outo = S.toks(3, "outown")
        def stage1(tt):
            i = tt % 3
            b = tt // 4
            pp = 2 * (tt % 2)
            S.dma("sp", lambda e: e.dma_start(out=XT[i][:], in_=res_d[tt * 128:(tt + 1) * 128, :]),
                  reads=[t_res[tt]] if t_res is not None else [], writes=[t_XT[i]])
            for half in range(2):
                ps = PS[pp + half]
                tps = t_PS[pp + half]
                for c in range(8):
                    S.op("pe", lambda e: e.matmul(
                        ps[:, :], lhsT=AOT[:, c, tt * 128:(tt + 1) * 128],
                        rhs=WO[:, c * 1024 + half * 512: c * 1024 + (half + 1) * 512],
                        start=(c == 0), stop=(c == 7)),
                        reads=[t_AOT[c][b], t_WO], writes=[tps])

        def stage1b(tt):
            i = tt % 3
            pp = 2 * (tt % 2)
            for half in range(2):
                ps = PS[pp + half]
                tps = t_PS[pp + half]
                S.op("dve", lambda e: e.tensor_tensor(
                    out=XT[i][:, half * 512:(half + 1) * 512], in0=ps[:, :], in1=XT[i][:, half * 512:(half + 1) * 512],
                    op=ALU.add), writes=[tps, t_XT[i]])

        def stage2a(tt):
            i = tt % 3
            if last_layer:
                norm_tile(XT[i], t_XT[i], XT[i][:], [t_XT[i]], XB[tt % 2][:], t_XB[tt % 2], slot=tt % 2)
                S.dma("sp", lambda e: e.dma_start(out=out_d[tt * 128:(tt + 1) * 128, :], in_=XT[i][:]),
                      reads=[t_XT[i]], writes=[t_out[tt]], owner=t_outo[i])
            else:
                S.dma("sp", lambda e: e.dma_start(out=x1_d[tt * 128:(tt + 1) * 128, :], in_=XT[i][:]),
                      reads=[t_XT[i]], writes=[t_x1[tt]], owner=t_x1o[i])
                if mode == "full":
                    to_hT_norm(XT[i], t_XT[i], tt)

        for tt in range(NT + 2):
            if tt < NT:
                stage1(tt)
            if 1 <= tt <= NT:
                stage2a(tt - 1)
            if tt < NT:
                stage1b(tt)
            if tt < NT and tt // 4 + 1 < NB:
                for c in (2 * (tt % 4), 2 * (tt % 4) + 1):
                    gate_chunk(c, tt // 4 + 1)
            if tt >= 2 and (not last_layer) and mode == "full":
                to_hT_tr(tt - 2)
        S.barrier()

    def _layers():
        if do0:
            VA0 = at("VA0", [128, 32, 128], BF16, o_big)
            LATt = at("LATt", [128, 3, S_LEN], BF16, o_big + 8192)
            SWT = at("SWT", [128, 2048], F32, o_big + 8192)
            PFw = [at(f"pfw{i}", [128, 256], F32, o_big + 16384 + 1024 * i) for i in range(2)]
            posi = at("posi", [128, S_LEN], I32, o_aot)
            kf = at("kf", [128, S_LEN], F32, o_aot + 16384)
            ANG = at("ANG", [128, S_LEN], F32, o_aot + 32768)
            t_pos, t_kf, t_ang = S.tok("posi"), S.tok("kf"), S.tok("ang")
            load_gain(e_g_in)
            phase_A(x_d)

            gq = sb("gq", [128, 2], F32)
            gkv = sb("gkv", [128, 1], F32)
            esink = sb("esink", [128, 8], F32)
            ropec = sb("ropec", [128, 2], F32)
            t_sm = S.tok("small0")
            for c2 in range(2):
                S.dma("sp", lambda e, c2=c2: e.dma_start(
                    out=gq[:, c2:c2 + 1], in_=e_g_q_a[c2 * 128:(c2 + 1) * 128].rearrange("(p o) -> p o", o=1)),
                    writes=[t_sm])
            S.dma("sp", lambda e: e.dma_start(out=gkv[:], in_=e_g_kv_a.rearrange("(p o) -> p o", o=1)), writes=[t_sm])
            S.dma("sp", lambda e: e.dma_start(out=esink[:], in_=bass.AP(e_sinks.tensor, 0, [[0, 128], [1, 8]])),
                  writes=[t_sm])
            S.dma("sp", lambda e: e.dma_start(out=ropec[:], in_=cst["c_rope"][:, :]), writes=[t_sm])
            S.op("act", lambda e: e.activation(out=esink[:], in_=esink[:], func=AF.Exp), writes=[t_sm])

            S.dma("sp", lambda e: e.dma_start(out=posi[64:128, :], in_=bass.AP(pos_d.tensor, 0, [[0, 64], [1, S_LEN]])),
                  writes=[t_pos])
            P6 = slice(64, 128)
            S.op("dve", lambda e: e.tensor_copy(out=ANG[P6, :], in_=posi[P6, :]), reads=[t_pos], writes=[t_ang])
            S.op("dve", lambda e: e.tensor_scalar(out=ANG[P6, :], in0=ANG[P6, :], scalar1=ropec[P6, 0:1],
                                                  scalar2=ropec[P6, 1:2], op0=ALU.mult, op1=ALU.add),
                 reads=[t_sm], writes=[t_ang])
            S.op("dve", lambda e: e.tensor_scalar(out=kf[P6, :], in0=ANG[P6, :], scalar1=1.0 / (2 * math.pi),
                                                  scalar2=0.5, op0=ALU.mult, op1=ALU.add),
                 reads=[t_ang], writes=[t_kf])
            S.op("dve", lambda e: e.tensor_copy(out=posi[P6, :], in_=kf[P6, :]), reads=[t_kf], writes=[t_pos])
            S.op("dve", lambda e: e.tensor_copy(out=kf[P6, :], in_=posi[P6, :]), reads=[t_pos], writes=[t_kf])
            C1 = 6.28125
            C2 = 2 * math.pi - C1
            S.op("dve", lambda e: e.scalar_tensor_tensor(out=ANG[P6, :], in0=kf[P6, :], scalar=-C1, in1=ANG[P6, :],
                                                         op0=ALU.mult, op1=ALU.add), reads=[t_kf], writes=[t_ang])
            S.op("dve", lambda e: e.scalar_tensor_tensor(out=ANG[P6, :], in0=kf[P6, :], scalar=-C2, in1=ANG[P6, :],
                                                         op0=ALU.mult, op1=ALU.add), reads=[t_kf], writes=[t_ang])
            S.op("dve", lambda e: e.tensor_single_scalar(out=kf[P6, :], in_=ANG[P6, :], scalar=-math.pi, op=ALU.is_lt),
                 reads=[t_ang], writes=[t_kf])
            S.op("dve", lambda e: e.scalar_tensor_tensor(out=ANG[P6, :], in0=kf[P6, :], scalar=2 * math.pi,
                                                         in1=ANG[P6, :], op0=ALU.mult, op1=ALU.add),
                 reads=[t_kf], writes=[t_ang])
            S.op("dve", lambda e: e.tensor_scalar(out=ANG[P6, :], in0=ANG[P6, :], scalar1=-3.1415925, scalar2=3.1415925,
                                                  op0=ALU.max, op1=ALU.min), writes=[t_ang])
            S.op("act", lambda e: e.activation(out=TRIG[P6, :], in_=ANG[P6, :], func=AF.Sin), reads=[t_ang],
                 writes=[t_TRIG])
            S.barrier()
            checkpoint(1)

            S.dma("sp", lambda e: e.dma_start(out=SWT[:, :], in_=cst["c_swt"].rearrange("p h r q -> p (h r q)")),
                  writes=[t_kf])
            S.op("dve", lambda e: e.memset(VA0[:, :, 64:128], 1.0), writes=[t_VA])
            SWA_Q0, SWA_K0, SWA_V0 = 416, 928, 1056
            WBkv = WB[:, 0:1024].rearrange("p (c s n) -> p c s n", c=8, s=2)
            for h in range(8):
                kv = h // 4
                if h % 4 == 0:
                    S.dma("pool", lambda e, kv=kv: e.dma_start(
                        out=WBkv[:, :, 0, :], in_=wview(e_w_in, SWA_K0 + kv * 64, SWA_K0 + (kv + 1) * 64)), writes=[t_WB])
                    S.dma("pool", lambda e, kv=kv: e.dma_start(
                        out=WBkv[:, :, 1, :], in_=wview(e_w_in, SWA_V0 + kv * 64, SWA_V0 + (kv + 1) * 64)), writes=[t_WB])
                    for b in range(NB):
                        ps = PS[4 + b % 2]
                        tps = t_PS[4 + b % 2]
                        proj_fm(ps, tps, lambda k: WB[:, k * 128:k * 128 + 64], 8,
                                lambda k, b=b: hT[:, k, b * 512:(b + 1) * 512], t_hT[4 * b:4 * b + 4], [t_WB], 64)
                        S.op("act", copy_op("act", KT[0:64, b * 512:(b + 1) * 512], ps[0:64, :]), writes=[tps, t_KT])
                        S.op("dve", copy_op("dve", KT[64:128, b * 512:(b + 1) * 512], ps[0:64, :]),
                             writes=[tps, t_KT])
                    for t8 in range(4):
                        ps = PS[4 + t8 % 2]
                        tps = t_PS[4 + t8 % 2]
                        for ti in range(8):
                            tt = t8 * 8 + ti
                            for k in range(8):
                                S.op("pe", lambda e, k=k, tt=tt, ti=ti, ps=ps: e.matmul(
                                    ps[:, ti * 64:(ti + 1) * 64], lhsT=hT[:, k, tt * 128:(tt + 1) * 128],
                                    rhs=WB[:, k * 128 + 64:k * 128 + 128], start=(k == 0), stop=(k == 7)),
                                    reads=[t_hT[tt], t_WB], writes=[tps])
                        eng = evac_engine()
                        S.op(eng, copy_op(eng, VA0[:, t8 * 8:(t8 + 1) * 8, 0:64],
                                          ps[:, :].rearrange("p (t d) -> p t d", t=8)), writes=[tps, t_VA])
                if h % 2 == 0:
                    S.dma("pool", lambda e, h=h: e.dma_start(
                        out=WA[:, 0:1024].rearrange("p (c n) -> p c n", c=8),
                        in_=wview(e_w_in, SWA_Q0 + h * 64, SWA_Q0 + (h + 2) * 64)), writes=[t_WA])
                    for b in range(NB):
                        ps = PS[4 + b % 2]
                        tps = t_PS[4 + b % 2]
                        proj_fm(ps, tps, lambda k: WA[:, k * 128:(k + 1) * 128], 8,
                                lambda k, b=b: hT[:, k, b * 512:(b + 1) * 512], t_hT[4 * b:4 * b + 4], [t_WA], 128)
                        eng = evac_engine()
                        S.op(eng, copy_op(eng, QT[0:128, b * 512:(b + 1) * 512], ps[0:128, :], scale=0.125),
                             writes=[tps, t_QT[b]])
                groups = []
                for g4 in range(NB):
                    n0 = 4 * g4
                    kbs = []
                    for j in range(max(0, n0 - 1), n0 + 4):
                        qa, qb = max(j, n0), min(j + 1, n0 + 3)
                        ta = 0 if j >= n0 else 128
                        tb = 256 if j + 1 <= n0 + 3 else 128
                        kbs.append((j, (qa - n0) * 128, (qb - n0 + 1) * 128,
                                    SWT[:, h * 256 + ta:h * 256 + tb]))
                    groups.append((g4 * 512, 512, kbs, g4))
                c = 4 + h // 2
                po = (h % 2) * 64

                def out_fn(g, c=c, po=po):
                    return AOT[po:po + 64, c, g * 512:(g + 1) * 512], [t_AOT[c][g]]
                attention(64, groups, None, 1.0, lambda j: VA0[:, j, :], out_fn,
                          den_add=esink[64:128, h:h + 1], fp32_tables=True, rd_extra=[t_sm, t_kf],
                          pf_bufs=PFw, skip_gc=True, r0=64 * (h % 2))
            S.barrier()
            checkpoint(2)

            W416 = WA[:, 0:8 * 416].rearrange("p (c n) -> p c n", c=8)
            S.dma("pool", lambda e: e.dma_start(out=W416, in_=wview(e_w_in, 0, 416)), writes=[t_WA])
            WROT = WB[:, 0:8 * 96].rearrange("p (c n) -> p c n", c=8)
            S.op("dve", lambda e: e.memset(WROT[:, :, 0:64], 0.0), writes=[t_WB])
            S.op("dve", lambda e: e.tensor_scalar(out=WROT[:, :, 64:80], in0=W416[:, :, 400:416], scalar1=-1.0,
                                                  scalar2=None, op0=ALU.mult), reads=[t_WA], writes=[t_WB])
            S.op("dve", lambda e: e.tensor_copy(out=WROT[:, :, 80:96], in_=W416[:, :, 384:400]), reads=[t_WA],
                 writes=[t_WB])
            for b in range(NB):
                bs = slice(b * 512, (b + 1) * 512)
                hsrc = lambda k, b=b: hT[:, k, b * 512:(b + 1) * 512]
                ht = t_hT[4 * b:4 * b + 4]
                proj_fm(PS[0], t_PS[0], lambda k: W416[:, k, 0:128], 8, hsrc, ht, [t_WA], 128)
                proj_fm(PS[1], t_PS[1], lambda k: W416[:, k, 128:256], 8, hsrc, ht, [t_WA], 128)
                proj_fm(PS[2], t_PS[2], lambda k: W416[:, k, 256:384], 8, hsrc, ht, [t_WA], 128)
                proj_fm(PS[3], t_PS[3], lambda k: W416[:, k, 320:416], 8, hsrc, ht, [t_WA], 96)
                proj_fm(PS[4], t_PS[4], lambda k: WROT[:, k, :], 8, hsrc, ht, [t_WB], 96)
                for n in range(3):
                    S.op("act", lambda e, n=n: e.activation(out=ET[n][:], in_=PS[n][:], func=AF.Square),
                         writes=[t_PS[n], t_ET[n]])
                S.op("pe", lambda e: e.matmul(PS[5][:, :], lhsT=onesf[:], rhs=ET[0][:], start=True, stop=False),
                     reads=[t_ET[0], t_cst], writes=[t_PS[5]])
                S.op("pe", lambda e: e.matmul(PS[5][:, :], lhsT=onesf[:], rhs=ET[1][:], start=False, stop=True),
                     reads=[t_ET[1], t_cst], writes=[t_PS[5]])
                S.op("pe", lambda e: e.matmul(PS[6][:, :], lhsT=onesf[:], rhs=ET[2][:], start=True, stop=True),
                     reads=[t_ET[2], t_cst], writes=[t_PS[6]])
                for (pi, n_, n) in ((5, 256.0, 0), (6, 128.0, 1)):
                    S.op("act", lambda e, pi=pi, n_=n_, n=n: e.activation(out=ET[n][:], in_=PS[pi][:], func=AF.Ln,
                                                                    bias=epsb[:, 0:1], scale=1.0 / n_),
                         reads=[t_eps], writes=[t_PS[pi], t_ET[n]])
                    S.op("act", lambda e, n=n: e.activation(out=ET[n][:], in_=ET[n][:], func=AF.Exp, scale=-0.5),
                         writes=[t_ET[n]])
                for (pi, chunk, gsc, n) in ((0, 0, gq[:, 0:1], 0), (1, 1, gq[:, 1:2], 0), (2, 2, gkv[:, 0:1], 1)):
                    S.op("dve", lambda e, pi=pi, chunk=chunk, gsc=gsc, n=n, bs=bs: e.scalar_tensor_tensor(
                        out=LATt[:, chunk, bs], in0=PS[pi][:], scalar=gsc, in1=ET[n][:], op0=ALU.mult, op1=ALU.mult),
                        reads=[t_ET[n], t_sm], writes=[t_PS[pi], t_LAT[b]])
                S.op("dve", lambda e, bs=bs: e.tensor_tensor(out=RC[0][64:96, :], in0=PS[3][64:96, :], in1=TRIG[64:96, bs],
                                                         op=ALU.mult), reads=[t_TRIG], writes=[t_PS[3], t_RC[0]])
                S.op("dve", lambda e, bs=bs: e.tensor_tensor(out=RC[1][64:96, :], in0=PS[4][64:96, :], in1=TRIG[96:128, bs],
                                                         op=ALU.mult), reads=[t_TRIG], writes=[t_PS[4], t_RC[1]])
                S.op("dve", lambda e, bs=bs: e.tensor_tensor(out=KT[64:96, bs], in0=RC[0][64:96, :], in1=RC[1][64:96, :],
                                                         op=ALU.add), reads=[t_RC[0], t_RC[1]], writes=[t_KTaug])
            S.barrier()
            checkpoint(3)

            WQ = WA[:, 0:2 * 768].rearrange("p (c n) -> p c n", c=2)
            S.dma("pool", lambda e: e.dma_start(out=WQ, in_=wview(e_w_q_up, 0, 768)), writes=[t_WA])
            WQR = WA[:, 1536:1536 + 2 * 768].rearrange("p (c n) -> p c n", c=2)
            WKV = WB[:, 0:1024]
            S.dma("pool", lambda e: e.dma_start(out=WKV, in_=e_w_kv_up[:, :]), writes=[t_WB])
            S.op("dve", lambda e: e.memset(WA[:, 1536:1536 + 2 * 768], 0.0), writes=[t_WA])
            for c2 in range(2):
                src = WQ[:, c2, :].rearrange("p (h d) -> p h d", h=8)
                dst = WQR[:, c2, :].rearrange("p (h d) -> p h d", h=8)
                S.op("dve", lambda e, src=src, dst=dst: e.tensor_scalar(out=dst[:, :, 64:80], in0=src[:, :, 80:96],
                                                                  scalar1=-1.0, scalar2=None, op0=ALU.mult),
                     writes=[t_WA])
                S.op("dve", lambda e, src=src, dst=dst: e.tensor_copy(out=dst[:, :, 80:96], in_=src[:, :, 64:80]),
                     writes=[t_WA])
            mla_scale = 96.0 ** -0.5
            for h in range(8):
                for b in range(NB):
                    ps = PS[4 + b % 2]
                    tps = t_PS[4 + b % 2]
                    S.op("pe", lambda e, b=b, h=h, ps=ps: e.matmul(ps[0:64, :], lhsT=WKV[:, h * 128:h * 128 + 64],
                                                             rhs=LATt[:, 2, b * 512:(b + 1) * 512], start=True, stop=True),
                         reads=[t_LAT[b], t_WB], writes=[tps])
                    eng = evac_engine()
                    S.op(eng, copy_op(eng, KT[0:64, b * 512:(b + 1) * 512], ps[0:64, :]), writes=[tps, t_KT])
                for t8 in range(4):
                    ps = PS[4 + t8 % 2]
                    tps = t_PS[4 + t8 % 2]
                    for ti in range(8):
                        tt = t8 * 8 + ti
                        S.op("pe", lambda e, tt=tt, ti=ti, h=h, ps=ps: e.matmul(
                            ps[:, ti * 64:(ti + 1) * 64], lhsT=LATt[:, 2, tt * 128:(tt + 1) * 128],
                            rhs=WKV[:, h * 128 + 64:h * 128 + 128], start=True, stop=True),
                            reads=[t_LAT[tt // 4], t_WB], writes=[tps])
                    eng = evac_engine()
                    S.op(eng, copy_op(eng, VA0[:, t8 * 8:(t8 + 1) * 8, 0:64],
                                      ps[:, :].rearrange("p (t d) -> p t d", t=8)), writes=[tps, t_VA])
                for b in range(NB):
                    bs = slice(b * 512, (b + 1) * 512)
                    p1, tp1 = PS[4], t_PS[4]
                    p2, tp2 = PS[5], t_PS[5]
                    for k in range(2):
                        S.op("pe", lambda e, k=k, h=h, bs=bs: e.matmul(p1[0:96, :], lhsT=WQ[:, k, h * 96:(h + 1) * 96],
                                                                rhs=LATt[:, k, bs], start=(k == 0), stop=(k == 1)),
                             reads=[t_LAT[b], t_WA], writes=[tp1])
                    for k in range(2):
                        S.op("pe", lambda e, k=k, h=h, bs=bs: e.matmul(p2[0:96, :], lhsT=WQR[:, k, h * 96:(h + 1) * 96],
                                                                rhs=LATt[:, k, bs], start=(k == 0), stop=(k == 1)),
                             reads=[t_LAT[b], t_WA], writes=[tp2])
                    S.op("act", lambda e, bs=bs: e.copy(out=QT[0:64, bs], in_=p1[0:64, :]),
                         writes=[tp1, t_QT[b]])
                    S.op("dve", lambda e, bs=bs: e.tensor_tensor(out=RC[0][64:96, :], in0=p1[64:96, :], in1=TRIG[64:96, bs],
                                                             op=ALU.mult), reads=[t_TRIG], writes=[tp1, t_RC[0]])
                    S.op("dve", lambda e, bs=bs: e.tensor_tensor(out=RC[1][64:96, :], in0=p2[64:96, :], in1=TRIG[96:128, bs],
                                                             op=ALU.mult), reads=[t_TRIG], writes=[tp2, t_RC[1]])
                    S.op("dve", lambda e, bs=bs: e.tensor_tensor(out=QT[64:96, bs], in0=RC[0][64:96, :], in1=RC[1][64:96, :],
                                                             op=ALU.add), reads=[t_RC[0], t_RC[1]], writes=[t_QT[b]])
                c = h // 2
                po = (h % 2) * 64

                def out_fn(g, c=c, po=po):
                    return AOT[po:po + 64, c, g * 512:(g + 1) * 512], [t_AOT[c][g]]
                attention(96, dense_groups(), None, mla_scale, lambda j: VA0[:, j, :], out_fn)

            checkpoint(4)
            checkpoint(5)
            out_phase(e_w_out, x_d, None, last_layer=False, next_gain_d=(o_g_in if do1 else None),
                      gate=(e_w_in, 1184))
            checkpoint(6)

        if do1:
            VA4 = at("VA4", [128, 32, 4, 128], BF16, o_big)
            if not do0:
                load_gain(o_g_in)
                phase_A(x1_d)
                S.barrier()
            Q0, K0, V0, F0, G0 = 0, 1024, 2048, 3072, 3088
            WF = WB[:, 0:128].rearrange("p (c n) -> p c n", c=8)
            S.dma("pool", lambda e: e.dma_start(out=WF, in_=wview(o_w_in, F0, F0 + 16)), writes=[t_WB])
            NL = at("NL", [128, 32, 16], F32, o_wa)
            trif = at("trif", [128, 128], F32, o_wa + 2048)
            identf = at("identf", [128, 128], F32, o_wa + 2560)
            CTf = at("CTf", [16, S_LEN], F32, o_big)
            r1 = at("r1", [16, S_LEN], F32, o_big + 16384)
            CS = at("CS", [128, S_LEN], BF16, o_trig)
            bfb = sb("bfb", [128, 16], F32)
            Ct = sb("Ct", [128, 32, 16], F32)
            Rs = sb("Rs", [128, 16], F32)
            zt = sb("zt", [128, 16], F32)
            t_bfb, t_NL, t_Ct, t_Rs, t_zt = S.tok("bfb"), S.toks(NT, "NL"), S.toks(NT, "Ct"), S.tok("Rs"), S.tok("zt")
            t_c1 = S.tok("cst1")
            S.dma("sp", lambda e: e.dma_start(out=trif[:], in_=cst["c_tri"][:, :]), writes=[t_c1])
            S.dma("sp", lambda e: e.dma_start(out=identf[:], in_=cst["c_ident"][:, :]), writes=[t_c1])
            S.dma("sp", lambda e: e.dma_start(out=bfb[:], in_=bass.AP(o_b_f.tensor, 0, [[0, 128], [1, 16]])),
                  writes=[t_bfb])
            Tt = at("Tt", [128, 32, 16], F32, o_wa + 3072)
            Pfx = at("Pfx", [128, 32, 16], F32, o_wa + 5120)
            t_NLa, t_Tt, t_Pfx = S.tok("NLa"), S.tok("Tt"), S.tok("Pfx")
            pf_, tpf_ = PS[4], t_PS[4]
            for tt in range(NT):
                for k in range(8):
                    S.op("pe", lambda e, k=k, tt=tt: e.matmul(pf_[:, tt * 16:(tt + 1) * 16],
                                                          lhsT=hT[:, k, tt * 128:(tt + 1) * 128],
                                                          rhs=WF[:, k, :], start=(k == 0), stop=(k == 7)),
                         reads=[t_hT[tt], t_WB], writes=[tpf_])
            S.op("dve", lambda e: e.tensor_tensor(out=NL[:, :, :], in0=pf_[:, :].rearrange("p (t h) -> p t h", t=32),
                                                  in1=bass.AP(bfb, 0, [[16, 128], [0, 32], [1, 16]]), op=ALU.add),
                 reads=[t_bfb], writes=[tpf_, t_NLa])
            S.op("act", lambda e: e.activation(out=NL[:, :, :], in_=NL[:, :, :], func=AF.Exp, scale=-1.0),
                 writes=[t_NLa])
            S.op("act", lambda e: e.activation(out=NL[:, :, :], in_=NL[:, :, :], func=AF.Ln, bias=epsb[:, 1:2],
                                               scale=1.0), reads=[t_eps], writes=[t_NLa])
            pT_, tpT_ = PS[5], t_PS[5]
            pC_, tpC_ = PS[6], t_PS[6]
            for tt in range(NT):
                S.op("pe", lambda e, tt=tt: e.matmul(pT_[:, tt * 16:(tt + 1) * 16], lhsT=onesf[:], rhs=NL[:, tt, :],
                                                     start=True, stop=True),
                     reads=[t_NLa, t_cst], writes=[tpT_])
            for tt in range(NT):
                S.op("pe", lambda e, tt=tt: e.matmul(pC_[:, tt * 16:(tt + 1) * 16], lhsT=trif[:], rhs=NL[:, tt, :],
                                                     start=True, stop=True),
                     reads=[t_NLa, t_c1], writes=[tpC_])
            S.op("dve", lambda e: e.tensor_copy(out=Tt[:, :, :], in_=pT_[:, :].rearrange("p (t h) -> p t h", t=32)),
                 writes=[tpT_, t_Tt])
            S.op("dve", lambda e: e.memset(Pfx[:, 0, :], 0.0), writes=[t_Pfx])
            for tt in range(1, NT):
                S.op("dve", lambda e, tt=tt: e.tensor_tensor(out=Pfx[:, tt, :], in0=Pfx[:, tt - 1, :],
                                                           in1=Tt[:, tt - 1, :], op=ALU.add),
                     reads=[t_Tt], writes=[t_Pfx])
            S.op("dve", lambda e: e.tensor_tensor(out=Ct[:, :, :], in0=pC_[:, :].rearrange("p (t h) -> p t h", t=32),
                                                  in1=Pfx[:, :, :], op=ALU.add),
                 reads=[t_Pfx], writes=[tpC_] + list(t_Ct))
            t_CS, t_r1, t_ctf = S.tok("CS"), S.tok("r1"), S.tok("ctf")
            for g4 in range(8):
                ps = PS[6]
                tps = t_PS[6]
                for ti in range(4):
                    tt = g4 * 4 + ti
                    S.op("pe", lambda e, tt=tt, ti=ti: e.matmul(ps[0:16, ti * 128:(ti + 1) * 128], lhsT=Ct[:, tt, :],
                                                            rhs=identf[:], start=True, stop=True),
                         reads=[t_Ct[tt], t_c1], writes=[tps])
                S.op("dve", lambda e, g4=g4: e.tensor_scalar(out=CTf[:, g4 * 512:(g4 + 1) * 512], in0=ps[0:16, :],
                                                          scalar1=-1.0, scalar2=None, op0=ALU.mult),
                     writes=[tps, t_ctf])
            tmpb = at("tmpb", [16, S_LEN], BF16, o_aot)
            t_tmpb = S.tok("tmpb")
            S.op("dve", lambda e: e.tensor_copy(out=CS[0:16, :], in_=CTf[:, :]), reads=[t_ctf], writes=[t_CS])
            S.op("dve", lambda e: e.tensor_tensor(out=r1[:, :], in0=CTf[:, :], in1=CS[0:16, :], op=ALU.subtract),
                 reads=[t_ctf, t_CS], writes=[t_r1])
            S.op("dve", lambda e: e.tensor_copy(out=tmpb[:, :], in_=r1[:, :]), reads=[t_r1], writes=[t_tmpb])
            S.op("dve", lambda e: e.tensor_copy(out=CS[32:48, :], in_=tmpb[:, :]), reads=[t_tmpb], writes=[t_CS])
            S.op("dve", lambda e: e.tensor_tensor(out=r1[:, :], in0=r1[:, :], in1=tmpb[:, :], op=ALU.subtract),
                 reads=[t_tmpb], writes=[t_r1])
            S.op("dve", lambda e: e.tensor_copy(out=CS[64:80, :], in_=r1[:, :]), reads=[t_r1], writes=[t_CS])
            S.barrier()
            checkpoint(7)
            if 'g' in DBG:
                S.op("pe", lambda e: e.matmul(PS[6][0:64, 0:16], lhsT=hT[:, 0, 0:64], rhs=hT[:, 0, 0:16],
                                              start=True, stop=True), reads=[t_hT[0]], writes=[t_PS[6]])
            if 'a' not in DBG:
                S.op("dve", lambda e: e.memset(KT[64:67, :], 1.0), writes=[t_KTaug])
            WBqk = WB[:, 0:1024].rearrange("p (c s n) -> p c s n", c=8, s=2)
            checkpoint(71)

            KT2 = at("KT2", [128, S_LEN], BF16, o_wa)
            KTb = [KT, KT2]
            t_KTb = [[S.tok("ktb0"), S.tok("ktb0aug")], [S.tok("ktb1"), S.tok("ktb1aug")]]
            S.op("dve", lambda e: e.memset(KT2[64:67, :], 1.0), writes=[t_KTb[1][1]])
            t_KTb[0][1] = t_KTaug
            WVb = at("WVb", [128, 1024], BF16, o_rc + 2048)
            t_WQb, t_WKb, t_WVb = S.tok("wqb"), S.tok("wkb"), S.tok("wvb")
            t_VAp = S.toks(2, "vap")
            S.op("dve", lambda e: e.memset(VA4[:, :, 0:2, 64:128], 1.0), writes=[t_VAp[0], t_VA])
            S.op("dve", lambda e: e.memset(VA4[:, :, 2:4, 64:128], 1.0), writes=[t_VAp[1], t_VA])
            rot = {"n": 0}

            def bank():
                rot["n"] += 1
                return 4 + rot["n"] % 2

            def load_wk(h):
                S.dma("pool", lambda e: e.dma_start(out=WBqk[:, :, 1, :],
                                                    in_=wview(o_w_in, K0 + h * 64, K0 + (h + 1) * 64)),
                      writes=[t_WKb])

            def load_wq(h):
                S.dma("pool", lambda e: e.dma_start(out=WBqk[:, :, 0, :],
                                                    in_=wview(o_w_in, Q0 + h * 64, Q0 + (h + 1) * 64)),
                      writes=[t_WQb])

            def load_wv(p):
                S.dma("pool", lambda e: e.dma_start(out=WVb[:, :].rearrange("p (c n) -> p c n", c=8),
                                                    in_=wview(o_w_in, V0 + p * 128, V0 + (p + 1) * 128)),
                      writes=[t_WVb])

            def k_block(h, b):
                pb = bank()
                proj_fm(PS[pb], t_PS[pb], lambda k: WB[:, k * 128 + 64:(k + 1) * 128], 8,
                        lambda k: hT[:, k, b * 512:(b + 1) * 512], t_hT[4 * b:4 * b + 4], [t_WKb], 64)
                S.op("dve", copy_op("dve", KTb[h % 2][0:64, b * 512:(b + 1) * 512], PS[pb][0:64, :]),
                     writes=[t_PS[pb], t_KTb[h % 2][0]])

            def q_block(h, b):
                pb = bank()
                both = h + 1 < 16
                M = 128 if both else 64
                proj_fm(PS[pb], t_PS[pb], lambda k: WB[:, k * 128:k * 128 + M], 8,
                        lambda k: hT[:, k, b * 512:(b + 1) * 512], t_hT[4 * b:4 * b + 4],
                        [t_WQb, t_WKb] if both else [t_WQb], M)
                S.op("dve", copy_op("dve", QT[0:64, b * 512:(b + 1) * 512], PS[pb][0:64, :], scale=0.125),
                     writes=[t_PS[pb], t_QT[b]])
                if both:
                    S.op("dve", copy_op("dve", KTb[(h + 1) % 2][0:64, b * 512:(b + 1) * 512], PS[pb][64:128, :]),
                         writes=[t_PS[pb], t_KTb[(h + 1) % 2][0]])

            def v_tiles(p, t2):
                pb = bank()
                ps = PS[pb]
                s0 = 2 * (p % 2)
                for ti in range(2):
                    tt = t2 * 2 + ti
                    for k in range(8):
                        S.op("pe", lambda e, k=k, tt=tt, ti=ti: e.matmul(
                            ps[:, ti * 128:(ti + 1) * 128], lhsT=hT[:, k, tt * 128:(tt + 1) * 128],
                            rhs=WVb[:, k * 128:(k + 1) * 128], start=(k == 0), stop=(k == 7)),
                            reads=[t_hT[tt], t_WVb], writes=[t_PS[pb]])
                for ti in range(2):
                    tt = t2 * 2 + ti
                    S.op("dve", copy_op("dve", VA4[:, tt, s0:s0 + 2, 0:64],
                                        ps[:, ti * 128:(ti + 1) * 128].rearrange("p (h d) -> p h d", h=2)),
                         writes=[t_PS[pb], t_VAp[p % 2]])

            load_wv(0)
            for t2 in range(16):
                v_tiles(0, t2)
            load_wk(0)
            for b in range(NB):
                k_block(0, b)
            for h in range(16):
                hh = h % 4
                load_wq(h)
                for r_ in range(3):
                    S.dma("sp", lambda e, h=h, r_=r_: e.dma_start(out=QT[64 + r_:65 + r_, :],
                                                                 in_=CS[32 * r_ + h:32 * r_ + h + 1, :]),
                          reads=[t_CS], writes=[t_QTaug])
                bgl = []
                if h + 1 < 16:
                    load_wk(h + 1)
                if h % 2 == 1 and h + 1 < 16:
                    load_wv((h + 1) // 2)
                    bgl += [(lambda h=h, t2=t2: v_tiles((h + 1) // 2, t2)) for t2 in range(16)]
                c = h // 2
                po = (h % 2) * 64

                def out_fn(g, c=c, po=po):
                    return AOT[po:po + 64, c, g * 512:(g + 1) * 512], [t_AOT[c][g]]
                attention(67, dense_groups(), lambda j, h=h: Ct[:, j, h:h + 1], 1.0,
                          lambda j, hh=hh: VA4[:, j, hh, :], out_fn, rd_extra=t_Ct,
                          KTt=(KTb[h % 2], t_KTb[h % 2]), pre_group=(lambda g, h=h: q_block(h, g)),
                          bg=bgl, va_tok=t_VAp[(h // 2) % 2], one_rc=True)

            checkpoint(8)
            S.barrier()
            checkpoint(9)
            out_phase(o_w_out, x1_d, (t_x1 if mode == "full" else None), last_layer=True, gate=(o_w_in, G0))
            S.wait_all("sp", t_out)
        else:
            S.wait_all("sp", t_x1)


    def checkpoint(k):
        if stop == k:
            raise _Stop()

    try:
        _layers()
    except _Stop:
        pass
    S.emit()
    return nc


_CACHE = {}


def _get(mode):
    if mode not in _CACHE:
        _CACHE[mode] = build_program(mode)
    return _CACHE[mode]


L0_KEYS = ["e_g_in", "e_w_in", "e_g_q_a", "e_w_q_up", "e_g_kv_a", "e_w_kv_up", "e_sinks", "e_w_out"]
L1_KEYS = ["o_g_in", "o_w_in", "o_b_f", "o_w_out"]


def _maps(inputs, n, mode, x1=None):
    consts = _constants()
    maps = []
    for b in range(n):
        m = dict(consts)
        if mode in ("full", "l0"):
            m["x"] = np.ascontiguousarray(inputs["x"][b])
            m["positions"] = np.ascontiguousarray(inputs["positions"][b]).astype(np.int32)
            for k in L0_KEYS:
                m[k] = np.ascontiguousarray(inputs[k][0])
        if mode in ("full", "l1"):
            for k in L1_KEYS:
                m[k] = np.ascontiguousarray(inputs[k][0])
            m["g_final"] = np.ascontiguousarray(inputs["g_final"])
        if mode == "l1":
            m["x1"] = np.ascontiguousarray(x1[b])
        maps.append(m)
    return maps


def kernel(**inputs):
    n = inputs["x"].shape[0]
    inputs = {k: np.asarray(v) for k, v in inputs.items()}
    nc = _get("full")
    res = run_bass_kernel_spmd(nc, _maps(inputs, n, "full"), core_ids=list(range(n)))
    return np.stack([np.asarray(r["out"]) for r in res.results], axis=0).astype(np.float32)
```

```python
import math
import os
DBG = os.environ.get('KDBG', '')
import numpy as np
import concourse.bass as bass
import concourse.mybir as mybir
from concourse.bass_utils import run_bass_kernel_spmd

F32 = mybir.dt.float32
BF16 = mybir.dt.bfloat16
I32 = mybir.dt.int32
AF = mybir.ActivationFunctionType
ALU = mybir.AluOpType

S_LEN = 4096
D = 1024
NT = 32
NB = 8
EPS = 1e-6
SEM_ROT = 12000


class Tok:
    __slots__ = ("name", "w", "r", "dsem")

    def __init__(self, name=""):
        self.name = name
        self.w = None
        self.r = {}
        self.dsem = None


class _Rec:
    def __init__(self):
        self.call = None

    def __getattr__(self, name):
        def f(*a, **k):
            assert self.call is None
            self.call = (name, a, k)
            return self
        return f


def _freeze(fn):
    r = _Rec()
    fn(r)
    assert r.call is not None
    return r.call


class Sched:
    ENGS = ("pe", "act", "dve", "pool", "sp")

    def __init__(self, nc):
        self.nc = nc
        self.sems = []
        self.semeng = {}
        self.ops = {e: [] for e in self.ENGS}
        self.esem = {}
        self.ecnt = {}
        self.seen = {e: {} for e in self.ENGS}
        self.semval = {}
        self.unsig = {e: False for e in self.ENGS}
        self.noself = {"pe"}
        for e in ("pe", "act", "dve", "pool"):
            self._new_esem(e)

    def _alloc(self, name, eng=None):
        h = self.nc.alloc_semaphore(name=name)
        self.sems.append(h)
        sid = len(self.sems) - 1
        self.semval[sid] = 0
        self.semeng[sid] = eng
        return sid

    def _new_esem(self, e):
        self.esem[e] = self._alloc(f"s_{e}_{len(self.sems)}", e)
        self.ecnt[e] = 0

    def tok(self, name=""):
        return Tok(name)

    def toks(self, n, name=""):
        return [Tok(f"{name}{i}") for i in range(n)]

    def _collect(self, e, reads, writes):
        need = {}

        def add(s, v):
            if need.get(s, 0) < v:
                need[s] = v
        for t in reads:
            if t.w is not None:
                add(*t.w)
        for t in writes:
            if t.w is not None:
                add(*t.w)
            for s, v in t.r.items():
                add(s, v)
        waits = []
        seen = self.seen[e]
        for s, v in need.items():
            if e in self.noself and self.semeng[s] == e:
                continue
            if seen.get(s, 0) >= v:
                continue
            seen[s] = v
            waits.append((s, v))
        return waits

    def _mark(self, s, val, reads, writes):
        for t in reads:
            if t.r.get(s, 0) < val:
                t.r[s] = val
        for t in writes:
            t.w = (s, val)
            t.r = {}

    def op(self, e, fn, reads=(), writes=(), sig=True):
        waits = self._collect(e, reads, writes)
        if sig and self.ecnt[e] >= SEM_ROT and not self.unsig[e]:
            self._new_esem(e)
        self.unsig[e] = not sig
        s = self.esem[e]
        val = self.ecnt[e] + 1
        if sig:
            self.ecnt[e] = val
            self.semval[s] = val
        self.ops[e].append((waits, _freeze(fn), (s, 1) if sig else None))
        self._mark(s, val, reads, writes)

    def barrier(self):
        cur = [(s, v) for s, v in self.semval.items() if v > 0]
        for e in self.ENGS:
            waits = []
            for s, v in cur:
                if (self.semeng[s] == e and e in self.noself) or self.seen[e].get(s, 0) >= v:
                    continue
                self.seen[e][s] = v
                waits.append((s, v))
            self.ops[e].append((waits, None, None))

    def dma(self, q, fn, reads=(), writes=(), owner=None):
        waits = self._collect(q, reads, writes)
        if owner is None:
            owner = writes[0] if writes else reads[0]
        if owner.dsem is None or self.semval[owner.dsem] >= SEM_ROT * 2:
            owner.dsem = self._alloc(f"d_{len(self.sems)}")
        s = owner.dsem
        self.semval[s] += 16
        val = self.semval[s]
        self.ops[q].append((waits, _freeze(fn), (s, 16)))
        self._mark(s, val, reads, writes)

    def wait_all(self, e, toks):
        waits = self._collect(e, [], toks)
        self.ops[e].append((waits, None, None))

    def emit(self):
        nc = self.nc
        sems = self.sems
        ops = self.ops

        def replay(eng, lst):
            for waits, fn, inc in lst:
                for s, v in waits:
                    eng.wait_ge(sems[s], v)
                if fn is None:
                    continue
                name, a, k = fn
                ins = getattr(eng, name)(*a, **k)
                if inc is not None:
                    ins.then_inc(sems[inc[0]], inc[1])

        with nc.Block() as block:
            @block.tensor
            def _(eng):
                replay(eng, ops["pe"])

            @block.scalar
            def _(eng):
                replay(eng, ops["act"])

            @block.vector
            def _(eng):
                replay(eng, ops["dve"])

            @block.gpsimd
            def _(eng):
                replay(eng, ops["pool"])

            @block.sync
            def _(eng):
                replay(eng, ops["sp"])


def _constants():
    c = {}
    c["c_ident"] = np.eye(128, dtype=np.float32)
    k = np.arange(128)[:, None]
    q = np.arange(128)[None, :]
    c["c_tri"] = (k <= q).astype(np.float32)
    c["c_ones"] = np.ones((128, 128), np.float32)
    qq = np.arange(512)[None, None, :]
    kk = np.arange(128)[:, None, None]
    rr = np.arange(4)[None, :, None]
    c["c_mask"] = ((rr * 128 + kk) <= qq).astype(np.float32)
    slopes = 2.0 ** (-8.0 * (np.arange(8, dtype=np.float64) + 1.0) / 8)
    t = np.zeros((128, 8, 2, 128), np.float64)
    kq = np.arange(128)[:, None]
    qv = np.arange(128)[None, :]
    for h in range(8):
        dist0 = 128 + qv - kq
        t[:, h, 1, :] = np.where(dist0 < 128, np.exp(-slopes[h] * dist0), 0.0)
        dist1 = qv - kq
        t[:, h, 0, :] = np.where(dist1 >= 0, np.exp(-slopes[h] * np.maximum(dist1, 0)), 0.0)
    c["c_swt"] = t.astype(np.float32)
    invf = 1.0 / (10000.0 ** (np.arange(0, 32, 2, dtype=np.float32) / 32))
    v = np.zeros((128, 2), np.float32)
    for p in range(64, 128):
        v[p, 0] = invf[(p - 64) % 16]
        v[p, 1] = (math.pi / 2) if p < 96 else 0.0
    c["c_rope"] = v
    return c


CONST_SHAPES = {"c_ident": [128, 128], "c_tri": [128, 128], "c_ones": [128, 128],
                "c_mask": [128, 4, 512], "c_swt": [128, 8, 2, 128], "c_rope": [128, 2]}


class _Stop(Exception):
    pass


def build_program(mode="full", stop=0):
    nc = bass.Bass("TRN2", target_bir_lowering=False)
    S = Sched(nc)
    do0 = mode in ("full", "l0")
    do1 = mode in ("full", "l1")

    def din(name, shape, dt=F32):
        return nc.dram_tensor(name, shape, dt, kind="ExternalInput").ap()

    cst = {k: din(k, v) for k, v in CONST_SHAPES.items()}
    if do0:
        x_d = din("x", [S_LEN, D])
        pos_d = din("positions", [S_LEN], I32)
        e_g_in = din("e_g_in", [D])
        e_w_in = din("e_w_in", [D, 2208])
        e_g_q_a = din("e_g_q_a", [256])
        e_w_q_up = din("e_w_q_up", [256, 768])
        e_g_kv_a = din("e_g_kv_a", [128])
        e_w_kv_up = din("e_w_kv_up", [128, 1024])
        e_sinks = din("e_sinks", [8])
        e_w_out = din("e_w_out", [D, D])
    if do1:
        o_g_in = din("o_g_in", [D])
        o_w_in = din("o_w_in", [D, 4112])
        o_b_f = din("o_b_f", [16])
        o_w_out = din("o_w_out", [D, D])
        g_final = din("g_final", [D])
        out_d = nc.dram_tensor("out", [S_LEN, D], F32, kind="ExternalOutput").ap()
    if mode == "full":
        x1_d = nc.dram_tensor("x1s", [S_LEN, D], F32).ap()
    elif mode == "l0":
        x1_d = nc.dram_tensor("x1", [S_LEN, D], F32, kind="ExternalOutput").ap()
    else:
        x1_d = din("x1", [S_LEN, D])
    t_x1 = S.toks(NT, "x1d")
    t_x1own = S.tok("x1own")
    t_outown = S.toks(2, "outown")
    t_out = S.toks(NT, "outd")

    def region(nbytes):
        st, _ = nc.bump_sbuf(nbytes)
        return st

    def at(name, shape, dt, off):
        return nc.alloc_sbuf_tensor_at(name, shape, dt, offset=off)

    def sb(name, shape, dt):
        return nc.alloc_sbuf_tensor(name, shape, dt)

    hT = sb("hT", [128, 8, S_LEN], BF16)
    t_hT = S.toks(NT, "hT")
    o_aot = region(65536)
    AOT = at("AOT", [128, 8, S_LEN], BF16, o_aot)
    t_AOT = [[S.tok(f"aot{c}_{b}") for b in range(NB)] for c in range(8)]
    o_big = region(32768)
    BIG = at("BIG", [128, 32 * 4 * 128], BF16, o_big)
    t_VA = S.tok("VA")
    t_LAT = S.toks(NB, "LAT")
    o_trig = region(8192)
    TRIG = at("TRIG", [128, S_LEN], BF16, o_trig)
    t_TRIG = S.tok("TRIG")
    o_pool = region(19456)
    QT = at("QT", [128, S_LEN], BF16, o_pool)
    t_QT = S.toks(NB, "QT")
    t_QTaug = S.tok("QTaug")
    KT = at("KT", [128, S_LEN], BF16, o_pool + 8192)
    t_KT = S.tok("KT")
    t_KTaug = S.tok("KTaug")
    NPT = 5
    PT = [at(f"pt{i}", [128, 512], BF16, o_pool + 16384 + 1024 * i) for i in range(3)]
    PT += [sb(f"ptx{i}", [128, 512], BF16) for i in range(NPT - 3)]
    t_PT = S.toks(NPT, "pt")
    XT = [at(f"xt{i}", [128, D], F32, o_pool + 4096 * i) for i in range(3)]
    t_XT = S.toks(3, "xt")
    XB = [at(f"xb{i}", [128, D], BF16, o_pool + 12288 + 2048 * i) for i in range(2)]
    t_XB = S.toks(2, "xb")
    ET = [at(f"et{i}", [128, 512], F32, o_pool + 2048 * i) for i in range(3)]
    t_ET = S.toks(3, "et")
    o_pf = region(1024)
    PF = [at(f"pf{i}", [128, 128], F32, o_pf + 512 * i) for i in range(2)]
    t_PF = S.toks(2, "pf")
    o_rc = region(4096)
    RC = [at(f"rc{i}", [128, 512], F32, o_rc + 2048 * i) for i in range(2)]
    t_RC = S.toks(2, "rc")
    GB = at("GB", [128, D], F32, o_rc)
    t_GB = S.tok("GB")
    o_wa = region(8192)
    WA = at("WA", [128, 3328], BF16, o_wa)
    t_WA = S.tok("WA")
    WB = sb("WB", [128, 1024], BF16)
    t_WB = S.tok("WB")
    MASK = sb("MASK", [128, 128], BF16)
    t_MASK = S.tok("MASK")
    identb = sb("identb", [128, 128], BF16)
    onesf = sb("onesf", [128, 128], F32)
    t_cst = S.tok("cst")
    stat = sb("stat", [128, 8], F32)
    t_statS = S.toks(2, "stat")
    epsb = sb("epsb", [128, 2], F32)
    t_eps = S.tok("eps")

    PS = [nc.alloc_psum_tensor(f"ps{i}", [128, 512], F32) for i in range(7)]
    t_PS = S.toks(7, "ps")
    PST = nc.alloc_psum_tensor("pst", [128, 8, 128], BF16)
    t_PST = S.tok("pst")

    S.dma("sp", lambda e: e.dma_start(out=onesf[:], in_=cst["c_ones"][:, :]), writes=[t_cst])
    t_cstb = S.tok("cstb")
    S.dma("pool", lambda e: e.dma_start(out=identb[:], in_=cst["c_ident"][:, :]), writes=[t_cstb])
    S.dma("pool", lambda e: e.dma_start(out=MASK[:], in_=cst["c_tri"][:, :]), writes=[t_MASK])
    S.op("dve", lambda e: e.memset(epsb[:, 0:1], EPS), writes=[t_eps])
    S.op("dve", lambda e: e.memset(epsb[:, 1:2], 1.0), writes=[t_eps])

    cnt = {"ev": 0}

    def evac_engine():
        cnt["ev"] += 1
        return "act" if cnt["ev"] % 2 else "dve"

    def copy_op(eng, out, in_, scale=None):
        if eng == "act":
            if scale is None:
                return lambda e: e.copy(out=out, in_=in_)
            return lambda e: e.mul(out=out, in_=in_, mul=scale)
        if scale is None:
            return lambda e: e.tensor_copy(out=out, in_=in_)
        return lambda e: e.tensor_scalar(out=out, in0=in_, scalar1=scale, scalar2=None, op0=ALU.mult)

    def load_gain(g_d):
        S.dma("sp", lambda e: e.dma_start(out=GB[:], in_=bass.AP(g_d.tensor, 0, [[0, 128], [1, D]])),
              writes=[t_GB])

    def rstd_from_ms(col, tst):
        S.op("act", lambda e: e.activation(out=stat[:, col + 1:col + 2], in_=stat[:, col:col + 1], func=AF.Ln,
                                           bias=epsb[:, 0:1], scale=1.0), reads=[t_eps], writes=[tst])
        S.op("act", lambda e: e.activation(out=stat[:, col + 1:col + 2], in_=stat[:, col + 1:col + 2],
                                           func=AF.Exp, scale=-0.5), writes=[tst])

    def norm_tile(xt, t_xt, out_ap, t_outs, junk_ap, t_junk, slot=0):
        c0 = 4 * slot
        tst = t_statS[slot]
        S.op("dve", lambda e: e.memset(stat[:, c0:c0 + 1], 0.0), writes=[tst])
        S.op("act", lambda e: e.activation(out=junk_ap, in_=xt[:], func=AF.Square, scale=1.0 / 32,
                                           accum_out=stat[:, c0:c0 + 1]), reads=[t_xt], writes=[t_junk, tst])
        rstd_from_ms(c0, tst)
        S.op("dve", lambda e: e.scalar_tensor_tensor(out=out_ap, in0=xt[:], scalar=stat[:, c0 + 1:c0 + 2], in1=GB[:],
                                                     op0=ALU.mult, op1=ALU.mult),
             reads=[t_xt, tst, t_GB], writes=list(t_outs))

    def to_hT_norm(xt, t_xt, tt):
        i = tt % 2
        norm_tile(xt, t_xt, XB[i][:], [t_XB[i]], XB[i][:], t_XB[i], slot=i)

    def to_hT_tr(tt):
        i = tt % 2
        for c in range(8):
            S.op("pe", lambda e, c=c: e.transpose(out=PST[:, c, :], in_=XB[i][:, c * 128:(c + 1) * 128],
                                                   identity=identb[:]),
                 reads=[t_XB[i], t_cstb], writes=[t_PST])
        S.op("dve", copy_op("dve", hT[:, :, tt * 128:(tt + 1) * 128], PST[:, :, :]), writes=[t_PST, t_hT[tt]])

    def phase_A(src_d):
        for tt in range(NT + 1):
            if tt < NT:
                i = tt % 3
                S.dma("sp", lambda e, tt=tt, i=i: e.dma_start(out=XT[i][:], in_=src_d[tt * 128:(tt + 1) * 128, :]),
                      writes=[t_XT[i]])
                to_hT_norm(XT[i], t_XT[i], tt)
            if tt >= 1:
                to_hT_tr(tt - 1)

    def wview(w_d, c0, c1):
        return w_d.rearrange("(c p) n -> p c n", p=128)[:, :, c0:c1]

    def proj_fm(ps, t_ps, w_ap_fn, nk, src_fn, src_toks, w_toks, M):
        for k in range(nk):
            S.op("pe", lambda e, k=k: e.matmul(ps[0:M, :], lhsT=w_ap_fn(k), rhs=src_fn(k),
                                               start=(k == 0), stop=(k == nk - 1)),
                 reads=list(src_toks) + list(w_toks), writes=[t_ps])

    def attention(KR, groups, bias_fn, scale, va_fn, out_fn, den_add=None, fp32_tables=False, rd_extra=(),
                  KTt=None, pre_group=None, bg=(), va_tok=None, one_rc=False, pf_bufs=None, skip_gc=False, r0=0,
                  sbanks=(0, 1, 6)):
        steps = []
        for gi, (q0, QW, kbs, gidx) in enumerate(groups):
            for n, kb in enumerate(kbs):
                if len(kb) == 3:
                    j, c0, tbl = kb
                    c1, t0, t1 = QW, c0, c0 + 128
                else:
                    j, c0, c1, tbl = kb
                    t0, t1 = c0, c1
                steps.append((gi, q0, QW, j, c0, c1, tbl, t0, t1, n == 0, n == len(kbs) - 1, gidx))

        SB_ = tuple(sbanks)
        NSB = len(SB_)
        LA = NSB - 1
        KTx, t_KTx = (KT, [t_KT, t_KTaug]) if KTt is None else KTt
        t_VAx = t_VA if va_tok is None else va_tok
        PFx = PF if pf_bufs is None else pf_bufs
        bg = list(bg)
        pending = []
        nsteps = len(steps)
        bg_stride = max(1, nsteps // (len(bg) + 1)) if bg else 0

        def emit_qk(i):
            gi, q0, QW, j, c0, c1, tbl, t0, t1, first, last, gidx = steps[i]
            sp = PS[SB_[i % NSB]]
            S.op("pe", lambda e: e.matmul(sp[:, c0:c1], lhsT=KTx[r0:r0 + KR, j * 128:(j + 1) * 128],
                                          rhs=QT[r0:r0 + KR, q0 + c0:q0 + c1], start=True, stop=True),
                 reads=list(t_KTx) + [t_QT[q0 // 512], t_QTaug], writes=[t_PS[SB_[i % NSB]]])

        first_idx = {}
        for i_, st_ in enumerate(steps):
            first_idx.setdefault(st_[0], i_)
        if pre_group is not None:
            pre_group(groups[0][3])
        for i0 in range(min(LA, len(steps))):
            emit_qk(i0)
        for i, (gi, q0, QW, j, c0, c1, tbl, t0, t1, first, last, gidx) in enumerate(steps):
            if i + LA < len(steps):
                emit_qk(i + LA)
            if pre_group is not None and i == first_idx[gi] + 1 and gi + 1 < len(groups):
                pre_group(groups[gi + 1][3])
            if bg and i > 0 and i % bg_stride == 0:
                bg.pop(0)()
            sp = PS[SB_[i % NSB]]
            tsp = t_PS[SB_[i % NSB]]
            pt = PT[i % NPT]
            tpt = t_PT[i % NPT]
            kw = {"scale": scale}
            b = bias_fn(j) if bias_fn is not None else None
            if b is not None:
                kw["bias"] = b
            if tbl is not None and fp32_tables:
                pf = PFx[i % 2]
                w = c1 - c0
                S.op("act", lambda e: e.activation(out=pf[:, 0:w], in_=sp[:, c0:c1], func=AF.Exp, **kw),
                     reads=list(rd_extra), writes=[tsp, t_PF[i % 2]])
                S.op("dve", lambda e: e.tensor_tensor(out=pt[:, c0:c1], in0=pf[:, 0:w], in1=tbl, op=ALU.mult),
                     reads=[t_PF[i % 2]] + list(rd_extra), writes=[tpt])
            else:
                S.op("act", lambda e: e.activation(out=pt[:, c0:c1], in_=sp[:, c0:c1], func=AF.Exp, **kw),
                     reads=list(rd_extra), writes=[tsp, tpt])
                if tbl is not None:
                    S.op("dve", lambda e: e.tensor_tensor(out=pt[:, t0:t1], in0=pt[:, t0:t1], in1=tbl, op=ALU.mult),
                         reads=[t_MASK], writes=[tpt])
            op_ = PS[2 + gi % 2]
            top = t_PS[2 + gi % 2]
            mkw = {"skip_group_check": True} if skip_gc else {}
            S.op("pe", lambda e: e.matmul(op_[:, c0:c1], lhsT=va_fn(j), rhs=pt[:, c0:c1], start=first, stop=last,
                                          **mkw),
                 reads=[tpt, t_VAx], writes=[top])
            if last:
                rc = RC[0 if one_rc else gi % 2]
                trc = t_RC[0 if one_rc else gi % 2]
                o_ap, o_toks = out_fn(gidx)
                if den_add is not None:
                    S.op("dve", lambda e: e.tensor_scalar(
                        out=rc[64:128, 0:QW], in0=op_[64:128, 0:QW], scalar1=den_add, scalar2=None, op0=ALU.add),
                        reads=list(rd_extra), writes=[top, trc])

                    def fin(rc=rc, trc=trc, op_=op_, top=top, QW=QW, o_ap=o_ap, o_toks=o_toks):
                        S.op("act", lambda e: e.activation(out=rc[64:128, 0:QW], in_=rc[64:128, 0:QW], func=AF.Ln),
                             writes=[trc])
                        S.op("act", lambda e: e.activation(out=rc[64:128, 0:QW], in_=rc[64:128, 0:QW], func=AF.Exp,
                                                           scale=-1.0), writes=[trc])
                        S.op("dve", lambda e: e.tensor_tensor(out=o_ap, in0=op_[0:64, 0:QW], in1=rc[64:128, 0:QW],
                                                              op=ALU.mult),
                             reads=[trc], writes=[top] + list(o_toks))
                    pending.append((i + 2, fin))
                else:
                    S.op("dve", lambda e: e.reciprocal(out=rc[64:128, 0:QW], in_=op_[64:128, 0:QW]),
                         writes=[top, trc])
                    S.op("dve", lambda e: e.tensor_tensor(out=o_ap, in0=op_[0:64, 0:QW], in1=rc[64:128, 0:QW],
                                                          op=ALU.mult),
                         reads=[trc], writes=[top] + list(o_toks))
            while pending and pending[0][0] <= i:
                pending.pop(0)[1]()
        while pending:
            pending.pop(0)[1]()
        while bg:
            bg.pop(0)()

    def _unused():
        pass

    def dense_groups():
        gs = []
        for g in range(NB):
            kbs = [(j, 0, None) for j in range(4 * g)] + [(4 * g + r, r * 128, MASK[:, :]) for r in range(4)]
            gs.append((g * 512, 512, kbs, g))
        return gs

    def gate_phase(w_in_d, goff):
        for c in range(8):
            S.dma("pool", lambda e, c=c: e.dma_start(
                out=WB[:, 0:1024].rearrange("p (c n) -> p c n", c=8),
                in_=wview(w_in_d, goff + c * 128, goff + (c + 1) * 128)), writes=[t_WB])
            for b in range(NB):
                ps = PS[4 + b % 2]
                tps = t_PS[4 + b % 2]
                proj_fm(ps, tps, lambda k: WB[:, k * 128:(k + 1) * 128], 8,
                        lambda k, b=b: hT[:, k, b * 512:(b + 1) * 512], t_hT[4 * b:4 * b + 4], [t_WB], 128)
                gt = PT[b % NPT]
                S.op("act", lambda e, gt=gt, ps=ps: e.activation(out=gt[:], in_=ps[:], func=AF.Silu),
                     writes=[tps, t_PT[b % NPT]])
                S.op("dve", lambda e, gt=gt, c=c, b=b: e.tensor_tensor(
                    out=AOT[:, c, b * 512:(b + 1) * 512], in0=AOT[:, c, b * 512:(b + 1) * 512], in1=gt[:],
                    op=ALU.mult), reads=[t_PT[b % NPT]], writes=[t_AOT[c][b]])

    def out_phase(w_out_d, res_d, t_res, last_layer, next_gain_d=None, gate=None):
        S.barrier()
        WO = at("WO_%d" % int(last_layer), [128, 8 * 1024], BF16, o_big)
        t_WO = S.tok("WO")
        WG = at("WG_%d" % int(last_layer), [128, 8 * 1024], BF16, o_big + 16384)
        t_WG = S.tok("WG")
        gw_d, goff = gate
        for c in range(8):
            S.dma("pool", lambda e, c=c: e.dma_start(
                out=WG[:, c * 1024:(c + 1) * 1024].rearrange("p (k n) -> p k n", k=8),
                in_=wview(gw_d, goff + c * 128, goff + (c + 1) * 128)), writes=[t_WG])
        S.dma("pool", lambda e: e.dma_start(out=WO[:].rearrange("p (c n) -> p c n", c=8),
                                            in_=wview(w_out_d, 0, D)), writes=[t_WO])
        gcnt = {"n": 0}

        def gate_chunk(c, b):
            gcnt["n"] += 1
            pb = 4 + gcnt["n"] % 2
            proj_fm(PS[pb], t_PS[pb], lambda k: WG[:, c * 1024 + k * 128:c * 1024 + (k + 1) * 128], 8,
                    lambda k: hT[:, k, b * 512:(b + 1) * 512], t_hT[4 * b:4 * b + 4], [t_WG], 128)
            gi_ = gcnt["n"] % NPT
            gt = PT[gi_]
            S.op("act", lambda e: e.activation(out=gt[:], in_=PS[pb][:], func=AF.Silu),
                 writes=[t_PS[pb], t_PT[gi_]])
            S.op("dve", lambda e: e.tensor_tensor(
                out=AOT[:, c, b * 512:(b + 1) * 512], in0=AOT[:, c, b * 512:(b + 1) * 512], in1=gt[:],
                op=ALU.mult), reads=[t_PT[gi_]], writes=[t_AOT[c][b]])

        for c in range(8):
            gate_chunk(c, 0)
        if last_layer:
            load_gain(g_final)
        elif next_gain_d is not None:
            load_gain(next_gain_d)
        t_x1o = S.toks(3, "x1own")
        t_outo = S.toks(3, "outown")
        def stage1(tt):
            i = tt % 3
            b = tt // 4
            pp = 2 * (tt % 2)
            S.dma("sp", lambda e: e.dma_start(out=XT[i][:], in_=res_d[tt * 128:(tt + 1) * 128, :]),
                  reads=[t_res[tt]] if t_res is not None else [], writes=[t_XT[i]])
            for half in range(2):
                ps = PS[pp + half]
                tps = t_PS[pp + half]
                for c in range(8):
                    S.op("pe", lambda e: e.matmul(
                        ps[:, :], lhsT=AOT[:, c, tt * 128:(tt + 1) * 128],
                        rhs=WO[:, c * 1024 + half * 512: c * 1024 + (half + 1) * 512],
                        start=(c == 0), stop=(c == 7)),
                        reads=[t_AOT[c][b], t_WO], writes=[tps])

        def stage1b(tt):
            i = tt % 3
            pp = 2 * (tt % 2)
            for half in range(2):
                ps = PS[pp + half]
                tps = t_PS[pp + half]
                S.op("dve", lambda e: e.tensor_tensor(
                    out=XT[i][:, half * 512:(half + 1) * 512], in0=ps[:, :], in1=XT[i][:, half * 512:(half + 1) * 512],
                    op=ALU.add), writes=[tps, t_XT[i]])

        def stage2a(tt):
            i = tt % 3
            if last_layer:
                norm_tile(XT[i], t_XT[i], XT[i][:], [t_XT[i]], XB[tt % 2][:], t_XB[tt % 2], slot=tt % 2)
                S.dma("sp", lambda e: e.dma_start(out=out_d[tt * 128:(tt + 1) * 128, :], in_=XT[i][:]),
                      reads=[t_XT[i]], writes=[t_out[tt]], owner=t_outo[i])
            else:
                S.dma("sp", lambda e: e.dma_start(out=x1_d[tt * 128:(tt + 1) * 128, :], in_=XT[i][:]),
                      reads=[t_XT[i]], writes=[t_x1[tt]], owner=t_x1o[i])
                if mode == "full":
                    to_hT_norm(XT[i], t_XT[i], tt)

        for tt in range(NT + 2):
            if tt < NT:
                stage1(tt)
            if 1 <= tt <= NT:
                stage2a(tt - 1)
            if tt < NT:
                stage1b(tt)
            if tt < NT and tt // 4 + 1 < NB:
                for c in (2 * (tt % 4), 2 * (tt % 4) + 1):
                    gate_chunk(c, tt // 4 + 1)
            if tt >= 2 and (not last_layer) and mode == "full":
                to_hT_tr(tt - 2)
        S.barrier()

    def _layers():
        if do0:
            VA0 = at("VA0", [128, 32, 128], BF16, o_big)
            LATt = at("LATt", [128, 3, S_LEN], BF16, o_big + 8192)
            SWT = at("SWT", [128, 2048], F32, o_big + 8192)
            PFw = [at(f"pfw{i}", [128, 256], F32, o_big + 16384 + 1024 * i) for i in range(2)]
            posi = at("posi", [128, S_LEN], I32, o_aot)
            kf = at("kf", [128, S_LEN], F32, o_aot + 16384)
            ANG = at("ANG", [128, S_LEN], F32, o_aot + 32768)
            t_pos, t_kf, t_ang = S.tok("posi"), S.tok("kf"), S.tok("ang")
            load_gain(e_g_in)
            phase_A(x_d)

            gq = sb("gq", [128, 2], F32)
            gkv = sb("gkv", [128, 1], F32)
            esink = sb("esink", [128, 8], F32)
            ropec = sb("ropec", [128, 2], F32)
            t_sm = S.tok("small0")
            for c2 in range(2):
                S.dma("sp", lambda e, c2=c2: e.dma_start(
                    out=gq[:, c2:c2 + 1], in_=e_g_q_a[c2 * 128:(c2 + 1) * 128].rearrange("(p o) -> p o", o=1)),
                    writes=[t_sm])
            S.dma("sp", lambda e: e.dma_start(out=gkv[:], in_=e_g_kv_a.rearrange("(p o) -> p o", o=1)), writes=[t_sm])
            S.dma("sp", lambda e: e.dma_start(out=esink[:], in_=bass.AP(e_sinks.tensor, 0, [[0, 128], [1, 8]])),
                  writes=[t_sm])
            S.dma("sp", lambda e: e.dma_start(out=ropec[:], in_=cst["c_rope"][:, :]), writes=[t_sm])
            S.op("act", lambda e: e.activation(out=esink[:], in_=esink[:], func=AF.Exp), writes=[t_sm])

            S.dma("sp", lambda e: e.dma_start(out=posi[64:128, :], in_=bass.AP(pos_d.tensor, 0, [[0, 64], [1, S_LEN]])),
                  writes=[t_pos])
            P6 = slice(64, 128)
            S.op("dve", lambda e: e.tensor_copy(out=ANG[P6, :], in_=posi[P6, :]), reads=[t_pos], writes=[t_ang])
            S.op("dve", lambda e: e.tensor_scalar(out=ANG[P6, :], in0=ANG[P6, :], scalar1=ropec[P6, 0:1],
                                                  scalar2=ropec[P6, 1:2], op0=ALU.mult, op1=ALU.add),
                 reads=[t_sm], writes=[t_ang])
            S.op("dve", lambda e: e.tensor_scalar(out=kf[P6, :], in0=ANG[P6, :], scalar1=1.0 / (2 * math.pi),
                                                  scalar2=0.5, op0=ALU.mult, op1=ALU.add),
                 reads=[t_ang], writes=[t_kf])
            S.op("dve", lambda e: e.tensor_copy(out=posi[P6, :], in_=kf[P6, :]), reads=[t_kf], writes=[t_pos])
            S.op("dve", lambda e: e.tensor_copy(out=kf[P6, :], in_=posi[P6, :]), reads=[t_pos], writes=[t_kf])
            C1 = 6.28125
            C2 = 2 * math.pi - C1
            S.op("dve", lambda e: e.scalar_tensor_tensor(out=ANG[P6, :], in0=kf[P6, :], scalar=-C1, in1=ANG[P6, :],
                                                         op0=ALU.mult, op1=ALU.add), reads=[t_kf], writes=[t_ang])
            S.op("dve", lambda e: e.scalar_tensor_tensor(out=ANG[P6, :], in0=kf[P6, :], scalar=-C2, in1=ANG[P6, :],
                                                         op0=ALU.mult, op1=ALU.add), reads=[t_kf], writes=[t_ang])
            S.op("dve", lambda e: e.tensor_single_scalar(out=kf[P6, :], in_=ANG[P6, :], scalar=-math.pi, op=ALU.is_lt),
                 reads=[t_ang], writes=[t_kf])
            S.op("dve", lambda e: e.scalar_tensor_tensor(out=ANG[P6, :], in0=kf[P6, :], scalar=2 * math.pi,
                                                         in1=ANG[P6, :], op0=ALU.mult, op1=ALU.add),
                 reads=[t_kf], writes=[t_ang])
            S.op("dve", lambda e: e.tensor_scalar(out=ANG[P6, :], in0=ANG[P6, :], scalar1=-3.1415925, scalar2=3.1415925,
                                                  op0=ALU.max, op1=ALU.min), writes=[t_ang])
            S.op("act", lambda e: e.activation(out=TRIG[P6, :], in_=ANG[P6, :], func=AF.Sin), reads=[t_ang],
                 writes=[t_TRIG])
            S.barrier()
            checkpoint(1)

            S.dma("sp", lambda e: e.dma_start(out=SWT[:, :], in_=cst["c_swt"].rearrange("p h r q -> p (h r q)")),
                  writes=[t_kf])
            S.op("dve", lambda e: e.memset(VA0[:, :, 64:128], 1.0), writes=[t_VA])
            SWA_Q0, SWA_K0, SWA_V0 = 416, 928, 1056
            WBkv = WB[:, 0:1024].rearrange("p (c s n) -> p c s n", c=8, s=2)
            for h in range(8):
                kv = h // 4
                if h % 4 == 0:
                    S.dma("pool", lambda e, kv=kv: e.dma_start(
                        out=WBkv[:, :, 0, :], in_=wview(e_w_in, SWA_K0 + kv * 64, SWA_K0 + (kv + 1) * 64)), writes=[t_WB])
                    S.dma("pool", lambda e, kv=kv: e.dma_start(
                        out=WBkv[:, :, 1, :], in_=wview(e_w_in, SWA_V0 + kv * 64, SWA_V0 + (kv + 1) * 64)), writes=[t_WB])
                    for b in range(NB):
                        ps = PS[4 + b % 2]
                        tps = t_PS[4 + b % 2]
                        proj_fm(ps, tps, lambda k: WB[:, k * 128:k * 128 + 64], 8,
                                lambda k, b=b: hT[:, k, b * 512:(b + 1) * 512], t_hT[4 * b:4 * b + 4], [t_WB], 64)
                        S.op("act", copy_op("act", KT[0:64, b * 512:(b + 1) * 512], ps[0:64, :]), writes=[tps, t_KT])
                        S.op("dve", copy_op("dve", KT[64:128, b * 512:(b + 1) * 512], ps[0:64, :]),
                             writes=[tps, t_KT])
                    for t8 in range(4):
                        ps = PS[4 + t8 % 2]
                        tps = t_PS[4 + t8 % 2]
                        for ti in range(8):
                            tt = t8 * 8 + ti
                            for k in range(8):
                                S.op("pe", lambda e, k=k, tt=tt, ti=ti, ps=ps: e.matmul(
                                    ps[:, ti * 64:(ti + 1) * 64], lhsT=hT[:, k, tt * 128:(tt + 1) * 128],
                                    rhs=WB[:, k * 128 + 64:k * 128 + 128], start=(k == 0), stop=(k == 7)),
                                    reads=[t_hT[tt], t_WB], writes=[tps])
                        eng = evac_engine()
                        S.op(eng, copy_op(eng, VA0[:, t8 * 8:(t8 + 1) * 8, 0:64],
                                          ps[:, :].rearrange("p (t d) -> p t d", t=8)), writes=[tps, t_VA])
                if h % 2 == 0:
                    S.dma("pool", lambda e, h=h: e.dma_start(
                        out=WA[:, 0:1024].rearrange("p (c n) -> p c n", c=8),
                        in_=wview(e_w_in, SWA_Q0 + h * 64, SWA_Q0 + (h + 2) * 64)), writes=[t_WA])
                    for b in range(NB):
                        ps = PS[4 + b % 2]
                        tps = t_PS[4 + b % 2]
                        proj_fm(ps, tps, lambda k: WA[:, k * 128:(k + 1) * 128], 8,
                                lambda k, b=b: hT[:, k, b * 512:(b + 1) * 512], t_hT[4 * b:4 * b + 4], [t_WA], 128)
                        eng = evac_engine()
                        S.op(eng, copy_op(eng, QT[0:128, b * 512:(b + 1) * 512], ps[0:128, :], scale=0.125),
                             writes=[tps, t_QT[b]])
                groups = []
                for g4 in range(NB):
                    n0 = 4 * g4
                    kbs = []
                    for j in range(max(0, n0 - 1), n0 + 4):
                        qa, qb = max(j, n0), min(j + 1, n0 + 3)
                        ta = 0 if j >= n0 else 128
                        tb = 256 if j + 1 <= n0 + 3 else 128
                        kbs.append((j, (qa - n0) * 128, (qb - n0 + 1) * 128,
                                    SWT[:, h * 256 + ta:h * 256 + tb]))
                    groups.append((g4 * 512, 512, kbs, g4))
                c = 4 + h // 2
                po = (h % 2) * 64

                def out_fn(g, c=c, po=po):
                    return AOT[po:po + 64, c, g * 512:(g + 1) * 512], [t_AOT[c][g]]
                attention(64, groups, None, 1.0, lambda j: VA0[:, j, :], out_fn,
                          den_add=esink[64:128, h:h + 1], fp32_tables=True, rd_extra=[t_sm, t_kf],
                          pf_bufs=PFw, skip_gc=True, r0=64 * (h % 2))
            S.barrier()
            checkpoint(2)

            W416 = WA[:, 0:8 * 416].rearrange("p (c n) -> p c n", c=8)
            S.dma("pool", lambda e: e.dma_start(out=W416, in_=wview(e_w_in, 0, 416)), writes=[t_WA])
            WROT = WB[:, 0:8 * 96].rearrange("p (c n) -> p c n", c=8)
            S.op("dve", lambda e: e.memset(WROT[:, :, 0:64], 0.0), writes=[t_WB])
            S.op("dve", lambda e: e.tensor_scalar(out=WROT[:, :, 64:80], in0=W416[:, :, 400:416], scalar1=-1.0,
                                                  scalar2=None, op0=ALU.mult), reads=[t_WA], writes=[t_WB])
            S.op("dve", lambda e: e.tensor_copy(out=WROT[:, :, 80:96], in_=W416[:, :, 384:400]), reads=[t_WA],
                 writes=[t_WB])
            for b in range(NB):
                bs = slice(b * 512, (b + 1) * 512)
                hsrc = lambda k, b=b: hT[:, k, b * 512:(b + 1) * 512]
                ht = t_hT[4 * b:4 * b + 4]
                proj_fm(PS[0], t_PS[0], lambda k: W416[:, k, 0:128], 8, hsrc, ht, [t_WA], 128)
                proj_fm(PS[1], t_PS[1], lambda k: W416[:, k, 128:256], 8, hsrc, ht, [t_WA], 128)
                proj_fm(PS[2], t_PS[2], lambda k: W416[:, k, 256:384], 8, hsrc, ht, [t_WA], 128)
                proj_fm(PS[3], t_PS[3], lambda k: W416[:, k, 320:416], 8, hsrc, ht, [t_WA], 96)
                proj_fm(PS[4], t_PS[4], lambda k: WROT[:, k, :], 8, hsrc, ht, [t_WB], 96)
                for n in range(3):
                    S.op("act", lambda e, n=n: e.activation(out=ET[n][:], in_=PS[n][:], func=AF.Square),
                         writes=[t_PS[n], t_ET[n]])
                S.op("pe", lambda e: e.matmul(PS[5][:, :], lhsT=onesf[:], rhs=ET[0][:], start=True, stop=False),
                     reads=[t_ET[0], t_cst], writes=[t_PS[5]])
                S.op("pe", lambda e: e.matmul(PS[5][:, :], lhsT=onesf[:], rhs=ET[1][:], start=False, stop=True),
                     reads=[t_ET[1], t_cst], writes=[t_PS[5]])
                S.op("pe", lambda e: e.matmul(PS[6][:, :], lhsT=onesf[:], rhs=ET[2][:], start=True, stop=True),
                     reads=[t_ET[2], t_cst], writes=[t_PS[6]])
                for (pi, n_, n) in ((5, 256.0, 0), (6, 128.0, 1)):
                    S.op("act", lambda e, pi=pi, n_=n_, n=n: e.activation(out=ET[n][:], in_=PS[pi][:], func=AF.Ln,
                                                                    bias=epsb[:, 0:1], scale=1.0 / n_),
                         reads=[t_eps], writes=[t_PS[pi], t_ET[n]])
                    S.op("act", lambda e, n=n: e.activation(out=ET[n][:], in_=ET[n][:], func=AF.Exp, scale=-0.5),
                         writes=[t_ET[n]])
                for (pi, chunk, gsc, n) in ((0, 0, gq[:, 0:1], 0), (1, 1, gq[:, 1:2], 0), (2, 2, gkv[:, 0:1], 1)):
                    S.op("dve", lambda e, pi=pi, chunk=chunk, gsc=gsc, n=n, bs=bs: e.scalar_tensor_tensor(
                        out=LATt[:, chunk, bs], in0=PS[pi][:], scalar=gsc, in1=ET[n][:], op0=ALU.mult, op1=ALU.mult),
                        reads=[t_ET[n], t_sm], writes=[t_PS[pi], t_LAT[b]])
                S.op("dve", lambda e, bs=bs: e.tensor_tensor(out=RC[0][64:96, :], in0=PS[3][64:96, :], in1=TRIG[64:96, bs],
                                                         op=ALU.mult), reads=[t_TRIG], writes=[t_PS[3], t_RC[0]])
                S.op("dve", lambda e, bs=bs: e.tensor_tensor(out=RC[1][64:96, :], in0=PS[4][64:96, :], in1=TRIG[96:128, bs],
                                                         op=ALU.mult), reads=[t_TRIG], writes=[t_PS[4], t_RC[1]])
                S.op("dve", lambda e, bs=bs: e.tensor_tensor(out=KT[64:96, bs], in0=RC[0][64:96, :], in1=RC[1][64:96, :],
                                                         op=ALU.add), reads=[t_RC[0], t_RC[1]], writes=[t_KTaug])
            S.barrier()
            checkpoint(3)

            WQ = WA[:, 0:2 * 768].rearrange("p (c n) -> p c n", c=2)
            S.dma("pool", lambda e: e.dma_start(out=WQ, in_=wview(e_w_q_up, 0, 768)), writes=[t_WA])
            WQR = WA[:, 1536:1536 + 2 * 768].rearrange("p (c n) -> p c n", c=2)
            WKV = WB[:, 0:1024]
            S.dma("pool", lambda e: e.dma_start(out=WKV, in_=e_w_kv_up[:, :]), writes=[t_WB])
            S.op("dve", lambda e: e.memset(WA[:, 1536:1536 + 2 * 768], 0.0), writes=[t_WA])
            for c2 in range(2):
                src = WQ[:, c2, :].rearrange("p (h d) -> p h d", h=8)
                dst = WQR[:, c2, :].rearrange("p (h d) -> p h d", h=8)
                S.op("dve", lambda e, src=src, dst=dst: e.tensor_scalar(out=dst[:, :, 64:80], in0=src[:, :, 80:96],
                                                                  scalar1=-1.0, scalar2=None, op0=ALU.mult),
                     writes=[t_WA])
                S.op("dve", lambda e, src=src, dst=dst: e.tensor_copy(out=dst[:, :, 80:96], in_=src[:, :, 64:80]),
                     writes=[t_WA])
            mla_scale = 96.0 ** -0.5
            for h in range(8):
                for b in range(NB):
                    ps = PS[4 + b % 2]
                    tps = t_PS[4 + b % 2]
                    S.op("pe", lambda e, b=b, h=h, ps=ps: e.matmul(ps[0:64, :], lhsT=WKV[:, h * 128:h * 128 + 64],
                                                             rhs=LATt[:, 2, b * 512:(b + 1) * 512], start=True, stop=True),
                         reads=[t_LAT[b], t_WB], writes=[tps])
                    eng = evac_engine()
                    S.op(eng, copy_op(eng, KT[0:64, b * 512:(b + 1) * 512], ps[0:64, :]), writes=[tps, t_KT])
                for t8 in range(4):
                    ps = PS[4 + t8 % 2]
                    tps = t_PS[4 + t8 % 2]
                    for ti in range(8):
                        tt = t8 * 8 + ti
                        S.op("pe", lambda e, tt=tt, ti=ti, h=h, ps=ps: e.matmul(
                            ps[:, ti * 64:(ti + 1) * 64], lhsT=LATt[:, 2, tt * 128:(tt + 1) * 128],
                            rhs=WKV[:, h * 128 + 64:h * 128 + 128], start=True, stop=True),
                            reads=[t_LAT[tt // 4], t_WB], writes=[tps])
                    eng = evac_engine()
                    S.op(eng, copy_op(eng, VA0[:, t8 * 8:(t8 + 1) * 8, 0:64],
                                      ps[:, :].rearrange("p (t d) -> p t d", t=8)), writes=[tps, t_VA])
                for b in range(NB):
                    bs = slice(b * 512, (b + 1) * 512)
                    p1, tp1 = PS[4], t_PS[4]
                    p2, tp2 = PS[5], t_PS[5]
                    for k in range(2):
                        S.op("pe", lambda e, k=k, h=h, bs=bs: e.matmul(p1[0:96, :], lhsT=WQ[:, k, h * 96:(h + 1) * 96],
                                                                rhs=LATt[:, k, bs], start=(k == 0), stop=(k == 1)),
                             reads=[t_LAT[b], t_WA], writes=[tp1])
                    for k in range(2):
                        S.op("pe", lambda e, k=k, h=h, bs=bs: e.matmul(p2[0:96, :], lhsT=WQR[:, k, h * 96:(h + 1) * 96],
                                                                rhs=LATt[:, k, bs], start=(k == 0), stop=(k == 1)),
                             reads=[t_LAT[b], t_WA], writes=[tp2])
                    S.op("act", lambda e, bs=bs: e.copy(out=QT[0:64, bs], in_=p1[0:64, :]),
                         writes=[tp1, t_QT[b]])
                    S.op("dve", lambda e, bs=bs: e.tensor_tensor(out=RC[0][64:96, :], in0=p1[64:96, :], in1=TRIG[64:96, bs],
                                                             op=ALU.mult), reads=[t_TRIG], writes=[tp1, t_RC[0]])
                    S.op("dve", lambda e, bs=bs: e.tensor_tensor(out=RC[1][64:96, :], in0=p2[64:96, :], in1=TRIG[96:128, bs],
                                                             op=ALU.mult), reads=[t_TRIG], writes=[tp2, t_RC[1]])
                    S.op("dve", lambda e, bs=bs: e.tensor_tensor(out=QT[64:96, bs], in0=RC[0][64:96, :], in1=RC[1][64:96, :],
                                                             op=ALU.add), reads=[t_RC[0], t_RC[1]], writes=[t_QT[b]])
                c = h // 2
                po = (h % 2) * 64

                def out_fn(g, c=c, po=po):
                    return AOT[po:po + 64, c, g * 512:(g + 1) * 512], [t_AOT[c][g]]
                attention(96, dense_groups(), None, mla_scale, lambda j: VA0[:, j, :], out_fn, sbanks=(0, 1, 6, 4))

            checkpoint(4)
            checkpoint(5)
            out_phase(e_w_out, x_d, None, last_layer=False, next_gain_d=(o_g_in if do1 else None),
                      gate=(e_w_in, 1184))
            checkpoint(6)

        if do1:
            VA4 = at("VA4", [128, 32, 4, 128], BF16, o_big)
            if not do0:
                load_gain(o_g_in)
                phase_A(x1_d)
                S.barrier()
            Q0, K0, V0, F0, G0 = 0, 1024, 2048, 3072, 3088
            WF = WB[:, 0:128].rearrange("p (c n) -> p c n", c=8)
            S.dma("pool", lambda e: e.dma_start(out=WF, in_=wview(o_w_in, F0, F0 + 16)), writes=[t_WB])
            NL = at("NL", [128, 32, 16], F32, o_wa)
            trif = at("trif", [128, 128], F32, o_wa + 2048)
            identf = at("identf", [128, 128], F32, o_wa + 2560)
            CTf = at("CTf", [16, S_LEN], F32, o_big)
            r1 = at("r1", [16, S_LEN], F32, o_big + 16384)
            CS = at("CS", [128, S_LEN], BF16, o_trig)
            bfb = sb("bfb", [128, 16], F32)
            Ct = sb("Ct", [128, 32, 16], F32)
            Rs = sb("Rs", [128, 16], F32)
            zt = sb("zt", [128, 16], F32)
            t_bfb, t_NL, t_Ct, t_Rs, t_zt = S.tok("bfb"), S.toks(NT, "NL"), S.toks(NT, "Ct"), S.tok("Rs"), S.tok("zt")
            t_c1 = S.tok("cst1")
            S.dma("sp", lambda e: e.dma_start(out=trif[:], in_=cst["c_tri"][:, :]), writes=[t_c1])
            S.dma("sp", lambda e: e.dma_start(out=identf[:], in_=cst["c_ident"][:, :]), writes=[t_c1])
            S.dma("sp", lambda e: e.dma_start(out=bfb[:], in_=bass.AP(o_b_f.tensor, 0, [[0, 128], [1, 16]])),
                  writes=[t_bfb])
            Tt = at("Tt", [128, 32, 16], F32, o_wa + 3072)
            Pfx = at("Pfx", [128, 32, 16], F32, o_wa + 5120)
            t_NLa, t_Tt, t_Pfx = S.tok("NLa"), S.tok("Tt"), S.tok("Pfx")
            pf_, tpf_ = PS[4], t_PS[4]
            for tt in range(NT):
                for k in range(8):
                    S.op("pe", lambda e, k=k, tt=tt: e.matmul(pf_[:, tt * 16:(tt + 1) * 16],
                                                          lhsT=hT[:, k, tt * 128:(tt + 1) * 128],
                                                          rhs=WF[:, k, :], start=(k == 0), stop=(k == 7)),
                         reads=[t_hT[tt], t_WB], writes=[tpf_])
            S.op("dve", lambda e: e.tensor_tensor(out=NL[:, :, :], in0=pf_[:, :].rearrange("p (t h) -> p t h", t=32),
                                                  in1=bass.AP(bfb, 0, [[16, 128], [0, 32], [1, 16]]), op=ALU.add),
                 reads=[t_bfb], writes=[tpf_, t_NLa])
            S.op("act", lambda e: e.activation(out=NL[:, :, :], in_=NL[:, :, :], func=AF.Exp, scale=-1.0),
                 writes=[t_NLa])
            S.op("act", lambda e: e.activation(out=NL[:, :, :], in_=NL[:, :, :], func=AF.Ln, bias=epsb[:, 1:2],
                                               scale=1.0), reads=[t_eps], writes=[t_NLa])
            pT_, tpT_ = PS[5], t_PS[5]
            pC_, tpC_ = PS[6], t_PS[6]
            for tt in range(NT):
                S.op("pe", lambda e, tt=tt: e.matmul(pT_[:, tt * 16:(tt + 1) * 16], lhsT=onesf[:], rhs=NL[:, tt, :],
                                                     start=True, stop=True),
                     reads=[t_NLa, t_cst], writes=[tpT_])
            for tt in range(NT):
                S.op("pe", lambda e, tt=tt: e.matmul(pC_[:, tt * 16:(tt + 1) * 16], lhsT=trif[:], rhs=NL[:, tt, :],
                                                     start=True, stop=True),
                     reads=[t_NLa, t_c1], writes=[tpC_])
            S.op("dve", lambda e: e.tensor_copy(out=Tt[:, :, :], in_=pT_[:, :].rearrange("p (t h) -> p t h", t=32)),
                 writes=[tpT_, t_Tt])
            S.op("dve", lambda e: e.memset(Pfx[:, 0, :], 0.0), writes=[t_Pfx])
            for tt in range(1, NT):
                S.op("dve", lambda e, tt=tt: e.tensor_tensor(out=Pfx[:, tt, :], in0=Pfx[:, tt - 1, :],
                                                           in1=Tt[:, tt - 1, :], op=ALU.add),
                     reads=[t_Tt], writes=[t_Pfx])
            S.op("dve", lambda e: e.tensor_tensor(out=Ct[:, :, :], in0=pC_[:, :].rearrange("p (t h) -> p t h", t=32),
                                                  in1=Pfx[:, :, :], op=ALU.add),
                 reads=[t_Pfx], writes=[tpC_] + list(t_Ct))
            t_CS, t_r1, t_ctf = S.tok("CS"), S.tok("r1"), S.tok("ctf")
            for g4 in range(8):
                ps = PS[6]
                tps = t_PS[6]
                for ti in range(4):
                    tt = g4 * 4 + ti
                    S.op("pe", lambda e, tt=tt, ti=ti: e.matmul(ps[0:16, ti * 128:(ti + 1) * 128], lhsT=Ct[:, tt, :],
                                                            rhs=identf[:], start=True, stop=True),
                         reads=[t_Ct[tt], t_c1], writes=[tps])
                S.op("dve", lambda e, g4=g4: e.tensor_scalar(out=CTf[:, g4 * 512:(g4 + 1) * 512], in0=ps[0:16, :],
                                                          scalar1=-1.0, scalar2=None, op0=ALU.mult),
                     writes=[tps, t_ctf])
            tmpb = at("tmpb", [16, S_LEN], BF16, o_aot)
            t_tmpb = S.tok("tmpb")
            S.op("dve", lambda e: e.tensor_copy(out=CS[0:16, :], in_=CTf[:, :]), reads=[t_ctf], writes=[t_CS])
            S.op("dve", lambda e: e.tensor_tensor(out=r1[:, :], in0=CTf[:, :], in1=CS[0:16, :], op=ALU.subtract),
                 reads=[t_ctf, t_CS], writes=[t_r1])
            S.op("dve", lambda e: e.tensor_copy(out=tmpb[:, :], in_=r1[:, :]), reads=[t_r1], writes=[t_tmpb])
            S.op("dve", lambda e: e.tensor_copy(out=CS[32:48, :], in_=tmpb[:, :]), reads=[t_tmpb], writes=[t_CS])
            S.op("dve", lambda e: e.tensor_tensor(out=r1[:, :], in0=r1[:, :], in1=tmpb[:, :], op=ALU.subtract),
                 reads=[t_tmpb], writes=[t_r1])
            S.op("dve", lambda e: e.tensor_copy(out=CS[64:80, :], in_=r1[:, :]), reads=[t_r1], writes=[t_CS])
            S.barrier()
            checkpoint(7)
            if 'g' in DBG:
                S.op("pe", lambda e: e.matmul(PS[6][0:64, 0:16], lhsT=hT[:, 0, 0:64], rhs=hT[:, 0, 0:16],
                                              start=True, stop=True), reads=[t_hT[0]], writes=[t_PS[6]])
            if 'a' not in DBG:
                S.op("dve", lambda e: e.memset(KT[64:67, :], 1.0), writes=[t_KTaug])
            WBqk = WB[:, 0:1024].rearrange("p (c s n) -> p c s n", c=8, s=2)
            checkpoint(71)

            KT2 = at("KT2", [128, S_LEN], BF16, o_wa)
            KTb = [KT, KT2]
            t_KTb = [[S.tok("ktb0"), S.tok("ktb0aug")], [S.tok("ktb1"), S.tok("ktb1aug")]]
            S.op("dve", lambda e: e.memset(KT2[64:67, :], 1.0), writes=[t_KTb[1][1]])
            t_KTb[0][1] = t_KTaug
            WVb = at("WVb", [128, 1024], BF16, o_rc + 2048)
            t_WQb, t_WKb, t_WVb = S.tok("wqb"), S.tok("wkb"), S.tok("wvb")
            t_VAp = S.toks(2, "vap")
            S.op("dve", lambda e: e.memset(VA4[:, :, 0:2, 64:128], 1.0), writes=[t_VAp[0], t_VA])
            S.op("dve", lambda e: e.memset(VA4[:, :, 2:4, 64:128], 1.0), writes=[t_VAp[1], t_VA])
            rot = {"n": 0}

            def bank():
                rot["n"] += 1
                return 4 + rot["n"] % 2

            def load_wk(h):
                S.dma("pool", lambda e: e.dma_start(out=WBqk[:, :, 1, :],
                                                    in_=wview(o_w_in, K0 + h * 64, K0 + (h + 1) * 64)),
                      writes=[t_WKb])

            def load_wq(h):
                S.dma("pool", lambda e: e.dma_start(out=WBqk[:, :, 0, :],
                                                    in_=wview(o_w_in, Q0 + h * 64, Q0 + (h + 1) * 64)),
                      writes=[t_WQb])

            def load_wv(p):
                S.dma("pool", lambda e: e.dma_start(out=WVb[:, :].rearrange("p (c n) -> p c n", c=8),
                                                    in_=wview(o_w_in, V0 + p * 128, V0 + (p + 1) * 128)),
                      writes=[t_WVb])

            def k_block(h, b):
                pb = bank()
                proj_fm(PS[pb], t_PS[pb], lambda k: WB[:, k * 128 + 64:(k + 1) * 128], 8,
                        lambda k: hT[:, k, b * 512:(b + 1) * 512], t_hT[4 * b:4 * b + 4], [t_WKb], 64)
                S.op("dve", copy_op("dve", KTb[h % 2][0:64, b * 512:(b + 1) * 512], PS[pb][0:64, :]),
                     writes=[t_PS[pb], t_KTb[h % 2][0]])

            def q_block(h, b):
                pb = bank()
                both = h + 1 < 16
                M = 128 if both else 64
                proj_fm(PS[pb], t_PS[pb], lambda k: WB[:, k * 128:k * 128 + M], 8,
                        lambda k: hT[:, k, b * 512:(b + 1) * 512], t_hT[4 * b:4 * b + 4],
                        [t_WQb, t_WKb] if both else [t_WQb], M)
                S.op("dve", copy_op("dve", QT[0:64, b * 512:(b + 1) * 512], PS[pb][0:64, :], scale=0.125),
                     writes=[t_PS[pb], t_QT[b]])
                if both:
                    S.op("dve", copy_op("dve", KTb[(h + 1) % 2][0:64, b * 512:(b + 1) * 512], PS[pb][64:128, :]),
                         writes=[t_PS[pb], t_KTb[(h + 1) % 2][0]])

            def v_tiles(p, t2):
                pb = bank()
                ps = PS[pb]
                s0 = 2 * (p % 2)
                for ti in range(2):
                    tt = t2 * 2 + ti
                    for k in range(8):
                        S.op("pe", lambda e, k=k, tt=tt, ti=ti: e.matmul(
                            ps[:, ti * 128:(ti + 1) * 128], lhsT=hT[:, k, tt * 128:(tt + 1) * 128],
                            rhs=WVb[:, k * 128:(k + 1) * 128], start=(k == 0), stop=(k == 7)),
                            reads=[t_hT[tt], t_WVb], writes=[t_PS[pb]])
                for ti in range(2):
                    tt = t2 * 2 + ti
                    S.op("dve", copy_op("dve", VA4[:, tt, s0:s0 + 2, 0:64],
                                        ps[:, ti * 128:(ti + 1) * 128].rearrange("p (h d) -> p h d", h=2)),
                         writes=[t_PS[pb], t_VAp[p % 2]])

            load_wv(0)
            for t2 in range(16):
                v_tiles(0, t2)
            load_wk(0)
            for b in range(NB):
                k_block(0, b)
            for h in range(16):
                hh = h % 4
                load_wq(h)
                for r_ in range(3):
                    S.dma("sp", lambda e, h=h, r_=r_: e.dma_start(out=QT[64 + r_:65 + r_, :],
                                                                 in_=CS[32 * r_ + h:32 * r_ + h + 1, :]),
                          reads=[t_CS], writes=[t_QTaug])
                bgl = []
                if h + 1 < 16:
                    load_wk(h + 1)
                if h % 2 == 1 and h + 1 < 16:
                    load_wv((h + 1) // 2)
                    bgl += [(lambda h=h, t2=t2: v_tiles((h + 1) // 2, t2)) for t2 in range(16)]
                c = h // 2
                po = (h % 2) * 64

                def out_fn(g, c=c, po=po):
                    return AOT[po:po + 64, c, g * 512:(g + 1) * 512], [t_AOT[c][g]]
                attention(67, dense_groups(), lambda j, h=h: Ct[:, j, h:h + 1], 1.0,
                          lambda j, hh=hh: VA4[:, j, hh, :], out_fn, rd_extra=t_Ct,
                          KTt=(KTb[h % 2], t_KTb[h % 2]), pre_group=(lambda g, h=h: q_block(h, g)),
                          bg=bgl, va_tok=t_VAp[(h // 2) % 2], one_rc=True)

            checkpoint(8)
            S.barrier()
            checkpoint(9)
            out_phase(o_w_out, x1_d, (t_x1 if mode == "full" else None), last_layer=True, gate=(o_w_in, G0))
            S.wait_all("sp", t_out)
        else:
            S.wait_all("sp", t_x1)


    def checkpoint(k):
        if stop == k:
            raise _Stop()

    try:
        _layers()
    except _Stop:
        pass
    S.emit()
    return nc


_CACHE = {}


def _get(mode):
    if mode not in _CACHE:
        _CACHE[mode] = build_program(mode)
    return _CACHE[mode]


L0_KEYS = ["e_g_in", "e_w_in", "e_g_q_a", "e_w_q_up", "e_g_kv_a", "e_w_kv_up", "e_sinks", "e_w_out"]
L1_KEYS = ["o_g_in", "o_w_in", "o_b_f", "o_w_out"]


def _maps(inputs, n, mode, x1=None):
    consts = _constants()
    maps = []
    for b in range(n):
        m = dict(consts)
        if mode in ("full", "l0"):
            m["x"] = np.ascontiguousarray(inputs["x"][b])
            m["positions"] = np.ascontiguousarray(inputs["positions"][b]).astype(np.int32)
            for k in L0_KEYS:
                m[k] = np.ascontiguousarray(inputs[k][0])
        if mode in ("full", "l1"):
            for k in L1_KEYS:
                m[k] = np.ascontiguousarray(inputs[k][0])
            m["g_final"] = np.ascontiguousarray(inputs["g_final"])
        if mode == "l1":
            m["x1"] = np.ascontiguousarray(x1[b])
        maps.append(m)
    return maps


def kernel(**inputs):
    n = inputs["x"].shape[0]
    inputs = {k: np.asarray(v) for k, v in inputs.items()}
    nc = _get("full")
    res = run_bass_kernel_spmd(nc, _maps(inputs, n, "full"), core_ids=list(range(n)))
    return np.stack([np.asarray(r["out"]) for r in res.results], axis=0).astype(np.float32)
```

```python
import math
import os
DBG = os.environ.get('KDBG', '')
import numpy as np
import concourse.bass as bass
import concourse.mybir as mybir
from concourse.bass_utils import run_bass_kernel_spmd

F32 = mybir.dt.float32
BF16 = mybir.dt.bfloat16
I32 = mybir.dt.int32
AF = mybir.ActivationFunctionType
ALU = mybir.AluOpType

S_LEN = 4096
D = 1024
NT = 32
NB = 8
EPS = 1e-6
SEM_ROT = 12000


class Tok:
    __slots__ = ("name", "w", "r", "dsem")

    def __init__(self, name=""):
        self.name = name
        self.w = None
        self.r = {}
        self.dsem = None


class _Rec:
    def __init__(self):
        self.call = None

    def __getattr__(self, name):
        def f(*a, **k):
            assert self.call is None
            self.call = (name, a, k)
            return self
        return f


def _freeze(fn):
    r = _Rec()
    fn(r)
    assert r.call is not None
    return r.call


class Sched:
    ENGS = ("pe", "act", "dve", "pool", "sp")

    def __init__(self, nc):
        self.nc = nc
        self.sems = []
        self.semeng = {}
        self.ops = {e: [] for e in self.ENGS}
        self.esem = {}
        self.ecnt = {}
        self.seen = {e: {} for e in self.ENGS}
        self.semval = {}
        self.unsig = {e: False for e in self.ENGS}
        self.noself = {"pe"}
        for e in ("pe", "act", "dve", "pool"):
            self._new_esem(e)

    def _alloc(self, name, eng=None):
        h = self.nc.alloc_semaphore(name=name)
        self.sems.append(h)
        sid = len(self.sems) - 1
        self.semval[sid] = 0
        self.semeng[sid] = eng
        return sid

    def _new_esem(self, e):
        self.esem[e] = self._alloc(f"s_{e}_{len(self.sems)}", e)
        self.ecnt[e] = 0

    def tok(self, name=""):
        return Tok(name)

    def toks(self, n, name=""):
        return [Tok(f"{name}{i}") for i in range(n)]

    def _collect(self, e, reads, writes):
        need = {}

        def add(s, v):
            if need.get(s, 0) < v:
                need[s] = v
        for t in reads:
            if t.w is not None:
                add(*t.w)
        for t in writes:
            if t.w is not None:
                add(*t.w)
            for s, v in t.r.items():
                add(s, v)
        waits = []
        seen = self.seen[e]
        for s, v in need.items():
            if e in self.noself and self.semeng[s] == e:
                continue
            if seen.get(s, 0) >= v:
                continue
            seen[s] = v
            waits.append((s, v))
        return waits

    def _mark(self, s, val, reads, writes):
        for t in reads:
            if t.r.get(s, 0) < val:
                t.r[s] = val
        for t in writes:
            t.w = (s, val)
            t.r = {}

    def op(self, e, fn, reads=(), writes=(), sig=True):
        waits = self._collect(e, reads, writes)
        if sig and self.ecnt[e] >= SEM_ROT and not self.unsig[e]:
            self._new_esem(e)
        self.unsig[e] = not sig
        s = self.esem[e]
        val = self.ecnt[e] + 1
        if sig:
            self.ecnt[e] = val
            self.semval[s] = val
        self.ops[e].append((waits, _freeze(fn), (s, 1) if sig else None))
        self._mark(s, val, reads, writes)

    def barrier(self):
        cur = [(s, v) for s, v in self.semval.items() if v > 0]
        for e in self.ENGS:
            waits = []
            for s, v in cur:
                if (self.semeng[s] == e and e in self.noself) or self.seen[e].get(s, 0) >= v:
                    continue
                self.seen[e][s] = v
                waits.append((s, v))
            self.ops[e].append((waits, None, None))

    def dma(self, q, fn, reads=(), writes=(), owner=None):
        waits = self._collect(q, reads, writes)
        if owner is None:
            owner = writes[0] if writes else reads[0]
        if owner.dsem is None or self.semval[owner.dsem] >= SEM_ROT * 2:
            owner.dsem = self._alloc(f"d_{len(self.sems)}")
        s = owner.dsem
        self.semval[s] += 16
        val = self.semval[s]
        self.ops[q].append((waits, _freeze(fn), (s, 16)))
        self._mark(s, val, reads, writes)

    def wait_all(self, e, toks):
        waits = self._collect(e, [], toks)
        self.ops[e].append((waits, None, None))

    def emit(self):
        nc = self.nc
        sems = self.sems
        ops = self.ops

        def replay(eng, lst):
            for waits, fn, inc in lst:
                for s, v in waits:
                    eng.wait_ge(sems[s], v)
                if fn is None:
                    continue
                name, a, k = fn
                ins = getattr(eng, name)(*a, **k)
                if inc is not None:
                    ins.then_inc(sems[inc[0]], inc[1])

        with nc.Block() as block:
            @block.tensor
            def _(eng):
                replay(eng, ops["pe"])

            @block.scalar
            def _(eng):
                replay(eng, ops["act"])

            @block.vector
            def _(eng):
                replay(eng, ops["dve"])

            @block.gpsimd
            def _(eng):
                replay(eng, ops["pool"])

            @block.sync
            def _(eng):
                replay(eng, ops["sp"])


def _constants():
    c = {}
    c["c_ident"] = np.eye(128, dtype=np.float32)
    k = np.arange(128)[:, None]
    q = np.arange(128)[None, :]
    c["c_tri"] = (k <= q).astype(np.float32)
    c["c_ones"] = np.ones((128, 128), np.float32)
    qq = np.arange(512)[None, None, :]
    kk = np.arange(128)[:, None, None]
    rr = np.arange(4)[None, :, None]
    c["c_mask"] = ((rr * 128 + kk) <= qq).astype(np.float32)
    slopes = 2.0 ** (-8.0 * (np.arange(8, dtype=np.float64) + 1.0) / 8)
    t = np.zeros((128, 8, 2, 128), np.float64)
    kq = np.arange(128)[:, None]
    qv = np.arange(128)[None, :]
    for h in range(8):
        dist0 = 128 + qv - kq
        t[:, h, 1, :] = np.where(dist0 < 128, np.exp(-slopes[h] * dist0), 0.0)
        dist1 = qv - kq
        t[:, h, 0, :] = np.where(dist1 >= 0, np.exp(-slopes[h] * np.maximum(dist1, 0)), 0.0)
    c["c_swt"] = t.astype(np.float32)
    invf = 1.0 / (10000.0 ** (np.arange(0, 32, 2, dtype=np.float32) / 32))
    v = np.zeros((128, 2), np.float32)
    for p in range(64, 128):
        v[p, 0] = invf[(p - 64) % 16]
        v[p, 1] = (math.pi / 2) if p < 96 else 0.0
    c["c_rope"] = v
    return c


CONST_SHAPES = {"c_ident": [128, 128], "c_tri": [128, 128], "c_ones": [128, 128],
                "c_mask": [128, 4, 512], "c_swt": [128, 8, 2, 128], "c_rope": [128, 2]}


class _Stop(Exception):
    pass


def build_program(mode="full", stop=0):
    nc = bass.Bass("TRN2", target_bir_lowering=False)
    S = Sched(nc)
    do0 = mode in ("full", "l0")
    do1 = mode in ("full", "l1")

    def din(name, shape, dt=F32):
        return nc.dram_tensor(name, shape, dt, kind="ExternalInput").ap()

    cst = {k: din(k, v) for k, v in CONST_SHAPES.items()}
    if do0:
        x_d = din("x", [S_LEN, D])
        pos_d = din("positions", [S_LEN], I32)
        e_g_in = din("e_g_in", [D])
        e_w_in = din("e_w_in", [D, 2208])
        e_g_q_a = din("e_g_q_a", [256])
        e_w_q_up = din("e_w_q_up", [256, 768])
        e_g_kv_a = din("e_g_kv_a", [128])
        e_w_kv_up = din("e_w_kv_up", [128, 1024])
        e_sinks = din("e_sinks", [8])
        e_w_out = din("e_w_out", [D, D])
    if do1:
        o_g_in = din("o_g_in", [D])
        o_w_in = din("o_w_in", [D, 4112])
        o_b_f = din("o_b_f", [16])
        o_w_out = din("o_w_out", [D, D])
        g_final = din("g_final", [D])
        out_d = nc.dram_tensor("out", [S_LEN, D], F32, kind="ExternalOutput").ap()
    if mode == "full":
        x1_d = nc.dram_tensor("x1s", [S_LEN, D], F32).ap()
    elif mode == "l0":
        x1_d = nc.dram_tensor("x1", [S_LEN, D], F32, kind="ExternalOutput").ap()
    else:
        x1_d = din("x1", [S_LEN, D])
    t_x1 = S.toks(NT, "x1d")
    t_x1own = S.tok("x1own")
    t_outown = S.toks(2, "outown")
    t_out = S.toks(NT, "outd")

    def region(nbytes):
        st, _ = nc.bump_sbuf(nbytes)
        return st

    def at(name, shape, dt, off):
        return nc.alloc_sbuf_tensor_at(name, shape, dt, offset=off)

    def sb(name, shape, dt):
        return nc.alloc_sbuf_tensor(name, shape, dt)

    hT = sb("hT", [128, 8, S_LEN], BF16)
    t_hT = S.toks(NT, "hT")
    o_aot = region(65536)
    AOT = at("AOT", [128, 8, S_LEN], BF16, o_aot)
    t_AOT = [[S.tok(f"aot{c}_{b}") for b in range(NB)] for c in range(8)]
    o_big = region(32768)
    BIG = at("BIG", [128, 32 * 4 * 128], BF16, o_big)
    t_VA = S.tok("VA")
    t_LAT = S.toks(NB, "LAT")
    o_trig = region(8192)
    TRIG = at("TRIG", [128, S_LEN], BF16, o_trig)
    t_TRIG = S.tok("TRIG")
    o_pool = region(19456)
    QT = at("QT", [128, S_LEN], BF16, o_pool)
    t_QT = S.toks(NB, "QT")
    t_QTaug = S.tok("QTaug")
    KT = at("KT", [128, S_LEN], BF16, o_pool + 8192)
    t_KT = S.tok("KT")
    t_KTaug = S.tok("KTaug")
    NPT = 5
    PT = [at(f"pt{i}", [128, 512], BF16, o_pool + 16384 + 1024 * i) for i in range(3)]
    PT += [sb(f"ptx{i}", [128, 512], BF16) for i in range(NPT - 3)]
    t_PT = S.toks(NPT, "pt")
    XT = [at(f"xt{i}", [128, D], F32, o_pool + 4096 * i) for i in range(3)]
    t_XT = S.toks(3, "xt")
    XB = [at(f"xb{i}", [128, D], BF16, o_pool + 12288 + 2048 * i) for i in range(2)]
    t_XB = S.toks(2, "xb")
    ET = [at(f"et{i}", [128, 512], F32, o_pool + 2048 * i) for i in range(3)]
    t_ET = S.toks(3, "et")
    o_pf = region(1024)
    PF = [at(f"pf{i}", [128, 128], F32, o_pf + 512 * i) for i in range(2)]
    t_PF = S.toks(2, "pf")
    o_rc = region(4096)
    RC = [at(f"rc{i}", [128, 512], F32, o_rc + 2048 * i) for i in range(2)]
    t_RC = S.toks(2, "rc")
    GB = at("GB", [128, D], F32, o_rc)
    t_GB = S.tok("GB")
    o_wa = region(8192)
    WA = at("WA", [128, 3328], BF16, o_wa)
    t_WA = S.tok("WA")
    WB = sb("WB", [128, 1024], BF16)
    t_WB = S.tok("WB")
    MASK = sb("MASK", [128, 128], BF16)
    t_MASK = S.tok("MASK")
    identb = sb("identb", [128, 128], BF16)
    onesf = sb("onesf", [128, 128], F32)
    t_cst = S.tok("cst")
    stat = sb("stat", [128, 8], F32)
    t_statS = S.toks(2, "stat")
    epsb = sb("epsb", [128, 2], F32)
    t_eps = S.tok("eps")

    PS = [nc.alloc_psum_tensor(f"ps{i}", [128, 512], F32) for i in range(7)]
    t_PS = S.toks(7, "ps")
    PST = nc.alloc_psum_tensor("pst", [128, 8, 128], BF16)
    t_PST = S.tok("pst")

    S.dma("sp", lambda e: e.dma_start(out=onesf[:], in_=cst["c_ones"][:, :]), writes=[t_cst])
    t_cstb = S.tok("cstb")
    S.dma("pool", lambda e: e.dma_start(out=identb[:], in_=cst["c_ident"][:, :]), writes=[t_cstb])
    S.dma("pool", lambda e: e.dma_start(out=MASK[:], in_=cst["c_tri"][:, :]), writes=[t_MASK])
    S.op("dve", lambda e: e.memset(epsb[:, 0:1], EPS), writes=[t_eps])
    S.op("dve", lambda e: e.memset(epsb[:, 1:2], 1.0), writes=[t_eps])

    cnt = {"ev": 0}

    def evac_engine():
        cnt["ev"] += 1
        return "act" if cnt["ev"] % 2 else "dve"

    def copy_op(eng, out, in_, scale=None):
        if eng == "act":
            if scale is None:
                return lambda e: e.copy(out=out, in_=in_)
            return lambda e: e.mul(out=out, in_=in_, mul=scale)
        if scale is None:
            return lambda e: e.tensor_copy(out=out, in_=in_)
        return lambda e: e.tensor_scalar(out=out, in0=in_, scalar1=scale, scalar2=None, op0=ALU.mult)

    def load_gain(g_d):
        S.dma("sp", lambda e: e.dma_start(out=GB[:], in_=bass.AP(g_d.tensor, 0, [[0, 128], [1, D]])),
              writes=[t_GB])

    def rstd_from_ms(col, tst):
        S.op("act", lambda e: e.activation(out=stat[:, col + 1:col + 2], in_=stat[:, col:col + 1], func=AF.Ln,
                                           bias=epsb[:, 0:1], scale=1.0), reads=[t_eps], writes=[tst])
        S.op("act", lambda e: e.activation(out=stat[:, col + 1:col + 2], in_=stat[:, col + 1:col + 2],
                                           func=AF.Exp, scale=-0.5), writes=[tst])

    def norm_tile(xt, t_xt, out_ap, t_outs, junk_ap, t_junk, slot=0):
        c0 = 4 * slot
        tst = t_statS[slot]
        S.op("dve", lambda e: e.memset(stat[:, c0:c0 + 1], 0.0), writes=[tst])
        S.op("act", lambda e: e.activation(out=junk_ap, in_=xt[:], func=AF.Square, scale=1.0 / 32,
                                           accum_out=stat[:, c0:c0 + 1]), reads=[t_xt], writes=[t_junk, tst])
        rstd_from_ms(c0, tst)
        S.op("dve", lambda e: e.scalar_tensor_tensor(out=out_ap, in0=xt[:], scalar=stat[:, c0 + 1:c0 + 2], in1=GB[:],
                                                     op0=ALU.mult, op1=ALU.mult),
             reads=[t_xt, tst, t_GB], writes=list(t_outs))

    def to_hT_norm(xt, t_xt, tt):
        i = tt % 2
        norm_tile(xt, t_xt, XB[i][:], [t_XB[i]], XB[i][:], t_XB[i], slot=i)

    def to_hT_tr(tt):
        i = tt % 2
        for c in range(8):
            S.op("pe", lambda e, c=c: e.transpose(out=PST[:, c, :], in_=XB[i][:, c * 128:(c + 1) * 128],
                                                   identity=identb[:]),
                 reads=[t_XB[i], t_cstb], writes=[t_PST])
        S.op("dve", copy_op("dve", hT[:, :, tt * 128:(tt + 1) * 128], PST[:, :, :]), writes=[t_PST, t_hT[tt]])

    def phase_A(src_d):
        for tt in range(NT + 1):
            if tt < NT:
                i = tt % 3
                S.dma("sp", lambda e, tt=tt, i=i: e.dma_start(out=XT[i][:], in_=src_d[tt * 128:(tt + 1) * 128, :]),
                      writes=[t_XT[i]])
                to_hT_norm(XT[i], t_XT[i], tt)
            if tt >= 1:
                to_hT_tr(tt - 1)

    def wview(w_d, c0, c1):
        return w_d.rearrange("(c p) n -> p c n", p=128)[:, :, c0:c1]

    def proj_fm(ps, t_ps, w_ap_fn, nk, src_fn, src_toks, w_toks, M):
        for k in range(nk):
            S.op("pe", lambda e, k=k: e.matmul(ps[0:M, :], lhsT=w_ap_fn(k), rhs=src_fn(k),
                                               start=(k == 0), stop=(k == nk - 1)),
                 reads=list(src_toks) + list(w_toks), writes=[t_ps])

    def attention(KR, groups, bias_fn, scale, va_fn, out_fn, den_add=None, fp32_tables=False, rd_extra=(),
                  KTt=None, pre_group=None, bg=(), va_tok=None, one_rc=False, pf_bufs=None, skip_gc=False, r0=0,
                  sbanks=(0, 1, 6)):
        steps = []
        for gi, (q0, QW, kbs, gidx) in enumerate(groups):
            for n, kb in enumerate(kbs):
                if len(kb) == 3:
                    j, c0, tbl = kb
                    c1, t0, t1 = QW, c0, c0 + 128
                else:
                    j, c0, c1, tbl = kb
                    t0, t1 = c0, c1
                steps.append((gi, q0, QW, j, c0, c1, tbl, t0, t1, n == 0, n == len(kbs) - 1, gidx))

        SB_ = tuple(sbanks)
        NSB = len(SB_)
        LA = NSB - 1
        KTx, t_KTx = (KT, [t_KT, t_KTaug]) if KTt is None else KTt
        t_VAx = t_VA if va_tok is None else va_tok
        PFx = PF if pf_bufs is None else pf_bufs
        bg = list(bg)
        pending = []
        nsteps = len(steps)
        bg_stride = max(1, nsteps // (len(bg) + 1)) if bg else 0

        def emit_qk(i):
            gi, q0, QW, j, c0, c1, tbl, t0, t1, first, last, gidx = steps[i]
            sp = PS[SB_[i % NSB]]
            S.op("pe", lambda e: e.matmul(sp[:, c0:c1], lhsT=KTx[r0:r0 + KR, j * 128:(j + 1) * 128],
                                          rhs=QT[r0:r0 + KR, q0 + c0:q0 + c1], start=True, stop=True),
                 reads=list(t_KTx) + [t_QT[q0 // 512], t_QTaug], writes=[t_PS[SB_[i % NSB]]])

        first_idx = {}
        for i_, st_ in enumerate(steps):
            first_idx.setdefault(st_[0], i_)
        if pre_group is not None:
            pre_group(groups[0][3])
        for i0 in range(min(LA, len(steps))):
            emit_qk(i0)
        for i, (gi, q0, QW, j, c0, c1, tbl, t0, t1, first, last, gidx) in enumerate(steps):
            if i + LA < len(steps):
                emit_qk(i + LA)
            if pre_group is not None and i == first_idx[gi] + 1 and gi + 1 < len(groups):
                pre_group(groups[gi + 1][3])
            if bg and i > 0 and i % bg_stride == 0:
                bg.pop(0)()
            sp = PS[SB_[i % NSB]]
            tsp = t_PS[SB_[i % NSB]]
            pt = PT[i % NPT]
            tpt = t_PT[i % NPT]
            kw = {"scale": scale}
            b = bias_fn(j) if bias_fn is not None else None
            if b is not None:
                kw["bias"] = b
            if tbl is not None and fp32_tables:
                pf = PFx[i % 2]
                w = c1 - c0
                S.op("act", lambda e: e.activation(out=pf[:, 0:w], in_=sp[:, c0:c1], func=AF.Exp, **kw),
                     reads=list(rd_extra), writes=[tsp, t_PF[i % 2]])
                S.op("dve", lambda e: e.tensor_tensor(out=pt[:, c0:c1], in0=pf[:, 0:w], in1=tbl, op=ALU.mult),
                     reads=[t_PF[i % 2]] + list(rd_extra), writes=[tpt])
            else:
                S.op("act", lambda e: e.activation(out=pt[:, c0:c1], in_=sp[:, c0:c1], func=AF.Exp, **kw),
                     reads=list(rd_extra), writes=[tsp, tpt])
                if tbl is not None:
                    S.op("dve", lambda e: e.tensor_tensor(out=pt[:, t0:t1], in0=pt[:, t0:t1], in1=tbl, op=ALU.mult),
                         reads=[t_MASK], writes=[tpt])
            op_ = PS[2 + gi % 2]
            top = t_PS[2 + gi % 2]
            mkw = {"skip_group_check": True} if skip_gc else {}
            S.op("pe", lambda e: e.matmul(op_[:, c0:c1], lhsT=va_fn(j), rhs=pt[:, c0:c1], start=first, stop=last,
                                          **mkw),
                 reads=[tpt, t_VAx], writes=[top])
            if last:
                rc = RC[0 if one_rc else gi % 2]
                trc = t_RC[0 if one_rc else gi % 2]
                o_ap, o_toks = out_fn(gidx)
                if den_add is not None:
                    S.op("dve", lambda e: e.tensor_scalar(
                        out=rc[64:128, 0:QW], in0=op_[64:128, 0:QW], scalar1=den_add, scalar2=None, op0=ALU.add),
                        reads=list(rd_extra), writes=[top, trc])

                    def fin(rc=rc, trc=trc, op_=op_, top=top, QW=QW, o_ap=o_ap, o_toks=o_toks):
                        S.op("act", lambda e: e.activation(out=rc[64:128, 0:QW], in_=rc[64:128, 0:QW], func=AF.Ln),
                             writes=[trc])
                        S.op("act", lambda e: e.activation(out=rc[64:128, 0:QW], in_=rc[64:128, 0:QW], func=AF.Exp,
                                                           scale=-1.0), writes=[trc])
                        S.op("dve", lambda e: e.tensor_tensor(out=o_ap, in0=op_[0:64, 0:QW], in1=rc[64:128, 0:QW],
                                                              op=ALU.mult),
                             reads=[trc], writes=[top] + list(o_toks))
                    pending.append((i + 2, fin))
                else:
                    S.op("dve", lambda e: e.reciprocal(out=rc[64:128, 0:QW], in_=op_[64:128, 0:QW]),
                         writes=[top, trc])
                    S.op("dve", lambda e: e.tensor_tensor(out=o_ap, in0=op_[0:64, 0:QW], in1=rc[64:128, 0:QW],
                                                          op=ALU.mult),
                         reads=[trc], writes=[top] + list(o_toks))
            while pending and pending[0][0] <= i:
                pending.pop(0)[1]()
        while pending:
            pending.pop(0)[1]()
        while bg:
            bg.pop(0)()

    def _unused():
        pass

    def dense_groups():
        gs = []
        for g in range(NB):
            kbs = [(j, 0, None) for j in range(4 * g)] + [(4 * g + r, r * 128, MASK[:, :]) for r in range(4)]
            gs.append((g * 512, 512, kbs, g))
        return gs

    def gate_phase(w_in_d, goff):
        for c in range(8):
            S.dma("pool", lambda e, c=c: e.dma_start(
                out=WB[:, 0:1024].rearrange("p (c n) -> p c n", c=8),
                in_=wview(w_in_d, goff + c * 128, goff + (c + 1) * 128)), writes=[t_WB])
            for b in range(NB):
                ps = PS[4 + b % 2]
                tps = t_PS[4 + b % 2]
                proj_fm(ps, tps, lambda k: WB[:, k * 128:(k + 1) * 128], 8,
                        lambda k, b=b: hT[:, k, b * 512:(b + 1) * 512], t_hT[4 * b:4 * b + 4], [t_WB], 128)
                gt = PT[b % NPT]
                S.op("act", lambda e, gt=gt, ps=ps: e.activation(out=gt[:], in_=ps[:], func=AF.Silu),
                     writes=[tps, t_PT[b % NPT]])
                S.op("dve", lambda e, gt=gt, c=c, b=b: e.tensor_tensor(
                    out=AOT[:, c, b * 512:(b + 1) * 512], in0=AOT[:, c, b * 512:(b + 1) * 512], in1=gt[:],
                    op=ALU.mult), reads=[t_PT[b % NPT]], writes=[t_AOT[c][b]])

    def out_phase(w_out_d, res_d, t_res, last_layer, next_gain_d=None, gate=None):
        S.barrier()
        WO = at("WO_%d" % int(last_layer), [128, 8 * 1024], BF16, o_big)
        t_WO = S.tok("WO")
        WG = at("WG_%d" % int(last_layer), [128, 8 * 1024], BF16, o_big + 16384)
        t_WG = S.tok("WG")
        gw_d, goff = gate
        for c in range(8):
            S.dma("pool", lambda e, c=c: e.dma_start(
                out=WG[:, c * 1024:(c + 1) * 1024].rearrange("p (k n) -> p k n", k=8),
                in_=wview(gw_d, goff + c * 128, goff + (c + 1) * 128)), writes=[t_WG])
        S.dma("pool", lambda e: e.dma_start(out=WO[:].rearrange("p (c n) -> p c n", c=8),
                                            in_=wview(w_out_d, 0, D)), writes=[t_WO])
        gcnt = {"n": 0}

        def gate_chunk(c, b):
            gcnt["n"] += 1
            pb = 4 + gcnt["n"] % 2
            proj_fm(PS[pb], t_PS[pb], lambda k: WG[:, c * 1024 + k * 128:c * 1024 + (k + 1) * 128], 8,
                    lambda k: hT[:, k, b * 512:(b + 1) * 512], t_hT[4 * b:4 * b + 4], [t_WG], 128)
            gi_ = gcnt["n"] % NPT
            gt = PT[gi_]
            S.op("act", lambda e: e.activation(out=gt[:], in_=PS[pb][:], func=AF.Silu),
                 writes=[t_PS[pb], t_PT[gi_]])
            S.op("dve", lambda e: e.tensor_tensor(
                out=AOT[:, c, b * 512:(b + 1) * 512], in0=AOT[:, c, b * 512:(b + 1) * 512], in1=gt[:],
                op=ALU.mult), reads=[t_PT[gi_]], writes=[t_AOT[c][b]])

        for c in range(8):
            gate_chunk(c, 0)
        if last_layer:
            load_gain(g_final)
        elif next_gain_d is not None:
            load_gain(next_gain_d)
        t_x1o = S.toks(3, "x1own")
        t_outo = S.toks(3, "outown")
        def stage1(tt):
            i = tt % 3
            b = tt // 4
            pp = 2 * (tt % 2)
            S.dma("sp", lambda e: e.dma_start(out=XT[i][:], in_=res_d[tt * 128:(tt + 1) * 128, :]),
                  reads=[t_res[tt]] if t_res is not None else [], writes=[t_XT[i]])
            for half in range(2):
                ps = PS[pp + half]
                tps = t_PS[pp + half]
                for c in range(8):
                    S.op("pe", lambda e: e.matmul(
                        ps[:, :], lhsT=AOT[:, c, tt * 128:(tt + 1) * 128],
                        rhs=WO[:, c * 1024 + half * 512: c * 1024 + (half + 1) * 512],
                        start=(c == 0), stop=(c == 7)),
                        reads=[t_AOT[c][b], t_WO], writes=[tps])

        def stage1b(tt):
            i = tt % 3
            pp = 2 * (tt % 2)
            for half in range(2):
                ps = PS[pp + half]
                tps = t_PS[pp + half]
                S.op("dve", lambda e: e.tensor_tensor(
                    out=XT[i][:, half * 512:(half + 1) * 512], in0=ps[:, :], in1=XT[i][:, half * 512:(half + 1) * 512],
                    op=ALU.add), writes=[tps, t_XT[i]])

        def stage2a(tt):
            i = tt % 3
            if last_layer:
                norm_tile(XT[i], t_XT[i], XT[i][:], [t_XT[i]], XB[tt % 2][:], t_XB[tt % 2], slot=tt % 2)
                S.dma("sp", lambda e: e.dma_start(out=out_d[tt * 128:(tt + 1) * 128, :], in_=XT[i][:]),
                      reads=[t_XT[i]], writes=[t_out[tt]], owner=t_outo[i])
            else:
                S.dma("sp", lambda e: e.dma_start(out=x1_d[tt * 128:(tt + 1) * 128, :], in_=XT[i][:]),
                      reads=[t_XT[i]], writes=[t_x1[tt]], owner=t_x1o[i])
                if mode == "full":
                    to_hT_norm(XT[i], t_XT[i], tt)

        for tt in range(NT + 2):
            if tt < NT:
                stage1(tt)
            if 1 <= tt <= NT:
                stage2a(tt - 1)
            if tt < NT:
                stage1b(tt)
            if tt < NT and tt // 4 + 1 < NB:
                for c in (2 * (tt % 4), 2 * (tt % 4) + 1):
                    gate_chunk(c, tt // 4 + 1)
            if tt >= 2 and (not last_layer) and mode == "full":
                to_hT_tr(tt - 2)
        S.barrier()

    def _layers():
        if do0:
            VA0 = at("VA0", [128, 32, 128], BF16, o_big)
            LATt = at("LATt", [128, 3, S_LEN], BF16, o_big + 8192)
            SWT = at("SWT", [128, 2048], F32, o_big + 8192)
            PFw = [at(f"pfw{i}", [128, 256], F32, o_big + 16384 + 1024 * i) for i in range(2)]
            posi = at("posi", [128, S_LEN], I32, o_aot)
            kf = at("kf", [128, S_LEN], F32, o_aot + 16384)
            ANG = at("ANG", [128, S_LEN], F32, o_aot + 32768)
            t_pos, t_kf, t_ang = S.tok("posi"), S.tok("kf"), S.tok("ang")
            load_gain(e_g_in)
            phase_A(x_d)

            gq = sb("gq", [128, 2], F32)
            gkv = sb("gkv", [128, 1], F32)
            esink = sb("esink", [128, 8], F32)
            ropec = sb("ropec", [128, 2], F32)
            t_sm = S.tok("small0")
            for c2 in range(2):
                S.dma("sp", lambda e, c2=c2: e.dma_start(
                    out=gq[:, c2:c2 + 1], in_=e_g_q_a[c2 * 128:(c2 + 1) * 128].rearrange("(p o) -> p o", o=1)),
                    writes=[t_sm])
            S.dma("sp", lambda e: e.dma_start(out=gkv[:], in_=e_g_kv_a.rearrange("(p o) -> p o", o=1)), writes=[t_sm])
            S.dma("sp", lambda e: e.dma_start(out=esink[:], in_=bass.AP(e_sinks.tensor, 0, [[0, 128], [1, 8]])),
                  writes=[t_sm])
            S.dma("sp", lambda e: e.dma_start(out=ropec[:], in_=cst["c_rope"][:, :]), writes=[t_sm])
            S.op("act", lambda e: e.activation(out=esink[:], in_=esink[:], func=AF.Exp), writes=[t_sm])

            S.dma("sp", lambda e: e.dma_start(out=posi[64:128, :], in_=bass.AP(pos_d.tensor, 0, [[0, 64], [1, S_LEN]])),
                  writes=[t_pos])
            P6 = slice(64, 128)
            S.op("dve", lambda e: e.tensor_copy(out=ANG[P6, :], in_=posi[P6, :]), reads=[t_pos], writes=[t_ang])
            S.op("dve", lambda e: e.tensor_scalar(out=ANG[P6, :], in0=ANG[P6, :], scalar1=ropec[P6, 0:1],
                                                  scalar2=ropec[P6, 1:2], op0=ALU.mult, op1=ALU.add),
                 reads=[t_sm], writes=[t_ang])
            S.op("dve", lambda e: e.tensor_scalar(out=kf[P6, :], in0=ANG[P6, :], scalar1=1.0 / (2 * math.pi),
                                                  scalar2=0.5, op0=ALU.mult, op1=ALU.add),
                 reads=[t_ang], writes=[t_kf])
            S.op("dve", lambda e: e.tensor_copy(out=posi[P6, :], in_=kf[P6, :]), reads=[t_kf], writes=[t_pos])
            S.op("dve", lambda e: e.tensor_copy(out=kf[P6, :], in_=posi[P6, :]), reads=[t_pos], writes=[t_kf])
            C1 = 6.28125
            C2 = 2 * math.pi - C1
            S.op("dve", lambda e: e.scalar_tensor_tensor(out=ANG[P6, :], in0=kf[P6, :], scalar=-C1, in1=ANG[P6, :],
                                                         op0=ALU.mult, op1=ALU.add), reads=[t_kf], writes=[t_ang])
            S.op("dve", lambda e: e.scalar_tensor_tensor(out=ANG[P6, :], in0=kf[P6, :], scalar=-C2, in1=ANG[P6, :],
                                                         op0=ALU.mult, op1=ALU.add), reads=[t_kf], writes=[t_ang])
            S.op("dve", lambda e: e.tensor_single_scalar(out=kf[P6, :], in_=ANG[P6, :], scalar=-math.pi, op=ALU.is_lt),
                 reads=[t_ang], writes=[t_kf])
            S.op("dve", lambda e: e.scalar_tensor_tensor(out=ANG[P6, :], in0=kf[P6, :], scalar=2 * math.pi,
                                                         in1=ANG[P6, :], op0=ALU.mult, op1=ALU.add),
                 reads=[t_kf], writes=[t_ang])
            S.op("dve", lambda e: e.tensor_scalar(out=ANG[P6, :], in0=ANG[P6, :], scalar1=-3.1415925, scalar2=3.1415925,
                                                  op0=ALU.max, op1=ALU.min), writes=[t_ang])
            S.op("act", lambda e: e.activation(out=TRIG[P6, :], in_=ANG[P6, :], func=AF.Sin), reads=[t_ang],
                 writes=[t_TRIG])
            S.barrier()
            checkpoint(1)

            S.dma("sp", lambda e: e.dma_start(out=SWT[:, :], in_=cst["c_swt"].rearrange("p h r q -> p (h r q)")),
                  writes=[t_kf])
            S.op("dve", lambda e: e.memset(VA0[:, :, 64:128], 1.0), writes=[t_VA])
            SWA_Q0, SWA_K0, SWA_V0 = 416, 928, 1056
            WBkv = WB[:, 0:1024].rearrange("p (c s n) -> p c s n", c=8, s=2)
            for h in range(8):
                kv = h // 4
                if h % 4 == 0:
                    S.dma("pool", lambda e, kv=kv: e.dma_start(
                        out=WBkv[:, :, 0, :], in_=wview(e_w_in, SWA_K0 + kv * 64, SWA_K0 + (kv + 1) * 64)), writes=[t_WB])
                    S.dma("pool", lambda e, kv=kv: e.dma_start(
                        out=WBkv[:, :, 1, :], in_=wview(e_w_in, SWA_V0 + kv * 64, SWA_V0 + (kv + 1) * 64)), writes=[t_WB])
                    for b in range(NB):
                        ps = PS[4 + b % 2]
                        tps = t_PS[4 + b % 2]
                        proj_fm(ps, tps, lambda k: WB[:, k * 128:k * 128 + 64], 8,
                                lambda k, b=b: hT[:, k, b * 512:(b + 1) * 512], t_hT[4 * b:4 * b + 4], [t_WB], 64)
                        S.op("act", copy_op("act", KT[0:64, b * 512:(b + 1) * 512], ps[0:64, :]), writes=[tps, t_KT])
                        S.op("dve", copy_op("dve", KT[64:128, b * 512:(b + 1) * 512], ps[0:64, :]),
                             writes=[tps, t_KT])
                    for t8 in range(4):
                        ps = PS[4 + t8 % 2]
                        tps = t_PS[4 + t8 % 2]
                        for ti in range(8):
                            tt = t8 * 8 + ti
                            for k in range(8):
                                S.op("pe", lambda e, k=k, tt=tt, ti=ti, ps=ps: e.matmul(
                                    ps[:, ti * 64:(ti + 1) * 64], lhsT=hT[:, k, tt * 128:(tt + 1) * 128],
                                    rhs=WB[:, k * 128 + 64:k * 128 + 128], start=(k == 0), stop=(k == 7)),
                                    reads=[t_hT[tt], t_WB], writes=[tps])
                        eng = evac_engine()
                        S.op(eng, copy_op(eng, VA0[:, t8 * 8:(t8 + 1) * 8, 0:64],
                                          ps[:, :].rearrange("p (t d) -> p t d", t=8)), writes=[tps, t_VA])
                if h % 2 == 0:
                    S.dma("pool", lambda e, h=h: e.dma_start(
                        out=WA[:, 0:1024].rearrange("p (c n) -> p c n", c=8),
                        in_=wview(e_w_in, SWA_Q0 + h * 64, SWA_Q0 + (h + 2) * 64)), writes=[t_WA])
                    for b in range(NB):
                        ps = PS[4 + b % 2]
                        tps = t_PS[4 + b % 2]
                        proj_fm(ps, tps, lambda k: WA[:, k * 128:(k + 1) * 128], 8,
                                lambda k, b=b: hT[:, k, b * 512:(b + 1) * 512], t_hT[4 * b:4 * b + 4], [t_WA], 128)
                        eng = evac_engine()
                        S.op(eng, copy_op(eng, QT[0:128, b * 512:(b + 1) * 512], ps[0:128, :], scale=0.125),
                             writes=[tps, t_QT[b]])
                groups = []
                for g4 in range(NB):
                    n0 = 4 * g4
                    kbs = []
                    for j in range(max(0, n0 - 1), n0 + 4):
                        qa, qb = max(j, n0), min(j + 1, n0 + 3)
                        ta = 0 if j >= n0 else 128
                        tb = 256 if j + 1 <= n0 + 3 else 128
                        kbs.append((j, (qa - n0) * 128, (qb - n0 + 1) * 128,
                                    SWT[:, h * 256 + ta:h * 256 + tb]))
                    groups.append((g4 * 512, 512, kbs, g4))
                c = 4 + h // 2
                po = (h % 2) * 64

                def out_fn(g, c=c, po=po):
                    return AOT[po:po + 64, c, g * 512:(g + 1) * 512], [t_AOT[c][g]]
                attention(64, groups, None, 1.0, lambda j: VA0[:, j, :], out_fn,
                          den_add=esink[64:128, h:h + 1], fp32_tables=True, rd_extra=[t_sm, t_kf],
                          pf_bufs=PFw, skip_gc=True, r0=64 * (h % 2))
            S.barrier()
            checkpoint(2)

            W416 = WA[:, 0:8 * 416].rearrange("p (c n) -> p c n", c=8)
            S.dma("pool", lambda e: e.dma_start(out=W416, in_=wview(e_w_in, 0, 416)), writes=[t_WA])
            WROT = WB[:, 0:8 * 96].rearrange("p (c n) -> p c n", c=8)
            S.op("dve", lambda e: e.memset(WROT[:, :, 0:64], 0.0), writes=[t_WB])
            S.op("dve", lambda e: e.tensor_scalar(out=WROT[:, :, 64:80], in0=W416[:, :, 400:416], scalar1=-1.0,
                                                  scalar2=None, op0=ALU.mult), reads=[t_WA], writes=[t_WB])
            S.op("dve", lambda e: e.tensor_copy(out=WROT[:, :, 80:96], in_=W416[:, :, 384:400]), reads=[t_WA],
                 writes=[t_WB])
            for b in range(NB):
                bs = slice(b * 512, (b + 1) * 512)
                hsrc = lambda k, b=b: hT[:, k, b * 512:(b + 1) * 512]
                ht = t_hT[4 * b:4 * b + 4]
                proj_fm(PS[0], t_PS[0], lambda k: W416[:, k, 0:128], 8, hsrc, ht, [t_WA], 128)
                proj_fm(PS[1], t_PS[1], lambda k: W416[:, k, 128:256], 8, hsrc, ht, [t_WA], 128)
                proj_fm(PS[2], t_PS[2], lambda k: W416[:, k, 256:384], 8, hsrc, ht, [t_WA], 128)
                proj_fm(PS[3], t_PS[3], lambda k: W416[:, k, 320:416], 8, hsrc, ht, [t_WA], 96)
                proj_fm(PS[4], t_PS[4], lambda k: WROT[:, k, :], 8, hsrc, ht, [t_WB], 96)
                for n in range(3):
                    S.op("act", lambda e, n=n: e.activation(out=ET[n][:], in_=PS[n][:], func=AF.Square),
                         writes=[t_PS[n], t_ET[n]])
                S.op("pe", lambda e: e.matmul(PS[5][:, :], lhsT=onesf[:], rhs=ET[0][:], start=True, stop=False),
                     reads=[t_ET[0], t_cst], writes=[t_PS[5]])
                S.op("pe", lambda e: e.matmul(PS[5][:, :], lhsT=onesf[:], rhs=ET[1][:], start=False, stop=True),
                     reads=[t_ET[1], t_cst], writes=[t_PS[5]])
                S.op("pe", lambda e: e.matmul(PS[6][:, :], lhsT=onesf[:], rhs=ET[2][:], start=True, stop=True),
                     reads=[t_ET[2], t_cst], writes=[t_PS[6]])
                for (pi, n_, n) in ((5, 256.0, 0), (6, 128.0, 1)):
                    S.op("act", lambda e, pi=pi, n_=n_, n=n: e.activation(out=ET[n][:], in_=PS[pi][:], func=AF.Ln,
                                                                    bias=epsb[:, 0:1], scale=1.0 / n_),
                         reads=[t_eps], writes=[t_PS[pi], t_ET[n]])
                    S.op("act", lambda e, n=n: e.activation(out=ET[n][:], in_=ET[n][:], func=AF.Exp, scale=-0.5),
                         writes=[t_ET[n]])
                for (pi, chunk, gsc, n) in ((0, 0, gq[:, 0:1], 0), (1, 1, gq[:, 1:2], 0), (2, 2, gkv[:, 0:1], 1)):
                    S.op("dve", lambda e, pi=pi, chunk=chunk, gsc=gsc, n=n, bs=bs: e.scalar_tensor_tensor(
                        out=LATt[:, chunk, bs], in0=PS[pi][:], scalar=gsc, in1=ET[n][:], op0=ALU.mult, op1=ALU.mult),
                        reads=[t_ET[n], t_sm], writes=[t_PS[pi], t_LAT[b]])
                S.op("dve", lambda e, bs=bs: e.tensor_tensor(out=RC[0][64:96, :], in0=PS[3][64:96, :], in1=TRIG[64:96, bs],
                                                         op=ALU.mult), reads=[t_TRIG], writes=[t_PS[3], t_RC[0]])
                S.op("dve", lambda e, bs=bs: e.tensor_tensor(out=RC[1][64:96, :], in0=PS[4][64:96, :], in1=TRIG[96:128, bs],
                                                         op=ALU.mult), reads=[t_TRIG], writes=[t_PS[4], t_RC[1]])
                S.op("dve", lambda e, bs=bs: e.tensor_tensor(out=KT[64:96, bs], in0=RC[0][64:96, :], in1=RC[1][64:96, :],
                                                         op=ALU.add), reads=[t_RC[0], t_RC[1]], writes=[t_KTaug])
            S.barrier()
            checkpoint(3)

            WQ = WA[:, 0:2 * 768].rearrange("p (c n) -> p c n", c=2)
            S.dma("pool", lambda e: e.dma_start(out=WQ, in_=wview(e_w_q_up, 0, 768)), writes=[t_WA])
            WQR = WA[:, 1536:1536 + 2 * 768].rearrange("p (c n) -> p c n", c=2)
            WKV = WB[:, 0:1024]
            S.dma("pool", lambda e: e.dma_start(out=WKV, in_=e_w_kv_up[:, :]), writes=[t_WB])
            S.op("dve", lambda e: e.memset(WA[:, 1536:1536 + 2 * 768], 0.0), writes=[t_WA])
            for c2 in range(2):
                src = WQ[:, c2, :].rearrange("p (h d) -> p h d", h=8)
                dst = WQR[:, c2, :].rearrange("p (h d) -> p h d", h=8)
                S.op("dve", lambda e, src=src, dst=dst: e.tensor_scalar(out=dst[:, :, 64:80], in0=src[:, :, 80:96],
                                                                  scalar1=-1.0, scalar2=None, op0=ALU.mult),
                     writes=[t_WA])
                S.op("dve", lambda e, src=src, dst=dst: e.tensor_copy(out=dst[:, :, 80:96], in_=src[:, :, 64:80]),
                     writes=[t_WA])
            mla_scale = 96.0 ** -0.5
            for h in range(8):
                for b in range(NB):
                    ps = PS[4 + b % 2]
                    tps = t_PS[4 + b % 2]
                    S.op("pe", lambda e, b=b, h=h, ps=ps: e.matmul(ps[0:64, :], lhsT=WKV[:, h * 128:h * 128 + 64],
                                                             rhs=LATt[:, 2, b * 512:(b + 1) * 512], start=True, stop=True),
                         reads=[t_LAT[b], t_WB], writes=[tps])
                    eng = evac_engine()
                    S.op(eng, copy_op(eng, KT[0:64, b * 512:(b + 1) * 512], ps[0:64, :]), writes=[tps, t_KT])
                for t8 in range(4):
                    ps = PS[4 + t8 % 2]
                    tps = t_PS[4 + t8 % 2]
                    for ti in range(8):
                        tt = t8 * 8 + ti
                        S.op("pe", lambda e, tt=tt, ti=ti, h=h, ps=ps: e.matmul(
                            ps[:, ti * 64:(ti + 1) * 64], lhsT=LATt[:, 2, tt * 128:(tt + 1) * 128],
                            rhs=WKV[:, h * 128 + 64:h * 128 + 128], start=True, stop=True),
                            reads=[t_LAT[tt // 4], t_WB], writes=[tps])
                    eng = evac_engine()
                    S.op(eng, copy_op(eng, VA0[:, t8 * 8:(t8 + 1) * 8, 0:64],
                                      ps[:, :].rearrange("p (t d) -> p t d", t=8)), writes=[tps, t_VA])
                for b in range(NB):
                    bs = slice(b * 512, (b + 1) * 512)
                    p1, tp1 = PS[4], t_PS[4]
                    p2, tp2 = PS[5], t_PS[5]
                    for k in range(2):
                        S.op("pe", lambda e, k=k, h=h, bs=bs: e.matmul(p1[0:96, :], lhsT=WQ[:, k, h * 96:(h + 1) * 96],
                                                                rhs=LATt[:, k, bs], start=(k == 0), stop=(k == 1)),
                             reads=[t_LAT[b], t_WA], writes=[tp1])
                    for k in range(2):
                        S.op("pe", lambda e, k=k, h=h, bs=bs: e.matmul(p2[0:96, :], lhsT=WQR[:, k, h * 96:(h + 1) * 96],
                                                                rhs=LATt[:, k, bs], start=(k == 0), stop=(k == 1)),
                             reads=[t_LAT[b], t_WA], writes=[tp2])
                    S.op("act", lambda e, bs=bs: e.copy(out=QT[0:64, bs], in_=p1[0:64, :]),
                         writes=[tp1, t_QT[b]])
                    S.op("dve", lambda e, bs=bs: e.tensor_tensor(out=RC[0][64:96, :], in0=p1[64:96, :], in1=TRIG[64:96, bs],
                                                             op=ALU.mult), reads=[t_TRIG], writes=[tp1, t_RC[0]])
                    S.op("dve", lambda e, bs=bs: e.tensor_tensor(out=RC[1][64:96, :], in0=p2[64:96, :], in1=TRIG[96:128, bs],
                                                             op=ALU.mult), reads=[t_TRIG], writes=[tp2, t_RC[1]])
                    S.op("dve", lambda e, bs=bs: e.tensor_tensor(out=QT[64:96, bs], in0=RC[0][64:96, :], in1=RC[1][64:96, :],
                                                             op=ALU.add), reads=[t_RC[0], t_RC[1]], writes=[t_QT[b]])
                c = h // 2
                po = (h % 2) * 64

                def out_fn(g, c=c, po=po):
                    return AOT[po:po + 64, c, g * 512:(g + 1) * 512], [t_AOT[c][g]]
                attention(96, dense_groups(), None, mla_scale, lambda j: VA0[:, j, :], out_fn, sbanks=(0, 1, 6, 4, 5))

            checkpoint(4)
            checkpoint(5)
            out_phase(e_w_out, x_d, None, last_layer=False, next_gain_d=(o_g_in if do1 else None),
                      gate=(e_w_in, 1184))
            checkpoint(6)

        if do1:
            VA4 = at("VA4", [128, 32, 4, 128], BF16, o_big)
            if not do0:
                load_gain(o_g_in)
                phase_A(x1_d)
                S.barrier()
            Q0, K0, V0, F0, G0 = 0, 1024, 2048, 3072, 3088
            WF = WB[:, 0:128].rearrange("p (c n) -> p c n", c=8)
            S.dma("pool", lambda e: e.dma_start(out=WF, in_=wview(o_w_in, F0, F0 + 16)), writes=[t_WB])
            NL = at("NL", [128, 32, 16], F32, o_wa)
            trif = at("trif", [128, 128], F32, o_wa + 2048)
            identf = at("identf", [128, 128], F32, o_wa + 2560)
            CTf = at("CTf", [16, S_LEN], F32, o_big)
            r1 = at("r1", [16, S_LEN], F32, o_big + 16384)
            CS = at("CS", [128, S_LEN], BF16, o_trig)
            bfb = sb("bfb", [128, 16], F32)
            Ct = sb("Ct", [128, 32, 16], F32)
            Rs = sb("Rs", [128, 16], F32)
            zt = sb("zt", [128, 16], F32)
            t_bfb, t_NL, t_Ct, t_Rs, t_zt = S.tok("bfb"), S.toks(NT, "NL"), S.toks(NT, "Ct"), S.tok("Rs"), S.tok("zt")
            t_c1 = S.tok("cst1")
            S.dma("sp", lambda e: e.dma_start(out=trif[:], in_=cst["c_tri"][:, :]), writes=[t_c1])
            S.dma("sp", lambda e: e.dma_start(out=identf[:], in_=cst["c_ident"][:, :]), writes=[t_c1])
            S.dma("sp", lambda e: e.dma_start(out=bfb[:], in_=bass.AP(o_b_f.tensor, 0, [[0, 128], [1, 16]])),
                  writes=[t_bfb])
            Tt = at("Tt", [128, 32, 16], F32, o_wa + 3072)
            Pfx = at("Pfx", [128, 32, 16], F32, o_wa + 5120)
            t_NLa, t_Tt, t_Pfx = S.tok("NLa"), S.tok("Tt"), S.tok("Pfx")
            pf_, tpf_ = PS[4], t_PS[4]
            for tt in range(NT):
                for k in range(8):
                    S.op("pe", lambda e, k=k, tt=tt: e.matmul(pf_[:, tt * 16:(tt + 1) * 16],
                                                          lhsT=hT[:, k, tt * 128:(tt + 1) * 128],
                                                          rhs=WF[:, k, :], start=(k == 0), stop=(k == 7)),
                         reads=[t_hT[tt], t_WB], writes=[tpf_])
            S.op("dve", lambda e: e.tensor_tensor(out=NL[:, :, :], in0=pf_[:, :].rearrange("p (t h) -> p t h", t=32),
                                                  in1=bass.AP(bfb, 0, [[16, 128], [0, 32], [1, 16]]), op=ALU.add),
                 reads=[t_bfb], writes=[tpf_, t_NLa])
            S.op("act", lambda e: e.activation(out=NL[:, :, :], in_=NL[:, :, :], func=AF.Exp, scale=-1.0),
                 writes=[t_NLa])
            S.op("act", lambda e: e.activation(out=NL[:, :, :], in_=NL[:, :, :], func=AF.Ln, bias=epsb[:, 1:2],
                                               scale=1.0), reads=[t_eps], writes=[t_NLa])
            pT_, tpT_ = PS[5], t_PS[5]
            pC_, tpC_ = PS[6], t_PS[6]
            for tt in range(NT):
                S.op("pe", lambda e, tt=tt: e.matmul(pT_[:, tt * 16:(tt + 1) * 16], lhsT=onesf[:], rhs=NL[:, tt, :],
                                                     start=True, stop=True),
                     reads=[t_NLa, t_cst], writes=[tpT_])
            for tt in range(NT):
                S.op("pe", lambda e, tt=tt: e.matmul(pC_[:, tt * 16:(tt + 1) * 16], lhsT=trif[:], rhs=NL[:, tt, :],
                                                     start=True, stop=True),
                     reads=[t_NLa, t_c1], writes=[tpC_])
            S.op("dve", lambda e: e.tensor_copy(out=Tt[:, :, :], in_=pT_[:, :].rearrange("p (t h) -> p t h", t=32)),
                 writes=[tpT_, t_Tt])
            S.op("dve", lambda e: e.memset(Pfx[:, 0, :], 0.0), writes=[t_Pfx])
            for tt in range(1, NT):
                S.op("dve", lambda e, tt=tt: e.tensor_tensor(out=Pfx[:, tt, :], in0=Pfx[:, tt - 1, :],
                                                           in1=Tt[:, tt - 1, :], op=ALU.add),
                     reads=[t_Tt], writes=[t_Pfx])
            S.op("dve", lambda e: e.tensor_tensor(out=Ct[:, :, :], in0=pC_[:, :].rearrange("p (t h) -> p t h", t=32),
                                                  in1=Pfx[:, :, :], op=ALU.add),
                 reads=[t_Pfx], writes=[tpC_] + list(t_Ct))
            t_CS, t_r1, t_ctf = S.tok("CS"), S.tok("r1"), S.tok("ctf")
            for g4 in range(8):
                ps = PS[6]
                tps = t_PS[6]
                for ti in range(4):
                    tt = g4 * 4 + ti
                    S.op("pe", lambda e, tt=tt, ti=ti: e.matmul(ps[0:16, ti * 128:(ti + 1) * 128], lhsT=Ct[:, tt, :],
                                                            rhs=identf[:], start=True, stop=True),
                         reads=[t_Ct[tt], t_c1], writes=[tps])
                S.op("dve", lambda e, g4=g4: e.tensor_scalar(out=CTf[:, g4 * 512:(g4 + 1) * 512], in0=ps[0:16, :],
                                                          scalar1=-1.0, scalar2=None, op0=ALU.mult),
                     writes=[tps, t_ctf])
            tmpb = at("tmpb", [16, S_LEN], BF16, o_aot)
            t_tmpb = S.tok("tmpb")
            S.op("dve", lambda e: e.tensor_copy(out=CS[0:16, :], in_=CTf[:, :]), reads=[t_ctf], writes=[t_CS])
            S.op("dve", lambda e: e.tensor_tensor(out=r1[:, :], in0=CTf[:, :], in1=CS[0:16, :], op=ALU.subtract),
                 reads=[t_ctf, t_CS], writes=[t_r1])
            S.op("dve", lambda e: e.tensor_copy(out=tmpb[:, :], in_=r1[:, :]), reads=[t_r1], writes=[t_tmpb])
            S.op("dve", lambda e: e.tensor_copy(out=CS[32:48, :], in_=tmpb[:, :]), reads=[t_tmpb], writes=[t_CS])
            S.op("dve", lambda e: e.tensor_tensor(out=r1[:, :], in0=r1[:, :], in1=tmpb[:, :], op=ALU.subtract),
                 reads=[t_tmpb], writes=[t_r1])
            S.op("dve", lambda e: e.tensor_copy(out=CS[64:80, :], in_=r1[:, :]), reads=[t_r1], writes=[t_CS])
            S.barrier()
            checkpoint(7)
            if 'g' in DBG:
                S.op("pe", lambda e: e.matmul(PS[6][0:64, 0:16], lhsT=hT[:, 0, 0:64], rhs=hT[:, 0, 0:16],
                                              start=True, stop=True), reads=[t_hT[0]], writes=[t_PS[6]])
            if 'a' not in DBG:
                S.op("dve", lambda e: e.memset(KT[64:67, :], 1.0), writes=[t_KTaug])
            WBqk = WB[:, 0:1024].rearrange("p (c s n) -> p c s n", c=8, s=2)
            checkpoint(71)

            KT2 = at("KT2", [128, S_LEN], BF16, o_wa)
            KTb = [KT, KT2]
            t_KTb = [[S.tok("ktb0"), S.tok("ktb0aug")], [S.tok("ktb1"), S.tok("ktb1aug")]]
            S.op("dve", lambda e: e.memset(KT2[64:67, :], 1.0), writes=[t_KTb[1][1]])
            t_KTb[0][1] = t_KTaug
            WVb = at("WVb", [128, 1024], BF16, o_rc + 2048)
            t_WQb, t_WKb, t_WVb = S.tok("wqb"), S.tok("wkb"), S.tok("wvb")
            t_VAp = S.toks(2, "vap")
            S.op("dve", lambda e: e.memset(VA4[:, :, 0:2, 64:128], 1.0), writes=[t_VAp[0], t_VA])
            S.op("dve", lambda e: e.memset(VA4[:, :, 2:4, 64:128], 1.0), writes=[t_VAp[1], t_VA])
            rot = {"n": 0}

            def bank():
                rot["n"] += 1
                return 4 + rot["n"] % 2

            def load_wk(h):
                S.dma("pool", lambda e: e.dma_start(out=WBqk[:, :, 1, :],
                                                    in_=wview(o_w_in, K0 + h * 64, K0 + (h + 1) * 64)),
                      writes=[t_WKb])

            def load_wq(h):
                S.dma("pool", lambda e: e.dma_start(out=WBqk[:, :, 0, :],
                                                    in_=wview(o_w_in, Q0 + h * 64, Q0 + (h + 1) * 64)),
                      writes=[t_WQb])

            def load_wv(p):
                S.dma("pool", lambda e: e.dma_start(out=WVb[:, :].rearrange("p (c n) -> p c n", c=8),
                                                    in_=wview(o_w_in, V0 + p * 128, V0 + (p + 1) * 128)),
                      writes=[t_WVb])

            def k_block(h, b):
                pb = bank()
                proj_fm(PS[pb], t_PS[pb], lambda k: WB[:, k * 128 + 64:(k + 1) * 128], 8,
                        lambda k: hT[:, k, b * 512:(b + 1) * 512], t_hT[4 * b:4 * b + 4], [t_WKb], 64)
                S.op("dve", copy_op("dve", KTb[h % 2][0:64, b * 512:(b + 1) * 512], PS[pb][0:64, :]),
                     writes=[t_PS[pb], t_KTb[h % 2][0]])

            def q_block(h, b):
                pb = bank()
                both = h + 1 < 16
                M = 128 if both else 64
                proj_fm(PS[pb], t_PS[pb], lambda k: WB[:, k * 128:k * 128 + M], 8,
                        lambda k: hT[:, k, b * 512:(b + 1) * 512], t_hT[4 * b:4 * b + 4],
                        [t_WQb, t_WKb] if both else [t_WQb], M)
                S.op("dve", copy_op("dve", QT[0:64, b * 512:(b + 1) * 512], PS[pb][0:64, :], scale=0.125),
                     writes=[t_PS[pb], t_QT[b]])
                if both:
                    S.op("dve", copy_op("dve", KTb[(h + 1) % 2][0:64, b * 512:(b + 1) * 512], PS[pb][64:128, :]),
                         writes=[t_PS[pb], t_KTb[(h + 1) % 2][0]])

            def v_tiles(p, t2):
                pb = bank()
                ps = PS[pb]
                s0 = 2 * (p % 2)
                for ti in range(2):
                    tt = t2 * 2 + ti
                    for k in range(8):
                        S.op("pe", lambda e, k=k, tt=tt, ti=ti: e.matmul(
                            ps[:, ti * 128:(ti + 1) * 128], lhsT=hT[:, k, tt * 128:(tt + 1) * 128],
                            rhs=WVb[:, k * 128:(k + 1) * 128], start=(k == 0), stop=(k == 7)),
                            reads=[t_hT[tt], t_WVb], writes=[t_PS[pb]])
                for ti in range(2):
                    tt = t2 * 2 + ti
                    S.op("dve", copy_op("dve", VA4[:, tt, s0:s0 + 2, 0:64],
                                        ps[:, ti * 128:(ti + 1) * 128].rearrange("p (h d) -> p h d", h=2)),
                         writes=[t_PS[pb], t_VAp[p % 2]])

            load_wv(0)
            for t2 in range(16):
                v_tiles(0, t2)
            load_wk(0)
            for b in range(NB):
                k_block(0, b)
            for h in range(16):
                hh = h % 4
                load_wq(h)
                for r_ in range(3):
                    S.dma("sp", lambda e, h=h, r_=r_: e.dma_start(out=QT[64 + r_:65 + r_, :],
                                                                 in_=CS[32 * r_ + h:32 * r_ + h + 1, :]),
                          reads=[t_CS], writes=[t_QTaug])
                bgl = []
                if h + 1 < 16:
                    load_wk(h + 1)
                if h % 2 == 1 and h + 1 < 16:
                    load_wv((h + 1) // 2)
                    bgl += [(lambda h=h, t2=t2: v_tiles((h + 1) // 2, t2)) for t2 in range(16)]
                c = h // 2
                po = (h % 2) * 64

                def out_fn(g, c=c, po=po):
                    return AOT[po:po + 64, c, g * 512:(g + 1) * 512], [t_AOT[c][g]]
                attention(67, dense_groups(), lambda j, h=h: Ct[:, j, h:h + 1], 1.0,
                          lambda j, hh=hh: VA4[:, j, hh, :], out_fn, rd_extra=t_Ct,
                          KTt=(KTb[h % 2], t_KTb[h % 2]), pre_group=(lambda g, h=h: q_block(h, g)),
                          bg=bgl, va_tok=t_VAp[(h // 2) % 2], one_rc=True)

            checkpoint(8)
            S.barrier()
            checkpoint(9)
            out_phase(o_w_out, x1_d, (t_x1 if mode == "full" else None), last_layer=True, gate=(o_w_in, G0))
            S.wait_all("sp", t_out)
        else:
            S.wait_all("sp", t_x1)


    def checkpoint(k):
        if stop == k:
            raise _Stop()

    try:
        _layers()
    except _Stop:
        pass
    S.emit()
    return nc


_CACHE = {}


def _get(mode):
    if mode not in _CACHE:
        _CACHE[mode] = build_program(mode)
    return _CACHE[mode]


L0_KEYS = ["e_g_in", "e_w_in", "e_g_q_a", "e_w_q_up", "e_g_kv_a", "e_w_kv_up", "e_sinks", "e_w_out"]
L1_KEYS = ["o_g_in", "o_w_in", "o_b_f", "o_w_out"]


def _maps(inputs, n, mode, x1=None):
    consts = _constants()
    maps = []
    for b in range(n):
        m = dict(consts)
        if mode in ("full", "l0"):
            m["x"] = np.ascontiguousarray(inputs["x"][b])
            m["positions"] = np.ascontiguousarray(inputs["positions"][b]).astype(np.int32)
            for k in L0_KEYS:
                m[k] = np.ascontiguousarray(inputs[k][0])
        if mode in ("full", "l1"):
            for k in L1_KEYS:
                m[k] = np.ascontiguousarray(inputs[k][0])
            m["g_final"] = np.ascontiguousarray(inputs["g_final"])
        if mode == "l1":
            m["x1"] = np.ascontiguousarray(x1[b])
        maps.append(m)
    return maps


def kernel(**inputs):
    n = inputs["x"].shape[0]
    inputs = {k: np.asarray(v) for k, v in inputs.items()}
    nc = _get("full")
    res = run_bass_kernel_spmd(nc, _maps(inputs, n, "full"), core_ids=list(range(n)))
    return np.stack([np.asarray(r["out"]) for r in res.results], axis=0).astype(np.float32)
```
